# Optimizing a Trainium2 kernel written in Bass

```python
import jax
import jax.numpy as jnp
from jax import lax
import numpy as np

D_MODEL = 1024
BATCH = 16
SEQ = 256
DEPTH = 4
DEC_BATCH = 2
DEC_SEQ = 1024
PAST_LEN = 256

GRID_W = 64
EPS = 1e-6
QBLOCK = 128
ROPE_BASE = 10000.0
H_A = 8
NOPE_DIM = 64
ROPE_DIM = 32
V_DIM = 64
QK_DIM = NOPE_DIM + ROPE_DIM
Q_RANK = 256
KV_RANK = 128
W_A = H_A * V_DIM
H_B = 4
DH_B = 128
W_B = H_B * DH_B
CHUNK = 64
W_C = 512
CONV_W = 3
H_D = 8
DH_D = 64
W_D = H_D * DH_D
WIN_R = 8
WIN_C = 16
EVEN_SPLITS = (Q_RANK, KV_RANK, ROPE_DIM, W_A, W_B, W_B, W_B, 2 * H_B, 2 * H_B, W_B, W_B)
EVEN_IN = Q_RANK + KV_RANK + ROPE_DIM + W_A + 5 * W_B + 4 * H_B
ODD_SPLITS = (W_C, W_C, W_C, W_C, W_D, W_D, W_D, W_D)
ODD_IN = 4 * W_C + 4 * W_D

kernel_name = 'hybrid_mla_mlstm_conv_natten_diffusion_step'


def rms_norm(x, g):
    xf = x.astype(jnp.float32)
    y = xf * lax.rsqrt(jnp.mean(xf * xf, axis=-1, keepdims=True) + EPS)
    return (y * g.astype(jnp.float32)).astype(x.dtype)


def split_cols(u, sizes):
    out, start = [], 0
    for sz in sizes:
        out.append(u[..., start:start + sz])
        start += sz
    return out


def modulation(cond, w, b):
    mod = jax.nn.silu(cond) @ w + b
    shift, scale, gate = jnp.split(mod[:, None, :], 3, axis=-1)
    return shift, scale, gate


def _rotate_half(x, ang):
    cos = jnp.cos(ang)[None, :, None, :].astype(x.dtype)
    sin = jnp.sin(ang)[None, :, None, :].astype(x.dtype)
    x1, x2 = jnp.split(x, 2, axis=-1)
    return jnp.concatenate([x1 * cos - x2 * sin, x1 * sin + x2 * cos], axis=-1)


def axial_rope(x):
    n_tok = x.shape[1]
    t = jnp.arange(n_tok)
    row = (t // GRID_W).astype(jnp.float32)
    col = (t % GRID_W).astype(jnp.float32)
    nf = ROPE_DIM // 4
    inv = ROPE_BASE ** (-jnp.arange(nf, dtype=jnp.float32) / nf)
    xr, xc = jnp.split(x, 2, axis=-1)
    return jnp.concatenate([_rotate_half(xr, row[:, None] * inv), _rotate_half(xc, col[:, None] * inv)], axis=-1)


def rope_tail(x):
    return jnp.concatenate([x[..., :NOPE_DIM], axial_rope(x[..., NOPE_DIM:])], axis=-1)


def block_attention(q, k, v):
    b, sq, h, d = q.shape
    nb = sq // QBLOCK
    scale = d ** -0.5
    qb = jnp.moveaxis(q.reshape(b, nb, QBLOCK, h, d), 1, 0)

    def one_block(qblk):
        s = jnp.einsum('bqhd,bkhd->bhqk', qblk, k, preferred_element_type=jnp.float32) * scale
        p = jax.nn.softmax(s, axis=-1).astype(v.dtype)
        return jnp.einsum('bhqk,bkhd->bqhd', p, v)

    out = lax.map(one_block, qb)
    return jnp.moveaxis(out, 0, 1).reshape(b, sq, h, v.shape[-1])


def mla_keys_values(ckv, kpe, w_kv_b, k_norm):
    b, s, _ = ckv.shape
    kv = (ckv @ w_kv_b).reshape(b, s, H_A, NOPE_DIM + V_DIM)
    k_nope, v = kv[..., :NOPE_DIM], kv[..., NOPE_DIM:]
    k_pe = jnp.broadcast_to(kpe[:, :, None, :], (b, s, H_A, ROPE_DIM))
    k = rms_norm(jnp.concatenate([k_nope, k_pe], axis=-1), k_norm)
    return k, v


def mlstm_chunkwise(q, k, v, i_pre, f_pre, C0, n0, m0):
    b, s, h, d = q.shape
    nc = s // CHUNK

    def chunks(a):
        return jnp.moveaxis(a.reshape((b, nc, CHUNK) + a.shape[2:]), (1, 3), (0, 2))

    xs = (chunks(q), chunks(k * (d ** -0.5)), chunks(v), chunks(i_pre), chunks(jax.nn.log_sigmoid(f_pre)))
    tri = jnp.tril(jnp.ones((CHUNK, CHUNK), dtype=bool))

    def step(carry, inp):
        C, n, m = carry
        qc, kc, vc, ic, lf = inp
        cum = jnp.cumsum(lf, axis=-1)
        inter = cum + m[..., None]
        dmat = jnp.where(tri, cum[..., :, None] - cum[..., None, :] + ic[..., None, :], -jnp.inf)
        m_t = jnp.maximum(inter, jnp.max(dmat, axis=-1))
        w_inter = jnp.exp(inter - m_t)
        sc = jnp.einsum('bhtd,bhsd->bhts', qc, kc) * jnp.exp(dmat - m_t[..., None])
        num = w_inter[..., None] * jnp.einsum('bhtd,bhde->bhte', qc, C) + jnp.einsum('bhts,bhse->bhte', sc, vc)
        qn = w_inter * jnp.einsum('bhtd,bhd->bht', qc, n) + jnp.sum(sc, axis=-1)
        h_out = num / jnp.maximum(jnp.abs(qn), jnp.exp(-m_t))[..., None]
        cum_last = cum[..., -1]
        dec = cum_last[..., None] - cum + ic
        m_new = jnp.maximum(cum_last + m, jnp.max(dec, axis=-1))
        w_state = jnp.exp(cum_last + m - m_new)
        wk = jnp.exp(dec - m_new[..., None])
        C_new = w_state[..., None, None] * C + jnp.einsum('bhs,bhsd,bhse->bhde', wk, kc, vc)
        n_new = w_state[..., None] * n + jnp.einsum('bhs,bhsd->bhd', wk, kc)
        return (C_new, n_new, m_new), h_out

    final, hs = lax.scan(step, (C0, n0, m0), xs)
    h_seq = jnp.moveaxis(hs, (0, 2), (1, 3)).reshape(b, s, h, d)
    return final, h_seq


def mlstm_bidirectional(q, k, v, i_pre, f_pre, C0, n0, m0):
    dt = q.dtype
    q, k, v, i_pre, f_pre = (a.astype(jnp.float32) for a in (q, k, v, i_pre, f_pre))
    C0, n0, m0 = (a.astype(jnp.float32) for a in (C0, n0, m0))
    (Cf, nf, mf), h_f = mlstm_chunkwise(q, k, v, i_pre[:, :, 0], f_pre[:, :, 0], C0[:, 0], n0[:, 0], m0[:, 0])

    def flip(a):
        return jnp.flip(a, axis=1)

    (Cb, nb, mb), h_b = mlstm_chunkwise(flip(q), flip(k), flip(v), flip(i_pre[:, :, 1]), flip(f_pre[:, :, 1]),
                                        C0[:, 1], n0[:, 1], m0[:, 1])
    h = (h_f + flip(h_b)).astype(dt)
    return h, jnp.stack([Cf, Cb], axis=1), jnp.stack([nf, nb], axis=1), jnp.stack([mf, mb], axis=1)


def short_conv(u, w, bias):
    up = jnp.pad(u, ((0, 0), (1, 1), (0, 0)))
    return up[:, :-2] * w[0] + up[:, 1:-1] * w[1] + up[:, 2:] * w[2] + bias


def neighbourhood_attention(q, k, v, k_ctx, v_ctx, rpb):
    b, n, h, d = q.shape
    rows = n // GRID_W
    kr = min(WIN_R, rows)
    scale = d ** -0.5
    r = jnp.arange(rows)
    band = jnp.clip(r - kr // 2, 0, rows - kr)[:, None] + jnp.arange(kr)[None, :]
    col = jnp.arange(GRID_W)
    c0 = jnp.clip(col - WIN_C // 2, 0, GRID_W - WIN_C)
    col_ok = (col[None, :] >= c0[:, None]) & (col[None, :] < c0[:, None] + WIN_C)
    dr = band - r[:, None] + (WIN_R - 1)
    dc = jnp.clip(col[None, :] - col[:, None], -(WIN_C - 1), WIN_C - 1) + (WIN_C - 1)
    bias = rpb[:, dr[:, None, :, None], dc[None, :, None, :]].astype(jnp.float32)
    qg = q.reshape(b, rows, GRID_W, h, d)
    kg = k.reshape(b, rows, GRID_W, h, d)[:, band]
    vg = v.reshape(b, rows, GRID_W, h, d)[:, band].reshape(b, rows, kr * GRID_W, h, d)
    s_win = jnp.einsum('brqhd,brkwhd->bhrqkw', qg, kg, preferred_element_type=jnp.float32) * scale + bias[None]
    s_win = jnp.where(col_ok[None, None, None, :, None, :], s_win, -jnp.inf).reshape(b, h, rows, GRID_W, kr * GRID_W)
    s_ctx = jnp.einsum('brqhd,bkhd->bhrqk', qg, k_ctx, preferred_element_type=jnp.float32) * scale
    p = jax.nn.softmax(jnp.concatenate([s_win, s_ctx], axis=-1), axis=-1).astype(v.dtype)
    p_win, p_ctx = p[..., :kr * GRID_W], p[..., kr * GRID_W:]
    out = jnp.einsum('bhrqk,brkhd->brqhd', p_win, vg) + jnp.einsum('bhrqk,bkhd->brqhd', p_ctx, v_ctx)
    return out.reshape(b, n, h, d)


def even_mixer(h, lp, ctx=None):
    b, s, _ = h.shape
    qa, kva, kpe, g_a, q_m, k_m, v_m, i_m, f_m, o_m, g_m = split_cols(h @ lp['w_in'], EVEN_SPLITS)
    q = rms_norm((rms_norm(qa, lp['q_a_norm']) @ lp['w_q_b']).reshape(b, s, H_A, QK_DIM), lp['q_norm'])
    ckv = rms_norm(kva, lp['kv_a_norm'])
    k, v = mla_keys_values(ckv, kpe, lp['w_kv_b'], lp['k_norm'])
    i_pre = i_m.reshape(b, s, 2, H_B) + lp['b_i']
    f_pre = f_m.reshape(b, s, 2, H_B) + lp['b_f']
    if ctx is None:
        C0 = jnp.zeros((b, 2, H_B, DH_B, DH_B), jnp.float32)
        n0 = jnp.zeros((b, 2, H_B, DH_B), jnp.float32)
        m0 = jnp.zeros((b, 2, H_B), jnp.float32)
    else:
        ckv_c, kpe_c, C0, n0, m0 = ctx
        q, k = rope_tail(q), rope_tail(k)
        k_c, v_c = mla_keys_values(ckv_c, kpe_c, lp['w_kv_b'], lp['k_norm'])
        k = jnp.concatenate([k_c, k], axis=1)
        v = jnp.concatenate([v_c, v], axis=1)
    att = block_attention(q, k, v).reshape(b, s, W_A) * jax.nn.silu(g_a)
    hm, C_fin, n_fin, m_fin = mlstm_bidirectional(
        q_m.reshape(b, s, H_B, DH_B), k_m.reshape(b, s, H_B, DH_B), v_m.reshape(b, s, H_B, DH_B),
        i_pre, f_pre, C0, n0, m0)
    hm = rms_norm(hm, lp['h_norm'].reshape(H_B, DH_B)).reshape(b, s, W_B)
    hm = hm * jax.nn.sigmoid(o_m) * jax.nn.silu(g_m)
    out = jnp.concatenate([att, hm], axis=-1) @ lp['w_out']
    return out, (ckv, kpe, C_fin.astype(h.dtype), n_fin.astype(h.dtype), m_fin.astype(h.dtype))


def odd_mixer(h, lp, ctx=None):
    b, s, _ = h.shape
    xc, bc, cc, g_c, q_d, k_d, v_d, g_d = split_cols(h @ lp['w_in'], ODD_SPLITS)
    conv_out = bc * short_conv(cc * xc, lp['conv_w'], lp['conv_b']) * jax.nn.silu(g_c)
    q = rms_norm(q_d.reshape(b, s, H_D, DH_D), lp['q_norm'])
    k = rms_norm(k_d.reshape(b, s, H_D, DH_D), lp['k_norm'])
    v = v_d.reshape(b, s, H_D, DH_D)
    if ctx is None:
        na = block_attention(q, k, v)
    else:
        k_c, v_c = ctx
        na = neighbourhood_attention(q, k, v, k_c, v_c, lp['rpb'])
    na = na.reshape(b, s, W_D) * jax.nn.silu(g_d)
    out = jnp.concatenate([conv_out, na], axis=-1) @ lp['w_out']
    return out, (k, v)


def setup_inputs(seed: int = 0) -> dict:
    key = jax.random.key(seed)
    keys = jax.random.split(key, 48)
    ks = iter([keys[i] for i in range(48)])

    def nrm(shape, scale=1.0):
        return scale * jax.random.normal(next(ks), shape, jnp.float32)

    def gain(shape):
        return 1.0 + nrm(shape, 0.05)

    ne = (DEPTH + 1) // 2
    no = DEPTH // 2
    D = D_MODEL
    return {
        'x_prompt': nrm((BATCH, SEQ, D)),
        'x_sample': nrm((DEC_BATCH, DEC_SEQ, D)),
        'c': nrm((DEC_BATCH, D)),
        'cache_mla_ckv': nrm((DEC_BATCH, ne, PAST_LEN, KV_RANK)),
        'cache_mla_kpe': nrm((DEC_BATCH, ne, PAST_LEN, ROPE_DIM)),
        'state_mlstm_C': nrm((DEC_BATCH, ne, 2, H_B, DH_B, DH_B), 0.05),
        'state_mlstm_n': nrm((DEC_BATCH, ne, 2, H_B, DH_B), 0.05),
        'state_mlstm_m': 2.0 + nrm((DEC_BATCH, ne, 2, H_B), 0.5),
        'cache_na_k': nrm((DEC_BATCH, no, PAST_LEN, H_D, DH_D)),
        'cache_na_v': nrm((DEC_BATCH, no, PAST_LEN, H_D, DH_D)),
        'c_ctx': nrm((D,)),
        'norm_w': gain((DEPTH, D)),
        'ada_w': nrm((DEPTH, D, 3 * D), 0.5 * D ** -0.5),
        'ada_b': nrm((DEPTH, 3 * D), 0.02),
        'ev_w_in': nrm((ne, D, EVEN_IN), D ** -0.5),
        'ev_q_a_norm': gain((ne, Q_RANK)),
        'ev_kv_a_norm': gain((ne, KV_RANK)),
        'ev_w_q_b': nrm((ne, Q_RANK, H_A * QK_DIM), Q_RANK ** -0.5),
        'ev_w_kv_b': nrm((ne, KV_RANK, H_A * (NOPE_DIM + V_DIM)), KV_RANK ** -0.5),
        'ev_q_norm': gain((ne, QK_DIM)),
        'ev_k_norm': gain((ne, QK_DIM)),
        'ev_b_i': -1.0 + nrm((ne, 2, H_B), 0.3),
        'ev_b_f': 3.0 + nrm((ne, 2, H_B), 0.5),
        'ev_h_norm': gain((ne, W_B)),
        'ev_w_out': nrm((ne, W_A + W_B, D), (W_A + W_B) ** -0.5),
        'od_w_in': nrm((no, D, ODD_IN), D ** -0.5),
        'od_conv_w': nrm((no, CONV_W, W_C), CONV_W ** -0.5),
        'od_conv_b': nrm((no, W_C), 0.02),
        'od_q_norm': gain((no, DH_D)),
        'od_k_norm': gain((no, DH_D)),
        'od_rpb': nrm((no, H_D, 2 * WIN_R - 1, 2 * WIN_C - 1), 0.5),
        'od_w_out': nrm((no, W_C + W_D, D), (W_C + W_D) ** -0.5),
    }


def reference(x_prompt, x_sample, c, cache_mla_ckv, cache_mla_kpe, state_mlstm_C, state_mlstm_n, state_mlstm_m,
              cache_na_k, cache_na_v, c_ctx, norm_w, ada_w, ada_b,
              ev_w_in, ev_q_a_norm, ev_kv_a_norm, ev_w_q_b, ev_w_kv_b, ev_q_norm, ev_k_norm, ev_b_i, ev_b_f,
              ev_h_norm, ev_w_out,
              od_w_in, od_conv_w, od_conv_b, od_q_norm, od_k_norm, od_rpb, od_w_out):
    yp, ys = x_prompt, x_sample
    l_ckv, l_kpe, l_C, l_n, l_m, l_k, l_v = [], [], [], [], [], [], []
    for l in range(DEPTH):
        j = l // 2
        sh_p, sc_p, g_p = modulation(c_ctx[None, :], ada_w[l], ada_b[l])
        sh_s, sc_s, g_s = modulation(c, ada_w[l], ada_b[l])
        hp = rms_norm(yp, norm_w[l]) * (1 + sc_p) + sh_p
        hs = rms_norm(ys, norm_w[l]) * (1 + sc_s) + sh_s
        if l % 2 == 0:
            lp = {'w_in': ev_w_in[j], 'q_a_norm': ev_q_a_norm[j], 'kv_a_norm': ev_kv_a_norm[j],
                  'w_q_b': ev_w_q_b[j], 'w_kv_b': ev_w_kv_b[j], 'q_norm': ev_q_norm[j], 'k_norm': ev_k_norm[j],
                  'b_i': ev_b_i[j], 'b_f': ev_b_f[j], 'h_norm': ev_h_norm[j], 'w_out': ev_w_out[j]}
            out_p, (ckv, kpe, C_s, n_s, m_s) = even_mixer(hp, lp)
            out_s, _ = even_mixer(hs, lp, ctx=(cache_mla_ckv[:, j], cache_mla_kpe[:, j], state_mlstm_C[:, j],
                                              state_mlstm_n[:, j], state_mlstm_m[:, j]))
            l_ckv.append(ckv)
            l_kpe.append(kpe)
            l_C.append(C_s)
            l_n.append(n_s)
            l_m.append(m_s)
        else:
            lp = {'w_in': od_w_in[j], 'conv_w': od_conv_w[j], 'conv_b': od_conv_b[j], 'q_norm': od_q_norm[j],
                  'k_norm': od_k_norm[j], 'rpb': od_rpb[j], 'w_out': od_w_out[j]}
            out_p, (k_p, v_p) = odd_mixer(hp, lp)
            out_s, _ = odd_mixer(hs, lp, ctx=(cache_na_k[:, j], cache_na_v[:, j]))
            l_k.append(k_p)
            l_v.append(v_p)
        yp = yp + g_p * out_p
        ys = ys + g_s * out_s
    new_mla_ckv = jnp.stack(l_ckv, axis=1)
    new_mla_kpe = jnp.stack(l_kpe, axis=1)
    new_mlstm_C = jnp.stack(l_C, axis=1)
    new_mlstm_n = jnp.stack(l_n, axis=1)
    new_mlstm_m = jnp.stack(l_m, axis=1)
    new_na_k = jnp.stack(l_k, axis=1)
    new_na_v = jnp.stack(l_v, axis=1)
    return (yp, ys, new_mla_ckv, new_mla_kpe, new_mlstm_C, new_mlstm_n, new_mlstm_m, new_na_k, new_na_v)
```

```python
import os
from contextlib import ExitStack

import numpy as np
import concourse.bass as bass
import concourse.mybir as mybir
from concourse.bass_utils import run_bass_kernel_spmd

F32 = mybir.dt.float32
BF16 = mybir.dt.bfloat16
I32 = mybir.dt.int32
AF = mybir.ActivationFunctionType
ALU = mybir.AluOpType
AX = mybir.AxisListType

D = 1024
NT = 12
NTOK = 1536
EPS = 1e-6
NEG = -30000.0
GRID_W = 64
ROPE_BASE = 10000.0

ENGS = ['pe', 'act', 'dve', 'pool', 'sp']
N_DMA_SEMS = 24
SAME_ENG_WINDOW = 6


class _Op:
    __slots__ = ('eng', 'fn', 'deps_c', 'deps_d', 'sig', 'idx', 'is_dma', 'dsem', 'dval')


class Sched:
    def __init__(self):
        self.ops = {e: [] for e in ENGS}
        self.bufs = {}
        self.dma_cnt = [0] * N_DMA_SEMS
        self.dma_last = [None] * N_DMA_SEMS
        self.dma_rr = 0
        self.out_tokens = []

    def reg(self, name, rng=None):
        self.bufs[name] = {'range': rng, 'subs': {}}

    def _overlapping(self, name):
        b = self.bufs[name]
        res = [name]
        if b['range'] is None:
            return res
        a0, a1 = b['range']
        for n2, b2 in self.bufs.items():
            if n2 == name or b2['range'] is None:
                continue
            c0, c1 = b2['range']
            if c0 < a1 and a0 < c1 and b2['subs']:
                res.append(n2)
        return res

    @staticmethod
    def _norm(key):
        if isinstance(key, tuple):
            return key[0], key[1]
        return key, None

    def _collect(self, key, is_write, dc, dd):
        name, sub = self._norm(key)
        if name not in self.bufs:
            self.reg(name)
        for n2 in self._overlapping(name):
            subs = self.bufs[n2]['subs']
            if n2 == name and sub is not None:
                cands = [s for s in (sub, None) if s in subs]
            else:
                cands = list(subs.keys())
            for s in cands:
                st = subs[s]
                toks = [st[0]] if st[0] is not None else []
                if is_write:
                    toks = toks + list(st[1].values())
                for t in toks:
                    if t[0] == 'dma':
                        dd[t[1]] = max(dd.get(t[1], 0), t[2])
                    else:
                        dc[t[0]] = max(dc.get(t[0], -1), t[1])

    def _update(self, key, is_write, tok):
        name, sub = self._norm(key)
        subs = self.bufs[name]['subs']
        if is_write:
            if sub is None:
                for n2 in self._overlapping(name):
                    if n2 != name:
                        self.bufs[n2]['subs'] = {}
                subs.clear()
            subs[sub] = [tok, {}]
        else:
            if sub not in subs:
                subs[sub] = [None, {}]
            rk = tok[0] if tok[0] != 'dma' else ('dma', tok[1])
            subs[sub][1][rk] = tok

    def op(self, eng, fn, reads=(), writes=()):
        o = _Op()
        o.eng, o.fn, o.sig, o.is_dma = eng, fn, False, False
        o.idx = len(self.ops[eng])
        dc, dd = {}, {}
        for k in reads:
            self._collect(k, False, dc, dd)
        for k in writes:
            self._collect(k, True, dc, dd)
        o.deps_c, o.deps_d = dc, dd
        tok = (eng, o.idx)
        self.ops[eng].append(o)
        for k in reads:
            self._update(k, False, tok)
        for k in writes:
            self._update(k, True, tok)
        return o

    def dma(self, eng, fn, reads=(), writes=(), is_output=False):
        o = _Op()
        o.eng, o.fn, o.sig, o.is_dma = eng, fn, False, True
        o.idx = len(self.ops[eng])
        s = self.dma_rr
        self.dma_rr = (self.dma_rr + 1) % N_DMA_SEMS
        dc, dd = {}, {}
        if self.dma_last[s] is not None:
            dd[s] = self.dma_cnt[s]
        for k in reads:
            self._collect(k, False, dc, dd)
        for k in writes:
            self._collect(k, True, dc, dd)
        self.dma_cnt[s] += 16
        o.dsem, o.dval = s, self.dma_cnt[s]
        self.dma_last[s] = o
        o.deps_c, o.deps_d = dc, dd
        tok = ('dma', s, o.dval)
        self.ops[eng].append(o)
        for k in reads:
            self._update(k, False, tok)
        for k in writes:
            self._update(k, True, tok)
        if is_output:
            self.out_tokens.append(tok)
        return o

    @staticmethod
    def _need(o, x, j):
        if x != o.eng or o.is_dma:
            return True
        if o.eng == 'pe':
            return False
        return (o.idx - j) <= SAME_ENG_WINDOW

    def emit(self, nc, es):
        for e in ENGS:
            for o in self.ops[e]:
                for x, j in o.deps_c.items():
                    if self._need(o, x, j):
                        self.ops[x][j].sig = True
        cnt = {}
        for e in ENGS:
            c = 0
            arr = []
            for o in self.ops[e]:
                if o.sig and not o.is_dma:
                    c += 1
                arr.append(c)
            cnt[e] = arr
        sems = {e: es.enter_context(nc.semaphore('s_' + e)) for e in ENGS}
        dsems = [es.enter_context(nc.semaphore('d%d' % i)) for i in range(N_DMA_SEMS)]
        block = es.enter_context(nc.Block())
        handles = {'pe': block.tensor, 'act': block.scalar, 'dve': block.vector,
                   'pool': block.gpsimd, 'sp': block.sync}
        nwaits = [0]

        def run_engine(e):
            def body(eng):
                waited_c = {}
                waited_d = {}
                for o in self.ops[e]:
                    for x, j in o.deps_c.items():
                        if not self._need(o, x, j):
                            continue
                        v = cnt[x][j]
                        if v > waited_c.get(x, 0):
                            eng.wait_ge(sems[x], v)
                            waited_c[x] = v
                            nwaits[0] += 1
                    for s, v in o.deps_d.items():
                        if v > waited_d.get(s, 0):
                            eng.wait_ge(dsems[s], v)
                            waited_d[s] = v
                            nwaits[0] += 1
                    ins = o.fn(eng)
                    if o.is_dma:
                        ins.then_inc(dsems[o.dsem], 16)
                    elif o.sig:
                        ins.then_inc(sems[e], 1)
                if e == 'sp':
                    fin = {}
                    for t in self.out_tokens:
                        fin[t[1]] = max(fin.get(t[1], 0), t[2])
                    for s, v in fin.items():
                        if v > waited_d.get(s, 0):
                            eng.wait_ge(dsems[s], v)
            return body

        for e in ENGS:
            handles[e](run_engine(e))
        return {e: len(self.ops[e]) for e in ENGS}, nwaits[0]


def MM(out, lhsT, rhs, start=True, stop=True):
    return lambda e: e.matmul(out, lhsT=lhsT, rhs=rhs, start=start, stop=stop)


def ACTF(out, in_, func, bias=None, scale=None, accum=None):
    def f(e):
        kw = {}
        if bias is not None:
            kw['bias'] = bias
        if scale is not None:
            kw['scale'] = scale
        if accum is not None:
            kw['accum_out'] = accum
        return e.activation(out=out, in_=in_, func=func, **kw)
    return f


def TT(out, a, b, op):
    return lambda e: e.tensor_tensor(out=out, in0=a, in1=b, op=op)


def TS(out, a, s1, op0, s2=None, op1=None):
    def f(e):
        if op1 is None:
            return e.tensor_scalar(out=out, in0=a, scalar1=s1, scalar2=None, op0=op0)
        return e.tensor_scalar(out=out, in0=a, scalar1=s1, scalar2=s2, op0=op0, op1=op1)
    return f


def STT(out, a, s, b, op0, op1):
    return lambda e: e.scalar_tensor_tensor(out=out, in0=a, scalar=s, in1=b, op0=op0, op1=op1)


def CP(out, in_):
    return lambda e: e.tensor_copy(out=out, in_=in_)


def RED(out, in_, op=ALU.add):
    return lambda e: e.tensor_reduce(out=out, in_=in_, axis=AX.X, op=op)


def MS(ap, v):
    return lambda e: e.memset(ap, v)


def RCP(out, in_):
    return lambda e: e.reciprocal(out=out, in_=in_)


def SCAN(out, d0, d1, init, op0, op1):
    return lambda e: e.tensor_tensor_scan(out=out, data0=d0, data1=d1, initial=init, op0=op0, op1=op1)


def DMA(out, in_, **kw):
    return lambda e: e.dma_start(out=out, in_=in_, **kw)


def ASEL(out, in_, pattern, cmp, fill, base, cm):
    return lambda e: e.affine_select(out=out, in_=in_, pattern=pattern, compare_op=cmp, fill=fill,
                                     base=base, channel_multiplier=cm)


class Buf:
    __slots__ = ('ap', 'key')

    def __init__(self, ap, key):
        self.ap, self.key = ap, key

    def k(self, sub):
        return (self.key, sub)


class Arena:
    BLK = 32

    def __init__(self, S, tensor, words):
        self.S, self.t, self.words = S, tensor, words
        self.top = 0
        self.ctr = 0
        self.peak = 0
        self.live = []
        self.blocks = [dict() for _ in range(words // self.BLK + 2)]

    def mark(self):
        return self.top

    @staticmethod
    def _fold(dst, tok):
        rk = tok[0] if tok[0] != 'dma' else ('dma', tok[1])
        old = dst.get(rk)
        if old is None or tok[-1] > old[-1]:
            dst[rk] = tok

    def release(self, m):
        while self.live and self.live[-1][0] >= m:
            off, w, key = self.live.pop()
            st = self.S.bufs.pop(key, None)
            if st is None:
                continue
            toks = []
            for sub, (wt, rd) in st['subs'].items():
                if wt is not None:
                    toks.append(wt)
                toks.extend(rd.values())
            for bi in range(off // self.BLK, (off + w - 1) // self.BLK + 1):
                blk = self.blocks[bi]
                for t_ in toks:
                    self._fold(blk, t_)
        self.top = m

    def alloc(self, name, shape, dtype=F32, parts=128):
        n = 1
        for s in shape:
            n *= s
        w = n if dtype == F32 else (n + 1) // 2
        w = (w + 1) // 2 * 2
        off = self.top
        self.top += w
        self.peak = max(self.peak, self.top)
        assert self.top <= self.words, "arena overflow %s: %d > %d" % (name, self.top, self.words)
        ap = self.t[0:parts, off:off + w]
        if dtype != F32:
            ap = ap.bitcast(dtype)
        ap = ap[:, 0:n]
        if len(shape) > 1:
            names = ["d%d" % i for i in range(len(shape))]
            kw = {names[i]: shape[i] for i in range(len(shape))}
            ap = ap.rearrange("p (" + " ".join(names) + ") -> p " + " ".join(names), **kw)
        self.ctr += 1
        key = "%s#%d" % (name, self.ctr)
        self.S.reg(key, None)
        inh = {}
        for bi in range(off // self.BLK, (off + w - 1) // self.BLK + 1):
            for t_ in self.blocks[bi].values():
                self._fold(inh, t_)
        if inh:
            self.S.bufs[key]['subs'][None] = [None, inh]
        self.live.append((off, w, key))
        return Buf(ap, key)


class PSum:
    def __init__(self, ps):
        self.ps = ps
        self.next = 0
        self.limit = 7

    def get(self, nb=1):
        if self.next + nb > self.limit:
            self.next = 0
        b = self.next
        self.next = (self.next + nb) % self.limit
        return b

    def ap(self, b, nb=1):
        return self.ps[:, b * 512:(b + nb) * 512]

    @staticmethod
    def keys(b, nb=1):
        return [('ps', b + i) for i in range(nb)]


SEQS = [(0, 2, False), (2, 2, False), (4, 8, True)]
ARENA_WORDS = 15400
WSLOT = 4096
NWSLOT = 3


def build_nc(depth=4, taps=(), stop=None):
    nc = bass.Bass("TRN2", target_bir_lowering=False)
    S = Sched()
    es = ExitStack()

    def din(name, shape):
        return nc.dram_tensor(name, list(shape), F32, kind="ExternalInput").ap()

    def dout(name, shape):
        return nc.dram_tensor(name, list(shape), F32, kind="ExternalOutput").ap()

    x_d = din("x", [NTOK, D])
    cond_d = din("cond", [2, D])
    ckvc_d = din("ckv_c", [2, 256, 128])
    kpec_d = din("kpe_c", [2, 256, 32])
    stC_d = din("stC", [2, 2, 4, 128, 128])
    stn_d = din("stn", [2, 2, 4, 128])
    stm_d = din("stm", [2, 2, 4])
    nakc_d = din("nak_c", [2, 256, 8, 64])
    navc_d = din("nav_c", [2, 256, 8, 64])
    normw_d = din("norm_w", [4, D])
    adaw_d = din("ada_w", [4, D, 3 * D])
    adab_d = din("ada_b", [4, 3 * D])
    evwin_d = din("ev_w_in", [2, D, 3504])
    evqan_d = din("ev_q_a_norm", [2, 256])
    evkvan_d = din("ev_kv_a_norm", [2, 128])
    evwqb_d = din("ev_w_q_b", [2, 256, 768])
    evwkvb_d = din("ev_w_kv_b", [2, 128, 1024])
    evqn_d = din("ev_q_norm", [2, 96])
    evkn_d = din("ev_k_norm", [2, 96])
    evbi_d = din("ev_b_i", [2, 2, 4])
    evbf_d = din("ev_b_f", [2, 2, 4])
    evhn_d = din("ev_h_norm", [2, 512])
    evwout_d = din("ev_w_out", [2, D, D])
    odwin_d = din("od_w_in", [2, D, 4096])
    odcw_d = din("od_conv_w", [2, 3, 512])
    odcb_d = din("od_conv_b", [2, 512])
    odqn_d = din("od_q_norm", [2, 64])
    odkn_d = din("od_k_norm", [2, 64])
    odwout_d = din("od_w_out", [2, D, D])
    rpbx_d = din("rpbx", [2, 8, 128, 1024])
    cmask_d = din("cmask", [2, 128, 1024])
    ropecs_d = din("rope_cs", [2, 1024, 32])
    lmask_d = din("lmask", [2, 128, 128])

    yp_d = dout("yp", [512, D])
    ys_d = dout("ys", [1024, D])
    ockv_d = dout("o_ckv", [2, 2, 256, 128])
    okpe_d = dout("o_kpe", [2, 2, 256, 32])
    oC_d = dout("o_C", [2, 2, 2, 4, 128, 128])
    on_d = dout("o_n", [2, 2, 2, 4, 128])
    om_d = dout("o_m", [2, 2, 2, 4])
    onak_d = dout("o_nak", [2, 2, 256, 8, 64])
    onav_d = dout("o_nav", [2, 2, 256, 8, 64])

    tap_list = []

    def sb(name, shape, dt):
        return es.enter_context(nc.sbuf_tensor(name, list(shape), dt))

    yT = sb("yT", [128, 8, NTOK], F32)
    hsT = sb("hsT", [128, 8, NTOK], BF16)
    mixT = sb("mixT", [128, 8, NTOK], BF16)
    wring = [sb("wring%d" % i, [128, WSLOT], BF16) for i in range(NWSLOT)]
    scb = sb("scb", [128, 8, 2], BF16)
    wmod = [sb("wmod%d" % i, [128, 8, 128], BF16) for i in range(2)]
    mrow = sb("mrow", [2, 128], F32)
    wqb = sb("wqb", [128, 2, 768], BF16)
    wkvb = sb("wkvb", [128, 1024], BF16)
    wgate = sb("wgate", [128, 8, 16], BF16)
    identf = sb("identf", [128, 128], F32)
    identb = sb("identb", [128, 128], BF16)
    onesb = sb("onesb", [128, 128], BF16)
    sel = sb("sel", [36, 4, 128], F32)
    lmaskb = sb("lmaskb", [128, 2, 128], BF16)
    ropeC = sb("ropeC", [128, 8, 32], F32)
    ropeS = sb("ropeS", [128, 8, 32], F32)
    cmaskb = sb("cmaskb", [128, 2, 1024], BF16)
    pvA = sb("pvA", [128, 128], F32)
    pvB = sb("pvB", [128, 64], F32)
    scond = sb("scond", [128, 16], F32)
    modT = sb("modT", [128, 4, 24, 2], F32)
    modA = sb("modA", [128, 4, 8, 2], F32)
    gq = sb("gq", [128, 96], F32)
    gk = sb("gk", [128, 96], F32)
    ghn = sb("ghn", [128, 512], F32)
    bif = sb("bif", [36, 2, 2], F32)
    m0t = sb("m0t", [36, 2], F32)
    convp = sb("convp", [128, 2, 4, 4], F32)
    arena_t = sb("arena", [128, ARENA_WORDS], F32)
    ps_t = es.enter_context(nc.psum_tensor("ps", [128, 4096], F32))

    A = Arena(S, arena_t, ARENA_WORDS)
    P = PSum(ps_t)
    PK = PSum.keys

    def TAP(name, ap, shape, reads, dt=F32):
        if name not in taps:
            return
        d = nc.dram_tensor("tap_" + name, list(shape), dt, kind="ExternalOutput").ap()
        S.dma('sp', DMA(d, ap), reads=reads, is_output=True)
        tap_list.append(name)

    def cols(t, n=1):
        return slice(128 * t, 128 * (t + n))

    S.op('pool', MS(identf[:], 1.0), writes=['identf'])
    S.op('pool', ASEL(identf[:], identf[:], [[-1, 128]], ALU.is_equal, 0.0, 0, 1),
         reads=['identf'], writes=['identf'])
    S.op('dve', CP(identb[:], identf[:]), reads=['identf'], writes=['identb'])
    S.op('pool', MS(onesb[:], 1.0), writes=['onesb'])
    S.op('pool', MS(sel[:], 1.0), writes=['sel'])
    for g0 in (0, 32):
        for h in range(4):
            S.op('pool', ASEL(sel[g0:g0 + 4, h, :], sel[g0:g0 + 4, h, :], [[0, 128]], ALU.is_equal, 0.0, -h, 1),
                 reads=['sel'], writes=['sel'])
    S.dma('pool', DMA(lmaskb[:], lmask_d.rearrange("d p q -> p d q")), writes=['lmaskb'])
    S.dma('pool', DMA(cmaskb[:], cmask_d.rearrange("d p q -> p d q")), writes=['cmaskb'])
    S.dma('sp', DMA(ropeC[:], ropecs_d[0].rearrange("(t p) e -> p t e", p=128)), writes=['ropeC'])
    S.dma('sp', DMA(ropeS[:], ropecs_d[1].rearrange("(t p) e -> p t e", p=128)), writes=['ropeS'])

    mk = A.mark()
    stgA = A.alloc("stgA", [128], F32)
    stgB = A.alloc("stgB", [128], F32, parts=64)
    S.dma('sp', DMA(stgA.ap[0:96, :], adab_d.rearrange("l (c p) -> (l c) p", p=128)), writes=[stgA.k(0)])
    S.dma('sp', DMA(stgA.ap[96:128, :], normw_d.rearrange("l (c p) -> (l c) p", p=128)), writes=[stgA.k(1)])
    S.dma('sp', DMA(stgB.ap[0:16, :], cond_d.rearrange("g (c p) -> (g c) p", p=128)), writes=[stgB.k(0)])
    S.dma('sp', DMA(stgB.ap[16:20, :], evqan_d.rearrange("l (c p) -> (l c) p", p=128)), writes=[stgB.k(1)])
    S.dma('sp', DMA(stgB.ap[20:22, :], evkvan_d), writes=[stgB.k(2)])
    S.dma('sp', DMA(stgB.ap[22:46, :], odcw_d.rearrange("l k (c p) -> (l k c) p", p=128)), writes=[stgB.k(3)])
    S.dma('sp', DMA(stgB.ap[46:54, :], odcb_d.rearrange("l (c p) -> (l c) p", p=128)), writes=[stgB.k(4)])
    b = P.get()
    S.op('pe', MM(P.ap(b)[:, 0:128], stgA.ap, identf[:]), reads=[stgA.key, 'identf'], writes=PK(b))
    S.op('dve', CP(pvA[:], P.ap(b)[:, 0:128]), reads=PK(b), writes=['pvA'])
    b = P.get()
    S.op('pe', MM(P.ap(b)[:, 0:54], stgB.ap[0:54, :], identf[0:54, 0:54]), reads=[stgB.key, 'identf'], writes=PK(b))
    S.op('dve', CP(pvB[:, 0:54], P.ap(b)[:, 0:54]), reads=PK(b), writes=['pvB'])
    A.release(mk)

    def pv_adab(l):
        return pvA[:, l * 24:(l + 1) * 24]

    def pv_normw(l):
        return pvA[:, 96 + l * 8:96 + (l + 1) * 8]

    def pv_qan(j, c2):
        return pvB[:, 16 + j * 2 + c2:16 + j * 2 + c2 + 1]

    def pv_kvan(j):
        return pvB[:, 20 + j:21 + j]

    mk = A.mark()
    tnh = A.alloc("tnh", [16], F32)
    S.op('act', ACTF(tnh.ap, pvB[:, 0:16], AF.Tanh, scale=0.5), reads=['pvB'], writes=[tnh.key])
    S.op('dve', STT(tnh.ap, tnh.ap, 1.0, pvB[:, 0:16], ALU.add, ALU.mult), reads=[tnh.key, 'pvB'], writes=[tnh.key])
    S.op('dve', TS(scond[:], tnh.ap, 0.5, ALU.mult), reads=[tnh.key], writes=['scond'])
    S.op('dve', CP(scb[:], scond[:].rearrange("p (g c) -> p c g", g=2)), reads=['scond'], writes=['scb'])
    A.release(mk)
    for j in range(2):
        wv = pvB[:, 22 + j * 12:22 + (j + 1) * 12].rearrange("p (k c) -> p c k", k=3)
        S.op('dve', TS(convp[:, j, :, 0:3], wv, 0.5, ALU.mult), reads=['pvB'], writes=[('convp', j)])
        bv = pvB[:, 46 + j * 4:46 + (j + 1) * 4]
        S.op('dve', TS(convp[:, j, :, 3], bv, 0.5, ALU.mult), reads=['pvB'], writes=[('convp', j)])
    for d_, g0 in ((0, 0), (1, 32)):
        S.dma('sp', DMA(bif[g0:g0 + 4, 0, :], evbi_d[:, d_, :].rearrange("j h -> h j"), allow_slow_non_contiguous=True),
              writes=['bif'])
        S.dma('sp', DMA(bif[g0:g0 + 4, 1, :], evbf_d[:, d_, :].rearrange("j h -> h j"), allow_slow_non_contiguous=True),
              writes=['bif'])
        S.dma('sp', DMA(m0t[g0:g0 + 4, :], stm_d[:, d_, :].rearrange("j h -> h j"), allow_slow_non_contiguous=True),
              writes=['m0t'])

    wstate = {'i': 0}
    pre_w = {}

    def load_w(pieces):
        i = wstate['i']
        wstate['i'] = (i + 1) % NWSLOT
        tot = sum(p.shape[1] for p in pieces)
        assert 8 * tot <= WSLOT, tot
        view = wring[i][:, 0:8 * tot].rearrange("p (k n) -> p k n", k=8)
        key = 'wring%d' % i
        off = 0
        for pi, pc in enumerate(pieces):
            n = pc.shape[1]
            S.dma('pool', DMA(view[:, :, off:off + n], pc.rearrange("(k p) n -> p k n", p=128)),
                  writes=[(key, pi)])
            off += n
        return view, key

    def rsqrt(dst, src, t1, t2, rk, wk):
        S.op('dve', TS(dst.bitcast(I32), src.bitcast(I32), -0.5, ALU.mult, float(0x5f3759df), ALU.add),
             reads=rk, writes=wk)
        for _ in range(2):
            S.op('dve', TT(t2, src, dst, ALU.mult), reads=rk + wk, writes=wk)
            S.op('dve', STT(t2, t2, -0.5, dst, ALU.mult, ALU.mult), reads=wk, writes=wk)
            S.op('dve', STT(dst, t2, 1.5, dst, ALU.add, ALU.mult), reads=wk, writes=wk)

    modq = []

    def queue_mod(l):
        def dma_piece(cg):
            S.dma('pool', DMA(wmod[cg % 2][:], adaw_d[l][:, cg * 128:(cg + 1) * 128].rearrange("(k p) n -> p k n", p=128)),
                  writes=['wmod%d' % (cg % 2)])

        def step(cg):
            def f():
                if cg == 0:
                    dma_piece(0)
                if cg + 1 < 24:
                    dma_piece(cg + 1)
                wv, wkey = wmod[cg % 2], 'wmod%d' % (cg % 2)
                for kc in range(8):
                    S.op('pe', MM(P.ap(7)[:, cg * 2:cg * 2 + 2], wv[:, kc, :], scb[:, kc, :], start=(kc == 0), stop=(kc == 7)),
                         reads=[wkey, 'scb'], writes=PK(7))
                if cg == 23:
                    pb = P.ap(7)
                    S.op('dve', TT(modT[:, l, :, :], pb[:, 0:48].rearrange("p (c g) -> p c g", g=2),
                                   pv_adab(l).unsqueeze(2).broadcast_to([128, 24, 2]), ALU.add),
                         reads=PK(7) + ['pvA'], writes=[('modT', l)])
                    S.op('dve', TS(modA[:, l, :, :], modT[:, l, 8:16, :], 1.0, ALU.add), reads=[('modT', l)], writes=[('modA', l)])
                    S.op('dve', TT(modA[:, l, :, :], modA[:, l, :, :], pv_normw(l).unsqueeze(2).broadcast_to([128, 8, 2]),
                                   ALU.mult), reads=[('modA', l), 'pvA'], writes=[('modA', l)])
            return f
        for cg in range(24):
            modq.append(step(cg))

    def pump(n=1):
        for _ in range(n):
            if modq:
                modq.pop(0)()

    def load_x():
        mk = A.mark()
        xin = [A.alloc("xin%d" % i, [1024], F32) for i in range(2)]
        for t in range(NT):
            xb = xin[t % 2]
            S.dma('sp', DMA(xb.ap, x_d[128 * t:128 * (t + 1), :]), writes=[xb.key])
            b = P.get(2)
            for c in range(8):
                S.op('pe', MM(P.ap(b, 2)[:, c * 128:(c + 1) * 128], xb.ap[:, c * 128:(c + 1) * 128], identf[:]),
                     reads=[xb.key, 'identf'], writes=PK(b, 2))
            eng = 'act' if t % 2 else 'dve'
            src = P.ap(b, 2).rearrange("p (c q) -> p c q", c=8)
            if eng == 'act':
                S.op('act', ACTF(yT[:, :, cols(t)], src, AF.Copy), reads=PK(b, 2), writes=[('yT', t)])
            else:
                S.op('dve', CP(yT[:, :, cols(t)], src), reads=PK(b, 2), writes=[('yT', t)])
            pump(2)
        A.release(mk)

    yo_bufs = []

    def store_tiles(tiles):
        if not yo_bufs:
            yo_bufs.extend([A.alloc("yo%d" % i, [1024], F32) for i in range(2)])
        for t in tiles:
            ob = yo_bufs[t % 2]
            b = P.get(2)
            for c in range(8):
                S.op('pe', MM(P.ap(b, 2)[:, c * 128:(c + 1) * 128], yT[:, c, cols(t)], identf[:]),
                     reads=[('yT', t), 'identf'], writes=PK(b, 2))
            if t % 2:
                S.op('act', ACTF(ob.ap, P.ap(b, 2), AF.Copy), reads=PK(b, 2), writes=[ob.key])
            else:
                S.op('dve', CP(ob.ap, P.ap(b, 2)), reads=PK(b, 2), writes=[ob.key])
            dst = yp_d[128 * t:128 * (t + 1), :] if t < 4 else ys_d[128 * (t - 4):128 * (t - 3), :]
            S.dma('sp', DMA(dst, ob.ap), reads=[ob.key], is_output=True)

    def store_y():
        store_tiles(range(NT))

    def phase_norm(l):
        mk = A.mark()
        sqs_ = [A.alloc("sq%d" % i, [8, 512], BF16) for i in range(3)]
        ms = A.alloc("ms", [3, 512], F32)
        rstd = A.alloc("rstd", [3, 512], F32)
        t1 = A.alloc("nt1", [3, 512], F32)
        t2 = A.alloc("nt2", [3, 512], F32)
        tmp = [A.alloc("ntmp%d" % i, [512], F32) for i in range(2)]
        for blk in range(3):
            cs = slice(blk * 512, (blk + 1) * 512)
            yk = [('yT', 4 * blk + i) for i in range(4)]
            sq = sqs_[blk]
            S.op('act', ACTF(sq.ap, yT[:, :, cs], AF.Square), reads=yk, writes=[sq.key])
            b = P.get()
            for c in range(8):
                S.op('pe', MM(P.ap(b), onesb[:], sq.ap[:, c, :], start=(c == 0), stop=(c == 7)),
                     reads=[sq.key, 'onesb'], writes=PK(b))
            S.op('dve', TS(ms.ap[:, blk, :], P.ap(b), 1.0 / D, ALU.mult, EPS, ALU.add), reads=PK(b), writes=[ms.k(blk)])
        rsqrt(rstd.ap, ms.ap, t1.ap, t2.ap, [ms.key], [rstd.key, t1.key, t2.key])
        for blk in range(3):
            g = 0 if blk == 0 else 1
            cs = slice(blk * 512, (blk + 1) * 512)
            yk = [('yT', 4 * blk + i) for i in range(4)]
            hk = [('hsT', 4 * blk + i) for i in range(4)]
            for c in range(8):
                tb = tmp[c % 2]
                S.op('dve', STT(tb.ap, yT[:, c, cs], modA[:, l, c, g:g + 1], rstd.ap[:, blk, :], ALU.mult, ALU.mult),
                     reads=yk + [('modA', l), rstd.key], writes=[tb.key])
                S.op('act', ACTF(hsT[:, c, cs], tb.ap, AF.Identity, bias=modT[:, l, c, g:g + 1], scale=1.0),
                     reads=[tb.key, ('modT', l)], writes=hk)
        A.release(mk)

    pre_out = [None]

    def preload_out(wout_d):
        pre_out[0] = load_w([wout_d[:, 0:512]])

    def phase_out(l, wout_d, store_blk=None):
        halves = [pre_out[0] if pre_out[0] is not None else load_w([wout_d[:, 0:512]]), None]
        pre_out[0] = None
        halves[1] = load_w([wout_d[:, 512:1024]])
        for blk in range(3):
            g = 0 if blk == 0 else 1
            cs = slice(blk * 512, (blk + 1) * 512)
            mk_ = [('mixT', 4 * blk + i) for i in range(4)]
            yk = [('yT', 4 * blk + i) for i in range(4)]
            for half in range(2):
                wv, wkey = halves[half]
                for dc in range(4):
                    c = half * 4 + dc
                    b = P.get()
                    for jc in range(8):
                        S.op('pe', MM(P.ap(b), wv[:, jc, dc * 128:(dc + 1) * 128], mixT[:, jc, cs],
                                      start=(jc == 0), stop=(jc == 7)),
                             reads=[wkey] + mk_, writes=PK(b))
                    S.op('dve', STT(yT[:, c, cs], P.ap(b), modT[:, l, 16 + c, g:g + 1], yT[:, c, cs], ALU.mult, ALU.add),
                         reads=PK(b) + [('modT', l)] + yk, writes=yk)
            if store_blk is not None:
                store_blk(blk)

    def attention(qT, kT, dk, vaug, att, NQ, blocks_for_q, bias_fn=None, head_hook=None):
        mk = A.mark()
        maxb = max(len(blocks_for_q(qi)) for qi in range(NQ))
        gsz = maxb if maxb <= 8 else (maxb + 1) // 2
        assert gsz <= 8
        npt = 3 if gsz <= 2 else 2
        PT = [A.alloc("PT%d" % i, [gsz * 128], BF16) for i in range(npt)]
        rd = A.alloc("rd", [4], F32)
        items = []
        for hh in range(4):
            for qi in range(NQ):
                bl = blocks_for_q(qi)
                grps = [bl[i:i + gsz] for i in range(0, len(bl), gsz)]
                for gi, g_ in enumerate(grps):
                    items.append((hh, qi, gi, len(grps), g_))
        state = {'pt': 0, 'ob': None, 'on': 0}
        P.limit = 5
        if P.next >= 5:
            P.next = 0

        def emit_S(it):
            hh, qi, gi, ng, g_ = it
            if head_hook is not None and qi == 0 and gi == 0:
                head_hook(hh)
            b = P.get(2)
            reg = P.ap(b, 2)
            for bi, kidx in enumerate(g_):
                bias = bias_fn(hh, qi, kidx) if bias_fn is not None else None
                S.op('pe', MM(reg[:, bi * 128:(bi + 1) * 128], kT.ap[0:dk, hh, kidx * 128:(kidx + 1) * 128],
                              qT.ap[0:dk, hh, qi * 128:(qi + 1) * 128], start=True, stop=(bias is None)),
                     reads=[kT.key, qT.key], writes=PK(b, 2))
                if bias is not None:
                    S.op('pe', MM(reg[:, bi * 128:(bi + 1) * 128], identb[:], bias[0], start=False, stop=True),
                         reads=['identb'] + bias[1], writes=PK(b, 2))
            pt = PT[state["pt"] % npt]
            state['pt'] += 1
            n = len(g_) * 128
            S.op('act', ACTF(pt.ap[:, 0:n], reg[:, 0:n], AF.Exp), reads=PK(b, 2), writes=[pt.key])
            return pt

        def emit_PV(it, pt):
            hh, qi, gi, ng, g_ = it
            if gi == 0 and qi % 4 == 0:
                state['ob'] = 5 + state['on'] % 2
                state['on'] += 1
            ob = state['ob']
            oq = qi % 4
            for bi, kidx in enumerate(g_):
                S.op('pe', MM(P.ap(ob)[:, oq * 66:oq * 66 + 65], pt.ap[:, bi * 128:(bi + 1) * 128],
                              vaug.ap[:, kidx, hh, 0:65], start=(gi == 0 and bi == 0),
                              stop=(gi == ng - 1 and bi == len(g_) - 1)),
                     reads=[pt.key, vaug.key], writes=PK(ob))
            if gi == ng - 1 and (oq == 3 or qi == NQ - 1):
                nq = oq + 1
                q0 = qi - oq
                ov = P.ap(ob)[:, 0:nq * 66].rearrange("p (q e) -> p q e", e=66)
                S.op('dve', RCP(rd.ap[:, 0:nq], ov[:, :, 64]), reads=PK(ob), writes=[rd.key])
                S.op('dve', TT(att.ap[:, q0:q0 + nq, hh, :], ov[:, :, 0:64],
                               rd.ap[:, 0:nq].unsqueeze(2).broadcast_to([128, nq, 64]), ALU.mult),
                     reads=PK(ob) + [rd.key], writes=[att.k(q) for q in range(q0, q0 + nq)])

        prev = None
        for it in items:
            pt = emit_S(it)
            if prev is not None:
                emit_PV(*prev)
            prev = (it, pt)
        emit_PV(*prev)
        P.limit = 7
        A.release(mk)

    def even_layer(l):
        j = l // 2
        W = evwin_d[j]
        S.dma('pool', DMA(wqb[:], evwqb_d[j].rearrange("(c p) n -> p c n", p=128)), writes=['wqb'])
        S.dma('pool', DMA(wkvb[:], evwkvb_d[j]), writes=['wkvb'])
        S.dma('pool', DMA(wgate[:], W[:, 2464:2480].rearrange("(k p) n -> p k n", p=128)), writes=['wgate'])
        S.dma('sp', DMA(gq[:, 0:96], evqn_d[j].partition_broadcast(128)), writes=['gq'])
        S.dma('sp', DMA(gk[:, 0:96], evkn_d[j].partition_broadcast(128)), writes=['gk'])
        S.dma('sp', DMA(ghn[:], evhn_d[j].partition_broadcast(128)), writes=['ghn'])
        S.op('dve', TS(gq[:, 0:96], gq[:, 0:96], 96 ** -0.5, ALU.mult), reads=['gq'], writes=['gq'])
        S.op('dve', TS(ghn[:], ghn[:], 0.25, ALU.mult), reads=['ghn'], writes=['ghn'])

        mk1 = A.mark()
        qanT = A.alloc("qanT", [2, NTOK], BF16)
        ckvT = A.alloc("ckvT", [NTOK + 256], BF16)
        kpeall = A.alloc("kpeall", [14, 32], F32)
        krg = A.alloc("krg", [14, 32], F32)
        sskpe = A.alloc("sskpe", [14], F32)
        mk2 = A.mark()
        wa, wakey = pre_w.pop('wa') if 'wa' in pre_w else load_w([W[:, 0:416]])
        wga_of = lambda hg_: load_w([W[:, 416 + hg_ * 256:416 + (hg_ + 1) * 256]])
        wga_next = wga_of(0)
        sq = A.alloc("sq1", [3, 512], BF16)
        ms = A.alloc("ms1", [2, 512], F32)
        rstd = A.alloc("rstd1", [2, 512], F32)
        n1 = A.alloc("n1", [2, 512], F32)
        n2 = A.alloc("n2", [2, 512], F32)
        ckvTf = A.alloc("ckvTf", [512], F32)
        for blk in range(3):
            cs = slice(blk * 512, (blk + 1) * 512)
            hk = [('hsT', 4 * blk + i) for i in range(4)]
            bs = []
            for ci in range(3):
                b = P.get()
                bs.append(b)
                for kc in range(8):
                    S.op('pe', MM(P.ap(b), wa[:, kc, ci * 128:(ci + 1) * 128], hsT[:, kc, cs],
                                  start=(kc == 0), stop=(kc == 7)), reads=[wakey] + hk, writes=PK(b))
                S.op('act', ACTF(sq.ap[:, ci, :], P.ap(b), AF.Square), reads=PK(b), writes=[sq.k(ci)])
            bq_ = P.get()
            for ci in range(2):
                S.op('pe', MM(P.ap(bq_), onesb[:], sq.ap[:, ci, :], start=(ci == 0), stop=(ci == 1)),
                     reads=[sq.k(ci), 'onesb'], writes=PK(bq_))
            bk_ = P.get()
            S.op('pe', MM(P.ap(bk_), onesb[:], sq.ap[:, 2, :]), reads=[sq.k(2), 'onesb'], writes=PK(bk_))
            S.op('dve', TS(ms.ap[:, 0, :], P.ap(bq_), 1.0 / 256, ALU.mult, EPS, ALU.add), reads=PK(bq_), writes=[ms.key])
            S.op('dve', TS(ms.ap[:, 1, :], P.ap(bk_), 1.0 / 128, ALU.mult, EPS, ALU.add), reads=PK(bk_), writes=[ms.key])
            rsqrt(rstd.ap, ms.ap, n1.ap, n2.ap, [ms.key], [rstd.key, n1.key, n2.key])
            for c2 in range(2):
                S.op('dve', STT(qanT.ap[:, c2, cs], P.ap(bs[c2]), pv_qan(j, c2), rstd.ap[:, 0, :], ALU.mult, ALU.mult),
                     reads=PK(bs[c2]) + ['pvB', rstd.key], writes=[qanT.k(4 * blk + i) for i in range(4)])
            S.op('dve', STT(ckvT.ap[:, cs], P.ap(bs[2]), pv_kvan(j), rstd.ap[:, 1, :], ALU.mult, ALU.mult),
                 reads=PK(bs[2]) + ['pvB', rstd.key], writes=[ckvT.k(4 * blk + i) for i in range(4)])
            if blk == 0:
                S.op('dve', STT(ckvTf.ap, P.ap(bs[2]), pv_kvan(j), rstd.ap[:, 1, :], ALU.mult, ALU.mult),
                     reads=PK(bs[2]) + ['pvB', rstd.key], writes=[ckvTf.key])
        b = P.get()
        for t in range(NT):
            for kc in range(8):
                S.op('pe', MM(P.ap(b)[:, t * 32:(t + 1) * 32], hsT[:, kc, cols(t)], wa[:, kc, 384:416],
                              start=(kc == 0), stop=(kc == 7)), reads=[wakey, ('hsT', t)], writes=PK(b))
        S.op('dve', CP(kpeall.ap[:, 0:12, :], P.ap(b)[:, 0:384].rearrange("p (t e) -> p t e", e=32)),
             reads=PK(b), writes=[kpeall.key])
        S.dma('sp', DMA(kpeall.ap[:, 12:14, :], kpec_d[j].rearrange("(t p) e -> p t e", p=128)), writes=[kpeall.key])
        for s_ in range(2):
            S.dma('sp', DMA(okpe_d[s_, j].rearrange("(t p) e -> p t e", p=128), kpeall.ap[:, 2 * s_:2 * s_ + 2, :]),
                  reads=[kpeall.key], is_output=True)
        ktmp = A.alloc("ktmp", [14, 32], F32)
        ktmp2 = A.alloc("ktmp2", [8, 32], F32)
        S.op('act', ACTF(ktmp.ap, kpeall.ap, AF.Square), reads=[kpeall.key], writes=[ktmp.key])
        S.op('dve', RED(sskpe.ap, ktmp.ap), reads=[ktmp.key], writes=[sskpe.key])
        S.op('dve', TT(krg.ap, kpeall.ap, gk[:, 64:96].unsqueeze(1).broadcast_to([128, 14, 32]), ALU.mult),
             reads=[kpeall.key, 'gk'], writes=[krg.key])
        S.op('dve', TT(ktmp.ap[:, 0:8, :], krg.ap[:, 4:12, :], ropeC[:], ALU.mult),
             reads=[krg.key, 'ropeC', ktmp.key], writes=[ktmp.key])
        for g in range(2):
            gs = slice(g * 16, (g + 1) * 16)
            S.op('dve', TT(ktmp2.ap[:, :, gs].rearrange("p t (a e) -> p t a e", a=2),
                           krg.ap[:, 4:12, gs].rearrange("p t (a e) -> p t a e", a=2)[:, :, ::-1, :],
                           ropeS[:, :, gs].rearrange("p t (a e) -> p t a e", a=2), ALU.mult),
                 reads=[krg.key, 'ropeS'], writes=[ktmp2.key])
        S.op('dve', TT(krg.ap[:, 4:12, :], ktmp.ap[:, 0:8, :], ktmp2.ap, ALU.add),
             reads=[ktmp.key, ktmp2.key], writes=[krg.key])
        ckvo = A.alloc("ckvo", [4, 128], F32)
        b = P.get()
        for t in range(4):
            S.op('pe', MM(P.ap(b)[:, t * 128:(t + 1) * 128], ckvTf.ap[:, cols(t)], identf[:]),
                 reads=[ckvTf.key, 'identf'], writes=PK(b))
        S.op('act', ACTF(ckvo.ap, P.ap(b).rearrange("p (t e) -> p t e", e=128), AF.Copy), reads=PK(b), writes=[ckvo.key])
        for s_ in range(2):
            S.dma('sp', DMA(ockv_d[s_, j].rearrange("(t p) e -> p t e", p=128), ckvo.ap[:, 2 * s_:2 * s_ + 2, :]),
                  reads=[ckvo.key], is_output=True)
        cc = A.alloc("cc", [2, 128], F32)
        S.dma('sp', DMA(cc.ap, ckvc_d[j].rearrange("(t p) e -> p t e", p=128)), writes=[cc.key])
        b = P.get()
        for t in range(2):
            S.op('pe', MM(P.ap(b)[:, t * 128:(t + 1) * 128], cc.ap[:, t, :], identf[:]),
                 reads=[cc.key, 'identf'], writes=PK(b))
        S.op('act', ACTF(ckvT.ap[:, NTOK:NTOK + 256], P.ap(b)[:, 0:256], AF.Copy), reads=PK(b),
             writes=[ckvT.k(12), ckvT.k(13)])
        A.release(mk2)

        if stop == 'mla_pre':
            A.release(mk1)
            return
        for hg in range(2):
            wga, wgakey = wga_next
            if hg == 0:
                wga_next = wga_of(1)
            for (t0, ntl, ctx) in SEQS:
                mk3 = A.mark()
                NQ = ntl
                ktiles = ([12, 13] if ctx else []) + list(range(t0, t0 + ntl))
                NK = len(ktiles)
                qT = A.alloc("qT", [4, NQ * 128], BF16)
                kT = A.alloc("kT", [4, NK * 128], BF16)
                vaug = A.alloc("vaug", [NK, 4, 66], BF16)
                att = A.alloc("att", [NQ, 4, 64], BF16)
                abuf = A.alloc("abuf", [NQ, 256], BF16)
                stage = [A.alloc("stage0", [8, 96], F32)] * 2
                qkn = [A.alloc("qkn%d" % i, [8, 96], BF16) for i in range(2)]
                sqs = A.alloc("sqs", [8, 96], BF16)
                ss = A.alloc("ss", [8], F32)
                ms8 = A.alloc("ms8", [8], F32)
                rs8 = A.alloc("rs8", [8], F32)
                na = A.alloc("na", [8], F32)
                nb_ = A.alloc("nb", [8], F32)
                r1 = A.alloc("r1", [4, 32], F32)
                r2 = A.alloc("r2", [4, 32], F32)
                tg = A.alloc("tg", [256], F32)
                mt = [A.alloc("mt%d" % i, [256], BF16) for i in range(2)]
                S.op('pool', MS(vaug.ap, 1.0), writes=[vaug.key])
                pendB = [None]
                for ki, t in enumerate(ktiles):
                    is_ctx = t >= 12
                    lo = 4 if is_ctx else 0
                    qi = ki - (2 if ctx else 0)
                    ccols = slice(NTOK + 128 * (t - 12), NTOK + 128 * (t - 11)) if is_ctx else cols(t)
                    st = stage[ki % 2]
                    qk = qkn[ki % 2]
                    bkv = P.get()
                    S.op('pe', MM(P.ap(bkv), ckvT.ap[:, ccols], wkvb[:, hg * 512:(hg + 1) * 512]),
                         reads=[ckvT.k(t), 'wkvb'], writes=PK(bkv))
                    kvv = P.ap(bkv).rearrange("p (h e) -> p h e", h=4)
                    if not is_ctx:
                        bq = P.get()
                        for c2 in range(2):
                            S.op('pe', MM(P.ap(bq)[:, 0:384], qanT.ap[:, c2, cols(t)], wqb[:, c2, hg * 384:(hg + 1) * 384],
                                          start=(c2 == 0), stop=(c2 == 1)), reads=[qanT.k(t), 'wqb'], writes=PK(bq))
                        qv = P.ap(bq)[:, 0:384].rearrange("p (h e) -> p h e", h=4)
                        bg = P.get()
                        for kc in range(8):
                            S.op('pe', MM(P.ap(bg)[:, 0:256], hsT[:, kc, cols(t)], wga[:, kc, :],
                                          start=(kc == 0), stop=(kc == 7)), reads=[('hsT', t), wgakey], writes=PK(bg))
                        S.op('act', ACTF(sqs.ap[:, 0:4, :], qv, AF.Square), reads=PK(bq), writes=[sqs.k(0)])
                    S.op('act', ACTF(sqs.ap[:, 4:8, 0:64], kvv[:, :, 0:64], AF.Square), reads=PK(bkv), writes=[sqs.k(1)])
                    if not is_ctx:
                        S.op('dve', RED(ss.ap[:, 0:4], sqs.ap[:, 0:4, :]), reads=[sqs.k(0)], writes=[ss.key])
                    S.op('dve', RED(ss.ap[:, 4:8], sqs.ap[:, 4:8, 0:64]), reads=[sqs.k(1)], writes=[ss.key])
                    S.op('dve', TS(ss.ap[:, 4:8], ss.ap[:, 4:8], sskpe.ap[:, t:t + 1], ALU.add),
                         reads=[ss.key, sskpe.key], writes=[ss.key])
                    S.op('dve', TS(ms8.ap[:, lo:8], ss.ap[:, lo:8], 1.0 / 96, ALU.mult, EPS, ALU.add),
                         reads=[ss.key], writes=[ms8.key])
                    if not is_ctx:
                        S.op('dve', TT(st.ap[:, 0:4, :], qv, gq[:, 0:96].unsqueeze(1).broadcast_to([128, 4, 96]), ALU.mult),
                             reads=PK(bq) + ['gq'], writes=[st.key])
                    S.op('dve', TT(st.ap[:, 4:8, 0:64], kvv[:, :, 0:64],
                                   gk[:, 0:64].unsqueeze(1).broadcast_to([128, 4, 64]), ALU.mult),
                         reads=PK(bkv) + ['gk'], writes=[st.key])
                    S.op('pool', CP(st.ap[:, 4:8, 64:96], krg.ap[:, t, :].unsqueeze(1).broadcast_to([128, 4, 32])),
                         reads=[krg.key], writes=[st.key])
                    S.op('act', ACTF(vaug.ap[:, ki, :, 0:64], kvv[:, :, 64:128], AF.Copy), reads=PK(bkv), writes=[vaug.key])
                    if ctx and not is_ctx:
                        tl = t - 4
                        S.op('dve', TT(r1.ap, st.ap[:, 0:4, 64:96], ropeC[:, tl, :].unsqueeze(1).broadcast_to([128, 4, 32]),
                                       ALU.mult), reads=[st.key, 'ropeC'], writes=[r1.key])
                        for g in range(2):
                            gs = slice(g * 16, (g + 1) * 16)
                            gs2 = slice(64 + g * 16, 64 + (g + 1) * 16)
                            S.op('dve', TT(r2.ap[:, :, gs].rearrange("p h (a e) -> p h a e", a=2),
                                           st.ap[:, 0:4, gs2].rearrange("p h (a e) -> p h a e", a=2)[:, :, ::-1, :],
                                           ropeS[:, tl, gs].rearrange("p (a e) -> p a e", a=2).unsqueeze(1)
                                           .broadcast_to([128, 4, 2, 8]), ALU.mult),
                                 reads=[st.key, 'ropeS'], writes=[r2.key])
                        S.op('dve', TT(st.ap[:, 0:4, 64:96], r1.ap, r2.ap, ALU.add), reads=[r1.key, r2.key], writes=[st.key])
                    rsqrt(rs8.ap[:, lo:8], ms8.ap[:, lo:8], na.ap[:, lo:8], nb_.ap[:, lo:8], [ms8.key],
                          [rs8.key, na.key, nb_.key])
                    S.op('dve', TT(qk.ap[:, lo:8, :], st.ap[:, lo:8, :],
                                   rs8.ap[:, lo:8].unsqueeze(2).broadcast_to([128, 8 - lo, 96]), ALU.mult),
                         reads=[st.key, rs8.key], writes=[qk.key])
                    if not is_ctx:
                        S.op('act', ACTF(tg.ap, P.ap(bg)[:, 0:256], AF.Tanh, scale=0.5), reads=PK(bg), writes=[tg.key])
                        S.op('dve', STT(abuf.ap[:, qi, :], tg.ap, 1.0, P.ap(bg)[:, 0:256], ALU.add, ALU.mult),
                             reads=[tg.key] + PK(bg), writes=[abuf.k(qi)])
                    def stageB(qk=qk, lo=lo, is_ctx=is_ctx, qi=qi, ki=ki):
                        bt = P.get(2)
                        for i in range(lo, 8):
                            S.op('pe', MM(P.ap(bt, 2)[0:96, i * 128:(i + 1) * 128], qk.ap[:, i, :], identb[:]),
                                 reads=[qk.key, 'identb'], writes=PK(bt, 2))
                        if not is_ctx:
                            S.op('act', ACTF(qT.ap[0:96, :, qi * 128:(qi + 1) * 128],
                                             P.ap(bt, 2)[0:96, 0:512].rearrange("p (h q) -> p h q", h=4), AF.Copy),
                                 reads=PK(bt, 2), writes=[qT.k(qi)])
                        S.op('dve', CP(kT.ap[0:96, :, ki * 128:(ki + 1) * 128],
                                       P.ap(bt, 2)[0:96, 512:1024].rearrange("p (h q) -> p h q", h=4)),
                             reads=PK(bt, 2), writes=[kT.k(ki)])
                    if pendB[0] is not None:
                        pendB[0]()
                    pendB[0] = stageB
                    pump(1)
                pendB[0]()
                pendB[0] = None
                attention(qT, kT, 96, vaug, att, NQ, lambda qi_, NK=NK: list(range(NK)))
                for qi in range(NQ):
                    t = t0 + qi
                    m_ = mt[qi % 2]
                    S.op('dve', TT(m_.ap, abuf.ap[:, qi, :], att.ap[:, qi, :, :].rearrange("p h e -> p (h e)"), ALU.mult),
                         reads=[abuf.k(qi), att.k(qi)], writes=[m_.key])
                    bT = P.get()
                    for i in range(2):
                        S.op('pe', MM(P.ap(bT)[:, i * 128:(i + 1) * 128], m_.ap[:, i * 128:(i + 1) * 128], identb[:]),
                             reads=[m_.key, 'identb'], writes=PK(bT))
                    S.op('act', ACTF(mixT[:, hg * 2:hg * 2 + 2, cols(t)],
                                     P.ap(bT)[:, 0:256].rearrange("p (c q) -> p c q", c=2), AF.Copy, scale=0.5),
                         reads=PK(bT), writes=[('mixT', t)])
                A.release(mk3)
        A.release(mk1)
        if stop == 'mla':
            return
        mlstm_phase(l)

    def mlstm_phase(l):
        j = l // 2
        W = evwin_d[j]
        mkA = A.mark()
        negR = A.alloc("negR", [NTOK], F32, parts=36)
        wint = A.alloc("wint", [NTOK], F32, parts=36)
        tmq = A.alloc("tmq", [12, 2, 3, 4], F32)
        cst = A.alloc("cst01", [2], F32, parts=36)
        w1_of = lambda h_: load_w([W[:, 928 + h_ * 128:928 + (h_ + 1) * 128], W[:, 1440 + h_ * 128:1440 + (h_ + 1) * 128]])
        w1_next = w1_of(0)
        qk_pair = [(A.alloc("qTbP%d" % i, [1024], BF16), A.alloc("kTbP%d" % i, [1024], BF16)) for i in range(2)]

        fm_done = set()

        def proj_fm(it_, w1v, w1key):
            if it_ in fm_done:
                return
            fm_done.add(it_)
            t0_, ntl_, _c = SEQS[it_ % 3]
            L_ = ntl_ * 128
            a_ = t0_ * 128
            qb_, kb_ = qk_pair[it_ % 2]
            for bi in range((L_ + 511) // 512):
                n = min(512, L_ - bi * 512)
                cs = slice(a_ + bi * 512, a_ + bi * 512 + n)
                lc = slice(bi * 512, bi * 512 + n)
                hk = [('hsT', (a_ + bi * 512) // 128 + i) for i in range(n // 128)]
                bq = P.get()
                for kc in range(8):
                    S.op('pe', MM(P.ap(bq)[:, 0:n], w1v[:, kc, 0:128], hsT[:, kc, cs], start=(kc == 0), stop=(kc == 7)),
                         reads=[w1key] + hk, writes=PK(bq))
                bk = P.get()
                for kc in range(8):
                    S.op('pe', MM(P.ap(bk)[:, 0:n], w1v[:, kc, 128:256], hsT[:, kc, cs], start=(kc == 0), stop=(kc == 7)),
                         reads=[w1key] + hk, writes=PK(bk))
                S.op('act', ACTF(qb_.ap[:, lc], P.ap(bq)[:, 0:n], AF.Copy), reads=PK(bq), writes=[qb_.k(bi)])
                S.op('dve', TS(kb_.ap[:, lc], P.ap(bk)[:, 0:n], 128 ** -0.5, ALU.mult), reads=PK(bk), writes=[kb_.k(bi)])

        mkB = A.mark()
        gtm = A.alloc("gtm", [12, 16], F32)
        T1 = A.alloc("T1", [NTOK], F32, parts=36)
        T2 = A.alloc("T2", [NTOK], F32, parts=36)
        T3 = A.alloc("T3", [NTOK], F32, parts=36)
        T4 = A.alloc("T4", [NTOK], F32, parts=36)
        T5 = A.alloc("T5", [NTOK], F32, parts=36)
        S.op('pool', MS(cst.ap[:, 0:1], 1.0), writes=[cst.key])
        S.op('pool', MS(cst.ap[:, 1:2], 0.0), writes=[cst.key])
        for Tb in (T1, T3):
            S.op('pool', MS(Tb.ap, 0.0), writes=[Tb.key])
        b = P.get()
        for t in range(NT):
            for kc in range(8):
                S.op('pe', MM(P.ap(b)[:, t * 16:(t + 1) * 16], hsT[:, kc, cols(t)], wgate[:, kc, :],
                              start=(kc == 0), stop=(kc == 7)), reads=[('hsT', t), 'wgate'], writes=PK(b))
        S.op('dve', CP(gtm.ap, P.ap(b)[:, 0:192].rearrange("p (t e) -> p t e", e=16)), reads=PK(b), writes=[gtm.key])
        for which, Tdst in ((0, T3), (1, T1)):
            b3 = P.get(3)
            for d_ in range(2):
                g0 = 32 * d_
                for t in range(NT):
                    S.op('pe', MM(P.ap(b3, 3)[g0:g0 + 4, t * 128:(t + 1) * 128],
                                  gtm.ap[:, t, which * 8 + d_ * 4:which * 8 + d_ * 4 + 4], identf[:]),
                         reads=[gtm.key, 'identf'], writes=PK(b3, 3))
            for d_ in range(2):
                g0 = 32 * d_
                S.op('act', ACTF(Tdst.ap[g0:g0 + 4, :], P.ap(b3, 3)[g0:g0 + 4, :], AF.Identity,
                                 bias=bif[g0:g0 + 4, which, j:j + 1], scale=1.0),
                     reads=PK(b3, 3) + ['bif'], writes=[Tdst.key])
        allk = [T1.key, T2.key, T3.key, T4.key, T5.key]
        S.op('dve', STT(T2.ap, T1.ap, -1.0, T1.ap, ALU.mult, ALU.max), reads=[T1.key], writes=[T2.key])
        S.op('act', ACTF(T2.ap, T2.ap, AF.Exp, scale=-1.0), reads=[T2.key], writes=[T2.key])
        S.op('act', ACTF(T2.ap, T2.ap, AF.Ln, bias=1.0, scale=1.0), reads=[T2.key], writes=[T2.key])
        S.op('dve', TS(T1.ap, T1.ap, 0.0, ALU.min), reads=[T1.key], writes=[T1.key])
        S.op('dve', TT(T1.ap, T1.ap, T2.ap, ALU.subtract), reads=[T1.key, T2.key], writes=[T1.key])
        segs = [(t0 * 128, ntl * 128, ctx) for (t0, ntl, ctx) in SEQS]

        def dirview(ap, d_, a, L):
            v = ap[32 * d_:32 * d_ + 4, a:a + L]
            return v[:, ::-1] if d_ else v

        for d_ in range(2):
            g0 = 32 * d_
            for (a, L, ctx) in segs:
                S.op('dve', SCAN(dirview(T2.ap, d_, a, L), cst.ap[g0:g0 + 4, 0:1].broadcast_to([4, L]),
                                 dirview(T1.ap, d_, a, L), 0.0, ALU.mult, ALU.add),
                     reads=[T1.key, cst.key], writes=[T2.key])
        S.op('dve', TT(T3.ap, T3.ap, T2.ap, ALU.subtract), reads=[T3.key, T2.key], writes=[T3.key])
        for d_ in range(2):
            g0 = 32 * d_
            for (a, L, ctx) in segs:
                init = m0t[g0:g0 + 4, j:j + 1] if ctx else 0.0
                S.op('dve', SCAN(dirview(T1.ap, d_, a, L), cst.ap[g0:g0 + 4, 1:2].broadcast_to([4, L]),
                                 dirview(T3.ap, d_, a, L), init, ALU.add, ALU.max),
                     reads=[T3.key, cst.key, 'm0t', T1.key], writes=[T1.key])
        S.op('dve', TT(T2.ap, T2.ap, T1.ap, ALU.add), reads=[T2.key, T1.key], writes=[T2.key])
        for s_ in range(2):
            a, L, _ = segs[s_]
            for d_ in range(2):
                g0 = 32 * d_
                idx = a + L - 1 if d_ == 0 else a
                S.dma('sp', DMA(om_d[s_, j, d_, :].rearrange("(h o) -> h o", o=1), T2.ap[g0:g0 + 4, idx:idx + 1]),
                      reads=[T2.key], is_output=True)
        for d_ in range(2):
            g0 = 32 * d_
            for (a, L, ctx) in segs:
                nck = L // 64
                first = slice(a, a + 64) if d_ == 0 else slice(a + L - 64, a + L)
                if ctx:
                    S.op('dve', CP(T4.ap[g0:g0 + 4, first], m0t[g0:g0 + 4, j:j + 1].broadcast_to([4, 64])),
                         reads=['m0t', T4.key], writes=[T4.key])
                else:
                    S.op('dve', MS(T4.ap[g0:g0 + 4, first], 0.0), reads=[T4.key], writes=[T4.key])
                if d_ == 0:
                    dst = T4.ap[g0:g0 + 4, a + 64:a + L].rearrange("p (k e) -> p k e", e=64)
                    src = T1.ap[g0:g0 + 4, a + 63:a + L - 1:64]
                    src2 = T1.ap[g0:g0 + 4, a + 63:a + L:64]
                else:
                    dst = T4.ap[g0:g0 + 4, a:a + L - 64].rearrange("p (k e) -> p k e", e=64)
                    src = T1.ap[g0:g0 + 4, a + 64:a + L:64]
                    src2 = T1.ap[g0:g0 + 4, a:a + L:64]
                S.op('dve', CP(dst, src.unsqueeze(2).broadcast_to([4, nck - 1, 64])), reads=[T1.key, T4.key], writes=[T4.key])
                S.op('dve', CP(T5.ap[g0:g0 + 4, a:a + L].rearrange("p (k e) -> p k e", e=64),
                               src2.unsqueeze(2).broadcast_to([4, nck, 64])), reads=[T1.key, T5.key], writes=[T5.key])
        for d_ in range(2):
            g0 = 32 * d_
            r = slice(g0, g0 + 4)
            S.op('dve', TT(T4.ap[r, :], T4.ap[r, :], T1.ap[r, :], ALU.subtract), reads=[T4.key, T1.key], writes=[T4.key])
            S.op('act', ACTF(wint.ap[r, :], T4.ap[r, :], AF.Exp), reads=[T4.key], writes=[wint.key])
            S.op('dve', TT(T5.ap[r, :], T3.ap[r, :], T5.ap[r, :], ALU.subtract), reads=[T3.key, T5.key], writes=[T5.key])
            S.op('act', ACTF(T5.ap[r, :], T5.ap[r, :], AF.Exp), reads=[T5.key], writes=[T5.key])
            S.op('act', ACTF(T2.ap[r, :], T2.ap[r, :], AF.Exp, scale=-1.0), reads=[T2.key], writes=[T2.key])
            S.op('dve', TS(negR.ap[r, :], T1.ap[r, :], -1.0, ALU.mult), reads=[T1.key], writes=[negR.key])
        if stop is None:
            proj_fm(0, w1_next[0], w1_next[1])
            proj_fm(1, w1_next[0], w1_next[1])
        b = P.get()
        for t in range(NT):
            for d_ in range(2):
                g0 = 32 * d_
                for q_, Tq in enumerate((T3, T5, T2)):
                    c0 = ((t * 2 + d_) * 3 + q_) * 4
                    S.op('pe', MM(P.ap(b)[:, c0:c0 + 4], Tq.ap[g0:g0 + 4, cols(t)], identf[g0:g0 + 4, g0:g0 + 4]),
                         reads=[Tq.key, 'identf'], writes=PK(b))
        S.op('dve', CP(tmq.ap, P.ap(b)[:, 0:288].rearrange("p (t d q h) -> p t d q h", t=12, d=2, q=3)),
             reads=PK(b), writes=[tmq.key])
        TAP('negR', negR.ap, [36, NTOK], [negR.key])
        TAP('wint', wint.ap, [36, NTOK], [wint.key])
        TAP('tmq', tmq.ap, [128, 12, 2, 3, 4], [tmq.key])
        A.release(mkB)
        if stop == 'rows':
            A.release(mkA)
            return

        ksc = 128 ** -0.5
        for h in range(4):
            w1, w1k = w1_next
            if h == 0:
                proj_fm(0, w1, w1k)
            w2, w2k = load_w([W[:, 1952 + h * 128:1952 + (h + 1) * 128], W[:, 2480 + h * 128:2480 + (h + 1) * 128],
                              W[:, 2992 + h * 128:2992 + (h + 1) * 128]])
            if h < 3:
                w1_next = w1_of(h + 1)
            for si, (t0, ntl, ctx) in enumerate(SEQS):
                if h == 3 and si == 2 and stop is None:
                    preload_out(evwout_d[j])
                mkC = A.mark()
                L = ntl * 128
                a = t0 * 128
                nck = L // 64
                it = h * 3 + si
                qTb, kTb = qk_pair[it % 2]
                ktm = A.alloc("ktm", [ntl, 128], BF16)
                va2 = A.alloc("va2", [ntl, 130], BF16)
                og = A.alloc("og", [ntl, 128], BF16)
                hacc = A.alloc("hacc", [ntl, 128], F32)
                qTw = A.alloc("qTw", [2, L], BF16)
                Cb = A.alloc("Cb", [2, 130], BF16)
                kw = A.alloc("kw", [2, ntl, 128], BF16)
                Cst = A.alloc("Cst", [2, 130], F32)
                wsb = A.alloc("wsb", [2, nck], F32)
                Dt = [A.alloc("Dt%d" % i, [128], BF16) for i in range(2)]
                scT = [A.alloc("scT%d" % i, [128], BF16) for i in range(4)]
                tog = A.alloc("tog", [256], F32)
                ag = A.alloc("ag", [128], F32)
                den = [A.alloc("den%d" % i, [4], F32) for i in range(2)]
                junk = A.alloc("junk", [128], F32)
                ssh = A.alloc("ssh", [ntl], F32)
                msh = A.alloc("msh", [ntl], F32)
                rsh = A.alloc("rsh", [ntl], F32)
                nh1 = A.alloc("nh1", [ntl], F32)
                nh2 = A.alloc("nh2", [ntl], F32)
                htmp = [A.alloc("htmp%d" % i, [128], F32) for i in range(2)]
                hmt = [A.alloc("hmt%d" % i, [128], BF16) for i in range(2)]
                S.op('pool', MS(va2.ap, 1.0), writes=[va2.key])
                for tt in range(ntl):
                    t = t0 + tt
                    bv = P.get()
                    for kc in range(8):
                        S.op('pe', MM(P.ap(bv)[:, 0:384], hsT[:, kc, cols(t)], w2[:, kc, :], start=(kc == 0), stop=(kc == 7)),
                             reads=[w2k, ('hsT', t)], writes=PK(bv))
                    bk2 = P.get()
                    for kc in range(8):
                        S.op('pe', MM(P.ap(bk2)[:, 0:128], hsT[:, kc, cols(t)], w1[:, kc, 128:256],
                                      start=(kc == 0), stop=(kc == 7)), reads=[w1k, ('hsT', t)], writes=PK(bk2))
                    S.op('act', ACTF(ktm.ap[:, tt, :], P.ap(bk2)[:, 0:128], AF.Copy, scale=ksc), reads=PK(bk2), writes=[ktm.k(tt)])
                    S.op('dve', CP(va2.ap[:, tt, 0:128], P.ap(bv)[:, 0:128]), reads=PK(bv), writes=[va2.k(tt)])
                    S.op('act', ACTF(tog.ap, P.ap(bv)[:, 128:384], AF.Tanh, scale=0.5), reads=PK(bv), writes=[tog.key])
                    S.op('dve', STT(ag.ap, tog.ap[:, 128:256], 1.0, P.ap(bv)[:, 256:384], ALU.add, ALU.mult),
                         reads=[tog.key] + PK(bv), writes=[ag.key])
                    S.op('dve', STT(og.ap[:, tt, :], tog.ap[:, 0:128], 1.0, ag.ap, ALU.add, ALU.mult),
                         reads=[tog.key, ag.key], writes=[og.k(tt)])
                if stop == 'mproj':
                    return
                if ctx:
                    for d_ in range(2):
                        S.dma('sp', DMA(Cst.ap[:, d_, 0:128], stC_d[j, d_, h]), writes=[Cst.k(d_)])
                        S.dma('sp', DMA(Cst.ap[:, d_, 128:129], stn_d[j, d_, h].rearrange("(p o) -> p o", o=1)),
                              writes=[Cst.k(d_)])
                else:
                    S.op('pool', MS(Cst.ap, 0.0), writes=[Cst.key])
                for d_ in range(2):
                    S.op('act', ACTF(Cb.ap[:, d_, 0:129], Cst.ap[:, d_, 0:129], AF.Copy), reads=[Cst.k(d_), Cst.key],
                         writes=[Cb.k(d_)])
                if stop == 'mSdma' and ctx:
                    return
                bw = P.get()
                wsr = A.alloc("wsr", [nck], F32, parts=36)
                for d_ in range(2):
                    g0 = 32 * d_
                    samp = wint.ap[g0:g0 + 4, a + 63:a + L:64] if d_ == 0 else wint.ap[g0:g0 + 4, a:a + L:64]
                    S.op('dve', CP(wsr.ap[g0:g0 + 4, :], samp), reads=[wint.key, wsr.key], writes=[wsr.key])
                    S.op('pe', MM(P.ap(bw)[:, d_ * nck:(d_ + 1) * nck], sel[g0:g0 + 4, h, :], wsr.ap[g0:g0 + 4, :]),
                         reads=[wsr.key, 'sel'], writes=PK(bw))
                S.op('dve', CP(wsb.ap, P.ap(bw)[:, 0:2 * nck].rearrange("p (d k) -> p d k", d=2)), reads=PK(bw), writes=[wsb.key])
                for d_ in range(2):
                    g0 = 32 * d_
                    for bi in range(L // 256):
                        n = 256
                        cs = slice(a + bi * 256, a + bi * 256 + n)
                        lc = slice(bi * 256, bi * 256 + n)
                        bb = P.get()
                        S.op('pe', MM(P.ap(bb)[:, 0:n], sel[g0:g0 + 4, h, :], wint.ap[g0:g0 + 4, cs]),
                             reads=[wint.key, 'sel'], writes=PK(bb))
                        S.op('dve', TT(qTw.ap[:, d_, lc], P.ap(bb)[:, 0:n], qTb.ap[:, lc], ALU.mult),
                             reads=PK(bb) + [qTb.key], writes=[qTw.k(d_)])
                if stop == 'mprep':
                    return
                cnt = {'dt': 0, 'sc': 0, 'den': 0}
                done = set()

                def intra(tt, d_):
                    t = t0 + tt
                    g0 = 32 * d_
                    lc = slice(tt * 128, (tt + 1) * 128)
                    bS = P.get()
                    S.op('pe', MM(P.ap(bS)[:, 0:128], kTb.ap[:, lc], qTb.ap[:, lc]), reads=[kTb.key, qTb.key], writes=PK(bS))
                    S.op('pe', MM(P.ap(bS)[:, 128:256], sel[g0:g0 + 4, h, :], negR.ap[g0:g0 + 4, cols(t)], start=True, stop=False),
                         reads=[negR.key, 'sel'], writes=PK(bS))
                    S.op('pe', MM(P.ap(bS)[:, 128:256], identb[:], lmaskb[:, d_, :], start=False, stop=True),
                         reads=['identb', 'lmaskb'], writes=PK(bS))
                    dt_ = Dt[cnt['dt'] % 2]
                    cnt['dt'] += 1
                    sc_ = scT[cnt['sc'] % 4]
                    cnt['sc'] += 1
                    S.op('act', ACTF(dt_.ap, P.ap(bS)[:, 128:256], AF.Exp, bias=tmq.ap[:, t, d_, 0, h:h + 1], scale=1.0),
                         reads=PK(bS) + [tmq.key], writes=[dt_.key])
                    S.op('dve', TT(sc_.ap, P.ap(bS)[:, 0:128], dt_.ap, ALU.mult), reads=PK(bS) + [dt_.key], writes=[sc_.key])
                    return sc_

                def num_open(tt, d_, sc_):
                    bN = P.get()
                    S.op('pe', MM(P.ap(bN)[:, 0:129], sc_.ap, va2.ap[:, tt, 0:129], start=True, stop=False),
                         reads=[sc_.key, va2.k(tt)], writes=PK(bN))
                    return bN

                def inter(tt, d_, hf, bN, last):
                    S.op('pe', MM(P.ap(bN)[hf * 64:(hf + 1) * 64, 0:129], qTw.ap[:, d_, tt * 128 + hf * 64:tt * 128 + (hf + 1) * 64],
                                  Cb.ap[:, d_, 0:129], start=False, stop=last),
                         reads=[qTw.k(d_), Cb.k(d_)], writes=PK(bN))

                def update(tt, d_, hf):
                    cidx = tt * 2 + hf
                    bU = P.get()
                    S.op('pe', MM(P.ap(bU)[:, 0:129], kw.ap[hf * 64:(hf + 1) * 64, d_, tt, :],
                                  va2.ap[hf * 64:(hf + 1) * 64, tt, 0:129]),
                         reads=[kw.k((d_, tt)), va2.k(tt)], writes=PK(bU))
                    S.op('dve', STT(Cb.ap[:, d_, 0:129], Cst.ap[:, d_, 0:129], wsb.ap[:, d_, cidx:cidx + 1],
                                    P.ap(bU)[:, 0:129], ALU.mult, ALU.add),
                         reads=[Cst.k(d_), wsb.key] + PK(bU), writes=[Cb.k(d_)])
                    S.op('dve', STT(Cst.ap[:, d_, 0:129], Cst.ap[:, d_, 0:129], wsb.ap[:, d_, cidx:cidx + 1],
                                    P.ap(bU)[:, 0:129], ALU.mult, ALU.add),
                         reads=[Cst.k(d_), wsb.key] + PK(bU), writes=[Cst.k(d_)])

                def epilogue(tt, d_, bN):
                    t = t0 + tt
                    dn = den[cnt['den'] % 2]
                    cnt['den'] += 1
                    qn = P.ap(bN)[:, 128:129]
                    S.op('dve', TS(dn.ap[:, 0:1], qn, tmq.ap[:, t, d_, 2, h:h + 1], ALU.max), reads=PK(bN) + [tmq.key], writes=[dn.key])
                    S.op('dve', STT(dn.ap[:, 1:2], qn, -1.0, dn.ap[:, 0:1], ALU.mult, ALU.max), reads=PK(bN) + [dn.key], writes=[dn.key])
                    S.op('dve', RCP(dn.ap[:, 2:3], dn.ap[:, 1:2]), reads=[dn.key], writes=[dn.key])
                    if tt not in done:
                        done.add(tt)
                        S.op('dve', TS(hacc.ap[:, tt, :], P.ap(bN)[:, 0:128], dn.ap[:, 2:3], ALU.mult),
                             reads=PK(bN) + [dn.key], writes=[hacc.k(tt)])
                    else:
                        S.op('dve', STT(hacc.ap[:, tt, :], P.ap(bN)[:, 0:128], dn.ap[:, 2:3], hacc.ap[:, tt, :], ALU.mult, ALU.add),
                             reads=PK(bN) + [dn.key, hacc.k(tt)], writes=[hacc.k(tt)])
                        S.op('act', ACTF(junk.ap, hacc.ap[:, tt, :], AF.Square, accum=ssh.ap[:, tt:tt + 1]),
                             reads=[hacc.k(tt)], writes=[junk.key, ssh.k(tt)])

                if stop == 'mSprep' and ctx:
                    return
                def emit_kw(d_, tt):
                    S.op('act', ACTF(kw.ap[:, d_, tt, :], ktm.ap[:, tt, :], AF.Copy,
                                     scale=tmq.ap[:, t0 + tt, d_, 1, h:h + 1]),
                         reads=[ktm.k(tt), tmq.key], writes=[kw.k((d_, tt))])

                for step in range(ntl):
                    tf, tb = step, ntl - 1 - step
                    emit_kw(0, tf)
                    emit_kw(1, tb)
                    scf = intra(tf, 0)
                    scb = intra(tb, 1)
                    bNf = num_open(tf, 0, scf)
                    inter(tf, 0, 0, bNf, False)
                    update(tf, 0, 0)
                    bNb = num_open(tb, 1, scb)
                    inter(tb, 1, 1, bNb, False)
                    update(tb, 1, 1)
                    inter(tf, 0, 1, bNf, True)
                    update(tf, 0, 1)
                    epilogue(tf, 0, bNf)
                    inter(tb, 1, 0, bNb, True)
                    update(tb, 1, 0)
                    epilogue(tb, 1, bNb)
                if h == 0 and si == 0:
                    TAP('hacc', hacc.ap, [128, ntl, 128], [hacc.key])
                    TAP('qTb', qTb.ap, [128, L], [qTb.key], BF16)
                    TAP('kTb', kTb.ap, [128, L], [kTb.key], BF16)
                    TAP('qTw', qTw.ap, [128, 2, L], [qTw.key], BF16)
                    TAP('og', og.ap, [128, ntl, 128], [og.key], BF16)
                if stop == 'mloop':
                    return
                if not ctx:
                    for d_ in range(2):
                        S.dma('sp', DMA(oC_d[si, j, d_, h], Cst.ap[:, d_, 0:128]), reads=[Cst.k(d_)], is_output=True)
                        S.dma('sp', DMA(on_d[si, j, d_, h, :].rearrange("(p o) -> p o", o=1), Cst.ap[:, d_, 128:129]),
                              reads=[Cst.k(d_)], is_output=True)
                if stop == 'mout':
                    return
                if si < 2:
                    proj_fm(it + 1, w1, w1k)
                elif h < 3:
                    proj_fm(it + 1, w1_next[0], w1_next[1])
                S.op('dve', TS(msh.ap, ssh.ap, 1.0 / 128, ALU.mult, EPS, ALU.add), reads=[ssh.key], writes=[msh.key])
                rsqrt(rsh.ap, msh.ap, nh1.ap, nh2.ap, [msh.key], [rsh.key, nh1.key, nh2.key])
                bT = None
                for tt in range(ntl):
                    t = t0 + tt
                    ht = htmp[tt % 2]
                    hm = hmt[tt % 2]
                    S.op('dve', STT(ht.ap, hacc.ap[:, tt, :], rsh.ap[:, tt:tt + 1], ghn[:, h * 128:(h + 1) * 128], ALU.mult, ALU.mult),
                         reads=[hacc.k(tt), rsh.key, 'ghn'], writes=[ht.key])
                    S.op('dve', TT(hm.ap, ht.ap, og.ap[:, tt, :], ALU.mult), reads=[ht.key, og.k(tt)], writes=[hm.key])
                    if tt % 4 == 0:
                        bT = P.get()
                    S.op('pe', MM(P.ap(bT)[:, (tt % 4) * 128:(tt % 4 + 1) * 128], hm.ap, identb[:]),
                         reads=[hm.key, 'identb'], writes=PK(bT))
                    if tt % 4 == 3 or tt == ntl - 1:
                        n = tt % 4 + 1
                        tfirst = t - (n - 1)
                        S.op('act', ACTF(mixT[:, 4 + h, cols(tfirst, n)], P.ap(bT)[:, 0:n * 128], AF.Copy),
                             reads=PK(bT), writes=[('mixT', tfirst + i) for i in range(n)])
                A.release(mkC)
                if stop is not None and stop.startswith('mstep') and (h * 3 + si + 1) >= int(stop[5:]):
                    return
        A.release(mkA)

    def na_blocks(qi):
        if qi in (0, 1):
            js = [0, 1, 2, 3]
        elif qi in (6, 7):
            js = [4, 5, 6, 7]
        else:
            js = list(range(qi - 2, qi + 3))
        return [0, 1] + [2 + jt for jt in js]

    def odd_layer(l):
        j = l // 2
        W = odwin_d[j]
        S.dma('sp', DMA(gq[:, 0:64], odqn_d[j].partition_broadcast(128)), writes=['gq'])
        S.dma('sp', DMA(gk[:, 0:64], odkn_d[j].partition_broadcast(128)), writes=['gk'])
        S.op('dve', TS(gq[:, 0:64], gq[:, 0:64], 0.125, ALU.mult), reads=['gq'], writes=['gq'])
        segs = [(t0 * 128, ntl * 128) for (t0, ntl, ctx) in SEQS]
        conv_w = lambda c_: load_w([W[:, i * 512 + c_ * 128:i * 512 + (c_ + 1) * 128] for i in range(4)])
        nxt = pre_w.pop('c0') if 'c0' in pre_w else conv_w(0)
        wqk_of = lambda hg_: load_w([W[:, 2048 + hg_ * 256:2048 + (hg_ + 1) * 256], W[:, 2560 + hg_ * 256:2560 + (hg_ + 1) * 256]])
        wvg_of = lambda hg_: load_w([W[:, 3072 + hg_ * 256:3072 + (hg_ + 1) * 256], W[:, 3584 + hg_ * 256:3584 + (hg_ + 1) * 256]])
        na_pre = {}
        for c4 in range(4):
            wv, wk_ = nxt
            if c4 < 3:
                nxt = conv_w(c4 + 1)
            else:
                na_pre['qk'] = wqk_of(0)
            mk = A.mark()
            u = A.alloc("u", [NTOK], F32)
            ab = A.alloc("ab", [NTOK], F32)
            y = A.alloc("y", [NTOK], F32)
            xs = A.alloc("xs", [512], F32)
            tgc = A.alloc("tgc", [512], F32)
            aa = A.alloc("aa", [512], F32)
            for blk in range(3):
                cs = slice(blk * 512, (blk + 1) * 512)
                hk = [('hsT', 4 * blk + i) for i in range(4)]
                bs = []
                for i in range(4):
                    b = P.get()
                    bs.append(b)
                    for kc in range(8):
                        S.op('pe', MM(P.ap(b), wv[:, kc, i * 128:(i + 1) * 128], hsT[:, kc, cs], start=(kc == 0), stop=(kc == 7)),
                             reads=[wk_] + hk, writes=PK(b))
                bx, bb, bc_, bg = bs
                S.op('act', ACTF(xs.ap, P.ap(bx), AF.Copy), reads=PK(bx), writes=[xs.key])
                S.op('dve', TT(u.ap[:, cs], P.ap(bc_), xs.ap, ALU.mult), reads=PK(bc_) + [xs.key], writes=[u.k(blk)])
                S.op('act', ACTF(tgc.ap, P.ap(bg), AF.Tanh, scale=0.5), reads=PK(bg), writes=[tgc.key])
                S.op('dve', STT(aa.ap, tgc.ap, 1.0, P.ap(bg), ALU.add, ALU.mult), reads=[tgc.key] + PK(bg), writes=[aa.key])
                S.op('dve', TT(ab.ap[:, cs], P.ap(bb), aa.ap, ALU.mult), reads=PK(bb) + [aa.key], writes=[ab.k(blk)])
                pump(1)
            S.op('dve', TS(y.ap, u.ap, convp[:, j, c4, 1:2], ALU.mult, convp[:, j, c4, 3:4], ALU.add),
                 reads=[u.key, ('convp', j)], writes=[y.key])
            for (a, L) in segs:
                S.op('dve', STT(y.ap[:, a + 1:a + L], u.ap[:, a:a + L - 1], convp[:, j, c4, 0:1], y.ap[:, a + 1:a + L], ALU.mult, ALU.add),
                     reads=[u.key, y.key, ('convp', j)], writes=[y.key])
                S.op('dve', STT(y.ap[:, a:a + L - 1], u.ap[:, a + 1:a + L], convp[:, j, c4, 2:3], y.ap[:, a:a + L - 1], ALU.mult, ALU.add),
                     reads=[u.key, y.key, ('convp', j)], writes=[y.key])
            S.op('dve', TT(mixT[:, c4, :], y.ap, ab.ap, ALU.mult), reads=[y.key, ab.key],
                 writes=[('mixT', t) for t in range(NT)])
            A.release(mk)
        if stop == 'conv':
            return
        for hg in range(2):
            wqk, wqkk = na_pre.pop('qk')
            wvg, wvgk = wvg_of(hg)
            if hg == 0:
                na_pre['qk'] = wqk_of(1)
            for si, (t0, ntl, ctx) in enumerate(SEQS):
                if hg == 1 and si == 2 and stop is None:
                    preload_out(odwout_d[j])
                mk3 = A.mark()
                NQ = ntl
                ktiles = ([12, 13] if ctx else []) + list(range(t0, t0 + ntl))
                NK = len(ktiles)
                qT = A.alloc("nqT", [4, NQ * 128], BF16)
                kT = A.alloc("nkT", [4, NK * 128], BF16)
                vaug = A.alloc("nvaug", [NK, 4, 66], BF16)
                att = A.alloc("natt", [NQ, 4, 64], BF16)
                abuf = A.alloc("nabuf", [NQ, 256], BF16)
                stage = [A.alloc("nstage%d" % i, [8, 64], F32) for i in range(2)]
                qkn = [A.alloc("nqkn%d" % i, [8, 64], BF16) for i in range(2)]
                sqs = A.alloc("nsqs", [8, 64], BF16)
                ss = A.alloc("nss", [8], F32)
                ms8 = A.alloc("nms8", [8], F32)
                rs8 = A.alloc("nrs8", [8], F32)
                na = A.alloc("nna", [8], F32)
                nb_ = A.alloc("nnb", [8], F32)
                tg = A.alloc("ntg", [256], F32)
                mt = [A.alloc("nmt%d" % i, [256], BF16) for i in range(2)]
                kvf = [A.alloc("kvf%d" % i, [2, 4, 64], F32) for i in range(2)]
                kvout = A.alloc("kvout", [2, 2, 4, 64], F32) if not ctx else None
                S.op('pool', MS(vaug.ap, 1.0), writes=[vaug.key])
                pendB = [None]
                for ki, t in enumerate(ktiles):
                    is_ctx = t >= 12
                    qi = ki - (2 if ctx else 0)
                    st = stage[ki % 2]
                    qk = qkn[ki % 2]
                    kf = kvf[ki % 2]
                    if is_ctx:
                        r0 = (t - 12) * 128
                        S.dma('sp', DMA(kf.ap[:, 0, :, :], nakc_d[j, r0:r0 + 128, hg * 4:(hg + 1) * 4, :]), writes=[kf.k(0)])
                        S.dma('sp', DMA(kf.ap[:, 1, :, :], navc_d[j, r0:r0 + 128, hg * 4:(hg + 1) * 4, :]), writes=[kf.k(1)])
                        S.op('dve', CP(qk.ap[:, 4:8, :], kf.ap[:, 0, :, :]), reads=[kf.k(0)], writes=[qk.key])
                        S.op('act', ACTF(vaug.ap[:, ki, :, 0:64], kf.ap[:, 1, :, :], AF.Copy), reads=[kf.k(1)], writes=[vaug.key])
                        lo = 4
                    else:
                        lo = 0
                        bqk = P.get()
                        for kc in range(8):
                            S.op('pe', MM(P.ap(bqk), hsT[:, kc, cols(t)], wqk[:, kc, :], start=(kc == 0), stop=(kc == 7)),
                                 reads=[('hsT', t), wqkk], writes=PK(bqk))
                        bvg = P.get()
                        for kc in range(8):
                            S.op('pe', MM(P.ap(bvg), hsT[:, kc, cols(t)], wvg[:, kc, :], start=(kc == 0), stop=(kc == 7)),
                                 reads=[('hsT', t), wvgk], writes=PK(bvg))
                        qkv = P.ap(bqk).rearrange("p (h e) -> p h e", e=64)
                        vv = P.ap(bvg)[:, 0:256].rearrange("p (h e) -> p h e", e=64)
                        S.op('act', ACTF(sqs.ap, qkv, AF.Square), reads=PK(bqk), writes=[sqs.key])
                        S.op('dve', RED(ss.ap, sqs.ap), reads=[sqs.key], writes=[ss.key])
                        S.op('dve', TS(ms8.ap, ss.ap, 1.0 / 64, ALU.mult, EPS, ALU.add), reads=[ss.key], writes=[ms8.key])
                        S.op('dve', TT(st.ap[:, 0:4, :], qkv[:, 0:4, :], gq[:, 0:64].unsqueeze(1).broadcast_to([128, 4, 64]), ALU.mult),
                             reads=PK(bqk) + ['gq'], writes=[st.key])
                        S.op('dve', TT(st.ap[:, 4:8, :], qkv[:, 4:8, :], gk[:, 0:64].unsqueeze(1).broadcast_to([128, 4, 64]), ALU.mult),
                             reads=PK(bqk) + ['gk'], writes=[st.key])
                        S.op('act', ACTF(vaug.ap[:, ki, :, 0:64], vv, AF.Copy), reads=PK(bvg), writes=[vaug.key])
                        if not ctx:
                            S.op('act', ACTF(kvout.ap[:, 1, qi, :, :], vv, AF.Copy), reads=PK(bvg), writes=[kvout.k((1, qi))])
                        rsqrt(rs8.ap, ms8.ap, na.ap, nb_.ap, [ms8.key], [rs8.key, na.key, nb_.key])
                        S.op('dve', TT(qk.ap, st.ap, rs8.ap.unsqueeze(2).broadcast_to([128, 8, 64]), ALU.mult),
                             reads=[st.key, rs8.key], writes=[qk.key])
                        S.op('act', ACTF(tg.ap, P.ap(bvg)[:, 256:512], AF.Tanh, scale=0.5), reads=PK(bvg), writes=[tg.key])
                        S.op('dve', STT(abuf.ap[:, qi, :], tg.ap, 1.0, P.ap(bvg)[:, 256:512], ALU.add, ALU.mult),
                             reads=[tg.key] + PK(bvg), writes=[abuf.k(qi)])
                        if not ctx:
                            for h_ in range(4):
                                S.op('act', ACTF(kvout.ap[:, 0, qi, h_, :], st.ap[:, 4 + h_, :], AF.Copy,
                                                 scale=rs8.ap[:, 4 + h_:5 + h_]),
                                     reads=[st.key, rs8.key], writes=[kvout.k((0, qi))])
                    def stageB(qk=qk, lo=lo, is_ctx=is_ctx, qi=qi, ki=ki):
                        bt = P.get(2)
                        for i in range(lo, 8):
                            S.op('pe', MM(P.ap(bt, 2)[0:64, i * 128:(i + 1) * 128], qk.ap[:, i, :], identb[:]),
                                 reads=[qk.key, 'identb'], writes=PK(bt, 2))
                        if not is_ctx:
                            S.op('act', ACTF(qT.ap[0:64, :, qi * 128:(qi + 1) * 128],
                                             P.ap(bt, 2)[0:64, 0:512].rearrange("p (h q) -> p h q", h=4), AF.Copy),
                                 reads=PK(bt, 2), writes=[qT.k(qi)])
                        S.op('dve', CP(kT.ap[0:64, :, ki * 128:(ki + 1) * 128],
                                       P.ap(bt, 2)[0:64, 512:1024].rearrange("p (h q) -> p h q", h=4)),
                             reads=PK(bt, 2), writes=[kT.k(ki)])
                    if pendB[0] is not None:
                        pendB[0]()
                    pendB[0] = stageB
                    pump(1)
                pendB[0]()
                pendB[0] = None
                if not ctx:
                    for kv_, dst_ in ((0, onak_d), (1, onav_d)):
                        for qo in range(NQ):
                            S.dma('pool', DMA(dst_[si, j, qo * 128:(qo + 1) * 128, hg * 4:(hg + 1) * 4, :],
                                              kvout.ap[:, kv_, qo, :, :]), reads=[kvout.key], is_output=True)
                if ctx:
                    xraw = [A.alloc("xraw%d" % i, [1024], BF16) for i in range(2)]
                    xtab = [[A.alloc("xtab%d_%d" % (i, k_), [1024], BF16) for k_ in range(2)] for i in range(2)]

                    def hook(hh, hg=hg):
                        hd = hg * 4 + hh
                        xr = xraw[hh % 2]
                        S.dma('pool', DMA(xr.ap, rpbx_d[j, hd]), writes=[xr.key])
                        for k_ in range(2):
                            S.op('dve', TT(xtab[hh % 2][k_].ap, xr.ap, cmaskb[:, k_, :], ALU.add),
                                 reads=[xr.key, 'cmaskb'], writes=[xtab[hh % 2][k_].key])

                    def bias_fn(hh, qi, kidx):
                        if kidx < 2:
                            return None
                        jt = kidx - 2
                        w0 = 7 - 2 * (jt - qi)
                        k_ = 0 if qi in (0, 1, 6, 7) else 1
                        tb_ = xtab[hh % 2][k_]
                        return (tb_.ap[:, w0 * 64:(w0 + 2) * 64], [tb_.key])

                    if stop == 'na_prep':
                        return
                    attention(qT, kT, 64, vaug, att, NQ, na_blocks, bias_fn=bias_fn, head_hook=hook)
                else:
                    attention(qT, kT, 64, vaug, att, NQ, lambda qi_, NK=NK: list(range(NK)))
                if stop == 'na_att':
                    return
                for qi in range(NQ):
                    t = t0 + qi
                    m_ = mt[qi % 2]
                    S.op('dve', TT(m_.ap, abuf.ap[:, qi, :], att.ap[:, qi, :, :].rearrange("p h e -> p (h e)"), ALU.mult),
                         reads=[abuf.k(qi), att.k(qi)], writes=[m_.key])
                    bT = P.get()
                    for i in range(2):
                        S.op('pe', MM(P.ap(bT)[:, i * 128:(i + 1) * 128], m_.ap[:, i * 128:(i + 1) * 128], identb[:]),
                             reads=[m_.key, 'identb'], writes=PK(bT))
                    S.op('act', ACTF(mixT[:, 4 + hg * 2:4 + hg * 2 + 2, cols(t)],
                                     P.ap(bT)[:, 0:256].rearrange("p (c q) -> p c q", c=2), AF.Copy, scale=0.5),
                         reads=PK(bT), writes=[('mixT', t)])
                A.release(mk3)
                if stop == 'na_%d' % si:
                    return

    queue_mod(0)
    load_x()
    pump(24)
    for l in range(depth):
        if stop is None:
            if l % 2 == 0:
                pre_w['wa'] = load_w([evwin_d[l // 2][:, 0:416]])
            else:
                Wo = odwin_d[l // 2]
                pre_w['c0'] = load_w([Wo[:, i * 512:i * 512 + 128] for i in range(4)])
        phase_norm(l)
        if stop == 'norm':
            break
        if l + 1 < depth:
            queue_mod(l + 1)
        TAP('hsT%d' % l, hsT[:], [128, 8, NTOK], [('hsT', t) for t in range(NT)], BF16)
        last = (l == depth - 1)
        sb_ = (lambda blk: store_tiles(range(4 * blk, 4 * blk + 4))) if last else None
        if l % 2 == 0:
            even_layer(l)
            TAP('mixT%d' % l, mixT[:], [128, 8, NTOK], [('mixT', t) for t in range(NT)], BF16)
            pump(24)
            phase_out(l, evwout_d[l // 2], sb_)
        else:
            odd_layer(l)
            pump(24)
            phase_out(l, odwout_d[l // 2], sb_)
    if depth == 0:
        store_y()
    counts, nw = S.emit(nc, es)
    info = {'ops': counts, 'waits': nw, 'arena_peak': A.peak, 'taps': tap_list}
    es.close()
    return nc, info


def _constants():
    n = np.arange(1024)
    row = (n // GRID_W).astype(np.float32)
    col = (n % GRID_W).astype(np.float32)
    inv = (np.float32(ROPE_BASE) ** (-np.arange(8, dtype=np.float32) / np.float32(8))).astype(np.float32)
    ar = (row[:, None] * inv[None, :]).astype(np.float32)
    ac = (col[:, None] * inv[None, :]).astype(np.float32)
    C = np.concatenate([np.cos(ar), np.cos(ar), np.cos(ac), np.cos(ac)], axis=1).astype(np.float32)
    Sg = np.concatenate([-np.sin(ar), np.sin(ar), -np.sin(ac), np.sin(ac)], axis=1).astype(np.float32)
    rope_cs = np.stack([C, Sg]).astype(np.float32)
    s = np.arange(128)[:, None]
    t = np.arange(128)[None, :]
    same = (s // 64) == (t // 64)
    lmask = np.stack([np.where(same & (s <= t), 0.0, NEG), np.where(same & (s >= t), 0.0, NEG)]).astype(np.float32)
    kl = np.arange(2)[:, None, None, None]
    kc = np.arange(64)[None, :, None, None]
    w = np.arange(16)[None, None, :, None]
    qc = np.arange(64)[None, None, None, :]
    dr = np.broadcast_to(7 - w + kl, (2, 64, 16, 64))
    dc = np.broadcast_to(kc - qc, (2, 64, 16, 64))
    c0 = np.clip(qc - 8, 0, 48)
    col_ok = np.broadcast_to((kc >= c0) & (kc < c0 + 16), (2, 64, 16, 64))
    ok_full = col_ok & (np.abs(dr) <= 7)
    ok_int = ok_full & (dr >= -4) & (dr <= 3)
    cmask = np.stack([np.where(ok_full, 0.0, NEG), np.where(ok_int, 0.0, NEG)]).astype(np.float32).reshape(2, 128, 1024)
    idx_r = np.clip(dr + 7, 0, 14).reshape(128, 1024)
    idx_c = np.clip(dc + 15, 0, 30).reshape(128, 1024)
    return rope_cs, lmask, cmask, idx_r, idx_c


_CACHE = {}


def _get_program(depth, taps=(), stop=None):
    key = (depth, tuple(taps), stop)
    if key not in _CACHE:
        _CACHE[key] = build_nc(depth, taps, stop)
    return _CACHE[key]


def kernel(x_prompt, x_sample, c, cache_mla_ckv, cache_mla_kpe, state_mlstm_C, state_mlstm_n, state_mlstm_m,
           cache_na_k, cache_na_v, c_ctx, norm_w, ada_w, ada_b,
           ev_w_in, ev_q_a_norm, ev_kv_a_norm, ev_w_q_b, ev_w_kv_b, ev_q_norm, ev_k_norm, ev_b_i, ev_b_f,
           ev_h_norm, ev_w_out,
           od_w_in, od_conv_w, od_conv_b, od_q_norm, od_k_norm, od_rpb, od_w_out, _depth=4, _taps=(), _raw=False, _stop=None):
    f = lambda a: np.ascontiguousarray(np.asarray(a), dtype=np.float32)
    x_prompt, x_sample, c, c_ctx = f(x_prompt), f(x_sample), f(c), f(c_ctx)
    rope_cs, lmask, cmask, idx_r, idx_c = _constants()
    od_rpb = f(od_rpb)
    rpbx = np.ascontiguousarray(od_rpb[:, :, idx_r, idx_c])
    shared = {
        "norm_w": f(norm_w), "ada_w": f(ada_w), "ada_b": f(ada_b),
        "ev_w_in": f(ev_w_in), "ev_q_a_norm": f(ev_q_a_norm), "ev_kv_a_norm": f(ev_kv_a_norm),
        "ev_w_q_b": f(ev_w_q_b), "ev_w_kv_b": f(ev_w_kv_b), "ev_q_norm": f(ev_q_norm), "ev_k_norm": f(ev_k_norm),
        "ev_b_i": f(ev_b_i), "ev_b_f": f(ev_b_f), "ev_h_norm": f(ev_h_norm), "ev_w_out": f(ev_w_out),
        "od_w_in": f(od_w_in), "od_conv_w": f(od_conv_w), "od_conv_b": f(od_conv_b), "od_q_norm": f(od_q_norm),
        "od_k_norm": f(od_k_norm), "od_w_out": f(od_w_out), "rpbx": rpbx, "cmask": cmask, "rope_cs": rope_cs,
        "lmask": lmask,
    }
    cm_ckv, cm_kpe = f(cache_mla_ckv), f(cache_mla_kpe)
    sC, sn, sm = f(state_mlstm_C), f(state_mlstm_n), f(state_mlstm_m)
    nk, nv = f(cache_na_k), f(cache_na_v)
    in_maps = []
    for core in range(8):
        b = core // 4
        m = dict(shared)
        m["x"] = np.ascontiguousarray(np.concatenate([x_prompt[2 * core], x_prompt[2 * core + 1], x_sample[b]], axis=0))
        m["cond"] = np.ascontiguousarray(np.stack([c_ctx, c[b]]))
        m["ckv_c"] = cm_ckv[b]
        m["kpe_c"] = cm_kpe[b]
        m["stC"] = sC[b]
        m["stn"] = sn[b]
        m["stm"] = sm[b]
        m["nak_c"] = nk[b]
        m["nav_c"] = nv[b]
        in_maps.append(m)
    nc, info = _get_program(_depth, _taps, _stop)
    res = run_bass_kernel_spmd(nc, in_maps, core_ids=list(range(8)))
    R = res.results
    if _raw:
        return R, info
    yp = np.concatenate([R[i]["yp"].reshape(2, 256, D) for i in range(8)], axis=0)
    ys = np.stack([R[0]["ys"], R[4]["ys"]], axis=0)
    cat = lambda k: np.concatenate([R[i][k] for i in range(8)], axis=0)
    return (yp.astype(np.float32), ys.astype(np.float32), cat("o_ckv"), cat("o_kpe"), cat("o_C"), cat("o_n"),
            cat("o_m"), cat("o_nak"), cat("o_nav"))
```

```python
import os
from contextlib import ExitStack

import numpy as np
import concourse.bass as bass
import concourse.mybir as mybir
from concourse.bass_utils import run_bass_kernel_spmd

F32 = mybir.dt.float32
BF16 = mybir.dt.bfloat16
I32 = mybir.dt.int32
AF = mybir.ActivationFunctionType
ALU = mybir.AluOpType
AX = mybir.AxisListType

D = 1024
NT = 12
NTOK = 1536
EPS = 1e-6
NEG = -30000.0
GRID_W = 64
ROPE_BASE = 10000.0

ENGS = ['pe', 'act', 'dve', 'pool', 'sp']
N_DMA_SEMS = 24
SAME_ENG_WINDOW = 6


class _Op:
    __slots__ = ('eng', 'fn', 'deps_c', 'deps_d', 'sig', 'idx', 'is_dma', 'dsem', 'dval')


class Sched:
    def __init__(self):
        self.ops = {e: [] for e in ENGS}
        self.bufs = {}
        self.dma_cnt = [0] * N_DMA_SEMS
        self.dma_last = [None] * N_DMA_SEMS
        self.dma_rr = 0
        self.out_tokens = []

    def reg(self, name, rng=None):
        self.bufs[name] = {'range': rng, 'subs': {}}

    def _overlapping(self, name):
        b = self.bufs[name]
        res = [name]
        if b['range'] is None:
            return res
        a0, a1 = b['range']
        for n2, b2 in self.bufs.items():
            if n2 == name or b2['range'] is None:
                continue
            c0, c1 = b2['range']
            if c0 < a1 and a0 < c1 and b2['subs']:
                res.append(n2)
        return res

    @staticmethod
    def _norm(key):
        if isinstance(key, tuple):
            return key[0], key[1]
        return key, None

    def _collect(self, key, is_write, dc, dd):
        name, sub = self._norm(key)
        if name not in self.bufs:
            self.reg(name)
        for n2 in self._overlapping(name):
            subs = self.bufs[n2]['subs']
            if n2 == name and sub is not None:
                cands = [s for s in (sub, None) if s in subs]
            else:
                cands = list(subs.keys())
            for s in cands:
                st = subs[s]
                toks = [st[0]] if st[0] is not None else []
                if is_write:
                    toks = toks + list(st[1].values())
                for t in toks:
                    if t[0] == 'dma':
                        dd[t[1]] = max(dd.get(t[1], 0), t[2])
                    else:
                        dc[t[0]] = max(dc.get(t[0], -1), t[1])

    def _update(self, key, is_write, tok):
        name, sub = self._norm(key)
        subs = self.bufs[name]['subs']
        if is_write:
            if sub is None:
                for n2 in self._overlapping(name):
                    if n2 != name:
                        self.bufs[n2]['subs'] = {}
                subs.clear()
            subs[sub] = [tok, {}]
        else:
            if sub not in subs:
                subs[sub] = [None, {}]
            rk = tok[0] if tok[0] != 'dma' else ('dma', tok[1])
            subs[sub][1][rk] = tok

    def op(self, eng, fn, reads=(), writes=()):
        o = _Op()
        o.eng, o.fn, o.sig, o.is_dma = eng, fn, False, False
        o.idx = len(self.ops[eng])
        dc, dd = {}, {}
        for k in reads:
            self._collect(k, False, dc, dd)
        for k in writes:
            self._collect(k, True, dc, dd)
        o.deps_c, o.deps_d = dc, dd
        tok = (eng, o.idx)
        self.ops[eng].append(o)
        for k in reads:
            self._update(k, False, tok)
        for k in writes:
            self._update(k, True, tok)
        return o

    def dma(self, eng, fn, reads=(), writes=(), is_output=False):
        o = _Op()
        o.eng, o.fn, o.sig, o.is_dma = eng, fn, False, True
        o.idx = len(self.ops[eng])
        s = self.dma_rr
        self.dma_rr = (self.dma_rr + 1) % N_DMA_SEMS
        dc, dd = {}, {}
        if self.dma_last[s] is not None:
            dd[s] = self.dma_cnt[s]
        for k in reads:
            self._collect(k, False, dc, dd)
        for k in writes:
            self._collect(k, True, dc, dd)
        self.dma_cnt[s] += 16
        o.dsem, o.dval = s, self.dma_cnt[s]
        self.dma_last[s] = o
        o.deps_c, o.deps_d = dc, dd
        tok = ('dma', s, o.dval)
        self.ops[eng].append(o)
        for k in reads:
            self._update(k, False, tok)
        for k in writes:
            self._update(k, True, tok)
        if is_output:
            self.out_tokens.append(tok)
        return o

    @staticmethod
    def _need(o, x, j):
        if x != o.eng or o.is_dma:
            return True
        if o.eng == 'pe':
            return False
        return (o.idx - j) <= SAME_ENG_WINDOW

    def emit(self, nc, es):
        for e in ENGS:
            for o in self.ops[e]:
                for x, j in o.deps_c.items():
                    if self._need(o, x, j):
                        self.ops[x][j].sig = True
        cnt = {}
        for e in ENGS:
            c = 0
            arr = []
            for o in self.ops[e]:
                if o.sig and not o.is_dma:
                    c += 1
                arr.append(c)
            cnt[e] = arr
        sems = {e: es.enter_context(nc.semaphore('s_' + e)) for e in ENGS}
        dsems = [es.enter_context(nc.semaphore('d%d' % i)) for i in range(N_DMA_SEMS)]
        block = es.enter_context(nc.Block())
        handles = {'pe': block.tensor, 'act': block.scalar, 'dve': block.vector,
                   'pool': block.gpsimd, 'sp': block.sync}
        nwaits = [0]

        def run_engine(e):
            def body(eng):
                waited_c = {}
                waited_d = {}
                for o in self.ops[e]:
                    for x, j in o.deps_c.items():
                        if not self._need(o, x, j):
                            continue
                        v = cnt[x][j]
                        if v > waited_c.get(x, 0):
                            eng.wait_ge(sems[x], v)
                            waited_c[x] = v
                            nwaits[0] += 1
                    for s, v in o.deps_d.items():
                        if v > waited_d.get(s, 0):
                            eng.wait_ge(dsems[s], v)
                            waited_d[s] = v
                            nwaits[0] += 1
                    ins = o.fn(eng)
                    if o.is_dma:
                        ins.then_inc(dsems[o.dsem], 16)
                    elif o.sig:
                        ins.then_inc(sems[e], 1)
                if e == 'sp':
                    fin = {}
                    for t in self.out_tokens:
                        fin[t[1]] = max(fin.get(t[1], 0), t[2])
                    for s, v in fin.items():
                        if v > waited_d.get(s, 0):
                            eng.wait_ge(dsems[s], v)
            return body

        for e in ENGS:
            handles[e](run_engine(e))
        return {e: len(self.ops[e]) for e in ENGS}, nwaits[0]


def MM(out, lhsT, rhs, start=True, stop=True):
    return lambda e: e.matmul(out, lhsT=lhsT, rhs=rhs, start=start, stop=stop)


def ACTF(out, in_, func, bias=None, scale=None, accum=None):
    def f(e):
        kw = {}
        if bias is not None:
            kw['bias'] = bias
        if scale is not None:
            kw['scale'] = scale
        if accum is not None:
            kw['accum_out'] = accum
        return e.activation(out=out, in_=in_, func=func, **kw)
    return f


def TT(out, a, b, op):
    return lambda e: e.tensor_tensor(out=out, in0=a, in1=b, op=op)


def TS(out, a, s1, op0, s2=None, op1=None):
    def f(e):
        if op1 is None:
            return e.tensor_scalar(out=out, in0=a, scalar1=s1, scalar2=None, op0=op0)
        return e.tensor_scalar(out=out, in0=a, scalar1=s1, scalar2=s2, op0=op0, op1=op1)
    return f


def STT(out, a, s, b, op0, op1):
    return lambda e: e.scalar_tensor_tensor(out=out, in0=a, scalar=s, in1=b, op0=op0, op1=op1)


def CP(out, in_):
    return lambda e: e.tensor_copy(out=out, in_=in_)


def RED(out, in_, op=ALU.add):
    return lambda e: e.tensor_reduce(out=out, in_=in_, axis=AX.X, op=op)


def MS(ap, v):
    return lambda e: e.memset(ap, v)


def RCP(out, in_):
    return lambda e: e.reciprocal(out=out, in_=in_)


def SCAN(out, d0, d1, init, op0, op1):
    return lambda e: e.tensor_tensor_scan(out=out, data0=d0, data1=d1, initial=init, op0=op0, op1=op1)


def DMA(out, in_, **kw):
    return lambda e: e.dma_start(out=out, in_=in_, **kw)


def ASEL(out, in_, pattern, cmp, fill, base, cm):
    return lambda e: e.affine_select(out=out, in_=in_, pattern=pattern, compare_op=cmp, fill=fill,
                                     base=base, channel_multiplier=cm)


class Buf:
    __slots__ = ('ap', 'key')

    def __init__(self, ap, key):
        self.ap, self.key = ap, key

    def k(self, sub):
        return (self.key, sub)


class Arena:
    BLK = 32

    def __init__(self, S, tensor, words):
        self.S, self.t, self.words = S, tensor, words
        self.top = 0
        self.ctr = 0
        self.peak = 0
        self.live = []
        self.blocks = [dict() for _ in range(words // self.BLK + 2)]

    def mark(self):
        return self.top

    @staticmethod
    def _fold(dst, tok):
        rk = tok[0] if tok[0] != 'dma' else ('dma', tok[1])
        old = dst.get(rk)
        if old is None or tok[-1] > old[-1]:
            dst[rk] = tok

    def release(self, m):
        while self.live and self.live[-1][0] >= m:
            off, w, key = self.live.pop()
            st = self.S.bufs.pop(key, None)
            if st is None:
                continue
            toks = []
            for sub, (wt, rd) in st['subs'].items():
                if wt is not None:
                    toks.append(wt)
                toks.extend(rd.values())
            for bi in range(off // self.BLK, (off + w - 1) // self.BLK + 1):
                blk = self.blocks[bi]
                for t_ in toks:
                    self._fold(blk, t_)
        self.top = m

    def alloc(self, name, shape, dtype=F32, parts=128):
        n = 1
        for s in shape:
            n *= s
        w = n if dtype == F32 else (n + 1) // 2
        w = (w + 1) // 2 * 2
        off = self.top
        self.top += w
        self.peak = max(self.peak, self.top)
        assert self.top <= self.words, "arena overflow %s: %d > %d" % (name, self.top, self.words)
        ap = self.t[0:parts, off:off + w]
        if dtype != F32:
            ap = ap.bitcast(dtype)
        ap = ap[:, 0:n]
        if len(shape) > 1:
            names = ["d%d" % i for i in range(len(shape))]
            kw = {names[i]: shape[i] for i in range(len(shape))}
            ap = ap.rearrange("p (" + " ".join(names) + ") -> p " + " ".join(names), **kw)
        self.ctr += 1
        key = "%s#%d" % (name, self.ctr)
        self.S.reg(key, None)
        inh = {}
        for bi in range(off // self.BLK, (off + w - 1) // self.BLK + 1):
            for t_ in self.blocks[bi].values():
                self._fold(inh, t_)
        if inh:
            self.S.bufs[key]['subs'][None] = [None, inh]
        self.live.append((off, w, key))
        return Buf(ap, key)


class PSum:
    def __init__(self, ps):
        self.ps = ps
        self.next = 0
        self.limit = 7

    def get(self, nb=1):
        if self.next + nb > self.limit:
            self.next = 0
        b = self.next
        self.next = (self.next + nb) % self.limit
        return b

    def ap(self, b, nb=1):
        return self.ps[:, b * 512:(b + nb) * 512]

    @staticmethod
    def keys(b, nb=1):
        return [('ps', b + i) for i in range(nb)]


SEQS = [(0, 2, False), (2, 2, False), (4, 8, True)]
ARENA_WORDS = 15400
WSLOT = 4096
NWSLOT = 3


def build_nc(depth=4, taps=(), stop=None):
    nc = bass.Bass("TRN2", target_bir_lowering=False)
    S = Sched()
    es = ExitStack()

    def din(name, shape):
        return nc.dram_tensor(name, list(shape), F32, kind="ExternalInput").ap()

    def dout(name, shape):
        return nc.dram_tensor(name, list(shape), F32, kind="ExternalOutput").ap()

    x_d = din("x", [NTOK, D])
    cond_d = din("cond", [2, D])
    ckvc_d = din("ckv_c", [2, 256, 128])
    kpec_d = din("kpe_c", [2, 256, 32])
    stC_d = din("stC", [2, 2, 4, 128, 128])
    stn_d = din("stn", [2, 2, 4, 128])
    stm_d = din("stm", [2, 2, 4])
    nakc_d = din("nak_c", [2, 256, 8, 64])
    navc_d = din("nav_c", [2, 256, 8, 64])
    normw_d = din("norm_w", [4, D])
    adaw_d = din("ada_w", [4, D, 3 * D])
    adab_d = din("ada_b", [4, 3 * D])
    evwin_d = din("ev_w_in", [2, D, 3504])
    evqan_d = din("ev_q_a_norm", [2, 256])
    evkvan_d = din("ev_kv_a_norm", [2, 128])
    evwqb_d = din("ev_w_q_b", [2, 256, 768])
    evwkvb_d = din("ev_w_kv_b", [2, 128, 1024])
    evqn_d = din("ev_q_norm", [2, 96])
    evkn_d = din("ev_k_norm", [2, 96])
    evbi_d = din("ev_b_i", [2, 2, 4])
    evbf_d = din("ev_b_f", [2, 2, 4])
    evhn_d = din("ev_h_norm", [2, 512])
    evwout_d = din("ev_w_out", [2, D, D])
    odwin_d = din("od_w_in", [2, D, 4096])
    odcw_d = din("od_conv_w", [2, 3, 512])
    odcb_d = din("od_conv_b", [2, 512])
    odqn_d = din("od_q_norm", [2, 64])
    odkn_d = din("od_k_norm", [2, 64])
    odwout_d = din("od_w_out", [2, D, D])
    rpbx_d = din("rpbx", [2, 8, 128, 1024])
    cmask_d = din("cmask", [2, 128, 1024])
    ropecs_d = din("rope_cs", [2, 1024, 32])
    lmask_d = din("lmask", [2, 128, 128])

    yp_d = dout("yp", [512, D])
    ys_d = dout("ys", [1024, D])
    ockv_d = dout("o_ckv", [2, 2, 256, 128])
    okpe_d = dout("o_kpe", [2, 2, 256, 32])
    oC_d = dout("o_C", [2, 2, 2, 4, 128, 128])
    on_d = dout("o_n", [2, 2, 2, 4, 128])
    om_d = dout("o_m", [2, 2, 2, 4])
    onak_d = dout("o_nak", [2, 2, 256, 8, 64])
    onav_d = dout("o_nav", [2, 2, 256, 8, 64])

    tap_list = []

    def sb(name, shape, dt):
        return es.enter_context(nc.sbuf_tensor(name, list(shape), dt))

    yT = sb("yT", [128, 8, NTOK], F32)
    hsT = sb("hsT", [128, 8, NTOK], BF16)
    mixT = sb("mixT", [128, 8, NTOK], BF16)
    wring = [sb("wring%d" % i, [128, WSLOT], BF16) for i in range(NWSLOT)]
    scb = sb("scb", [128, 8, 2], BF16)
    wmod = [sb("wmod%d" % i, [128, 8, 128], BF16) for i in range(2)]
    mrow = sb("mrow", [2, 128], F32)
    wqb = sb("wqb", [128, 2, 768], BF16)
    wkvb = sb("wkvb", [128, 1024], BF16)
    wgate = sb("wgate", [128, 8, 16], BF16)
    identf = sb("identf", [128, 128], F32)
    identb = sb("identb", [128, 128], BF16)
    onesb = sb("onesb", [128, 128], BF16)
    sel = sb("sel", [36, 4, 128], F32)
    lmaskb = sb("lmaskb", [128, 2, 128], BF16)
    ropeC = sb("ropeC", [128, 8, 32], F32)
    ropeS = sb("ropeS", [128, 8, 32], F32)
    cmaskb = sb("cmaskb", [128, 2, 1024], BF16)
    pvA = sb("pvA", [128, 128], F32)
    pvB = sb("pvB", [128, 64], F32)
    scond = sb("scond", [128, 16], F32)
    modT = sb("modT", [128, 4, 24, 2], F32)
    modA = sb("modA", [128, 4, 8, 2], F32)
    gq = sb("gq", [128, 96], F32)
    gk = sb("gk", [128, 96], F32)
    ghn = sb("ghn", [128, 512], F32)
    bif = sb("bif", [36, 2, 2], F32)
    m0t = sb("m0t", [36, 2], F32)
    convp = sb("convp", [128, 2, 4, 4], F32)
    arena_t = sb("arena", [128, ARENA_WORDS], F32)
    ps_t = es.enter_context(nc.psum_tensor("ps", [128, 4096], F32))

    A = Arena(S, arena_t, ARENA_WORDS)
    P = PSum(ps_t)
    PK = PSum.keys

    def TAP(name, ap, shape, reads, dt=F32):
        if name not in taps:
            return
        d = nc.dram_tensor("tap_" + name, list(shape), dt, kind="ExternalOutput").ap()
        S.dma('sp', DMA(d, ap), reads=reads, is_output=True)
        tap_list.append(name)

    def cols(t, n=1):
        return slice(128 * t, 128 * (t + n))

    S.op('pool', MS(identf[:], 1.0), writes=['identf'])
    S.op('pool', ASEL(identf[:], identf[:], [[-1, 128]], ALU.is_equal, 0.0, 0, 1),
         reads=['identf'], writes=['identf'])
    S.op('dve', CP(identb[:], identf[:]), reads=['identf'], writes=['identb'])
    S.op('pool', MS(onesb[:], 1.0), writes=['onesb'])
    S.op('pool', MS(sel[:], 1.0), writes=['sel'])
    for g0 in (0, 32):
        for h in range(4):
            S.op('pool', ASEL(sel[g0:g0 + 4, h, :], sel[g0:g0 + 4, h, :], [[0, 128]], ALU.is_equal, 0.0, -h, 1),
                 reads=['sel'], writes=['sel'])
    S.dma('pool', DMA(lmaskb[:], lmask_d.rearrange("d p q -> p d q")), writes=['lmaskb'])
    S.dma('pool', DMA(cmaskb[:], cmask_d.rearrange("d p q -> p d q")), writes=['cmaskb'])
    S.dma('sp', DMA(ropeC[:], ropecs_d[0].rearrange("(t p) e -> p t e", p=128)), writes=['ropeC'])
    S.dma('sp', DMA(ropeS[:], ropecs_d[1].rearrange("(t p) e -> p t e", p=128)), writes=['ropeS'])

    mk = A.mark()
    stgA = A.alloc("stgA", [128], F32)
    stgB = A.alloc("stgB", [128], F32, parts=64)
    S.dma('sp', DMA(stgA.ap[0:96, :], adab_d.rearrange("l (c p) -> (l c) p", p=128)), writes=[stgA.k(0)])
    S.dma('sp', DMA(stgA.ap[96:128, :], normw_d.rearrange("l (c p) -> (l c) p", p=128)), writes=[stgA.k(1)])
    S.dma('sp', DMA(stgB.ap[0:16, :], cond_d.rearrange("g (c p) -> (g c) p", p=128)), writes=[stgB.k(0)])
    S.dma('sp', DMA(stgB.ap[16:20, :], evqan_d.rearrange("l (c p) -> (l c) p", p=128)), writes=[stgB.k(1)])
    S.dma('sp', DMA(stgB.ap[20:22, :], evkvan_d), writes=[stgB.k(2)])
    S.dma('sp', DMA(stgB.ap[22:46, :], odcw_d.rearrange("l k (c p) -> (l k c) p", p=128)), writes=[stgB.k(3)])
    S.dma('sp', DMA(stgB.ap[46:54, :], odcb_d.rearrange("l (c p) -> (l c) p", p=128)), writes=[stgB.k(4)])
    b = P.get()
    S.op('pe', MM(P.ap(b)[:, 0:128], stgA.ap, identf[:]), reads=[stgA.key, 'identf'], writes=PK(b))
    S.op('dve', CP(pvA[:], P.ap(b)[:, 0:128]), reads=PK(b), writes=['pvA'])
    b = P.get()
    S.op('pe', MM(P.ap(b)[:, 0:54], stgB.ap[0:54, :], identf[0:54, 0:54]), reads=[stgB.key, 'identf'], writes=PK(b))
    S.op('dve', CP(pvB[:, 0:54], P.ap(b)[:, 0:54]), reads=PK(b), writes=['pvB'])
    A.release(mk)

    def pv_adab(l):
        return pvA[:, l * 24:(l + 1) * 24]

    def pv_normw(l):
        return pvA[:, 96 + l * 8:96 + (l + 1) * 8]

    def pv_qan(j, c2):
        return pvB[:, 16 + j * 2 + c2:16 + j * 2 + c2 + 1]

    def pv_kvan(j):
        return pvB[:, 20 + j:21 + j]

    mk = A.mark()
    tnh = A.alloc("tnh", [16], F32)
    S.op('act', ACTF(tnh.ap, pvB[:, 0:16], AF.Tanh, scale=0.5), reads=['pvB'], writes=[tnh.key])
    S.op('dve', STT(tnh.ap, tnh.ap, 1.0, pvB[:, 0:16], ALU.add, ALU.mult), reads=[tnh.key, 'pvB'], writes=[tnh.key])
    S.op('dve', TS(scond[:], tnh.ap, 0.5, ALU.mult), reads=[tnh.key], writes=['scond'])
    S.op('dve', CP(scb[:], scond[:].rearrange("p (g c) -> p c g", g=2)), reads=['scond'], writes=['scb'])
    A.release(mk)
    for j in range(2):
        wv = pvB[:, 22 + j * 12:22 + (j + 1) * 12].rearrange("p (k c) -> p c k", k=3)
        S.op('dve', TS(convp[:, j, :, 0:3], wv, 0.5, ALU.mult), reads=['pvB'], writes=[('convp', j)])
        bv = pvB[:, 46 + j * 4:46 + (j + 1) * 4]
        S.op('dve', TS(convp[:, j, :, 3], bv, 0.5, ALU.mult), reads=['pvB'], writes=[('convp', j)])
    for d_, g0 in ((0, 0), (1, 32)):
        S.dma('sp', DMA(bif[g0:g0 + 4, 0, :], evbi_d[:, d_, :].rearrange("j h -> h j"), allow_slow_non_contiguous=True),
              writes=['bif'])
        S.dma('sp', DMA(bif[g0:g0 + 4, 1, :], evbf_d[:, d_, :].rearrange("j h -> h j"), allow_slow_non_contiguous=True),
              writes=['bif'])
        S.dma('sp', DMA(m0t[g0:g0 + 4, :], stm_d[:, d_, :].rearrange("j h -> h j"), allow_slow_non_contiguous=True),
              writes=['m0t'])

    wstate = {'i': 0}
    pre_w = {}

    def load_w(pieces):
        i = wstate['i']
        wstate['i'] = (i + 1) % NWSLOT
        tot = sum(p.shape[1] for p in pieces)
        assert 8 * tot <= WSLOT, tot
        view = wring[i][:, 0:8 * tot].rearrange("p (k n) -> p k n", k=8)
        key = 'wring%d' % i
        off = 0
        for pi, pc in enumerate(pieces):
            n = pc.shape[1]
            S.dma('pool', DMA(view[:, :, off:off + n], pc.rearrange("(k p) n -> p k n", p=128)),
                  writes=[(key, pi)])
            off += n
        return view, key

    def rsqrt(dst, src, t1, t2, rk, wk):
        S.op('dve', TS(dst.bitcast(I32), src.bitcast(I32), -0.5, ALU.mult, float(0x5f3759df), ALU.add),
             reads=rk, writes=wk)
        for _ in range(2):
            S.op('dve', TT(t2, src, dst, ALU.mult), reads=rk + wk, writes=wk)
            S.op('dve', STT(t2, t2, -0.5, dst, ALU.mult, ALU.mult), reads=wk, writes=wk)
            S.op('dve', STT(dst, t2, 1.5, dst, ALU.add, ALU.mult), reads=wk, writes=wk)

    modq = []

    def queue_mod(l):
        def dma_piece(cg):
            S.dma('pool', DMA(wmod[cg % 2][:], adaw_d[l][:, cg * 128:(cg + 1) * 128].rearrange("(k p) n -> p k n", p=128)),
                  writes=['wmod%d' % (cg % 2)])

        def step(cg):
            def f():
                if cg == 0:
                    dma_piece(0)
                if cg + 1 < 24:
                    dma_piece(cg + 1)
                wv, wkey = wmod[cg % 2], 'wmod%d' % (cg % 2)
                for kc in range(8):
                    S.op('pe', MM(P.ap(7)[:, cg * 2:cg * 2 + 2], wv[:, kc, :], scb[:, kc, :], start=(kc == 0), stop=(kc == 7)),
                         reads=[wkey, 'scb'], writes=PK(7))
                if cg == 23:
                    pb = P.ap(7)
                    S.op('dve', TT(modT[:, l, :, :], pb[:, 0:48].rearrange("p (c g) -> p c g", g=2),
                                   pv_adab(l).unsqueeze(2).broadcast_to([128, 24, 2]), ALU.add),
                         reads=PK(7) + ['pvA'], writes=[('modT', l)])
                    S.op('dve', TS(modA[:, l, :, :], modT[:, l, 8:16, :], 1.0, ALU.add), reads=[('modT', l)], writes=[('modA', l)])
                    S.op('dve', TT(modA[:, l, :, :], modA[:, l, :, :], pv_normw(l).unsqueeze(2).broadcast_to([128, 8, 2]),
                                   ALU.mult), reads=[('modA', l), 'pvA'], writes=[('modA', l)])
            return f
        for cg in range(24):
            modq.append(step(cg))

    def pump(n=1):
        for _ in range(n):
            if modq:
                modq.pop(0)()

    def load_x():
        mk = A.mark()
        xin = [A.alloc("xin%d" % i, [1024], F32) for i in range(2)]
        for t in range(NT):
            xb = xin[t % 2]
            S.dma('sp', DMA(xb.ap, x_d[128 * t:128 * (t + 1), :]), writes=[xb.key])
            b = P.get(2)
            for c in range(8):
                S.op('pe', MM(P.ap(b, 2)[:, c * 128:(c + 1) * 128], xb.ap[:, c * 128:(c + 1) * 128], identf[:]),
                     reads=[xb.key, 'identf'], writes=PK(b, 2))
            eng = 'act' if t % 2 else 'dve'
            src = P.ap(b, 2).rearrange("p (c q) -> p c q", c=8)
            if eng == 'act':
                S.op('act', ACTF(yT[:, :, cols(t)], src, AF.Copy), reads=PK(b, 2), writes=[('yT', t)])
            else:
                S.op('dve', CP(yT[:, :, cols(t)], src), reads=PK(b, 2), writes=[('yT', t)])
            pump(2)
        A.release(mk)

    yo_bufs = []

    def store_tiles(tiles):
        if not yo_bufs:
            yo_bufs.extend([A.alloc("yo%d" % i, [1024], F32) for i in range(2)])
        for t in tiles:
            ob = yo_bufs[t % 2]
            b = P.get(2)
            for c in range(8):
                S.op('pe', MM(P.ap(b, 2)[:, c * 128:(c + 1) * 128], yT[:, c, cols(t)], identf[:]),
                     reads=[('yT', t), 'identf'], writes=PK(b, 2))
            if t % 2:
                S.op('act', ACTF(ob.ap, P.ap(b, 2), AF.Copy), reads=PK(b, 2), writes=[ob.key])
            else:
                S.op('dve', CP(ob.ap, P.ap(b, 2)), reads=PK(b, 2), writes=[ob.key])
            dst = yp_d[128 * t:128 * (t + 1), :] if t < 4 else ys_d[128 * (t - 4):128 * (t - 3), :]
            S.dma('sp', DMA(dst, ob.ap), reads=[ob.key], is_output=True)

    def store_y():
        store_tiles(range(NT))

    def phase_norm(l):
        mk = A.mark()
        sqs_ = [A.alloc("sq%d" % i, [8, 512], BF16) for i in range(3)]
        ms = A.alloc("ms", [3, 512], F32)
        rstd = A.alloc("rstd", [3, 512], F32)
        t1 = A.alloc("nt1", [3, 512], F32)
        t2 = A.alloc("nt2", [3, 512], F32)
        tmp = [A.alloc("ntmp%d" % i, [512], F32) for i in range(2)]
        for blk in range(3):
            cs = slice(blk * 512, (blk + 1) * 512)
            yk = [('yT', 4 * blk + i) for i in range(4)]
            sq = sqs_[blk]
            S.op('act', ACTF(sq.ap, yT[:, :, cs], AF.Square), reads=yk, writes=[sq.key])
            b = P.get()
            for c in range(8):
                S.op('pe', MM(P.ap(b), onesb[:], sq.ap[:, c, :], start=(c == 0), stop=(c == 7)),
                     reads=[sq.key, 'onesb'], writes=PK(b))
            S.op('dve', TS(ms.ap[:, blk, :], P.ap(b), 1.0 / D, ALU.mult, EPS, ALU.add), reads=PK(b), writes=[ms.k(blk)])
        rsqrt(rstd.ap, ms.ap, t1.ap, t2.ap, [ms.key], [rstd.key, t1.key, t2.key])
        for blk in range(3):
            g = 0 if blk == 0 else 1
            cs = slice(blk * 512, (blk + 1) * 512)
            yk = [('yT', 4 * blk + i) for i in range(4)]
            hk = [('hsT', 4 * blk + i) for i in range(4)]
            for c in range(8):
                tb = tmp[c % 2]
                S.op('dve', STT(tb.ap, yT[:, c, cs], modA[:, l, c, g:g + 1], rstd.ap[:, blk, :], ALU.mult, ALU.mult),
                     reads=yk + [('modA', l), rstd.key], writes=[tb.key])
                S.op('act', ACTF(hsT[:, c, cs], tb.ap, AF.Identity, bias=modT[:, l, c, g:g + 1], scale=1.0),
                     reads=[tb.key, ('modT', l)], writes=hk)
        A.release(mk)

    pre_out = [None]

    def preload_out(wout_d):
        pre_out[0] = load_w([wout_d[:, 0:512]])

    def phase_out(l, wout_d, store_blk=None):
        halves = [pre_out[0] if pre_out[0] is not None else load_w([wout_d[:, 0:512]]), None]
        pre_out[0] = None
        halves[1] = load_w([wout_d[:, 512:1024]])
        for blk in range(3):
            g = 0 if blk == 0 else 1
            cs = slice(blk * 512, (blk + 1) * 512)
            mk_ = [('mixT', 4 * blk + i) for i in range(4)]
            yk = [('yT', 4 * blk + i) for i in range(4)]
            for half in range(2):
                wv, wkey = halves[half]
                for dc in range(4):
                    c = half * 4 + dc
                    b = P.get()
                    for jc in range(8):
                        S.op('pe', MM(P.ap(b), wv[:, jc, dc * 128:(dc + 1) * 128], mixT[:, jc, cs],
                                      start=(jc == 0), stop=(jc == 7)),
                             reads=[wkey] + mk_, writes=PK(b))
                    S.op('dve', STT(yT[:, c, cs], P.ap(b), modT[:, l, 16 + c, g:g + 1], yT[:, c, cs], ALU.mult, ALU.add),
                         reads=PK(b) + [('modT', l)] + yk, writes=yk)
            if store_blk is not None:
                store_blk(blk)

    def attention(qT, kT, dk, vaug, att, NQ, blocks_for_q, bias_fn=None, head_hook=None):
        mk = A.mark()
        maxb = max(len(blocks_for_q(qi)) for qi in range(NQ))
        gsz = maxb if maxb <= 8 else (maxb + 1) // 2
        assert gsz <= 8
        npt = 3 if gsz <= 2 else 2
        PT = [A.alloc("PT%d" % i, [gsz * 128], BF16) for i in range(npt)]
        rd = A.alloc("rd", [4], F32)
        items = []
        for hh in range(4):
            for qi in range(NQ):
                bl = blocks_for_q(qi)
                grps = [bl[i:i + gsz] for i in range(0, len(bl), gsz)]
                for gi, g_ in enumerate(grps):
                    items.append((hh, qi, gi, len(grps), g_))
        state = {'pt': 0, 'ob': None, 'on': 0}
        P.limit = 5
        if P.next >= 5:
            P.next = 0

        def emit_S(it):
            hh, qi, gi, ng, g_ = it
            if head_hook is not None and qi == 0 and gi == 0:
                head_hook(hh)
            b = P.get(2)
            reg = P.ap(b, 2)
            for bi, kidx in enumerate(g_):
                bias = bias_fn(hh, qi, kidx) if bias_fn is not None else None
                S.op('pe', MM(reg[:, bi * 128:(bi + 1) * 128], kT.ap[0:dk, hh, kidx * 128:(kidx + 1) * 128],
                              qT.ap[0:dk, hh, qi * 128:(qi + 1) * 128], start=True, stop=(bias is None)),
                     reads=[kT.key, qT.key], writes=PK(b, 2))
                if bias is not None:
                    S.op('pe', MM(reg[:, bi * 128:(bi + 1) * 128], identb[:], bias[0], start=False, stop=True),
                         reads=['identb'] + bias[1], writes=PK(b, 2))
            pt = PT[state["pt"] % npt]
            state['pt'] += 1
            n = len(g_) * 128
            S.op('act', ACTF(pt.ap[:, 0:n], reg[:, 0:n], AF.Exp), reads=PK(b, 2), writes=[pt.key])
            return pt

        def emit_PV(it, pt):
            hh, qi, gi, ng, g_ = it
            if gi == 0 and qi % 4 == 0:
                state['ob'] = 5 + state['on'] % 2
                state['on'] += 1
            ob = state['ob']
            oq = qi % 4
            for bi, kidx in enumerate(g_):
                S.op('pe', MM(P.ap(ob)[:, oq * 66:oq * 66 + 65], pt.ap[:, bi * 128:(bi + 1) * 128],
                              vaug.ap[:, kidx, hh, 0:65], start=(gi == 0 and bi == 0),
                              stop=(gi == ng - 1 and bi == len(g_) - 1)),
                     reads=[pt.key, vaug.key], writes=PK(ob))
            if gi == ng - 1 and (oq == 3 or qi == NQ - 1):
                nq = oq + 1
                q0 = qi - oq
                ov = P.ap(ob)[:, 0:nq * 66].rearrange("p (q e) -> p q e", e=66)
                S.op('dve', RCP(rd.ap[:, 0:nq], ov[:, :, 64]), reads=PK(ob), writes=[rd.key])
                S.op('dve', TT(att.ap[:, q0:q0 + nq, hh, :], ov[:, :, 0:64],
                               rd.ap[:, 0:nq].unsqueeze(2).broadcast_to([128, nq, 64]), ALU.mult),
                     reads=PK(ob) + [rd.key], writes=[att.k(q) for q in range(q0, q0 + nq)])

        prev = None
        for it in items:
            pt = emit_S(it)
            if prev is not None:
                emit_PV(*prev)
            prev = (it, pt)
        emit_PV(*prev)
        P.limit = 7
        A.release(mk)

    def even_layer(l):
        j = l // 2
        W = evwin_d[j]
        S.dma('pool', DMA(wqb[:], evwqb_d[j].rearrange("(c p) n -> p c n", p=128)), writes=['wqb'])
        S.dma('pool', DMA(wkvb[:], evwkvb_d[j]), writes=['wkvb'])
        S.dma('pool', DMA(wgate[:], W[:, 2464:2480].rearrange("(k p) n -> p k n", p=128)), writes=['wgate'])
        S.dma('sp', DMA(gq[:, 0:96], evqn_d[j].partition_broadcast(128)), writes=['gq'])
        S.dma('sp', DMA(gk[:, 0:96], evkn_d[j].partition_broadcast(128)), writes=['gk'])
        S.dma('sp', DMA(ghn[:], evhn_d[j].partition_broadcast(128)), writes=['ghn'])
        S.op('dve', TS(gq[:, 0:96], gq[:, 0:96], 96 ** -0.5, ALU.mult), reads=['gq'], writes=['gq'])
        S.op('dve', TS(ghn[:], ghn[:], 0.25, ALU.mult), reads=['ghn'], writes=['ghn'])

        mk1 = A.mark()
        qanT = A.alloc("qanT", [2, NTOK], BF16)
        ckvT = A.alloc("ckvT", [NTOK + 256], BF16)
        kpeall = A.alloc("kpeall", [14, 32], F32)
        krg = A.alloc("krg", [14, 32], F32)
        sskpe = A.alloc("sskpe", [14], F32)
        mk2 = A.mark()
        wa, wakey = pre_w.pop('wa') if 'wa' in pre_w else load_w([W[:, 0:416]])
        wga_of = lambda hg_: load_w([W[:, 416 + hg_ * 256:416 + (hg_ + 1) * 256]])
        wga_next = wga_of(0)
        sq = A.alloc("sq1", [3, 512], BF16)
        ms = A.alloc("ms1", [2, 512], F32)
        rstd = A.alloc("rstd1", [2, 512], F32)
        n1 = A.alloc("n1", [2, 512], F32)
        n2 = A.alloc("n2", [2, 512], F32)
        ckvTf = A.alloc("ckvTf", [512], F32)
        for blk in range(3):
            cs = slice(blk * 512, (blk + 1) * 512)
            hk = [('hsT', 4 * blk + i) for i in range(4)]
            bs = []
            for ci in range(3):
                b = P.get()
                bs.append(b)
                for kc in range(8):
                    S.op('pe', MM(P.ap(b), wa[:, kc, ci * 128:(ci + 1) * 128], hsT[:, kc, cs],
                                  start=(kc == 0), stop=(kc == 7)), reads=[wakey] + hk, writes=PK(b))
                S.op('act', ACTF(sq.ap[:, ci, :], P.ap(b), AF.Square), reads=PK(b), writes=[sq.k(ci)])
            bq_ = P.get()
            for ci in range(2):
                S.op('pe', MM(P.ap(bq_), onesb[:], sq.ap[:, ci, :], start=(ci == 0), stop=(ci == 1)),
                     reads=[sq.k(ci), 'onesb'], writes=PK(bq_))
            bk_ = P.get()
            S.op('pe', MM(P.ap(bk_), onesb[:], sq.ap[:, 2, :]), reads=[sq.k(2), 'onesb'], writes=PK(bk_))
            S.op('dve', TS(ms.ap[:, 0, :], P.ap(bq_), 1.0 / 256, ALU.mult, EPS, ALU.add), reads=PK(bq_), writes=[ms.key])
            S.op('dve', TS(ms.ap[:, 1, :], P.ap(bk_), 1.0 / 128, ALU.mult, EPS, ALU.add), reads=PK(bk_), writes=[ms.key])
            rsqrt(rstd.ap, ms.ap, n1.ap, n2.ap, [ms.key], [rstd.key, n1.key, n2.key])
            for c2 in range(2):
                S.op('dve', STT(qanT.ap[:, c2, cs], P.ap(bs[c2]), pv_qan(j, c2), rstd.ap[:, 0, :], ALU.mult, ALU.mult),
                     reads=PK(bs[c2]) + ['pvB', rstd.key], writes=[qanT.k(4 * blk + i) for i in range(4)])
            S.op('dve', STT(ckvT.ap[:, cs], P.ap(bs[2]), pv_kvan(j), rstd.ap[:, 1, :], ALU.mult, ALU.mult),
                 reads=PK(bs[2]) + ['pvB', rstd.key], writes=[ckvT.k(4 * blk + i) for i in range(4)])
            if blk == 0:
                S.op('dve', STT(ckvTf.ap, P.ap(bs[2]), pv_kvan(j), rstd.ap[:, 1, :], ALU.mult, ALU.mult),
                     reads=PK(bs[2]) + ['pvB', rstd.key], writes=[ckvTf.key])
        b = P.get()
        for t in range(NT):
            for kc in range(8):
                S.op('pe', MM(P.ap(b)[:, t * 32:(t + 1) * 32], hsT[:, kc, cols(t)], wa[:, kc, 384:416],
                              start=(kc == 0), stop=(kc == 7)), reads=[wakey, ('hsT', t)], writes=PK(b))
        S.op('dve', CP(kpeall.ap[:, 0:12, :], P.ap(b)[:, 0:384].rearrange("p (t e) -> p t e", e=32)),
             reads=PK(b), writes=[kpeall.key])
        S.dma('sp', DMA(kpeall.ap[:, 12:14, :], kpec_d[j].rearrange("(t p) e -> p t e", p=128)), writes=[kpeall.key])
        for s_ in range(2):
            S.dma('sp', DMA(okpe_d[s_, j].rearrange("(t p) e -> p t e", p=128), kpeall.ap[:, 2 * s_:2 * s_ + 2, :]),
                  reads=[kpeall.key], is_output=True)
        ktmp = A.alloc("ktmp", [14, 32], F32)
        ktmp2 = A.alloc("ktmp2", [8, 32], F32)
        S.op('act', ACTF(ktmp.ap, kpeall.ap, AF.Square), reads=[kpeall.key], writes=[ktmp.key])
        S.op('dve', RED(sskpe.ap, ktmp.ap), reads=[ktmp.key], writes=[sskpe.key])
        S.op('dve', TT(krg.ap, kpeall.ap, gk[:, 64:96].unsqueeze(1).broadcast_to([128, 14, 32]), ALU.mult),
             reads=[kpeall.key, 'gk'], writes=[krg.key])
        S.op('dve', TT(ktmp.ap[:, 0:8, :], krg.ap[:, 4:12, :], ropeC[:], ALU.mult),
             reads=[krg.key, 'ropeC', ktmp.key], writes=[ktmp.key])
        for g in range(2):
            gs = slice(g * 16, (g + 1) * 16)
            S.op('dve', TT(ktmp2.ap[:, :, gs].rearrange("p t (a e) -> p t a e", a=2),
                           krg.ap[:, 4:12, gs].rearrange("p t (a e) -> p t a e", a=2)[:, :, ::-1, :],
                           ropeS[:, :, gs].rearrange("p t (a e) -> p t a e", a=2), ALU.mult),
                 reads=[krg.key, 'ropeS'], writes=[ktmp2.key])
        S.op('dve', TT(krg.ap[:, 4:12, :], ktmp.ap[:, 0:8, :], ktmp2.ap, ALU.add),
             reads=[ktmp.key, ktmp2.key], writes=[krg.key])
        ckvo = A.alloc("ckvo", [4, 128], F32)
        b = P.get()
        for t in range(4):
            S.op('pe', MM(P.ap(b)[:, t * 128:(t + 1) * 128], ckvTf.ap[:, cols(t)], identf[:]),
                 reads=[ckvTf.key, 'identf'], writes=PK(b))
        S.op('act', ACTF(ckvo.ap, P.ap(b).rearrange("p (t e) -> p t e", e=128), AF.Copy), reads=PK(b), writes=[ckvo.key])
        for s_ in range(2):
            S.dma('sp', DMA(ockv_d[s_, j].rearrange("(t p) e -> p t e", p=128), ckvo.ap[:, 2 * s_:2 * s_ + 2, :]),
                  reads=[ckvo.key], is_output=True)
        cc = A.alloc("cc", [2, 128], F32)
        S.dma('sp', DMA(cc.ap, ckvc_d[j].rearrange("(t p) e -> p t e", p=128)), writes=[cc.key])
        b = P.get()
        for t in range(2):
            S.op('pe', MM(P.ap(b)[:, t * 128:(t + 1) * 128], cc.ap[:, t, :], identf[:]),
                 reads=[cc.key, 'identf'], writes=PK(b))
        S.op('act', ACTF(ckvT.ap[:, NTOK:NTOK + 256], P.ap(b)[:, 0:256], AF.Copy), reads=PK(b),
             writes=[ckvT.k(12), ckvT.k(13)])
        A.release(mk2)

        if stop == 'mla_pre':
            A.release(mk1)
            return
        for hg in range(2):
            wga, wgakey = wga_next
            if hg == 0:
                wga_next = wga_of(1)
            for (t0, ntl, ctx) in SEQS:
                mk3 = A.mark()
                NQ = ntl
                ktiles = ([12, 13] if ctx else []) + list(range(t0, t0 + ntl))
                NK = len(ktiles)
                qT = A.alloc("qT", [4, NQ * 128], BF16)
                kT = A.alloc("kT", [4, NK * 128], BF16)
                vaug = A.alloc("vaug", [NK, 4, 66], BF16)
                att = A.alloc("att", [NQ, 4, 64], BF16)
                abuf = A.alloc("abuf", [NQ, 256], BF16)
                stage = [A.alloc("stage0", [8, 96], F32)] * 2
                qkn = [A.alloc("qkn%d" % i, [8, 96], BF16) for i in range(2)]
                sqs = A.alloc("sqs", [8, 96], BF16)
                ss = A.alloc("ss", [8], F32)
                ms8 = A.alloc("ms8", [8], F32)
                rs8 = A.alloc("rs8", [8], F32)
                na = A.alloc("na", [8], F32)
                nb_ = A.alloc("nb", [8], F32)
                r1 = A.alloc("r1", [4, 32], F32)
                r2 = A.alloc("r2", [4, 32], F32)
                tg = A.alloc("tg", [256], F32)
                mt = [A.alloc("mt%d" % i, [256], BF16) for i in range(2)]
                S.op('pool', MS(vaug.ap, 1.0), writes=[vaug.key])
                pendB = [None]
                for ki, t in enumerate(ktiles):
                    is_ctx = t >= 12
                    lo = 4 if is_ctx else 0
                    qi = ki - (2 if ctx else 0)
                    ccols = slice(NTOK + 128 * (t - 12), NTOK + 128 * (t - 11)) if is_ctx else cols(t)
                    st = stage[ki % 2]
                    qk = qkn[ki % 2]
                    bkv = P.get()
                    S.op('pe', MM(P.ap(bkv), ckvT.ap[:, ccols], wkvb[:, hg * 512:(hg + 1) * 512]),
                         reads=[ckvT.k(t), 'wkvb'], writes=PK(bkv))
                    kvv = P.ap(bkv).rearrange("p (h e) -> p h e", h=4)
                    if not is_ctx:
                        bq = P.get()
                        for c2 in range(2):
                            S.op('pe', MM(P.ap(bq)[:, 0:384], qanT.ap[:, c2, cols(t)], wqb[:, c2, hg * 384:(hg + 1) * 384],
                                          start=(c2 == 0), stop=(c2 == 1)), reads=[qanT.k(t), 'wqb'], writes=PK(bq))
                        qv = P.ap(bq)[:, 0:384].rearrange("p (h e) -> p h e", h=4)
                        bg = P.get()
                        for kc in range(8):
                            S.op('pe', MM(P.ap(bg)[:, 0:256], hsT[:, kc, cols(t)], wga[:, kc, :],
                                          start=(kc == 0), stop=(kc == 7)), reads=[('hsT', t), wgakey], writes=PK(bg))
                        S.op('act', ACTF(sqs.ap[:, 0:4, :], qv, AF.Square), reads=PK(bq), writes=[sqs.k(0)])
                    S.op('act', ACTF(sqs.ap[:, 4:8, 0:64], kvv[:, :, 0:64], AF.Square), reads=PK(bkv), writes=[sqs.k(1)])
                    if not is_ctx:
                        S.op('dve', RED(ss.ap[:, 0:4], sqs.ap[:, 0:4, :]), reads=[sqs.k(0)], writes=[ss.key])
                    S.op('dve', RED(ss.ap[:, 4:8], sqs.ap[:, 4:8, 0:64]), reads=[sqs.k(1)], writes=[ss.key])
                    S.op('dve', TS(ss.ap[:, 4:8], ss.ap[:, 4:8], sskpe.ap[:, t:t + 1], ALU.add),
                         reads=[ss.key, sskpe.key], writes=[ss.key])
                    S.op('dve', TS(ms8.ap[:, lo:8], ss.ap[:, lo:8], 1.0 / 96, ALU.mult, EPS, ALU.add),
                         reads=[ss.key], writes=[ms8.key])
                    if not is_ctx:
                        S.op('dve', TT(st.ap[:, 0:4, :], qv, gq[:, 0:96].unsqueeze(1).broadcast_to([128, 4, 96]), ALU.mult),
                             reads=PK(bq) + ['gq'], writes=[st.key])
                    S.op('dve', TT(st.ap[:, 4:8, 0:64], kvv[:, :, 0:64],
                                   gk[:, 0:64].unsqueeze(1).broadcast_to([128, 4, 64]), ALU.mult),
                         reads=PK(bkv) + ['gk'], writes=[st.key])
                    S.op('pool', CP(st.ap[:, 4:8, 64:96], krg.ap[:, t, :].unsqueeze(1).broadcast_to([128, 4, 32])),
                         reads=[krg.key], writes=[st.key])
                    S.op('act', ACTF(vaug.ap[:, ki, :, 0:64], kvv[:, :, 64:128], AF.Copy), reads=PK(bkv), writes=[vaug.key])
                    if ctx and not is_ctx:
                        tl = t - 4
                        S.op('dve', TT(r1.ap, st.ap[:, 0:4, 64:96], ropeC[:, tl, :].unsqueeze(1).broadcast_to([128, 4, 32]),
                                       ALU.mult), reads=[st.key, 'ropeC'], writes=[r1.key])
                        for g in range(2):
                            gs = slice(g * 16, (g + 1) * 16)
                            gs2 = slice(64 + g * 16, 64 + (g + 1) * 16)
                            S.op('dve', TT(r2.ap[:, :, gs].rearrange("p h (a e) -> p h a e", a=2),
                                           st.ap[:, 0:4, gs2].rearrange("p h (a e) -> p h a e", a=2)[:, :, ::-1, :],
                                           ropeS[:, tl, gs].rearrange("p (a e) -> p a e", a=2).unsqueeze(1)
                                           .broadcast_to([128, 4, 2, 8]), ALU.mult),
                                 reads=[st.key, 'ropeS'], writes=[r2.key])
                        S.op('dve', TT(st.ap[:, 0:4, 64:96], r1.ap, r2.ap, ALU.add), reads=[r1.key, r2.key], writes=[st.key])
                    rsqrt(rs8.ap[:, lo:8], ms8.ap[:, lo:8], na.ap[:, lo:8], nb_.ap[:, lo:8], [ms8.key],
                          [rs8.key, na.key, nb_.key])
                    S.op('dve', TT(qk.ap[:, lo:8, :], st.ap[:, lo:8, :],
                                   rs8.ap[:, lo:8].unsqueeze(2).broadcast_to([128, 8 - lo, 96]), ALU.mult),
                         reads=[st.key, rs8.key], writes=[qk.key])
                    if not is_ctx:
                        S.op('act', ACTF(tg.ap, P.ap(bg)[:, 0:256], AF.Tanh, scale=0.5), reads=PK(bg), writes=[tg.key])
                        S.op('dve', STT(abuf.ap[:, qi, :], tg.ap, 1.0, P.ap(bg)[:, 0:256], ALU.add, ALU.mult),
                             reads=[tg.key] + PK(bg), writes=[abuf.k(qi)])
                    def stageB(qk=qk, lo=lo, is_ctx=is_ctx, qi=qi, ki=ki):
                        bt = P.get(2)
                        for i in range(lo, 8):
                            S.op('pe', MM(P.ap(bt, 2)[0:96, i * 128:(i + 1) * 128], qk.ap[:, i, :], identb[:]),
                                 reads=[qk.key, 'identb'], writes=PK(bt, 2))
                        if not is_ctx:
                            S.op('act', ACTF(qT.ap[0:96, :, qi * 128:(qi + 1) * 128],
                                             P.ap(bt, 2)[0:96, 0:512].rearrange("p (h q) -> p h q", h=4), AF.Copy),
                                 reads=PK(bt, 2), writes=[qT.k(qi)])
                        S.op('dve', CP(kT.ap[0:96, :, ki * 128:(ki + 1) * 128],
                                       P.ap(bt, 2)[0:96, 512:1024].rearrange("p (h q) -> p h q", h=4)),
                             reads=PK(bt, 2), writes=[kT.k(ki)])
                    if pendB[0] is not None:
                        pendB[0]()
                    pendB[0] = stageB
                    pump(1)
                pendB[0]()
                pendB[0] = None
                attention(qT, kT, 96, vaug, att, NQ, lambda qi_, NK=NK: list(range(NK)))
                for qi in range(NQ):
                    t = t0 + qi
                    m_ = mt[qi % 2]
                    S.op('dve', TT(m_.ap, abuf.ap[:, qi, :], att.ap[:, qi, :, :].rearrange("p h e -> p (h e)"), ALU.mult),
                         reads=[abuf.k(qi), att.k(qi)], writes=[m_.key])
                    bT = P.get()
                    for i in range(2):
                        S.op('pe', MM(P.ap(bT)[:, i * 128:(i + 1) * 128], m_.ap[:, i * 128:(i + 1) * 128], identb[:]),
                             reads=[m_.key, 'identb'], writes=PK(bT))
                    S.op('act', ACTF(mixT[:, hg * 2:hg * 2 + 2, cols(t)],
                                     P.ap(bT)[:, 0:256].rearrange("p (c q) -> p c q", c=2), AF.Copy, scale=0.5),
                         reads=PK(bT), writes=[('mixT', t)])
                A.release(mk3)
        A.release(mk1)
        if stop == 'mla':
            return
        mlstm_phase(l)

    def mlstm_phase(l):
        j = l // 2
        W = evwin_d[j]
        mkA = A.mark()
        negR = A.alloc("negR", [NTOK], F32, parts=36)
        wint = A.alloc("wint", [NTOK], F32, parts=36)
        tmq = A.alloc("tmq", [12, 2, 3, 4], F32)
        cst = A.alloc("cst01", [2], F32, parts=36)
        w1_of = lambda h_: load_w([W[:, 928 + h_ * 128:928 + (h_ + 1) * 128], W[:, 1440 + h_ * 128:1440 + (h_ + 1) * 128]])
        w1_next = w1_of(0)
        qk_pair = [(A.alloc("qTbP%d" % i, [1024], BF16), A.alloc("kTbP%d" % i, [1024], BF16)) for i in range(2)]

        fm_done = set()

        def proj_fm(it_, w1v, w1key):
            if it_ in fm_done:
                return
            fm_done.add(it_)
            t0_, ntl_, _c = SEQS[it_ % 3]
            L_ = ntl_ * 128
            a_ = t0_ * 128
            qb_, kb_ = qk_pair[it_ % 2]
            for bi in range((L_ + 511) // 512):
                n = min(512, L_ - bi * 512)
                cs = slice(a_ + bi * 512, a_ + bi * 512 + n)
                lc = slice(bi * 512, bi * 512 + n)
                hk = [('hsT', (a_ + bi * 512) // 128 + i) for i in range(n // 128)]
                bq = P.get()
                for kc in range(8):
                    S.op('pe', MM(P.ap(bq)[:, 0:n], w1v[:, kc, 0:128], hsT[:, kc, cs], start=(kc == 0), stop=(kc == 7)),
                         reads=[w1key] + hk, writes=PK(bq))
                bk = P.get()
                for kc in range(8):
                    S.op('pe', MM(P.ap(bk)[:, 0:n], w1v[:, kc, 128:256], hsT[:, kc, cs], start=(kc == 0), stop=(kc == 7)),
                         reads=[w1key] + hk, writes=PK(bk))
                S.op('act', ACTF(qb_.ap[:, lc], P.ap(bq)[:, 0:n], AF.Copy), reads=PK(bq), writes=[qb_.k(bi)])
                S.op('dve', TS(kb_.ap[:, lc], P.ap(bk)[:, 0:n], 128 ** -0.5, ALU.mult), reads=PK(bk), writes=[kb_.k(bi)])

        mkB = A.mark()
        gtm = A.alloc("gtm", [12, 16], F32)
        T1 = A.alloc("T1", [NTOK], F32, parts=36)
        T2 = A.alloc("T2", [NTOK], F32, parts=36)
        T3 = A.alloc("T3", [NTOK], F32, parts=36)
        T4 = A.alloc("T4", [NTOK], F32, parts=36)
        T5 = A.alloc("T5", [NTOK], F32, parts=36)
        S.op('pool', MS(cst.ap[:, 0:1], 1.0), writes=[cst.key])
        S.op('pool', MS(cst.ap[:, 1:2], 0.0), writes=[cst.key])
        for Tb in (T1, T3):
            S.op('pool', MS(Tb.ap, 0.0), writes=[Tb.key])
        b = P.get()
        for t in range(NT):
            for kc in range(8):
                S.op('pe', MM(P.ap(b)[:, t * 16:(t + 1) * 16], hsT[:, kc, cols(t)], wgate[:, kc, :],
                              start=(kc == 0), stop=(kc == 7)), reads=[('hsT', t), 'wgate'], writes=PK(b))
        S.op('dve', CP(gtm.ap, P.ap(b)[:, 0:192].rearrange("p (t e) -> p t e", e=16)), reads=PK(b), writes=[gtm.key])
        for which, Tdst in ((0, T3), (1, T1)):
            b3 = P.get(3)
            for d_ in range(2):
                g0 = 32 * d_
                for t in range(NT):
                    S.op('pe', MM(P.ap(b3, 3)[g0:g0 + 4, t * 128:(t + 1) * 128],
                                  gtm.ap[:, t, which * 8 + d_ * 4:which * 8 + d_ * 4 + 4], identf[:]),
                         reads=[gtm.key, 'identf'], writes=PK(b3, 3))
            for d_ in range(2):
                g0 = 32 * d_
                S.op('act', ACTF(Tdst.ap[g0:g0 + 4, :], P.ap(b3, 3)[g0:g0 + 4, :], AF.Identity,
                                 bias=bif[g0:g0 + 4, which, j:j + 1], scale=1.0),
                     reads=PK(b3, 3) + ['bif'], writes=[Tdst.key])
        allk = [T1.key, T2.key, T3.key, T4.key, T5.key]
        S.op('dve', STT(T2.ap, T1.ap, -1.0, T1.ap, ALU.mult, ALU.max), reads=[T1.key], writes=[T2.key])
        S.op('act', ACTF(T2.ap, T2.ap, AF.Exp, scale=-1.0), reads=[T2.key], writes=[T2.key])
        S.op('act', ACTF(T2.ap, T2.ap, AF.Ln, bias=1.0, scale=1.0), reads=[T2.key], writes=[T2.key])
        S.op('dve', TS(T1.ap, T1.ap, 0.0, ALU.min), reads=[T1.key], writes=[T1.key])
        S.op('dve', TT(T1.ap, T1.ap, T2.ap, ALU.subtract), reads=[T1.key, T2.key], writes=[T1.key])
        segs = [(t0 * 128, ntl * 128, ctx) for (t0, ntl, ctx) in SEQS]

        def dirview(ap, d_, a, L):
            v = ap[32 * d_:32 * d_ + 4, a:a + L]
            return v[:, ::-1] if d_ else v

        for d_ in range(2):
            g0 = 32 * d_
            for (a, L, ctx) in segs:
                S.op('dve', SCAN(dirview(T2.ap, d_, a, L), cst.ap[g0:g0 + 4, 0:1].broadcast_to([4, L]),
                                 dirview(T1.ap, d_, a, L), 0.0, ALU.mult, ALU.add),
                     reads=[T1.key, cst.key], writes=[T2.key])
        S.op('dve', TT(T3.ap, T3.ap, T2.ap, ALU.subtract), reads=[T3.key, T2.key], writes=[T3.key])
        for d_ in range(2):
            g0 = 32 * d_
            for (a, L, ctx) in segs:
                init = m0t[g0:g0 + 4, j:j + 1] if ctx else 0.0
                S.op('dve', SCAN(dirview(T1.ap, d_, a, L), cst.ap[g0:g0 + 4, 1:2].broadcast_to([4, L]),
                                 dirview(T3.ap, d_, a, L), init, ALU.add, ALU.max),
                     reads=[T3.key, cst.key, 'm0t', T1.key], writes=[T1.key])
        S.op('dve', TT(T2.ap, T2.ap, T1.ap, ALU.add), reads=[T2.key, T1.key], writes=[T2.key])
        for s_ in range(2):
            a, L, _ = segs[s_]
            for d_ in range(2):
                g0 = 32 * d_
                idx = a + L - 1 if d_ == 0 else a
                S.dma('sp', DMA(om_d[s_, j, d_, :].rearrange("(h o) -> h o", o=1), T2.ap[g0:g0 + 4, idx:idx + 1]),
                      reads=[T2.key], is_output=True)
        for d_ in range(2):
            g0 = 32 * d_
            for (a, L, ctx) in segs:
                nck = L // 64
                first = slice(a, a + 64) if d_ == 0 else slice(a + L - 64, a + L)
                if ctx:
                    S.op('dve', CP(T4.ap[g0:g0 + 4, first], m0t[g0:g0 + 4, j:j + 1].broadcast_to([4, 64])),
                         reads=['m0t', T4.key], writes=[T4.key])
                else:
                    S.op('dve', MS(T4.ap[g0:g0 + 4, first], 0.0), reads=[T4.key], writes=[T4.key])
                if d_ == 0:
                    dst = T4.ap[g0:g0 + 4, a + 64:a + L].rearrange("p (k e) -> p k e", e=64)
                    src = T1.ap[g0:g0 + 4, a + 63:a + L - 1:64]
                    src2 = T1.ap[g0:g0 + 4, a + 63:a + L:64]
                else:
                    dst = T4.ap[g0:g0 + 4, a:a + L - 64].rearrange("p (k e) -> p k e", e=64)
                    src = T1.ap[g0:g0 + 4, a + 64:a + L:64]
                    src2 = T1.ap[g0:g0 + 4, a:a + L:64]
                S.op('dve', CP(dst, src.unsqueeze(2).broadcast_to([4, nck - 1, 64])), reads=[T1.key, T4.key], writes=[T4.key])
                S.op('dve', CP(T5.ap[g0:g0 + 4, a:a + L].rearrange("p (k e) -> p k e", e=64),
                               src2.unsqueeze(2).broadcast_to([4, nck, 64])), reads=[T1.key, T5.key], writes=[T5.key])
        for d_ in range(2):
            g0 = 32 * d_
            r = slice(g0, g0 + 4)
            S.op('dve', TT(T4.ap[r, :], T4.ap[r, :], T1.ap[r, :], ALU.subtract), reads=[T4.key, T1.key], writes=[T4.key])
            S.op('act', ACTF(wint.ap[r, :], T4.ap[r, :], AF.Exp), reads=[T4.key], writes=[wint.key])
            S.op('dve', TT(T5.ap[r, :], T3.ap[r, :], T5.ap[r, :], ALU.subtract), reads=[T3.key, T5.key], writes=[T5.key])
            S.op('act', ACTF(T5.ap[r, :], T5.ap[r, :], AF.Exp), reads=[T5.key], writes=[T5.key])
            S.op('act', ACTF(T2.ap[r, :], T2.ap[r, :], AF.Exp, scale=-1.0), reads=[T2.key], writes=[T2.key])
            S.op('dve', TS(negR.ap[r, :], T1.ap[r, :], -1.0, ALU.mult), reads=[T1.key], writes=[negR.key])
        if stop is None:
            proj_fm(0, w1_next[0], w1_next[1])
            proj_fm(1, w1_next[0], w1_next[1])
        b = P.get()
        for t in range(NT):
            for d_ in range(2):
                g0 = 32 * d_
                for q_, Tq in enumerate((T3, T5, T2)):
                    c0 = ((t * 2 + d_) * 3 + q_) * 4
                    S.op('pe', MM(P.ap(b)[:, c0:c0 + 4], Tq.ap[g0:g0 + 4, cols(t)], identf[g0:g0 + 4, g0:g0 + 4]),
                         reads=[Tq.key, 'identf'], writes=PK(b))
        S.op('dve', CP(tmq.ap, P.ap(b)[:, 0:288].rearrange("p (t d q h) -> p t d q h", t=12, d=2, q=3)),
             reads=PK(b), writes=[tmq.key])
        TAP('negR', negR.ap, [36, NTOK], [negR.key])
        TAP('wint', wint.ap, [36, NTOK], [wint.key])
        TAP('tmq', tmq.ap, [128, 12, 2, 3, 4], [tmq.key])
        A.release(mkB)
        if stop == 'rows':
            A.release(mkA)
            return

        ksc = 128 ** -0.5
        for h in range(4):
            w1, w1k = w1_next
            if h == 0:
                proj_fm(0, w1, w1k)
            w2, w2k = load_w([W[:, 1952 + h * 128:1952 + (h + 1) * 128], W[:, 2480 + h * 128:2480 + (h + 1) * 128],
                              W[:, 2992 + h * 128:2992 + (h + 1) * 128]])
            if h < 3:
                w1_next = w1_of(h + 1)
            for si, (t0, ntl, ctx) in enumerate(SEQS):
                if h == 3 and si == 2 and stop is None:
                    preload_out(evwout_d[j])
                mkC = A.mark()
                L = ntl * 128
                a = t0 * 128
                nck = L // 64
                it = h * 3 + si
                qTb, kTb = qk_pair[it % 2]
                ktm = A.alloc("ktm", [ntl, 128], BF16)
                va2 = A.alloc("va2", [ntl, 130], BF16)
                og = A.alloc("og", [ntl, 128], BF16)
                hacc = A.alloc("hacc", [ntl, 128], F32)
                qTw = A.alloc("qTw", [2, L], BF16)
                Cb = A.alloc("Cb", [2, 130], BF16)
                kw = A.alloc("kw", [2, ntl, 128], BF16)
                Cst = A.alloc("Cst", [2, 130], F32)
                wsb = A.alloc("wsb", [2, nck], F32)
                Dt = [A.alloc("Dt%d" % i, [128], BF16) for i in range(2)]
                scT = [A.alloc("scT%d" % i, [128], BF16) for i in range(4)]
                tog = A.alloc("tog", [256], F32)
                ag = A.alloc("ag", [128], F32)
                den = [A.alloc("den%d" % i, [4], F32) for i in range(2)]
                junk = A.alloc("junk", [128], F32)
                ssh = A.alloc("ssh", [ntl], F32)
                msh = A.alloc("msh", [ntl], F32)
                rsh = A.alloc("rsh", [ntl], F32)
                nh1 = A.alloc("nh1", [ntl], F32)
                nh2 = A.alloc("nh2", [ntl], F32)
                htmp = [A.alloc("htmp%d" % i, [128], F32) for i in range(2)]
                hmt = [A.alloc("hmt%d" % i, [128], BF16) for i in range(2)]
                S.op('pool', MS(va2.ap, 1.0), writes=[va2.key])
                for tt in range(ntl):
                    t = t0 + tt
                    bv = P.get()
                    for kc in range(8):
                        S.op('pe', MM(P.ap(bv)[:, 0:384], hsT[:, kc, cols(t)], w2[:, kc, :], start=(kc == 0), stop=(kc == 7)),
                             reads=[w2k, ('hsT', t)], writes=PK(bv))
                    bk2 = P.get()
                    for kc in range(8):
                        S.op('pe', MM(P.ap(bk2)[:, 0:128], hsT[:, kc, cols(t)], w1[:, kc, 128:256],
                                      start=(kc == 0), stop=(kc == 7)), reads=[w1k, ('hsT', t)], writes=PK(bk2))
                    S.op('act', ACTF(ktm.ap[:, tt, :], P.ap(bk2)[:, 0:128], AF.Copy, scale=ksc), reads=PK(bk2), writes=[ktm.k(tt)])
                    S.op('dve', CP(va2.ap[:, tt, 0:128], P.ap(bv)[:, 0:128]), reads=PK(bv), writes=[va2.k(tt)])
                    S.op('act', ACTF(tog.ap, P.ap(bv)[:, 128:384], AF.Tanh, scale=0.5), reads=PK(bv), writes=[tog.key])
                    S.op('dve', STT(ag.ap, tog.ap[:, 128:256], 1.0, P.ap(bv)[:, 256:384], ALU.add, ALU.mult),
                         reads=[tog.key] + PK(bv), writes=[ag.key])
                    S.op('dve', STT(og.ap[:, tt, :], tog.ap[:, 0:128], 1.0, ag.ap, ALU.add, ALU.mult),
                         reads=[tog.key, ag.key], writes=[og.k(tt)])
                if stop == 'mproj':
                    return
                if ctx:
                    for d_ in range(2):
                        S.dma('sp', DMA(Cst.ap[:, d_, 0:128], stC_d[j, d_, h]), writes=[Cst.k(d_)])
                        S.dma('sp', DMA(Cst.ap[:, d_, 128:129], stn_d[j, d_, h].rearrange("(p o) -> p o", o=1)),
                              writes=[Cst.k(d_)])
                else:
                    S.op('pool', MS(Cst.ap, 0.0), writes=[Cst.key])
                for d_ in range(2):
                    S.op('act', ACTF(Cb.ap[:, d_, 0:129], Cst.ap[:, d_, 0:129], AF.Copy), reads=[Cst.k(d_), Cst.key],
                         writes=[Cb.k(d_)])
                if stop == 'mSdma' and ctx:
                    return
                bw = P.get()
                wsr = A.alloc("wsr", [nck], F32, parts=36)
                for d_ in range(2):
                    g0 = 32 * d_
                    samp = wint.ap[g0:g0 + 4, a + 63:a + L:64] if d_ == 0 else wint.ap[g0:g0 + 4, a:a + L:64]
                    S.op('dve', CP(wsr.ap[g0:g0 + 4, :], samp), reads=[wint.key, wsr.key], writes=[wsr.key])
                    S.op('pe', MM(P.ap(bw)[:, d_ * nck:(d_ + 1) * nck], sel[g0:g0 + 4, h, :], wsr.ap[g0:g0 + 4, :]),
                         reads=[wsr.key, 'sel'], writes=PK(bw))
                S.op('dve', CP(wsb.ap, P.ap(bw)[:, 0:2 * nck].rearrange("p (d k) -> p d k", d=2)), reads=PK(bw), writes=[wsb.key])
                for d_ in range(2):
                    g0 = 32 * d_
                    for bi in range(L // 256):
                        n = 256
                        cs = slice(a + bi * 256, a + bi * 256 + n)
                        lc = slice(bi * 256, bi * 256 + n)
                        bb = P.get()
                        S.op('pe', MM(P.ap(bb)[:, 0:n], sel[g0:g0 + 4, h, :], wint.ap[g0:g0 + 4, cs]),
                             reads=[wint.key, 'sel'], writes=PK(bb))
                        S.op('dve', TT(qTw.ap[:, d_, lc], P.ap(bb)[:, 0:n], qTb.ap[:, lc], ALU.mult),
                             reads=PK(bb) + [qTb.key], writes=[qTw.k(d_)])
                if stop == 'mprep':
                    return
                cnt = {'dt': 0, 'sc': 0, 'den': 0}
                done = set()

                def intra(tt, d_):
                    t = t0 + tt
                    g0 = 32 * d_
                    lc = slice(tt * 128, (tt + 1) * 128)
                    bS = P.get()
                    S.op('pe', MM(P.ap(bS)[:, 0:128], kTb.ap[:, lc], qTb.ap[:, lc]), reads=[kTb.key, qTb.key], writes=PK(bS))
                    S.op('pe', MM(P.ap(bS)[:, 128:256], sel[g0:g0 + 4, h, :], negR.ap[g0:g0 + 4, cols(t)], start=True, stop=False),
                         reads=[negR.key, 'sel'], writes=PK(bS))
                    S.op('pe', MM(P.ap(bS)[:, 128:256], identb[:], lmaskb[:, d_, :], start=False, stop=True),
                         reads=['identb', 'lmaskb'], writes=PK(bS))
                    dt_ = Dt[cnt['dt'] % 2]
                    cnt['dt'] += 1
                    sc_ = scT[cnt['sc'] % 4]
                    cnt['sc'] += 1
                    S.op('act', ACTF(dt_.ap, P.ap(bS)[:, 128:256], AF.Exp, bias=tmq.ap[:, t, d_, 0, h:h + 1], scale=1.0),
                         reads=PK(bS) + [tmq.key], writes=[dt_.key])
                    S.op('dve', TT(sc_.ap, P.ap(bS)[:, 0:128], dt_.ap, ALU.mult), reads=PK(bS) + [dt_.key], writes=[sc_.key])
                    return sc_

                def num_open(tt, d_, sc_):
                    bN = P.get()
                    S.op('pe', MM(P.ap(bN)[:, 0:129], sc_.ap, va2.ap[:, tt, 0:129], start=True, stop=False),
                         reads=[sc_.key, va2.k(tt)], writes=PK(bN))
                    return bN

                def inter(tt, d_, hf, bN, last):
                    S.op('pe', MM(P.ap(bN)[hf * 64:(hf + 1) * 64, 0:129], qTw.ap[:, d_, tt * 128 + hf * 64:tt * 128 + (hf + 1) * 64],
                                  Cb.ap[:, d_, 0:129], start=False, stop=last),
                         reads=[qTw.k(d_), Cb.k(d_)], writes=PK(bN))

                def update(tt, d_, hf):
                    cidx = tt * 2 + hf
                    bU = P.get()
                    S.op('pe', MM(P.ap(bU)[:, 0:129], kw.ap[hf * 64:(hf + 1) * 64, d_, tt, :],
                                  va2.ap[hf * 64:(hf + 1) * 64, tt, 0:129]),
                         reads=[kw.k((d_, tt)), va2.k(tt)], writes=PK(bU))
                    S.op('dve', STT(Cb.ap[:, d_, 0:129], Cst.ap[:, d_, 0:129], wsb.ap[:, d_, cidx:cidx + 1],
                                    P.ap(bU)[:, 0:129], ALU.mult, ALU.add),
                         reads=[Cst.k(d_), wsb.key] + PK(bU), writes=[Cb.k(d_)])
                    S.op('dve', STT(Cst.ap[:, d_, 0:129], Cst.ap[:, d_, 0:129], wsb.ap[:, d_, cidx:cidx + 1],
                                    P.ap(bU)[:, 0:129], ALU.mult, ALU.add),
                         reads=[Cst.k(d_), wsb.key] + PK(bU), writes=[Cst.k(d_)])

                def epilogue(tt, d_, bN):
                    t = t0 + tt
                    dn = den[cnt['den'] % 2]
                    cnt['den'] += 1
                    qn = P.ap(bN)[:, 128:129]
                    S.op('dve', TS(dn.ap[:, 0:1], qn, tmq.ap[:, t, d_, 2, h:h + 1], ALU.max), reads=PK(bN) + [tmq.key], writes=[dn.key])
                    S.op('dve', STT(dn.ap[:, 1:2], qn, -1.0, dn.ap[:, 0:1], ALU.mult, ALU.max), reads=PK(bN) + [dn.key], writes=[dn.key])
                    S.op('dve', RCP(dn.ap[:, 2:3], dn.ap[:, 1:2]), reads=[dn.key], writes=[dn.key])
                    if tt not in done:
                        done.add(tt)
                        S.op('dve', TS(hacc.ap[:, tt, :], P.ap(bN)[:, 0:128], dn.ap[:, 2:3], ALU.mult),
                             reads=PK(bN) + [dn.key], writes=[hacc.k(tt)])
                    else:
                        S.op('dve', STT(hacc.ap[:, tt, :], P.ap(bN)[:, 0:128], dn.ap[:, 2:3], hacc.ap[:, tt, :], ALU.mult, ALU.add),
                             reads=PK(bN) + [dn.key, hacc.k(tt)], writes=[hacc.k(tt)])
                        S.op('act', ACTF(junk.ap, hacc.ap[:, tt, :], AF.Square, accum=ssh.ap[:, tt:tt + 1]),
                             reads=[hacc.k(tt)], writes=[junk.key, ssh.k(tt)])

                if stop == 'mSprep' and ctx:
                    return
                def emit_kw(d_, tt):
                    S.op('act', ACTF(kw.ap[:, d_, tt, :], ktm.ap[:, tt, :], AF.Copy,
                                     scale=tmq.ap[:, t0 + tt, d_, 1, h:h + 1]),
                         reads=[ktm.k(tt), tmq.key], writes=[kw.k((d_, tt))])

                emit_kw(0, 0)
                emit_kw(1, ntl - 1)
                for step in range(ntl):
                    tf, tb = step, ntl - 1 - step
                    scf = intra(tf, 0)
                    scb = intra(tb, 1)
                    if step + 1 < ntl:
                        emit_kw(0, step + 1)
                        emit_kw(1, ntl - 2 - step)
                    bNf = num_open(tf, 0, scf)
                    inter(tf, 0, 0, bNf, False)
                    update(tf, 0, 0)
                    bNb = num_open(tb, 1, scb)
                    inter(tb, 1, 1, bNb, False)
                    update(tb, 1, 1)
                    inter(tf, 0, 1, bNf, True)
                    update(tf, 0, 1)
                    epilogue(tf, 0, bNf)
                    inter(tb, 1, 0, bNb, True)
                    update(tb, 1, 0)
                    epilogue(tb, 1, bNb)
                if h == 0 and si == 0:
                    TAP('hacc', hacc.ap, [128, ntl, 128], [hacc.key])
                    TAP('qTb', qTb.ap, [128, L], [qTb.key], BF16)
                    TAP('kTb', kTb.ap, [128, L], [kTb.key], BF16)
                    TAP('qTw', qTw.ap, [128, 2, L], [qTw.key], BF16)
                    TAP('og', og.ap, [128, ntl, 128], [og.key], BF16)
                if stop == 'mloop':
                    return
                if not ctx:
                    for d_ in range(2):
                        S.dma('sp', DMA(oC_d[si, j, d_, h], Cst.ap[:, d_, 0:128]), reads=[Cst.k(d_)], is_output=True)
                        S.dma('sp', DMA(on_d[si, j, d_, h, :].rearrange("(p o) -> p o", o=1), Cst.ap[:, d_, 128:129]),
                              reads=[Cst.k(d_)], is_output=True)
                if stop == 'mout':
                    return
                if si < 2:
                    proj_fm(it + 1, w1, w1k)
                elif h < 3:
                    proj_fm(it + 1, w1_next[0], w1_next[1])
                S.op('dve', TS(msh.ap, ssh.ap, 1.0 / 128, ALU.mult, EPS, ALU.add), reads=[ssh.key], writes=[msh.key])
                rsqrt(rsh.ap, msh.ap, nh1.ap, nh2.ap, [msh.key], [rsh.key, nh1.key, nh2.key])
                bT = None
                for tt in range(ntl):
                    t = t0 + tt
                    ht = htmp[tt % 2]
                    hm = hmt[tt % 2]
                    S.op('dve', STT(ht.ap, hacc.ap[:, tt, :], rsh.ap[:, tt:tt + 1], ghn[:, h * 128:(h + 1) * 128], ALU.mult, ALU.mult),
                         reads=[hacc.k(tt), rsh.key, 'ghn'], writes=[ht.key])
                    S.op('dve', TT(hm.ap, ht.ap, og.ap[:, tt, :], ALU.mult), reads=[ht.key, og.k(tt)], writes=[hm.key])
                    if tt % 4 == 0:
                        bT = P.get()
                    S.op('pe', MM(P.ap(bT)[:, (tt % 4) * 128:(tt % 4 + 1) * 128], hm.ap, identb[:]),
                         reads=[hm.key, 'identb'], writes=PK(bT))
                    if tt % 4 == 3 or tt == ntl - 1:
                        n = tt % 4 + 1
                        tfirst = t - (n - 1)
                        S.op('act', ACTF(mixT[:, 4 + h, cols(tfirst, n)], P.ap(bT)[:, 0:n * 128], AF.Copy),
                             reads=PK(bT), writes=[('mixT', tfirst + i) for i in range(n)])
                A.release(mkC)
                if stop is not None and stop.startswith('mstep') and (h * 3 + si + 1) >= int(stop[5:]):
                    return
        A.release(mkA)

    def na_blocks(qi):
        if qi in (0, 1):
            js = [0, 1, 2, 3]
        elif qi in (6, 7):
            js = [4, 5, 6, 7]
        else:
            js = list(range(qi - 2, qi + 3))
        return [0, 1] + [2 + jt for jt in js]

    def odd_layer(l):
        j = l // 2
        W = odwin_d[j]
        S.dma('sp', DMA(gq[:, 0:64], odqn_d[j].partition_broadcast(128)), writes=['gq'])
        S.dma('sp', DMA(gk[:, 0:64], odkn_d[j].partition_broadcast(128)), writes=['gk'])
        S.op('dve', TS(gq[:, 0:64], gq[:, 0:64], 0.125, ALU.mult), reads=['gq'], writes=['gq'])
        segs = [(t0 * 128, ntl * 128) for (t0, ntl, ctx) in SEQS]
        conv_w = lambda c_: load_w([W[:, i * 512 + c_ * 128:i * 512 + (c_ + 1) * 128] for i in range(4)])
        nxt = pre_w.pop('c0') if 'c0' in pre_w else conv_w(0)
        wqk_of = lambda hg_: load_w([W[:, 2048 + hg_ * 256:2048 + (hg_ + 1) * 256], W[:, 2560 + hg_ * 256:2560 + (hg_ + 1) * 256]])
        wvg_of = lambda hg_: load_w([W[:, 3072 + hg_ * 256:3072 + (hg_ + 1) * 256], W[:, 3584 + hg_ * 256:3584 + (hg_ + 1) * 256]])
        na_pre = {}
        for c4 in range(4):
            wv, wk_ = nxt
            if c4 < 3:
                nxt = conv_w(c4 + 1)
            else:
                na_pre['qk'] = wqk_of(0)
            mk = A.mark()
            u = A.alloc("u", [NTOK], F32)
            ab = A.alloc("ab", [NTOK], F32)
            y = A.alloc("y", [NTOK], F32)
            xs = A.alloc("xs", [512], F32)
            tgc = A.alloc("tgc", [512], F32)
            aa = A.alloc("aa", [512], F32)
            for blk in range(3):
                cs = slice(blk * 512, (blk + 1) * 512)
                hk = [('hsT', 4 * blk + i) for i in range(4)]
                bs = []
                for i in range(4):
                    b = P.get()
                    bs.append(b)
                    for kc in range(8):
                        S.op('pe', MM(P.ap(b), wv[:, kc, i * 128:(i + 1) * 128], hsT[:, kc, cs], start=(kc == 0), stop=(kc == 7)),
                             reads=[wk_] + hk, writes=PK(b))
                bx, bb, bc_, bg = bs
                S.op('act', ACTF(xs.ap, P.ap(bx), AF.Copy), reads=PK(bx), writes=[xs.key])
                S.op('dve', TT(u.ap[:, cs], P.ap(bc_), xs.ap, ALU.mult), reads=PK(bc_) + [xs.key], writes=[u.k(blk)])
                S.op('act', ACTF(tgc.ap, P.ap(bg), AF.Tanh, scale=0.5), reads=PK(bg), writes=[tgc.key])
                S.op('dve', STT(aa.ap, tgc.ap, 1.0, P.ap(bg), ALU.add, ALU.mult), reads=[tgc.key] + PK(bg), writes=[aa.key])
                S.op('dve', TT(ab.ap[:, cs], P.ap(bb), aa.ap, ALU.mult), reads=PK(bb) + [aa.key], writes=[ab.k(blk)])
                pump(1)
            S.op('dve', TS(y.ap, u.ap, convp[:, j, c4, 1:2], ALU.mult, convp[:, j, c4, 3:4], ALU.add),
                 reads=[u.key, ('convp', j)], writes=[y.key])
            for (a, L) in segs:
                S.op('dve', STT(y.ap[:, a + 1:a + L], u.ap[:, a:a + L - 1], convp[:, j, c4, 0:1], y.ap[:, a + 1:a + L], ALU.mult, ALU.add),
                     reads=[u.key, y.key, ('convp', j)], writes=[y.key])
                S.op('dve', STT(y.ap[:, a:a + L - 1], u.ap[:, a + 1:a + L], convp[:, j, c4, 2:3], y.ap[:, a:a + L - 1], ALU.mult, ALU.add),
                     reads=[u.key, y.key, ('convp', j)], writes=[y.key])
            S.op('dve', TT(mixT[:, c4, :], y.ap, ab.ap, ALU.mult), reads=[y.key, ab.key],
                 writes=[('mixT', t) for t in range(NT)])
            A.release(mk)
        if stop == 'conv':
            return
        for hg in range(2):
            wqk, wqkk = na_pre.pop('qk')
            wvg, wvgk = wvg_of(hg)
            if hg == 0:
                na_pre['qk'] = wqk_of(1)
            for si, (t0, ntl, ctx) in enumerate(SEQS):
                if hg == 1 and si == 2 and stop is None:
                    preload_out(odwout_d[j])
                mk3 = A.mark()
                NQ = ntl
                ktiles = ([12, 13] if ctx else []) + list(range(t0, t0 + ntl))
                NK = len(ktiles)
                qT = A.alloc("nqT", [4, NQ * 128], BF16)
                kT = A.alloc("nkT", [4, NK * 128], BF16)
                vaug = A.alloc("nvaug", [NK, 4, 66], BF16)
                att = A.alloc("natt", [NQ, 4, 64], BF16)
                abuf = A.alloc("nabuf", [NQ, 256], BF16)
                stage = [A.alloc("nstage%d" % i, [8, 64], F32) for i in range(2)]
                qkn = [A.alloc("nqkn%d" % i, [8, 64], BF16) for i in range(2)]
                sqs = A.alloc("nsqs", [8, 64], BF16)
                ss = A.alloc("nss", [8], F32)
                ms8 = A.alloc("nms8", [8], F32)
                rs8 = A.alloc("nrs8", [8], F32)
                na = A.alloc("nna", [8], F32)
                nb_ = A.alloc("nnb", [8], F32)
                tg = A.alloc("ntg", [256], F32)
                mt = [A.alloc("nmt%d" % i, [256], BF16) for i in range(2)]
                kvf = [A.alloc("kvf%d" % i, [2, 4, 64], F32) for i in range(2)]
                kvout = A.alloc("kvout", [2, 2, 4, 64], F32) if not ctx else None
                S.op('pool', MS(vaug.ap, 1.0), writes=[vaug.key])
                pendB = [None]
                for ki, t in enumerate(ktiles):
                    is_ctx = t >= 12
                    qi = ki - (2 if ctx else 0)
                    st = stage[ki % 2]
                    qk = qkn[ki % 2]
                    kf = kvf[ki % 2]
                    if is_ctx:
                        r0 = (t - 12) * 128
                        S.dma('sp', DMA(kf.ap[:, 0, :, :], nakc_d[j, r0:r0 + 128, hg * 4:(hg + 1) * 4, :]), writes=[kf.k(0)])
                        S.dma('sp', DMA(kf.ap[:, 1, :, :], navc_d[j, r0:r0 + 128, hg * 4:(hg + 1) * 4, :]), writes=[kf.k(1)])
                        S.op('dve', CP(qk.ap[:, 4:8, :], kf.ap[:, 0, :, :]), reads=[kf.k(0)], writes=[qk.key])
                        S.op('act', ACTF(vaug.ap[:, ki, :, 0:64], kf.ap[:, 1, :, :], AF.Copy), reads=[kf.k(1)], writes=[vaug.key])
                        lo = 4
                    else:
                        lo = 0
                        bqk = P.get()
                        for kc in range(8):
                            S.op('pe', MM(P.ap(bqk), hsT[:, kc, cols(t)], wqk[:, kc, :], start=(kc == 0), stop=(kc == 7)),
                                 reads=[('hsT', t), wqkk], writes=PK(bqk))
                        bvg = P.get()
                        for kc in range(8):
                            S.op('pe', MM(P.ap(bvg), hsT[:, kc, cols(t)], wvg[:, kc, :], start=(kc == 0), stop=(kc == 7)),
                                 reads=[('hsT', t), wvgk], writes=PK(bvg))
                        qkv = P.ap(bqk).rearrange("p (h e) -> p h e", e=64)
                        vv = P.ap(bvg)[:, 0:256].rearrange("p (h e) -> p h e", e=64)
                        S.op('act', ACTF(sqs.ap, qkv, AF.Square), reads=PK(bqk), writes=[sqs.key])
                        S.op('dve', RED(ss.ap, sqs.ap), reads=[sqs.key], writes=[ss.key])
                        S.op('dve', TS(ms8.ap, ss.ap, 1.0 / 64, ALU.mult, EPS, ALU.add), reads=[ss.key], writes=[ms8.key])
                        S.op('dve', TT(st.ap[:, 0:4, :], qkv[:, 0:4, :], gq[:, 0:64].unsqueeze(1).broadcast_to([128, 4, 64]), ALU.mult),
                             reads=PK(bqk) + ['gq'], writes=[st.key])
                        S.op('dve', TT(st.ap[:, 4:8, :], qkv[:, 4:8, :], gk[:, 0:64].unsqueeze(1).broadcast_to([128, 4, 64]), ALU.mult),
                             reads=PK(bqk) + ['gk'], writes=[st.key])
                        S.op('act', ACTF(vaug.ap[:, ki, :, 0:64], vv, AF.Copy), reads=PK(bvg), writes=[vaug.key])
                        if not ctx:
                            S.op('act', ACTF(kvout.ap[:, 1, qi, :, :], vv, AF.Copy), reads=PK(bvg), writes=[kvout.k((1, qi))])
                        rsqrt(rs8.ap, ms8.ap, na.ap, nb_.ap, [ms8.key], [rs8.key, na.key, nb_.key])
                        S.op('dve', TT(qk.ap, st.ap, rs8.ap.unsqueeze(2).broadcast_to([128, 8, 64]), ALU.mult),
                             reads=[st.key, rs8.key], writes=[qk.key])
                        S.op('act', ACTF(tg.ap, P.ap(bvg)[:, 256:512], AF.Tanh, scale=0.5), reads=PK(bvg), writes=[tg.key])
                        S.op('dve', STT(abuf.ap[:, qi, :], tg.ap, 1.0, P.ap(bvg)[:, 256:512], ALU.add, ALU.mult),
                             reads=[tg.key] + PK(bvg), writes=[abuf.k(qi)])
                        if not ctx:
                            for h_ in range(4):
                                S.op('act', ACTF(kvout.ap[:, 0, qi, h_, :], st.ap[:, 4 + h_, :], AF.Copy,
                                                 scale=rs8.ap[:, 4 + h_:5 + h_]),
                                     reads=[st.key, rs8.key], writes=[kvout.k((0, qi))])
                    def stageB(qk=qk, lo=lo, is_ctx=is_ctx, qi=qi, ki=ki):
                        bt = P.get(2)
                        for i in range(lo, 8):
                            S.op('pe', MM(P.ap(bt, 2)[0:64, i * 128:(i + 1) * 128], qk.ap[:, i, :], identb[:]),
                                 reads=[qk.key, 'identb'], writes=PK(bt, 2))
                        if not is_ctx:
                            S.op('act', ACTF(qT.ap[0:64, :, qi * 128:(qi + 1) * 128],
                                             P.ap(bt, 2)[0:64, 0:512].rearrange("p (h q) -> p h q", h=4), AF.Copy),
                                 reads=PK(bt, 2), writes=[qT.k(qi)])
                        S.op('dve', CP(kT.ap[0:64, :, ki * 128:(ki + 1) * 128],
                                       P.ap(bt, 2)[0:64, 512:1024].rearrange("p (h q) -> p h q", h=4)),
                             reads=PK(bt, 2), writes=[kT.k(ki)])
                    if pendB[0] is not None:
                        pendB[0]()
                    pendB[0] = stageB
                    pump(1)
                pendB[0]()
                pendB[0] = None
                if not ctx:
                    for kv_, dst_ in ((0, onak_d), (1, onav_d)):
                        for qo in range(NQ):
                            S.dma('pool', DMA(dst_[si, j, qo * 128:(qo + 1) * 128, hg * 4:(hg + 1) * 4, :],
                                              kvout.ap[:, kv_, qo, :, :]), reads=[kvout.key], is_output=True)
                if ctx:
                    xraw = [A.alloc("xraw%d" % i, [1024], BF16) for i in range(2)]
                    xtab = [[A.alloc("xtab%d_%d" % (i, k_), [1024], BF16) for k_ in range(2)] for i in range(2)]

                    def hook(hh, hg=hg):
                        hd = hg * 4 + hh
                        xr = xraw[hh % 2]
                        S.dma('pool', DMA(xr.ap, rpbx_d[j, hd]), writes=[xr.key])
                        for k_ in range(2):
                            S.op('dve', TT(xtab[hh % 2][k_].ap, xr.ap, cmaskb[:, k_, :], ALU.add),
                                 reads=[xr.key, 'cmaskb'], writes=[xtab[hh % 2][k_].key])

                    def bias_fn(hh, qi, kidx):
                        if kidx < 2:
                            return None
                        jt = kidx - 2
                        w0 = 7 - 2 * (jt - qi)
                        k_ = 0 if qi in (0, 1, 6, 7) else 1
                        tb_ = xtab[hh % 2][k_]
                        return (tb_.ap[:, w0 * 64:(w0 + 2) * 64], [tb_.key])

                    if stop == 'na_prep':
                        return
                    attention(qT, kT, 64, vaug, att, NQ, na_blocks, bias_fn=bias_fn, head_hook=hook)
                else:
                    attention(qT, kT, 64, vaug, att, NQ, lambda qi_, NK=NK: list(range(NK)))
                if stop == 'na_att':
                    return
                for qi in range(NQ):
                    t = t0 + qi
                    m_ = mt[qi % 2]
                    S.op('dve', TT(m_.ap, abuf.ap[:, qi, :], att.ap[:, qi, :, :].rearrange("p h e -> p (h e)"), ALU.mult),
                         reads=[abuf.k(qi), att.k(qi)], writes=[m_.key])
                    bT = P.get()
                    for i in range(2):
                        S.op('pe', MM(P.ap(bT)[:, i * 128:(i + 1) * 128], m_.ap[:, i * 128:(i + 1) * 128], identb[:]),
                             reads=[m_.key, 'identb'], writes=PK(bT))
                    S.op('act', ACTF(mixT[:, 4 + hg * 2:4 + hg * 2 + 2, cols(t)],
                                     P.ap(bT)[:, 0:256].rearrange("p (c q) -> p c q", c=2), AF.Copy, scale=0.5),
                         reads=PK(bT), writes=[('mixT', t)])
                A.release(mk3)
                if stop == 'na_%d' % si:
                    return

    queue_mod(0)
    load_x()
    pump(24)
    for l in range(depth):
        if stop is None:
            if l % 2 == 0:
                pre_w['wa'] = load_w([evwin_d[l // 2][:, 0:416]])
            else:
                Wo = odwin_d[l // 2]
                pre_w['c0'] = load_w([Wo[:, i * 512:i * 512 + 128] for i in range(4)])
        phase_norm(l)
        if stop == 'norm':
            break
        if l + 1 < depth:
            queue_mod(l + 1)
        TAP('hsT%d' % l, hsT[:], [128, 8, NTOK], [('hsT', t) for t in range(NT)], BF16)
        last = (l == depth - 1)
        sb_ = (lambda blk: store_tiles(range(4 * blk, 4 * blk + 4))) if last else None
        if l % 2 == 0:
            even_layer(l)
            TAP('mixT%d' % l, mixT[:], [128, 8, NTOK], [('mixT', t) for t in range(NT)], BF16)
            pump(24)
            phase_out(l, evwout_d[l // 2], sb_)
        else:
            odd_layer(l)
            pump(24)
            phase_out(l, odwout_d[l // 2], sb_)
    if depth == 0:
        store_y()
    counts, nw = S.emit(nc, es)
    info = {'ops': counts, 'waits': nw, 'arena_peak': A.peak, 'taps': tap_list}
    es.close()
    return nc, info


def _constants():
    n = np.arange(1024)
    row = (n // GRID_W).astype(np.float32)
    col = (n % GRID_W).astype(np.float32)
    inv = (np.float32(ROPE_BASE) ** (-np.arange(8, dtype=np.float32) / np.float32(8))).astype(np.float32)
    ar = (row[:, None] * inv[None, :]).astype(np.float32)
    ac = (col[:, None] * inv[None, :]).astype(np.float32)
    C = np.concatenate([np.cos(ar), np.cos(ar), np.cos(ac), np.cos(ac)], axis=1).astype(np.float32)
    Sg = np.concatenate([-np.sin(ar), np.sin(ar), -np.sin(ac), np.sin(ac)], axis=1).astype(np.float32)
    rope_cs = np.stack([C, Sg]).astype(np.float32)
    s = np.arange(128)[:, None]
    t = np.arange(128)[None, :]
    same = (s // 64) == (t // 64)
    lmask = np.stack([np.where(same & (s <= t), 0.0, NEG), np.where(same & (s >= t), 0.0, NEG)]).astype(np.float32)
    kl = np.arange(2)[:, None, None, None]
    kc = np.arange(64)[None, :, None, None]
    w = np.arange(16)[None, None, :, None]
    qc = np.arange(64)[None, None, None, :]
    dr = np.broadcast_to(7 - w + kl, (2, 64, 16, 64))
    dc = np.broadcast_to(kc - qc, (2, 64, 16, 64))
    c0 = np.clip(qc - 8, 0, 48)
    col_ok = np.broadcast_to((kc >= c0) & (kc < c0 + 16), (2, 64, 16, 64))
    ok_full = col_ok & (np.abs(dr) <= 7)
    ok_int = ok_full & (dr >= -4) & (dr <= 3)
    cmask = np.stack([np.where(ok_full, 0.0, NEG), np.where(ok_int, 0.0, NEG)]).astype(np.float32).reshape(2, 128, 1024)
    idx_r = np.clip(dr + 7, 0, 14).reshape(128, 1024)
    idx_c = np.clip(dc + 15, 0, 30).reshape(128, 1024)
    return rope_cs, lmask, cmask, idx_r, idx_c


_CACHE = {}


def _get_program(depth, taps=(), stop=None):
    key = (depth, tuple(taps), stop)
    if key not in _CACHE:
        _CACHE[key] = build_nc(depth, taps, stop)
    return _CACHE[key]


def kernel(x_prompt, x_sample, c, cache_mla_ckv, cache_mla_kpe, state_mlstm_C, state_mlstm_n, state_mlstm_m,
           cache_na_k, cache_na_v, c_ctx, norm_w, ada_w, ada_b,
           ev_w_in, ev_q_a_norm, ev_kv_a_norm, ev_w_q_b, ev_w_kv_b, ev_q_norm, ev_k_norm, ev_b_i, ev_b_f,
           ev_h_norm, ev_w_out,
           od_w_in, od_conv_w, od_conv_b, od_q_norm, od_k_norm, od_rpb, od_w_out, _depth=4, _taps=(), _raw=False, _stop=None):
    f = lambda a: np.ascontiguousarray(np.asarray(a), dtype=np.float32)
    x_prompt, x_sample, c, c_ctx = f(x_prompt), f(x_sample), f(c), f(c_ctx)
    rope_cs, lmask, cmask, idx_r, idx_c = _constants()
    od_rpb = f(od_rpb)
    rpbx = np.ascontiguousarray(od_rpb[:, :, idx_r, idx_c])
    shared = {
        "norm_w": f(norm_w), "ada_w": f(ada_w), "ada_b": f(ada_b),
        "ev_w_in": f(ev_w_in), "ev_q_a_norm": f(ev_q_a_norm), "ev_kv_a_norm": f(ev_kv_a_norm),
        "ev_w_q_b": f(ev_w_q_b), "ev_w_kv_b": f(ev_w_kv_b), "ev_q_norm": f(ev_q_norm), "ev_k_norm": f(ev_k_norm),
        "ev_b_i": f(ev_b_i), "ev_b_f": f(ev_b_f), "ev_h_norm": f(ev_h_norm), "ev_w_out": f(ev_w_out),
        "od_w_in": f(od_w_in), "od_conv_w": f(od_conv_w), "od_conv_b": f(od_conv_b), "od_q_norm": f(od_q_norm),
        "od_k_norm": f(od_k_norm), "od_w_out": f(od_w_out), "rpbx": rpbx, "cmask": cmask, "rope_cs": rope_cs,
        "lmask": lmask,
    }
    cm_ckv, cm_kpe = f(cache_mla_ckv), f(cache_mla_kpe)
    sC, sn, sm = f(state_mlstm_C), f(state_mlstm_n), f(state_mlstm_m)
    nk, nv = f(cache_na_k), f(cache_na_v)
    in_maps = []
    for core in range(8):
        b = core // 4
        m = dict(shared)
        m["x"] = np.ascontiguousarray(np.concatenate([x_prompt[2 * core], x_prompt[2 * core + 1], x_sample[b]], axis=0))
        m["cond"] = np.ascontiguousarray(np.stack([c_ctx, c[b]]))
        m["ckv_c"] = cm_ckv[b]
        m["kpe_c"] = cm_kpe[b]
        m["stC"] = sC[b]
        m["stn"] = sn[b]
        m["stm"] = sm[b]
        m["nak_c"] = nk[b]
        m["nav_c"] = nv[b]
        in_maps.append(m)
    nc, info = _get_program(_depth, _taps, _stop)
    res = run_bass_kernel_spmd(nc, in_maps, core_ids=list(range(8)))
    R = res.results
    if _raw:
        return R, info
    yp = np.concatenate([R[i]["yp"].reshape(2, 256, D) for i in range(8)], axis=0)
    ys = np.stack([R[0]["ys"], R[4]["ys"]], axis=0)
    cat = lambda k: np.concatenate([R[i][k] for i in range(8)], axis=0)
    return (yp.astype(np.float32), ys.astype(np.float32), cat("o_ckv"), cat("o_kpe"), cat("o_C"), cat("o_n"),
            cat("o_m"), cat("o_nak"), cat("o_nav"))
```

```python
import os
from contextlib import ExitStack

import numpy as np
import concourse.bass as bass
import concourse.mybir as mybir
from concourse.bass_utils import run_bass_kernel_spmd

F32 = mybir.dt.float32
BF16 = mybir.dt.bfloat16
I32 = mybir.dt.int32
AF = mybir.ActivationFunctionType
ALU = mybir.AluOpType
AX = mybir.AxisListType

D = 1024
NT = 12
NTOK = 1536
EPS = 1e-6
NEG = -30000.0
GRID_W = 64
ROPE_BASE = 10000.0

ENGS = ['pe', 'act', 'dve', 'pool', 'sp']
N_DMA_SEMS = 24
SAME_ENG_WINDOW = 6


class _Op:
    __slots__ = ('eng', 'fn', 'deps_c', 'deps_d', 'sig', 'idx', 'is_dma', 'dsem', 'dval')


class Sched:
    def __init__(self):
        self.ops = {e: [] for e in ENGS}
        self.bufs = {}
        self.dma_cnt = [0] * N_DMA_SEMS
        self.dma_last = [None] * N_DMA_SEMS
        self.dma_rr = 0
        self.out_tokens = []

    def reg(self, name, rng=None):
        self.bufs[name] = {'range': rng, 'subs': {}}

    def _overlapping(self, name):
        b = self.bufs[name]
        res = [name]
        if b['range'] is None:
            return res
        a0, a1 = b['range']
        for n2, b2 in self.bufs.items():
            if n2 == name or b2['range'] is None:
                continue
            c0, c1 = b2['range']
            if c0 < a1 and a0 < c1 and b2['subs']:
                res.append(n2)
        return res

    @staticmethod
    def _norm(key):
        if isinstance(key, tuple):
            return key[0], key[1]
        return key, None

    def _collect(self, key, is_write, dc, dd):
        name, sub = self._norm(key)
        if name not in self.bufs:
            self.reg(name)
        for n2 in self._overlapping(name):
            subs = self.bufs[n2]['subs']
            if n2 == name and sub is not None:
                cands = [s for s in (sub, None) if s in subs]
            else:
                cands = list(subs.keys())
            for s in cands:
                st = subs[s]
                toks = [st[0]] if st[0] is not None else []
                if is_write:
                    toks = toks + list(st[1].values())
                for t in toks:
                    if t[0] == 'dma':
                        dd[t[1]] = max(dd.get(t[1], 0), t[2])
                    else:
                        dc[t[0]] = max(dc.get(t[0], -1), t[1])

    def _update(self, key, is_write, tok):
        name, sub = self._norm(key)
        subs = self.bufs[name]['subs']
        if is_write:
            if sub is None:
                for n2 in self._overlapping(name):
                    if n2 != name:
                        self.bufs[n2]['subs'] = {}
                subs.clear()
            subs[sub] = [tok, {}]
        else:
            if sub not in subs:
                subs[sub] = [None, {}]
            rk = tok[0] if tok[0] != 'dma' else ('dma', tok[1])
            subs[sub][1][rk] = tok

    def op(self, eng, fn, reads=(), writes=()):
        o = _Op()
        o.eng, o.fn, o.sig, o.is_dma = eng, fn, False, False
        o.idx = len(self.ops[eng])
        dc, dd = {}, {}
        for k in reads:
            self._collect(k, False, dc, dd)
        for k in writes:
            self._collect(k, True, dc, dd)
        o.deps_c, o.deps_d = dc, dd
        tok = (eng, o.idx)
        self.ops[eng].append(o)
        for k in reads:
            self._update(k, False, tok)
        for k in writes:
            self._update(k, True, tok)
        return o

    def dma(self, eng, fn, reads=(), writes=(), is_output=False):
        o = _Op()
        o.eng, o.fn, o.sig, o.is_dma = eng, fn, False, True
        o.idx = len(self.ops[eng])
        s = self.dma_rr
        self.dma_rr = (self.dma_rr + 1) % N_DMA_SEMS
        dc, dd = {}, {}
        if self.dma_last[s] is not None:
            dd[s] = self.dma_cnt[s]
        for k in reads:
            self._collect(k, False, dc, dd)
        for k in writes:
            self._collect(k, True, dc, dd)
        self.dma_cnt[s] += 16
        o.dsem, o.dval = s, self.dma_cnt[s]
        self.dma_last[s] = o
        o.deps_c, o.deps_d = dc, dd
        tok = ('dma', s, o.dval)
        self.ops[eng].append(o)
        for k in reads:
            self._update(k, False, tok)
        for k in writes:
            self._update(k, True, tok)
        if is_output:
            self.out_tokens.append(tok)
        return o

    @staticmethod
    def _need(o, x, j):
        if x != o.eng or o.is_dma:
            return True
        if o.eng == 'pe':
            return False
        return (o.idx - j) <= SAME_ENG_WINDOW

    def emit(self, nc, es):
        for e in ENGS:
            for o in self.ops[e]:
                for x, j in o.deps_c.items():
                    if self._need(o, x, j):
                        self.ops[x][j].sig = True
        cnt = {}
        for e in ENGS:
            c = 0
            arr = []
            for o in self.ops[e]:
                if o.sig and not o.is_dma:
                    c += 1
                arr.append(c)
            cnt[e] = arr
        sems = {e: es.enter_context(nc.semaphore('s_' + e)) for e in ENGS}
        dsems = [es.enter_context(nc.semaphore('d%d' % i)) for i in range(N_DMA_SEMS)]
        block = es.enter_context(nc.Block())
        handles = {'pe': block.tensor, 'act': block.scalar, 'dve': block.vector,
                   'pool': block.gpsimd, 'sp': block.sync}
        nwaits = [0]

        def run_engine(e):
            def body(eng):
                waited_c = {}
                waited_d = {}
                for o in self.ops[e]:
                    for x, j in o.deps_c.items():
                        if not self._need(o, x, j):
                            continue
                        v = cnt[x][j]
                        if v > waited_c.get(x, 0):
                            eng.wait_ge(sems[x], v)
                            waited_c[x] = v
                            nwaits[0] += 1
                    for s, v in o.deps_d.items():
                        if v > waited_d.get(s, 0):
                            eng.wait_ge(dsems[s], v)
                            waited_d[s] = v
                            nwaits[0] += 1
                    ins = o.fn(eng)
                    if o.is_dma:
                        ins.then_inc(dsems[o.dsem], 16)
                    elif o.sig:
                        ins.then_inc(sems[e], 1)
                if e == 'sp':
                    fin = {}
                    for t in self.out_tokens:
                        fin[t[1]] = max(fin.get(t[1], 0), t[2])
                    for s, v in fin.items():
                        if v > waited_d.get(s, 0):
                            eng.wait_ge(dsems[s], v)
            return body

        for e in ENGS:
            handles[e](run_engine(e))
        return {e: len(self.ops[e]) for e in ENGS}, nwaits[0]


def MM(out, lhsT, rhs, start=True, stop=True):
    return lambda e: e.matmul(out, lhsT=lhsT, rhs=rhs, start=start, stop=stop)


def ACTF(out, in_, func, bias=None, scale=None, accum=None):
    def f(e):
        kw = {}
        if bias is not None:
            kw['bias'] = bias
        if scale is not None:
            kw['scale'] = scale
        if accum is not None:
            kw['accum_out'] = accum
        return e.activation(out=out, in_=in_, func=func, **kw)
    return f


def TT(out, a, b, op):
    return lambda e: e.tensor_tensor(out=out, in0=a, in1=b, op=op)


def TS(out, a, s1, op0, s2=None, op1=None):
    def f(e):
        if op1 is None:
            return e.tensor_scalar(out=out, in0=a, scalar1=s1, scalar2=None, op0=op0)
        return e.tensor_scalar(out=out, in0=a, scalar1=s1, scalar2=s2, op0=op0, op1=op1)
    return f


def STT(out, a, s, b, op0, op1):
    return lambda e: e.scalar_tensor_tensor(out=out, in0=a, scalar=s, in1=b, op0=op0, op1=op1)


def CP(out, in_):
    return lambda e: e.tensor_copy(out=out, in_=in_)


def RED(out, in_, op=ALU.add):
    return lambda e: e.tensor_reduce(out=out, in_=in_, axis=AX.X, op=op)


def MS(ap, v):
    return lambda e: e.memset(ap, v)


def RCP(out, in_):
    return lambda e: e.reciprocal(out=out, in_=in_)


def SCAN(out, d0, d1, init, op0, op1):
    return lambda e: e.tensor_tensor_scan(out=out, data0=d0, data1=d1, initial=init, op0=op0, op1=op1)


def DMA(out, in_, **kw):
    return lambda e: e.dma_start(out=out, in_=in_, **kw)


def ASEL(out, in_, pattern, cmp, fill, base, cm):
    return lambda e: e.affine_select(out=out, in_=in_, pattern=pattern, compare_op=cmp, fill=fill,
                                     base=base, channel_multiplier=cm)


class Buf:
    __slots__ = ('ap', 'key')

    def __init__(self, ap, key):
        self.ap, self.key = ap, key

    def k(self, sub):
        return (self.key, sub)


class Arena:
    BLK = 32

    def __init__(self, S, tensor, words):
        self.S, self.t, self.words = S, tensor, words
        self.top = 0
        self.ctr = 0
        self.peak = 0
        self.live = []
        self.blocks = [dict() for _ in range(words // self.BLK + 2)]

    def mark(self):
        return self.top

    @staticmethod
    def _fold(dst, tok):
        rk = tok[0] if tok[0] != 'dma' else ('dma', tok[1])
        old = dst.get(rk)
        if old is None or tok[-1] > old[-1]:
            dst[rk] = tok

    def release(self, m):
        while self.live and self.live[-1][0] >= m:
            off, w, key = self.live.pop()
            st = self.S.bufs.pop(key, None)
            if st is None:
                continue
            toks = []
            for sub, (wt, rd) in st['subs'].items():
                if wt is not None:
                    toks.append(wt)
                toks.extend(rd.values())
            for bi in range(off // self.BLK, (off + w - 1) // self.BLK + 1):
                blk = self.blocks[bi]
                for t_ in toks:
                    self._fold(blk, t_)
        self.top = m

    def alloc(self, name, shape, dtype=F32, parts=128):
        n = 1
        for s in shape:
            n *= s
        w = n if dtype == F32 else (n + 1) // 2
        w = (w + 1) // 2 * 2
        off = self.top
        self.top += w
        self.peak = max(self.peak, self.top)
        assert self.top <= self.words, "arena overflow %s: %d > %d" % (name, self.top, self.words)
        ap = self.t[0:parts, off:off + w]
        if dtype != F32:
            ap = ap.bitcast(dtype)
        ap = ap[:, 0:n]
        if len(shape) > 1:
            names = ["d%d" % i for i in range(len(shape))]
            kw = {names[i]: shape[i] for i in range(len(shape))}
            ap = ap.rearrange("p (" + " ".join(names) + ") -> p " + " ".join(names), **kw)
        self.ctr += 1
        key = "%s#%d" % (name, self.ctr)
        self.S.reg(key, None)
        inh = {}
        for bi in range(off // self.BLK, (off + w - 1) // self.BLK + 1):
            for t_ in self.blocks[bi].values():
                self._fold(inh, t_)
        if inh:
            self.S.bufs[key]['subs'][None] = [None, inh]
        self.live.append((off, w, key))
        return Buf(ap, key)


class PSum:
    def __init__(self, ps):
        self.ps = ps
        self.next = 0
        self.limit = 7

    def get(self, nb=1):
        if self.next + nb > self.limit:
            self.next = 0
        b = self.next
        self.next = (self.next + nb) % self.limit
        return b

    def ap(self, b, nb=1):
        return self.ps[:, b * 512:(b + nb) * 512]

    @staticmethod
    def keys(b, nb=1):
        return [('ps', b + i) for i in range(nb)]


SEQS = [(0, 2, False), (2, 2, False), (4, 8, True)]
ARENA_WORDS = 15400
WSLOT = 4096
NWSLOT = 3


def build_nc(depth=4, taps=(), stop=None):
    nc = bass.Bass("TRN2", target_bir_lowering=False)
    S = Sched()
    es = ExitStack()

    def din(name, shape):
        return nc.dram_tensor(name, list(shape), F32, kind="ExternalInput").ap()

    def dout(name, shape):
        return nc.dram_tensor(name, list(shape), F32, kind="ExternalOutput").ap()

    x_d = din("x", [NTOK, D])
    cond_d = din("cond", [2, D])
    ckvc_d = din("ckv_c", [2, 256, 128])
    kpec_d = din("kpe_c", [2, 256, 32])
    stC_d = din("stC", [2, 2, 4, 128, 128])
    stn_d = din("stn", [2, 2, 4, 128])
    stm_d = din("stm", [2, 2, 4])
    nakc_d = din("nak_c", [2, 256, 8, 64])
    navc_d = din("nav_c", [2, 256, 8, 64])
    normw_d = din("norm_w", [4, D])
    adaw_d = din("ada_w", [4, D, 3 * D])
    adab_d = din("ada_b", [4, 3 * D])
    evwin_d = din("ev_w_in", [2, D, 3504])
    evqan_d = din("ev_q_a_norm", [2, 256])
    evkvan_d = din("ev_kv_a_norm", [2, 128])
    evwqb_d = din("ev_w_q_b", [2, 256, 768])
    evwkvb_d = din("ev_w_kv_b", [2, 128, 1024])
    evqn_d = din("ev_q_norm", [2, 96])
    evkn_d = din("ev_k_norm", [2, 96])
    evbi_d = din("ev_b_i", [2, 2, 4])
    evbf_d = din("ev_b_f", [2, 2, 4])
    evhn_d = din("ev_h_norm", [2, 512])
    evwout_d = din("ev_w_out", [2, D, D])
    odwin_d = din("od_w_in", [2, D, 4096])
    odcw_d = din("od_conv_w", [2, 3, 512])
    odcb_d = din("od_conv_b", [2, 512])
    odqn_d = din("od_q_norm", [2, 64])
    odkn_d = din("od_k_norm", [2, 64])
    odwout_d = din("od_w_out", [2, D, D])
    rpbx_d = din("rpbx", [2, 8, 128, 1024])
    cmask_d = din("cmask", [2, 128, 1024])
    ropecs_d = din("rope_cs", [2, 1024, 32])
    lmask_d = din("lmask", [2, 128, 128])

    yp_d = dout("yp", [512, D])
    ys_d = dout("ys", [1024, D])
    ockv_d = dout("o_ckv", [2, 2, 256, 128])
    okpe_d = dout("o_kpe", [2, 2, 256, 32])
    oC_d = dout("o_C", [2, 2, 2, 4, 128, 128])
    on_d = dout("o_n", [2, 2, 2, 4, 128])
    om_d = dout("o_m", [2, 2, 2, 4])
    onak_d = dout("o_nak", [2, 2, 256, 8, 64])
    onav_d = dout("o_nav", [2, 2, 256, 8, 64])

    tap_list = []

    def sb(name, shape, dt):
        return es.enter_context(nc.sbuf_tensor(name, list(shape), dt))

    yT = sb("yT", [128, 8, NTOK], F32)
    hsT = sb("hsT", [128, 8, NTOK], BF16)
    mixT = sb("mixT", [128, 8, NTOK], BF16)
    wring = [sb("wring%d" % i, [128, WSLOT], BF16) for i in range(NWSLOT)]
    scb = sb("scb", [128, 8, 2], BF16)
    wmod = [sb("wmod%d" % i, [128, 8, 128], BF16) for i in range(2)]
    mrow = sb("mrow", [2, 128], F32)
    wqb = sb("wqb", [128, 2, 768], BF16)
    wkvb = sb("wkvb", [128, 1024], BF16)
    wgate = sb("wgate", [128, 8, 16], BF16)
    identf = sb("identf", [128, 128], F32)
    identb = sb("identb", [128, 128], BF16)
    onesb = sb("onesb", [128, 128], BF16)
    sel = sb("sel", [36, 4, 128], F32)
    lmaskb = sb("lmaskb", [128, 2, 128], BF16)
    ropeC = sb("ropeC", [128, 8, 32], F32)
    ropeS = sb("ropeS", [128, 8, 32], F32)
    cmaskb = sb("cmaskb", [128, 2, 1024], BF16)
    pvA = sb("pvA", [128, 128], F32)
    pvB = sb("pvB", [128, 64], F32)
    scond = sb("scond", [128, 16], F32)
    modT = sb("modT", [128, 4, 24, 2], F32)
    modA = sb("modA", [128, 4, 8, 2], F32)
    gq = sb("gq", [128, 96], F32)
    gk = sb("gk", [128, 96], F32)
    ghn = sb("ghn", [128, 512], F32)
    bif = sb("bif", [36, 2, 2], F32)
    m0t = sb("m0t", [36, 2], F32)
    convp = sb("convp", [128, 2, 4, 4], F32)
    arena_t = sb("arena", [128, ARENA_WORDS], F32)
    ps_t = es.enter_context(nc.psum_tensor("ps", [128, 4096], F32))

    A = Arena(S, arena_t, ARENA_WORDS)
    P = PSum(ps_t)
    PK = PSum.keys

    def TAP(name, ap, shape, reads, dt=F32):
        if name not in taps:
            return
        d = nc.dram_tensor("tap_" + name, list(shape), dt, kind="ExternalOutput").ap()
        S.dma('sp', DMA(d, ap), reads=reads, is_output=True)
        tap_list.append(name)

    def cols(t, n=1):
        return slice(128 * t, 128 * (t + n))

    S.op('pool', MS(identf[:], 1.0), writes=['identf'])
    S.op('pool', ASEL(identf[:], identf[:], [[-1, 128]], ALU.is_equal, 0.0, 0, 1),
         reads=['identf'], writes=['identf'])
    S.op('dve', CP(identb[:], identf[:]), reads=['identf'], writes=['identb'])
    S.op('pool', MS(onesb[:], 1.0), writes=['onesb'])
    S.op('pool', MS(sel[:], 1.0), writes=['sel'])
    for g0 in (0, 32):
        for h in range(4):
            S.op('pool', ASEL(sel[g0:g0 + 4, h, :], sel[g0:g0 + 4, h, :], [[0, 128]], ALU.is_equal, 0.0, -h, 1),
                 reads=['sel'], writes=['sel'])
    S.dma('pool', DMA(lmaskb[:], lmask_d.rearrange("d p q -> p d q")), writes=['lmaskb'])
    S.dma('pool', DMA(cmaskb[:], cmask_d.rearrange("d p q -> p d q")), writes=['cmaskb'])
    S.dma('sp', DMA(ropeC[:], ropecs_d[0].rearrange("(t p) e -> p t e", p=128)), writes=['ropeC'])
    S.dma('sp', DMA(ropeS[:], ropecs_d[1].rearrange("(t p) e -> p t e", p=128)), writes=['ropeS'])

    mk = A.mark()
    stgA = A.alloc("stgA", [128], F32)
    stgB = A.alloc("stgB", [128], F32, parts=64)
    S.dma('sp', DMA(stgA.ap[0:96, :], adab_d.rearrange("l (c p) -> (l c) p", p=128)), writes=[stgA.k(0)])
    S.dma('sp', DMA(stgA.ap[96:128, :], normw_d.rearrange("l (c p) -> (l c) p", p=128)), writes=[stgA.k(1)])
    S.dma('sp', DMA(stgB.ap[0:16, :], cond_d.rearrange("g (c p) -> (g c) p", p=128)), writes=[stgB.k(0)])
    S.dma('sp', DMA(stgB.ap[16:20, :], evqan_d.rearrange("l (c p) -> (l c) p", p=128)), writes=[stgB.k(1)])
    S.dma('sp', DMA(stgB.ap[20:22, :], evkvan_d), writes=[stgB.k(2)])
    S.dma('sp', DMA(stgB.ap[22:46, :], odcw_d.rearrange("l k (c p) -> (l k c) p", p=128)), writes=[stgB.k(3)])
    S.dma('sp', DMA(stgB.ap[46:54, :], odcb_d.rearrange("l (c p) -> (l c) p", p=128)), writes=[stgB.k(4)])
    b = P.get()
    S.op('pe', MM(P.ap(b)[:, 0:128], stgA.ap, identf[:]), reads=[stgA.key, 'identf'], writes=PK(b))
    S.op('dve', CP(pvA[:], P.ap(b)[:, 0:128]), reads=PK(b), writes=['pvA'])
    b = P.get()
    S.op('pe', MM(P.ap(b)[:, 0:54], stgB.ap[0:54, :], identf[0:54, 0:54]), reads=[stgB.key, 'identf'], writes=PK(b))
    S.op('dve', CP(pvB[:, 0:54], P.ap(b)[:, 0:54]), reads=PK(b), writes=['pvB'])
    A.release(mk)

    def pv_adab(l):
        return pvA[:, l * 24:(l + 1) * 24]

    def pv_normw(l):
        return pvA[:, 96 + l * 8:96 + (l + 1) * 8]

    def pv_qan(j, c2):
        return pvB[:, 16 + j * 2 + c2:16 + j * 2 + c2 + 1]

    def pv_kvan(j):
        return pvB[:, 20 + j:21 + j]

    mk = A.mark()
    tnh = A.alloc("tnh", [16], F32)
    S.op('act', ACTF(tnh.ap, pvB[:, 0:16], AF.Tanh, scale=0.5), reads=['pvB'], writes=[tnh.key])
    S.op('dve', STT(tnh.ap, tnh.ap, 1.0, pvB[:, 0:16], ALU.add, ALU.mult), reads=[tnh.key, 'pvB'], writes=[tnh.key])
    S.op('dve', TS(scond[:], tnh.ap, 0.5, ALU.mult), reads=[tnh.key], writes=['scond'])
    S.op('dve', CP(scb[:], scond[:].rearrange("p (g c) -> p c g", g=2)), reads=['scond'], writes=['scb'])
    A.release(mk)
    for j in range(2):
        wv = pvB[:, 22 + j * 12:22 + (j + 1) * 12].rearrange("p (k c) -> p c k", k=3)
        S.op('dve', TS(convp[:, j, :, 0:3], wv, 0.5, ALU.mult), reads=['pvB'], writes=[('convp', j)])
        bv = pvB[:, 46 + j * 4:46 + (j + 1) * 4]
        S.op('dve', TS(convp[:, j, :, 3], bv, 0.5, ALU.mult), reads=['pvB'], writes=[('convp', j)])
    for d_, g0 in ((0, 0), (1, 32)):
        S.dma('sp', DMA(bif[g0:g0 + 4, 0, :], evbi_d[:, d_, :].rearrange("j h -> h j"), allow_slow_non_contiguous=True),
              writes=['bif'])
        S.dma('sp', DMA(bif[g0:g0 + 4, 1, :], evbf_d[:, d_, :].rearrange("j h -> h j"), allow_slow_non_contiguous=True),
              writes=['bif'])
        S.dma('sp', DMA(m0t[g0:g0 + 4, :], stm_d[:, d_, :].rearrange("j h -> h j"), allow_slow_non_contiguous=True),
              writes=['m0t'])

    wstate = {'i': 0}
    pre_w = {}

    def load_w(pieces):
        i = wstate['i']
        wstate['i'] = (i + 1) % NWSLOT
        tot = sum(p.shape[1] for p in pieces)
        assert 8 * tot <= WSLOT, tot
        view = wring[i][:, 0:8 * tot].rearrange("p (k n) -> p k n", k=8)
        key = 'wring%d' % i
        off = 0
        for pi, pc in enumerate(pieces):
            n = pc.shape[1]
            S.dma('pool', DMA(view[:, :, off:off + n], pc.rearrange("(k p) n -> p k n", p=128)),
                  writes=[(key, pi)])
            off += n
        return view, key

    def rsqrt(dst, src, t1, t2, rk, wk):
        S.op('dve', TS(dst.bitcast(I32), src.bitcast(I32), -0.5, ALU.mult, float(0x5f3759df), ALU.add),
             reads=rk, writes=wk)
        for _ in range(2):
            S.op('dve', TT(t2, src, dst, ALU.mult), reads=rk + wk, writes=wk)
            S.op('dve', STT(t2, t2, -0.5, dst, ALU.mult, ALU.mult), reads=wk, writes=wk)
            S.op('dve', STT(dst, t2, 1.5, dst, ALU.add, ALU.mult), reads=wk, writes=wk)

    modq = []

    def queue_mod(l):
        def dma_piece(cg):
            S.dma('pool', DMA(wmod[cg % 2][:], adaw_d[l][:, cg * 128:(cg + 1) * 128].rearrange("(k p) n -> p k n", p=128)),
                  writes=['wmod%d' % (cg % 2)])

        def step(cg):
            def f():
                if cg == 0:
                    dma_piece(0)
                if cg + 1 < 24:
                    dma_piece(cg + 1)
                wv, wkey = wmod[cg % 2], 'wmod%d' % (cg % 2)
                for kc in range(8):
                    S.op('pe', MM(P.ap(7)[:, cg * 2:cg * 2 + 2], wv[:, kc, :], scb[:, kc, :], start=(kc == 0), stop=(kc == 7)),
                         reads=[wkey, 'scb'], writes=PK(7))
                if cg == 23:
                    pb = P.ap(7)
                    S.op('dve', TT(modT[:, l, :, :], pb[:, 0:48].rearrange("p (c g) -> p c g", g=2),
                                   pv_adab(l).unsqueeze(2).broadcast_to([128, 24, 2]), ALU.add),
                         reads=PK(7) + ['pvA'], writes=[('modT', l)])
                    S.op('dve', TS(modA[:, l, :, :], modT[:, l, 8:16, :], 1.0, ALU.add), reads=[('modT', l)], writes=[('modA', l)])
                    S.op('dve', TT(modA[:, l, :, :], modA[:, l, :, :], pv_normw(l).unsqueeze(2).broadcast_to([128, 8, 2]),
                                   ALU.mult), reads=[('modA', l), 'pvA'], writes=[('modA', l)])
            return f
        for cg in range(24):
            modq.append(step(cg))

    def pump(n=1):
        for _ in range(n):
            if modq:
                modq.pop(0)()

    def load_x():
        mk = A.mark()
        xin = [A.alloc("xin%d" % i, [1024], F32) for i in range(2)]
        for t in range(NT):
            xb = xin[t % 2]
            S.dma('sp', DMA(xb.ap, x_d[128 * t:128 * (t + 1), :]), writes=[xb.key])
            b = P.get(2)
            for c in range(8):
                S.op('pe', MM(P.ap(b, 2)[:, c * 128:(c + 1) * 128], xb.ap[:, c * 128:(c + 1) * 128], identf[:]),
                     reads=[xb.key, 'identf'], writes=PK(b, 2))
            eng = 'act' if t % 2 else 'dve'
            src = P.ap(b, 2).rearrange("p (c q) -> p c q", c=8)
            if eng == 'act':
                S.op('act', ACTF(yT[:, :, cols(t)], src, AF.Copy), reads=PK(b, 2), writes=[('yT', t)])
            else:
                S.op('dve', CP(yT[:, :, cols(t)], src), reads=PK(b, 2), writes=[('yT', t)])
            pump(2)
        A.release(mk)

    yo_bufs = []

    def store_tiles(tiles):
        if not yo_bufs:
            yo_bufs.extend([A.alloc("yo%d" % i, [1024], F32) for i in range(2)])
        for t in tiles:
            ob = yo_bufs[t % 2]
            b = P.get(2)
            for c in range(8):
                S.op('pe', MM(P.ap(b, 2)[:, c * 128:(c + 1) * 128], yT[:, c, cols(t)], identf[:]),
                     reads=[('yT', t), 'identf'], writes=PK(b, 2))
            if t % 2:
                S.op('act', ACTF(ob.ap, P.ap(b, 2), AF.Copy), reads=PK(b, 2), writes=[ob.key])
            else:
                S.op('dve', CP(ob.ap, P.ap(b, 2)), reads=PK(b, 2), writes=[ob.key])
            dst = yp_d[128 * t:128 * (t + 1), :] if t < 4 else ys_d[128 * (t - 4):128 * (t - 3), :]
            S.dma('sp', DMA(dst, ob.ap), reads=[ob.key], is_output=True)

    def store_y():
        store_tiles(range(NT))

    def phase_norm(l):
        mk = A.mark()
        sqs_ = [A.alloc("sq%d" % i, [8, 512], BF16) for i in range(3)]
        ms = A.alloc("ms", [3, 512], F32)
        rstd = A.alloc("rstd", [3, 512], F32)
        t1 = A.alloc("nt1", [3, 512], F32)
        t2 = A.alloc("nt2", [3, 512], F32)
        tmp = [A.alloc("ntmp%d" % i, [512], F32) for i in range(2)]
        for blk in range(3):
            cs = slice(blk * 512, (blk + 1) * 512)
            yk = [('yT', 4 * blk + i) for i in range(4)]
            sq = sqs_[blk]
            S.op('act', ACTF(sq.ap, yT[:, :, cs], AF.Square), reads=yk, writes=[sq.key])
            b = P.get()
            for c in range(8):
                S.op('pe', MM(P.ap(b), onesb[:], sq.ap[:, c, :], start=(c == 0), stop=(c == 7)),
                     reads=[sq.key, 'onesb'], writes=PK(b))
            S.op('dve', TS(ms.ap[:, blk, :], P.ap(b), 1.0 / D, ALU.mult, EPS, ALU.add), reads=PK(b), writes=[ms.k(blk)])
        rsqrt(rstd.ap, ms.ap, t1.ap, t2.ap, [ms.key], [rstd.key, t1.key, t2.key])
        for blk in range(3):
            g = 0 if blk == 0 else 1
            cs = slice(blk * 512, (blk + 1) * 512)
            yk = [('yT', 4 * blk + i) for i in range(4)]
            hk = [('hsT', 4 * blk + i) for i in range(4)]
            for c in range(8):
                tb = tmp[c % 2]
                S.op('dve', STT(tb.ap, yT[:, c, cs], modA[:, l, c, g:g + 1], rstd.ap[:, blk, :], ALU.mult, ALU.mult),
                     reads=yk + [('modA', l), rstd.key], writes=[tb.key])
                S.op('act', ACTF(hsT[:, c, cs], tb.ap, AF.Identity, bias=modT[:, l, c, g:g + 1], scale=1.0),
                     reads=[tb.key, ('modT', l)], writes=hk)
        A.release(mk)

    pre_out = [None]

    def preload_out(wout_d):
        pre_out[0] = load_w([wout_d[:, 0:512]])

    def phase_out(l, wout_d, store_blk=None):
        halves = [pre_out[0] if pre_out[0] is not None else load_w([wout_d[:, 0:512]]), None]
        pre_out[0] = None
        halves[1] = load_w([wout_d[:, 512:1024]])
        for blk in range(3):
            g = 0 if blk == 0 else 1
            cs = slice(blk * 512, (blk + 1) * 512)
            mk_ = [('mixT', 4 * blk + i) for i in range(4)]
            yk = [('yT', 4 * blk + i) for i in range(4)]
            for half in range(2):
                wv, wkey = halves[half]
                for dc in range(4):
                    c = half * 4 + dc
                    b = P.get()
                    for jc in range(8):
                        S.op('pe', MM(P.ap(b), wv[:, jc, dc * 128:(dc + 1) * 128], mixT[:, jc, cs],
                                      start=(jc == 0), stop=(jc == 7)),
                             reads=[wkey] + mk_, writes=PK(b))
                    S.op('dve', STT(yT[:, c, cs], P.ap(b), modT[:, l, 16 + c, g:g + 1], yT[:, c, cs], ALU.mult, ALU.add),
                         reads=PK(b) + [('modT', l)] + yk, writes=yk)
            if store_blk is not None:
                store_blk(blk)

    def attention(qT, kT, dk, vaug, att, NQ, blocks_for_q, bias_fn=None, head_hook=None):
        mk = A.mark()
        maxb = max(len(blocks_for_q(qi)) for qi in range(NQ))
        gsz = maxb if maxb <= 8 else (maxb + 1) // 2
        assert gsz <= 8
        npt = 3 if gsz <= 2 else 2
        PT = [A.alloc("PT%d" % i, [gsz * 128], BF16) for i in range(npt)]
        rd = A.alloc("rd", [4], F32)
        items = []
        for hh in range(4):
            for qi in range(NQ):
                bl = blocks_for_q(qi)
                grps = [bl[i:i + gsz] for i in range(0, len(bl), gsz)]
                for gi, g_ in enumerate(grps):
                    items.append((hh, qi, gi, len(grps), g_))
        state = {'pt': 0, 'ob': None, 'on': 0}
        P.limit = 5
        if P.next >= 5:
            P.next = 0

        def emit_S(it):
            hh, qi, gi, ng, g_ = it
            if head_hook is not None and qi == 0 and gi == 0:
                head_hook(hh)
            b = P.get(2)
            reg = P.ap(b, 2)
            for bi, kidx in enumerate(g_):
                bias = bias_fn(hh, qi, kidx) if bias_fn is not None else None
                S.op('pe', MM(reg[:, bi * 128:(bi + 1) * 128], kT.ap[0:dk, hh, kidx * 128:(kidx + 1) * 128],
                              qT.ap[0:dk, hh, qi * 128:(qi + 1) * 128], start=True, stop=(bias is None)),
                     reads=[kT.key, qT.key], writes=PK(b, 2))
                if bias is not None:
                    S.op('pe', MM(reg[:, bi * 128:(bi + 1) * 128], identb[:], bias[0], start=False, stop=True),
                         reads=['identb'] + bias[1], writes=PK(b, 2))
            pt = PT[state["pt"] % npt]
            state['pt'] += 1
            n = len(g_) * 128
            S.op('act', ACTF(pt.ap[:, 0:n], reg[:, 0:n], AF.Exp), reads=PK(b, 2), writes=[pt.key])
            return pt

        def emit_PV(it, pt):
            hh, qi, gi, ng, g_ = it
            if gi == 0 and qi % 4 == 0:
                state['ob'] = 5 + state['on'] % 2
                state['on'] += 1
            ob = state['ob']
            oq = qi % 4
            for bi, kidx in enumerate(g_):
                S.op('pe', MM(P.ap(ob)[:, oq * 66:oq * 66 + 65], pt.ap[:, bi * 128:(bi + 1) * 128],
                              vaug.ap[:, kidx, hh, 0:65], start=(gi == 0 and bi == 0),
                              stop=(gi == ng - 1 and bi == len(g_) - 1)),
                     reads=[pt.key, vaug.key], writes=PK(ob))
            if gi == ng - 1 and (oq == 3 or qi == NQ - 1):
                nq = oq + 1
                q0 = qi - oq
                ov = P.ap(ob)[:, 0:nq * 66].rearrange("p (q e) -> p q e", e=66)
                S.op('dve', RCP(rd.ap[:, 0:nq], ov[:, :, 64]), reads=PK(ob), writes=[rd.key])
                S.op('dve', TT(att.ap[:, q0:q0 + nq, hh, :], ov[:, :, 0:64],
                               rd.ap[:, 0:nq].unsqueeze(2).broadcast_to([128, nq, 64]), ALU.mult),
                     reads=PK(ob) + [rd.key], writes=[att.k(q) for q in range(q0, q0 + nq)])

        prev = None
        for it in items:
            pt = emit_S(it)
            if prev is not None:
                emit_PV(*prev)
            prev = (it, pt)
        emit_PV(*prev)
        P.limit = 7
        A.release(mk)

    def even_layer(l):
        j = l // 2
        W = evwin_d[j]
        S.dma('pool', DMA(wqb[:], evwqb_d[j].rearrange("(c p) n -> p c n", p=128)), writes=['wqb'])
        S.dma('pool', DMA(wkvb[:], evwkvb_d[j]), writes=['wkvb'])
        S.dma('pool', DMA(wgate[:], W[:, 2464:2480].rearrange("(k p) n -> p k n", p=128)), writes=['wgate'])
        S.dma('sp', DMA(gq[:, 0:96], evqn_d[j].partition_broadcast(128)), writes=['gq'])
        S.dma('sp', DMA(gk[:, 0:96], evkn_d[j].partition_broadcast(128)), writes=['gk'])
        S.dma('sp', DMA(ghn[:], evhn_d[j].partition_broadcast(128)), writes=['ghn'])
        S.op('dve', TS(gq[:, 0:96], gq[:, 0:96], 96 ** -0.5, ALU.mult), reads=['gq'], writes=['gq'])
        S.op('dve', TS(ghn[:], ghn[:], 0.25, ALU.mult), reads=['ghn'], writes=['ghn'])

        mk1 = A.mark()
        qanT = A.alloc("qanT", [2, NTOK], BF16)
        ckvT = A.alloc("ckvT", [NTOK + 256], BF16)
        kpeall = A.alloc("kpeall", [14, 32], F32)
        krg = A.alloc("krg", [14, 32], F32)
        sskpe = A.alloc("sskpe", [14], F32)
        mk2 = A.mark()
        wa, wakey = pre_w.pop('wa') if 'wa' in pre_w else load_w([W[:, 0:416]])
        wga_of = lambda hg_: load_w([W[:, 416 + hg_ * 256:416 + (hg_ + 1) * 256]])
        wga_next = wga_of(0)
        sq = A.alloc("sq1", [3, 512], BF16)
        ms = A.alloc("ms1", [2, 512], F32)
        rstd = A.alloc("rstd1", [2, 512], F32)
        n1 = A.alloc("n1", [2, 512], F32)
        n2 = A.alloc("n2", [2, 512], F32)
        ckvTf = A.alloc("ckvTf", [512], F32)
        for blk in range(3):
            cs = slice(blk * 512, (blk + 1) * 512)
            hk = [('hsT', 4 * blk + i) for i in range(4)]
            bs = []
            for ci in range(3):
                b = P.get()
                bs.append(b)
                for kc in range(8):
                    S.op('pe', MM(P.ap(b), wa[:, kc, ci * 128:(ci + 1) * 128], hsT[:, kc, cs],
                                  start=(kc == 0), stop=(kc == 7)), reads=[wakey] + hk, writes=PK(b))
                S.op('act', ACTF(sq.ap[:, ci, :], P.ap(b), AF.Square), reads=PK(b), writes=[sq.k(ci)])
            bq_ = P.get()
            for ci in range(2):
                S.op('pe', MM(P.ap(bq_), onesb[:], sq.ap[:, ci, :], start=(ci == 0), stop=(ci == 1)),
                     reads=[sq.k(ci), 'onesb'], writes=PK(bq_))
            bk_ = P.get()
            S.op('pe', MM(P.ap(bk_), onesb[:], sq.ap[:, 2, :]), reads=[sq.k(2), 'onesb'], writes=PK(bk_))
            S.op('dve', TS(ms.ap[:, 0, :], P.ap(bq_), 1.0 / 256, ALU.mult, EPS, ALU.add), reads=PK(bq_), writes=[ms.key])
            S.op('dve', TS(ms.ap[:, 1, :], P.ap(bk_), 1.0 / 128, ALU.mult, EPS, ALU.add), reads=PK(bk_), writes=[ms.key])
            rsqrt(rstd.ap, ms.ap, n1.ap, n2.ap, [ms.key], [rstd.key, n1.key, n2.key])
            for c2 in range(2):
                S.op('dve', STT(qanT.ap[:, c2, cs], P.ap(bs[c2]), pv_qan(j, c2), rstd.ap[:, 0, :], ALU.mult, ALU.mult),
                     reads=PK(bs[c2]) + ['pvB', rstd.key], writes=[qanT.k(4 * blk + i) for i in range(4)])
            S.op('dve', STT(ckvT.ap[:, cs], P.ap(bs[2]), pv_kvan(j), rstd.ap[:, 1, :], ALU.mult, ALU.mult),
                 reads=PK(bs[2]) + ['pvB', rstd.key], writes=[ckvT.k(4 * blk + i) for i in range(4)])
            if blk == 0:
                S.op('dve', STT(ckvTf.ap, P.ap(bs[2]), pv_kvan(j), rstd.ap[:, 1, :], ALU.mult, ALU.mult),
                     reads=PK(bs[2]) + ['pvB', rstd.key], writes=[ckvTf.key])
        b = P.get()
        for t in range(NT):
            for kc in range(8):
                S.op('pe', MM(P.ap(b)[:, t * 32:(t + 1) * 32], hsT[:, kc, cols(t)], wa[:, kc, 384:416],
                              start=(kc == 0), stop=(kc == 7)), reads=[wakey, ('hsT', t)], writes=PK(b))
        S.op('dve', CP(kpeall.ap[:, 0:12, :], P.ap(b)[:, 0:384].rearrange("p (t e) -> p t e", e=32)),
             reads=PK(b), writes=[kpeall.key])
        S.dma('sp', DMA(kpeall.ap[:, 12:14, :], kpec_d[j].rearrange("(t p) e -> p t e", p=128)), writes=[kpeall.key])
        for s_ in range(2):
            S.dma('sp', DMA(okpe_d[s_, j].rearrange("(t p) e -> p t e", p=128), kpeall.ap[:, 2 * s_:2 * s_ + 2, :]),
                  reads=[kpeall.key], is_output=True)
        ktmp = A.alloc("ktmp", [14, 32], F32)
        ktmp2 = A.alloc("ktmp2", [8, 32], F32)
        S.op('act', ACTF(ktmp.ap, kpeall.ap, AF.Square), reads=[kpeall.key], writes=[ktmp.key])
        S.op('dve', RED(sskpe.ap, ktmp.ap), reads=[ktmp.key], writes=[sskpe.key])
        S.op('dve', TT(krg.ap, kpeall.ap, gk[:, 64:96].unsqueeze(1).broadcast_to([128, 14, 32]), ALU.mult),
             reads=[kpeall.key, 'gk'], writes=[krg.key])
        S.op('dve', TT(ktmp.ap[:, 0:8, :], krg.ap[:, 4:12, :], ropeC[:], ALU.mult),
             reads=[krg.key, 'ropeC', ktmp.key], writes=[ktmp.key])
        for g in range(2):
            gs = slice(g * 16, (g + 1) * 16)
            S.op('dve', TT(ktmp2.ap[:, :, gs].rearrange("p t (a e) -> p t a e", a=2),
                           krg.ap[:, 4:12, gs].rearrange("p t (a e) -> p t a e", a=2)[:, :, ::-1, :],
                           ropeS[:, :, gs].rearrange("p t (a e) -> p t a e", a=2), ALU.mult),
                 reads=[krg.key, 'ropeS'], writes=[ktmp2.key])
        S.op('dve', TT(krg.ap[:, 4:12, :], ktmp.ap[:, 0:8, :], ktmp2.ap, ALU.add),
             reads=[ktmp.key, ktmp2.key], writes=[krg.key])
        ckvo = A.alloc("ckvo", [4, 128], F32)
        b = P.get()
        for t in range(4):
            S.op('pe', MM(P.ap(b)[:, t * 128:(t + 1) * 128], ckvTf.ap[:, cols(t)], identf[:]),
                 reads=[ckvTf.key, 'identf'], writes=PK(b))
        S.op('act', ACTF(ckvo.ap, P.ap(b).rearrange("p (t e) -> p t e", e=128), AF.Copy), reads=PK(b), writes=[ckvo.key])
        for s_ in range(2):
            S.dma('sp', DMA(ockv_d[s_, j].rearrange("(t p) e -> p t e", p=128), ckvo.ap[:, 2 * s_:2 * s_ + 2, :]),
                  reads=[ckvo.key], is_output=True)
        cc = A.alloc("cc", [2, 128], F32)
        S.dma('sp', DMA(cc.ap, ckvc_d[j].rearrange("(t p) e -> p t e", p=128)), writes=[cc.key])
        b = P.get()
        for t in range(2):
            S.op('pe', MM(P.ap(b)[:, t * 128:(t + 1) * 128], cc.ap[:, t, :], identf[:]),
                 reads=[cc.key, 'identf'], writes=PK(b))
        S.op('act', ACTF(ckvT.ap[:, NTOK:NTOK + 256], P.ap(b)[:, 0:256], AF.Copy), reads=PK(b),
             writes=[ckvT.k(12), ckvT.k(13)])
        A.release(mk2)

        if stop == 'mla_pre':
            A.release(mk1)
            return
        for hg in range(2):
            wga, wgakey = wga_next
            if hg == 0:
                wga_next = wga_of(1)
            for (t0, ntl, ctx) in SEQS:
                mk3 = A.mark()
                NQ = ntl
                ktiles = ([12, 13] if ctx else []) + list(range(t0, t0 + ntl))
                NK = len(ktiles)
                qT = A.alloc("qT", [4, NQ * 128], BF16)
                kT = A.alloc("kT", [4, NK * 128], BF16)
                vaug = A.alloc("vaug", [NK, 4, 66], BF16)
                att = A.alloc("att", [NQ, 4, 64], BF16)
                abuf = A.alloc("abuf", [NQ, 256], BF16)
                stage = [A.alloc("stage0", [8, 96], F32)] * 2
                qkn = [A.alloc("qkn%d" % i, [8, 96], BF16) for i in range(2)]
                sqs = A.alloc("sqs", [8, 96], BF16)
                ss = A.alloc("ss", [8], F32)
                ms8 = A.alloc("ms8", [8], F32)
                rs8 = A.alloc("rs8", [8], F32)
                na = A.alloc("na", [8], F32)
                nb_ = A.alloc("nb", [8], F32)
                r1 = A.alloc("r1", [4, 32], F32)
                r2 = A.alloc("r2", [4, 32], F32)
                tg = A.alloc("tg", [256], F32)
                mt = [A.alloc("mt%d" % i, [256], BF16) for i in range(2)]
                S.op('pool', MS(vaug.ap, 1.0), writes=[vaug.key])
                pendB = [None]
                for ki, t in enumerate(ktiles):
                    is_ctx = t >= 12
                    lo = 4 if is_ctx else 0
                    qi = ki - (2 if ctx else 0)
                    ccols = slice(NTOK + 128 * (t - 12), NTOK + 128 * (t - 11)) if is_ctx else cols(t)
                    st = stage[ki % 2]
                    qk = qkn[ki % 2]
                    bkv = P.get()
                    S.op('pe', MM(P.ap(bkv), ckvT.ap[:, ccols], wkvb[:, hg * 512:(hg + 1) * 512]),
                         reads=[ckvT.k(t), 'wkvb'], writes=PK(bkv))
                    kvv = P.ap(bkv).rearrange("p (h e) -> p h e", h=4)
                    if not is_ctx:
                        bq = P.get()
                        for c2 in range(2):
                            S.op('pe', MM(P.ap(bq)[:, 0:384], qanT.ap[:, c2, cols(t)], wqb[:, c2, hg * 384:(hg + 1) * 384],
                                          start=(c2 == 0), stop=(c2 == 1)), reads=[qanT.k(t), 'wqb'], writes=PK(bq))
                        qv = P.ap(bq)[:, 0:384].rearrange("p (h e) -> p h e", h=4)
                        bg = P.get()
                        for kc in range(8):
                            S.op('pe', MM(P.ap(bg)[:, 0:256], hsT[:, kc, cols(t)], wga[:, kc, :],
                                          start=(kc == 0), stop=(kc == 7)), reads=[('hsT', t), wgakey], writes=PK(bg))
                        S.op('act', ACTF(sqs.ap[:, 0:4, :], qv, AF.Square), reads=PK(bq), writes=[sqs.k(0)])
                    S.op('act', ACTF(sqs.ap[:, 4:8, 0:64], kvv[:, :, 0:64], AF.Square), reads=PK(bkv), writes=[sqs.k(1)])
                    if not is_ctx:
                        S.op('dve', RED(ss.ap[:, 0:4], sqs.ap[:, 0:4, :]), reads=[sqs.k(0)], writes=[ss.key])
                    S.op('dve', RED(ss.ap[:, 4:8], sqs.ap[:, 4:8, 0:64]), reads=[sqs.k(1)], writes=[ss.key])
                    S.op('dve', TS(ss.ap[:, 4:8], ss.ap[:, 4:8], sskpe.ap[:, t:t + 1], ALU.add),
                         reads=[ss.key, sskpe.key], writes=[ss.key])
                    S.op('dve', TS(ms8.ap[:, lo:8], ss.ap[:, lo:8], 1.0 / 96, ALU.mult, EPS, ALU.add),
                         reads=[ss.key], writes=[ms8.key])
                    if not is_ctx:
                        S.op('dve', TT(st.ap[:, 0:4, :], qv, gq[:, 0:96].unsqueeze(1).broadcast_to([128, 4, 96]), ALU.mult),
                             reads=PK(bq) + ['gq'], writes=[st.key])
                    S.op('dve', TT(st.ap[:, 4:8, 0:64], kvv[:, :, 0:64],
                                   gk[:, 0:64].unsqueeze(1).broadcast_to([128, 4, 64]), ALU.mult),
                         reads=PK(bkv) + ['gk'], writes=[st.key])
                    S.op('pool', CP(st.ap[:, 4:8, 64:96], krg.ap[:, t, :].unsqueeze(1).broadcast_to([128, 4, 32])),
                         reads=[krg.key], writes=[st.key])
                    S.op('act', ACTF(vaug.ap[:, ki, :, 0:64], kvv[:, :, 64:128], AF.Copy), reads=PK(bkv), writes=[vaug.key])
                    if ctx and not is_ctx:
                        tl = t - 4
                        S.op('dve', TT(r1.ap, st.ap[:, 0:4, 64:96], ropeC[:, tl, :].unsqueeze(1).broadcast_to([128, 4, 32]),
                                       ALU.mult), reads=[st.key, 'ropeC'], writes=[r1.key])
                        for g in range(2):
                            gs = slice(g * 16, (g + 1) * 16)
                            gs2 = slice(64 + g * 16, 64 + (g + 1) * 16)
                            S.op('dve', TT(r2.ap[:, :, gs].rearrange("p h (a e) -> p h a e", a=2),
                                           st.ap[:, 0:4, gs2].rearrange("p h (a e) -> p h a e", a=2)[:, :, ::-1, :],
                                           ropeS[:, tl, gs].rearrange("p (a e) -> p a e", a=2).unsqueeze(1)
                                           .broadcast_to([128, 4, 2, 8]), ALU.mult),
                                 reads=[st.key, 'ropeS'], writes=[r2.key])
                        S.op('dve', TT(st.ap[:, 0:4, 64:96], r1.ap, r2.ap, ALU.add), reads=[r1.key, r2.key], writes=[st.key])
                    rsqrt(rs8.ap[:, lo:8], ms8.ap[:, lo:8], na.ap[:, lo:8], nb_.ap[:, lo:8], [ms8.key],
                          [rs8.key, na.key, nb_.key])
                    S.op('dve', TT(qk.ap[:, lo:8, :], st.ap[:, lo:8, :],
                                   rs8.ap[:, lo:8].unsqueeze(2).broadcast_to([128, 8 - lo, 96]), ALU.mult),
                         reads=[st.key, rs8.key], writes=[qk.key])
                    if not is_ctx:
                        S.op('act', ACTF(tg.ap, P.ap(bg)[:, 0:256], AF.Tanh, scale=0.5), reads=PK(bg), writes=[tg.key])
                        S.op('dve', STT(abuf.ap[:, qi, :], tg.ap, 1.0, P.ap(bg)[:, 0:256], ALU.add, ALU.mult),
                             reads=[tg.key] + PK(bg), writes=[abuf.k(qi)])
                    def stageB(qk=qk, lo=lo, is_ctx=is_ctx, qi=qi, ki=ki):
                        bt = P.get(2)
                        for i in range(lo, 8):
                            S.op('pe', MM(P.ap(bt, 2)[0:96, i * 128:(i + 1) * 128], qk.ap[:, i, :], identb[:]),
                                 reads=[qk.key, 'identb'], writes=PK(bt, 2))
                        if not is_ctx:
                            S.op('act', ACTF(qT.ap[0:96, :, qi * 128:(qi + 1) * 128],
                                             P.ap(bt, 2)[0:96, 0:512].rearrange("p (h q) -> p h q", h=4), AF.Copy),
                                 reads=PK(bt, 2), writes=[qT.k(qi)])
                        S.op('act', ACTF(kT.ap[0:96, :, ki * 128:(ki + 1) * 128],
                                         P.ap(bt, 2)[0:96, 512:1024].rearrange("p (h q) -> p h q", h=4), AF.Copy),
                             reads=PK(bt, 2), writes=[kT.k(ki)])
                    if pendB[0] is not None:
                        pendB[0]()
                    pendB[0] = stageB
                    pump(1)
                pendB[0]()
                pendB[0] = None
                attention(qT, kT, 96, vaug, att, NQ, lambda qi_, NK=NK: list(range(NK)))
                for qi in range(NQ):
                    t = t0 + qi
                    m_ = mt[qi % 2]
                    S.op('dve', TT(m_.ap, abuf.ap[:, qi, :], att.ap[:, qi, :, :].rearrange("p h e -> p (h e)"), ALU.mult),
                         reads=[abuf.k(qi), att.k(qi)], writes=[m_.key])
                    bT = P.get()
                    for i in range(2):
                        S.op('pe', MM(P.ap(bT)[:, i * 128:(i + 1) * 128], m_.ap[:, i * 128:(i + 1) * 128], identb[:]),
                             reads=[m_.key, 'identb'], writes=PK(bT))
                    S.op('act', ACTF(mixT[:, hg * 2:hg * 2 + 2, cols(t)],
                                     P.ap(bT)[:, 0:256].rearrange("p (c q) -> p c q", c=2), AF.Copy, scale=0.5),
                         reads=PK(bT), writes=[('mixT', t)])
                A.release(mk3)
        A.release(mk1)
        if stop == 'mla':
            return
        mlstm_phase(l)

    def mlstm_phase(l):
        j = l // 2
        W = evwin_d[j]
        mkA = A.mark()
        negR = A.alloc("negR", [NTOK], F32, parts=36)
        wint = A.alloc("wint", [NTOK], F32, parts=36)
        tmq = A.alloc("tmq", [12, 2, 3, 4], F32)
        cst = A.alloc("cst01", [2], F32, parts=36)
        w1_of = lambda h_: load_w([W[:, 928 + h_ * 128:928 + (h_ + 1) * 128], W[:, 1440 + h_ * 128:1440 + (h_ + 1) * 128]])
        w1_next = w1_of(0)
        qk_pair = [(A.alloc("qTbP%d" % i, [1024], BF16), A.alloc("kTbP%d" % i, [1024], BF16)) for i in range(2)]

        fm_done = set()

        def proj_fm(it_, w1v, w1key):
            if it_ in fm_done:
                return
            fm_done.add(it_)
            t0_, ntl_, _c = SEQS[it_ % 3]
            L_ = ntl_ * 128
            a_ = t0_ * 128
            qb_, kb_ = qk_pair[it_ % 2]
            for bi in range((L_ + 511) // 512):
                n = min(512, L_ - bi * 512)
                cs = slice(a_ + bi * 512, a_ + bi * 512 + n)
                lc = slice(bi * 512, bi * 512 + n)
                hk = [('hsT', (a_ + bi * 512) // 128 + i) for i in range(n // 128)]
                bq = P.get()
                for kc in range(8):
                    S.op('pe', MM(P.ap(bq)[:, 0:n], w1v[:, kc, 0:128], hsT[:, kc, cs], start=(kc == 0), stop=(kc == 7)),
                         reads=[w1key] + hk, writes=PK(bq))
                bk = P.get()
                for kc in range(8):
                    S.op('pe', MM(P.ap(bk)[:, 0:n], w1v[:, kc, 128:256], hsT[:, kc, cs], start=(kc == 0), stop=(kc == 7)),
                         reads=[w1key] + hk, writes=PK(bk))
                S.op('act', ACTF(qb_.ap[:, lc], P.ap(bq)[:, 0:n], AF.Copy), reads=PK(bq), writes=[qb_.k(bi)])
                S.op('dve', TS(kb_.ap[:, lc], P.ap(bk)[:, 0:n], 128 ** -0.5, ALU.mult), reads=PK(bk), writes=[kb_.k(bi)])

        mkB = A.mark()
        gtm = A.alloc("gtm", [12, 16], F32)
        T1 = A.alloc("T1", [NTOK], F32, parts=36)
        T2 = A.alloc("T2", [NTOK], F32, parts=36)
        T3 = A.alloc("T3", [NTOK], F32, parts=36)
        T4 = A.alloc("T4", [NTOK], F32, parts=36)
        T5 = A.alloc("T5", [NTOK], F32, parts=36)
        S.op('pool', MS(cst.ap[:, 0:1], 1.0), writes=[cst.key])
        S.op('pool', MS(cst.ap[:, 1:2], 0.0), writes=[cst.key])
        for Tb in (T1, T3):
            S.op('pool', MS(Tb.ap, 0.0), writes=[Tb.key])
        b = P.get()
        for t in range(NT):
            for kc in range(8):
                S.op('pe', MM(P.ap(b)[:, t * 16:(t + 1) * 16], hsT[:, kc, cols(t)], wgate[:, kc, :],
                              start=(kc == 0), stop=(kc == 7)), reads=[('hsT', t), 'wgate'], writes=PK(b))
        S.op('dve', CP(gtm.ap, P.ap(b)[:, 0:192].rearrange("p (t e) -> p t e", e=16)), reads=PK(b), writes=[gtm.key])
        for which, Tdst in ((0, T3), (1, T1)):
            b3 = P.get(3)
            for d_ in range(2):
                g0 = 32 * d_
                for t in range(NT):
                    S.op('pe', MM(P.ap(b3, 3)[g0:g0 + 4, t * 128:(t + 1) * 128],
                                  gtm.ap[:, t, which * 8 + d_ * 4:which * 8 + d_ * 4 + 4], identf[:]),
                         reads=[gtm.key, 'identf'], writes=PK(b3, 3))
            for d_ in range(2):
                g0 = 32 * d_
                S.op('act', ACTF(Tdst.ap[g0:g0 + 4, :], P.ap(b3, 3)[g0:g0 + 4, :], AF.Identity,
                                 bias=bif[g0:g0 + 4, which, j:j + 1], scale=1.0),
                     reads=PK(b3, 3) + ['bif'], writes=[Tdst.key])
        allk = [T1.key, T2.key, T3.key, T4.key, T5.key]
        S.op('dve', STT(T2.ap, T1.ap, -1.0, T1.ap, ALU.mult, ALU.max), reads=[T1.key], writes=[T2.key])
        S.op('act', ACTF(T2.ap, T2.ap, AF.Exp, scale=-1.0), reads=[T2.key], writes=[T2.key])
        S.op('act', ACTF(T2.ap, T2.ap, AF.Ln, bias=1.0, scale=1.0), reads=[T2.key], writes=[T2.key])
        S.op('dve', TS(T1.ap, T1.ap, 0.0, ALU.min), reads=[T1.key], writes=[T1.key])
        S.op('dve', TT(T1.ap, T1.ap, T2.ap, ALU.subtract), reads=[T1.key, T2.key], writes=[T1.key])
        segs = [(t0 * 128, ntl * 128, ctx) for (t0, ntl, ctx) in SEQS]

        def dirview(ap, d_, a, L):
            v = ap[32 * d_:32 * d_ + 4, a:a + L]
            return v[:, ::-1] if d_ else v

        for d_ in range(2):
            g0 = 32 * d_
            for (a, L, ctx) in segs:
                S.op('dve', SCAN(dirview(T2.ap, d_, a, L), cst.ap[g0:g0 + 4, 0:1].broadcast_to([4, L]),
                                 dirview(T1.ap, d_, a, L), 0.0, ALU.mult, ALU.add),
                     reads=[T1.key, cst.key], writes=[T2.key])
        S.op('dve', TT(T3.ap, T3.ap, T2.ap, ALU.subtract), reads=[T3.key, T2.key], writes=[T3.key])
        for d_ in range(2):
            g0 = 32 * d_
            for (a, L, ctx) in segs:
                init = m0t[g0:g0 + 4, j:j + 1] if ctx else 0.0
                S.op('dve', SCAN(dirview(T1.ap, d_, a, L), cst.ap[g0:g0 + 4, 1:2].broadcast_to([4, L]),
                                 dirview(T3.ap, d_, a, L), init, ALU.add, ALU.max),
                     reads=[T3.key, cst.key, 'm0t', T1.key], writes=[T1.key])
        S.op('dve', TT(T2.ap, T2.ap, T1.ap, ALU.add), reads=[T2.key, T1.key], writes=[T2.key])
        for s_ in range(2):
            a, L, _ = segs[s_]
            for d_ in range(2):
                g0 = 32 * d_
                idx = a + L - 1 if d_ == 0 else a
                S.dma('sp', DMA(om_d[s_, j, d_, :].rearrange("(h o) -> h o", o=1), T2.ap[g0:g0 + 4, idx:idx + 1]),
                      reads=[T2.key], is_output=True)
        for d_ in range(2):
            g0 = 32 * d_
            for (a, L, ctx) in segs:
                nck = L // 64
                first = slice(a, a + 64) if d_ == 0 else slice(a + L - 64, a + L)
                if ctx:
                    S.op('dve', CP(T4.ap[g0:g0 + 4, first], m0t[g0:g0 + 4, j:j + 1].broadcast_to([4, 64])),
                         reads=['m0t', T4.key], writes=[T4.key])
                else:
                    S.op('dve', MS(T4.ap[g0:g0 + 4, first], 0.0), reads=[T4.key], writes=[T4.key])
                if d_ == 0:
                    dst = T4.ap[g0:g0 + 4, a + 64:a + L].rearrange("p (k e) -> p k e", e=64)
                    src = T1.ap[g0:g0 + 4, a + 63:a + L - 1:64]
                    src2 = T1.ap[g0:g0 + 4, a + 63:a + L:64]
                else:
                    dst = T4.ap[g0:g0 + 4, a:a + L - 64].rearrange("p (k e) -> p k e", e=64)
                    src = T1.ap[g0:g0 + 4, a + 64:a + L:64]
                    src2 = T1.ap[g0:g0 + 4, a:a + L:64]
                S.op('dve', CP(dst, src.unsqueeze(2).broadcast_to([4, nck - 1, 64])), reads=[T1.key, T4.key], writes=[T4.key])
                S.op('dve', CP(T5.ap[g0:g0 + 4, a:a + L].rearrange("p (k e) -> p k e", e=64),
                               src2.unsqueeze(2).broadcast_to([4, nck, 64])), reads=[T1.key, T5.key], writes=[T5.key])
        for d_ in range(2):
            g0 = 32 * d_
            r = slice(g0, g0 + 4)
            S.op('dve', TT(T4.ap[r, :], T4.ap[r, :], T1.ap[r, :], ALU.subtract), reads=[T4.key, T1.key], writes=[T4.key])
            S.op('act', ACTF(wint.ap[r, :], T4.ap[r, :], AF.Exp), reads=[T4.key], writes=[wint.key])
            S.op('dve', TT(T5.ap[r, :], T3.ap[r, :], T5.ap[r, :], ALU.subtract), reads=[T3.key, T5.key], writes=[T5.key])
            S.op('act', ACTF(T5.ap[r, :], T5.ap[r, :], AF.Exp), reads=[T5.key], writes=[T5.key])
            S.op('act', ACTF(T2.ap[r, :], T2.ap[r, :], AF.Exp, scale=-1.0), reads=[T2.key], writes=[T2.key])
            S.op('dve', TS(negR.ap[r, :], T1.ap[r, :], -1.0, ALU.mult), reads=[T1.key], writes=[negR.key])
        if stop is None:
            proj_fm(0, w1_next[0], w1_next[1])
            proj_fm(1, w1_next[0], w1_next[1])
        b = P.get()
        for t in range(NT):
            for d_ in range(2):
                g0 = 32 * d_
                for q_, Tq in enumerate((T3, T5, T2)):
                    c0 = ((t * 2 + d_) * 3 + q_) * 4
                    S.op('pe', MM(P.ap(b)[:, c0:c0 + 4], Tq.ap[g0:g0 + 4, cols(t)], identf[g0:g0 + 4, g0:g0 + 4]),
                         reads=[Tq.key, 'identf'], writes=PK(b))
        S.op('dve', CP(tmq.ap, P.ap(b)[:, 0:288].rearrange("p (t d q h) -> p t d q h", t=12, d=2, q=3)),
             reads=PK(b), writes=[tmq.key])
        TAP('negR', negR.ap, [36, NTOK], [negR.key])
        TAP('wint', wint.ap, [36, NTOK], [wint.key])
        TAP('tmq', tmq.ap, [128, 12, 2, 3, 4], [tmq.key])
        A.release(mkB)
        if stop == 'rows':
            A.release(mkA)
            return

        ksc = 128 ** -0.5
        for h in range(4):
            w1, w1k = w1_next
            if h == 0:
                proj_fm(0, w1, w1k)
            w2, w2k = load_w([W[:, 1952 + h * 128:1952 + (h + 1) * 128], W[:, 2480 + h * 128:2480 + (h + 1) * 128],
                              W[:, 2992 + h * 128:2992 + (h + 1) * 128]])
            if h < 3:
                w1_next = w1_of(h + 1)
            for si, (t0, ntl, ctx) in enumerate(SEQS):
                if h == 3 and si == 2 and stop is None:
                    preload_out(evwout_d[j])
                mkC = A.mark()
                L = ntl * 128
                a = t0 * 128
                nck = L // 64
                it = h * 3 + si
                qTb, kTb = qk_pair[it % 2]
                ktm = A.alloc("ktm", [ntl, 128], BF16)
                va2 = A.alloc("va2", [ntl, 130], BF16)
                og = A.alloc("og", [ntl, 128], BF16)
                hacc = A.alloc("hacc", [ntl, 128], F32)
                qTw = A.alloc("qTw", [2, L], BF16)
                Cb = A.alloc("Cb", [2, 130], BF16)
                kw = A.alloc("kw", [2, ntl, 128], BF16)
                Cst = A.alloc("Cst", [2, 130], F32)
                wsb = A.alloc("wsb", [2, nck], F32)
                Dt = [A.alloc("Dt%d" % i, [128], BF16) for i in range(2)]
                scT = [A.alloc("scT%d" % i, [128], BF16) for i in range(4)]
                tog = A.alloc("tog", [256], F32)
                ag = A.alloc("ag", [128], F32)
                den = [A.alloc("den%d" % i, [4], F32) for i in range(2)]
                junk = A.alloc("junk", [128], F32)
                ssh = A.alloc("ssh", [ntl], F32)
                msh = A.alloc("msh", [ntl], F32)
                rsh = A.alloc("rsh", [ntl], F32)
                nh1 = A.alloc("nh1", [ntl], F32)
                nh2 = A.alloc("nh2", [ntl], F32)
                htmp = [A.alloc("htmp%d" % i, [128], F32) for i in range(2)]
                hmt = [A.alloc("hmt%d" % i, [128], BF16) for i in range(2)]
                S.op('pool', MS(va2.ap, 1.0), writes=[va2.key])
                for tt in range(ntl):
                    t = t0 + tt
                    bv = P.get()
                    for kc in range(8):
                        S.op('pe', MM(P.ap(bv)[:, 0:384], hsT[:, kc, cols(t)], w2[:, kc, :], start=(kc == 0), stop=(kc == 7)),
                             reads=[w2k, ('hsT', t)], writes=PK(bv))
                    bk2 = P.get()
                    for kc in range(8):
                        S.op('pe', MM(P.ap(bk2)[:, 0:128], hsT[:, kc, cols(t)], w1[:, kc, 128:256],
                                      start=(kc == 0), stop=(kc == 7)), reads=[w1k, ('hsT', t)], writes=PK(bk2))
                    S.op('act', ACTF(ktm.ap[:, tt, :], P.ap(bk2)[:, 0:128], AF.Copy, scale=ksc), reads=PK(bk2), writes=[ktm.k(tt)])
                    S.op('dve', CP(va2.ap[:, tt, 0:128], P.ap(bv)[:, 0:128]), reads=PK(bv), writes=[va2.k(tt)])
                    S.op('act', ACTF(tog.ap, P.ap(bv)[:, 128:384], AF.Tanh, scale=0.5), reads=PK(bv), writes=[tog.key])
                    S.op('dve', STT(ag.ap, tog.ap[:, 128:256], 1.0, P.ap(bv)[:, 256:384], ALU.add, ALU.mult),
                         reads=[tog.key] + PK(bv), writes=[ag.key])
                    S.op('dve', STT(og.ap[:, tt, :], tog.ap[:, 0:128], 1.0, ag.ap, ALU.add, ALU.mult),
                         reads=[tog.key, ag.key], writes=[og.k(tt)])
                if stop == 'mproj':
                    return
                if ctx:
                    for d_ in range(2):
                        S.dma('sp', DMA(Cst.ap[:, d_, 0:128], stC_d[j, d_, h]), writes=[Cst.k(d_)])
                        S.dma('sp', DMA(Cst.ap[:, d_, 128:129], stn_d[j, d_, h].rearrange("(p o) -> p o", o=1)),
                              writes=[Cst.k(d_)])
                else:
                    S.op('pool', MS(Cst.ap, 0.0), writes=[Cst.key])
                for d_ in range(2):
                    S.op('act', ACTF(Cb.ap[:, d_, 0:129], Cst.ap[:, d_, 0:129], AF.Copy), reads=[Cst.k(d_), Cst.key],
                         writes=[Cb.k(d_)])
                if stop == 'mSdma' and ctx:
                    return
                bw = P.get()
                wsr = A.alloc("wsr", [nck], F32, parts=36)
                for d_ in range(2):
                    g0 = 32 * d_
                    samp = wint.ap[g0:g0 + 4, a + 63:a + L:64] if d_ == 0 else wint.ap[g0:g0 + 4, a:a + L:64]
                    S.op('dve', CP(wsr.ap[g0:g0 + 4, :], samp), reads=[wint.key, wsr.key], writes=[wsr.key])
                    S.op('pe', MM(P.ap(bw)[:, d_ * nck:(d_ + 1) * nck], sel[g0:g0 + 4, h, :], wsr.ap[g0:g0 + 4, :]),
                         reads=[wsr.key, 'sel'], writes=PK(bw))
                S.op('dve', CP(wsb.ap, P.ap(bw)[:, 0:2 * nck].rearrange("p (d k) -> p d k", d=2)), reads=PK(bw), writes=[wsb.key])
                for d_ in range(2):
                    g0 = 32 * d_
                    for bi in range(L // 256):
                        n = 256
                        cs = slice(a + bi * 256, a + bi * 256 + n)
                        lc = slice(bi * 256, bi * 256 + n)
                        bb = P.get()
                        S.op('pe', MM(P.ap(bb)[:, 0:n], sel[g0:g0 + 4, h, :], wint.ap[g0:g0 + 4, cs]),
                             reads=[wint.key, 'sel'], writes=PK(bb))
                        S.op('dve', TT(qTw.ap[:, d_, lc], P.ap(bb)[:, 0:n], qTb.ap[:, lc], ALU.mult),
                             reads=PK(bb) + [qTb.key], writes=[qTw.k(d_)])
                    for tt in range(ntl):
                        S.op('act', ACTF(kw.ap[:, d_, tt, :], ktm.ap[:, tt, :], AF.Copy,
                                         scale=tmq.ap[:, t0 + tt, d_, 1, h:h + 1]),
                             reads=[ktm.k(tt), tmq.key], writes=[kw.k((d_, tt))])
                if stop == 'mprep':
                    return
                cnt = {'dt': 0, 'sc': 0, 'den': 0}
                done = set()

                def intra(tt, d_):
                    t = t0 + tt
                    g0 = 32 * d_
                    lc = slice(tt * 128, (tt + 1) * 128)
                    bS = P.get()
                    S.op('pe', MM(P.ap(bS)[:, 0:128], kTb.ap[:, lc], qTb.ap[:, lc]), reads=[kTb.key, qTb.key], writes=PK(bS))
                    S.op('pe', MM(P.ap(bS)[:, 128:256], sel[g0:g0 + 4, h, :], negR.ap[g0:g0 + 4, cols(t)], start=True, stop=False),
                         reads=[negR.key, 'sel'], writes=PK(bS))
                    S.op('pe', MM(P.ap(bS)[:, 128:256], identb[:], lmaskb[:, d_, :], start=False, stop=True),
                         reads=['identb', 'lmaskb'], writes=PK(bS))
                    dt_ = Dt[cnt['dt'] % 2]
                    cnt['dt'] += 1
                    sc_ = scT[cnt['sc'] % 4]
                    cnt['sc'] += 1
                    S.op('act', ACTF(dt_.ap, P.ap(bS)[:, 128:256], AF.Exp, bias=tmq.ap[:, t, d_, 0, h:h + 1], scale=1.0),
                         reads=PK(bS) + [tmq.key], writes=[dt_.key])
                    S.op('dve', TT(sc_.ap, P.ap(bS)[:, 0:128], dt_.ap, ALU.mult), reads=PK(bS) + [dt_.key], writes=[sc_.key])
                    return sc_

                def num_open(tt, d_, sc_):
                    bN = P.get()
                    S.op('pe', MM(P.ap(bN)[:, 0:129], sc_.ap, va2.ap[:, tt, 0:129], start=True, stop=False),
                         reads=[sc_.key, va2.k(tt)], writes=PK(bN))
                    return bN

                def inter(tt, d_, hf, bN, last):
                    S.op('pe', MM(P.ap(bN)[hf * 64:(hf + 1) * 64, 0:129], qTw.ap[:, d_, tt * 128 + hf * 64:tt * 128 + (hf + 1) * 64],
                                  Cb.ap[:, d_, 0:129], start=False, stop=last),
                         reads=[qTw.k(d_), Cb.k(d_)], writes=PK(bN))

                def update(tt, d_, hf):
                    cidx = tt * 2 + hf
                    bU = P.get()
                    S.op('pe', MM(P.ap(bU)[:, 0:129], kw.ap[hf * 64:(hf + 1) * 64, d_, tt, :],
                                  va2.ap[hf * 64:(hf + 1) * 64, tt, 0:129]),
                         reads=[kw.k((d_, tt)), va2.k(tt)], writes=PK(bU))
                    S.op('dve', STT(Cb.ap[:, d_, 0:129], Cst.ap[:, d_, 0:129], wsb.ap[:, d_, cidx:cidx + 1],
                                    P.ap(bU)[:, 0:129], ALU.mult, ALU.add),
                         reads=[Cst.k(d_), wsb.key] + PK(bU), writes=[Cb.k(d_)])
                    S.op('dve', STT(Cst.ap[:, d_, 0:129], Cst.ap[:, d_, 0:129], wsb.ap[:, d_, cidx:cidx + 1],
                                    P.ap(bU)[:, 0:129], ALU.mult, ALU.add),
                         reads=[Cst.k(d_), wsb.key] + PK(bU), writes=[Cst.k(d_)])

                def epilogue(tt, d_, bN):
                    t = t0 + tt
                    dn = den[cnt['den'] % 2]
                    cnt['den'] += 1
                    qn = P.ap(bN)[:, 128:129]
                    S.op('dve', TS(dn.ap[:, 0:1], qn, tmq.ap[:, t, d_, 2, h:h + 1], ALU.max), reads=PK(bN) + [tmq.key], writes=[dn.key])
                    S.op('dve', STT(dn.ap[:, 1:2], qn, -1.0, dn.ap[:, 0:1], ALU.mult, ALU.max), reads=PK(bN) + [dn.key], writes=[dn.key])
                    S.op('dve', RCP(dn.ap[:, 2:3], dn.ap[:, 1:2]), reads=[dn.key], writes=[dn.key])
                    if tt not in done:
                        done.add(tt)
                        S.op('dve', TS(hacc.ap[:, tt, :], P.ap(bN)[:, 0:128], dn.ap[:, 2:3], ALU.mult),
                             reads=PK(bN) + [dn.key], writes=[hacc.k(tt)])
                    else:
                        S.op('dve', STT(hacc.ap[:, tt, :], P.ap(bN)[:, 0:128], dn.ap[:, 2:3], hacc.ap[:, tt, :], ALU.mult, ALU.add),
                             reads=PK(bN) + [dn.key, hacc.k(tt)], writes=[hacc.k(tt)])
                        S.op('act', ACTF(junk.ap, hacc.ap[:, tt, :], AF.Square, accum=ssh.ap[:, tt:tt + 1]),
                             reads=[hacc.k(tt)], writes=[junk.key, ssh.k(tt)])

                if stop == 'mSprep' and ctx:
                    return
                for step in range(ntl):
                    tf, tb = step, ntl - 1 - step
                    scf = intra(tf, 0)
                    scb = intra(tb, 1)
                    bNf = num_open(tf, 0, scf)
                    inter(tf, 0, 0, bNf, False)
                    update(tf, 0, 0)
                    bNb = num_open(tb, 1, scb)
                    inter(tb, 1, 1, bNb, False)
                    update(tb, 1, 1)
                    inter(tf, 0, 1, bNf, True)
                    update(tf, 0, 1)
                    epilogue(tf, 0, bNf)
                    inter(tb, 1, 0, bNb, True)
                    update(tb, 1, 0)
                    epilogue(tb, 1, bNb)
                if h == 0 and si == 0:
                    TAP('hacc', hacc.ap, [128, ntl, 128], [hacc.key])
                    TAP('qTb', qTb.ap, [128, L], [qTb.key], BF16)
                    TAP('kTb', kTb.ap, [128, L], [kTb.key], BF16)
                    TAP('qTw', qTw.ap, [128, 2, L], [qTw.key], BF16)
                    TAP('og', og.ap, [128, ntl, 128], [og.key], BF16)
                if stop == 'mloop':
                    return
                if not ctx:
                    for d_ in range(2):
                        S.dma('sp', DMA(oC_d[si, j, d_, h], Cst.ap[:, d_, 0:128]), reads=[Cst.k(d_)], is_output=True)
                        S.dma('sp', DMA(on_d[si, j, d_, h, :].rearrange("(p o) -> p o", o=1), Cst.ap[:, d_, 128:129]),
                              reads=[Cst.k(d_)], is_output=True)
                if stop == 'mout':
                    return
                if si < 2:
                    proj_fm(it + 1, w1, w1k)
                elif h < 3:
                    proj_fm(it + 1, w1_next[0], w1_next[1])
                S.op('dve', TS(msh.ap, ssh.ap, 1.0 / 128, ALU.mult, EPS, ALU.add), reads=[ssh.key], writes=[msh.key])
                rsqrt(rsh.ap, msh.ap, nh1.ap, nh2.ap, [msh.key], [rsh.key, nh1.key, nh2.key])
                bT = None
                for tt in range(ntl):
                    t = t0 + tt
                    ht = htmp[tt % 2]
                    hm = hmt[tt % 2]
                    S.op('dve', STT(ht.ap, hacc.ap[:, tt, :], rsh.ap[:, tt:tt + 1], ghn[:, h * 128:(h + 1) * 128], ALU.mult, ALU.mult),
                         reads=[hacc.k(tt), rsh.key, 'ghn'], writes=[ht.key])
                    S.op('dve', TT(hm.ap, ht.ap, og.ap[:, tt, :], ALU.mult), reads=[ht.key, og.k(tt)], writes=[hm.key])
                    if tt % 4 == 0:
                        bT = P.get()
                    S.op('pe', MM(P.ap(bT)[:, (tt % 4) * 128:(tt % 4 + 1) * 128], hm.ap, identb[:]),
                         reads=[hm.key, 'identb'], writes=PK(bT))
                    if tt % 4 == 3 or tt == ntl - 1:
                        n = tt % 4 + 1
                        tfirst = t - (n - 1)
                        S.op('act', ACTF(mixT[:, 4 + h, cols(tfirst, n)], P.ap(bT)[:, 0:n * 128], AF.Copy),
                             reads=PK(bT), writes=[('mixT', tfirst + i) for i in range(n)])
                A.release(mkC)
                if stop is not None and stop.startswith('mstep') and (h * 3 + si + 1) >= int(stop[5:]):
                    return
        A.release(mkA)

    def na_blocks(qi):
        if qi in (0, 1):
            js = [0, 1, 2, 3]
        elif qi in (6, 7):
            js = [4, 5, 6, 7]
        else:
            js = list(range(qi - 2, qi + 3))
        return [0, 1] + [2 + jt for jt in js]

    def odd_layer(l):
        j = l // 2
        W = odwin_d[j]
        S.dma('sp', DMA(gq[:, 0:64], odqn_d[j].partition_broadcast(128)), writes=['gq'])
        S.dma('sp', DMA(gk[:, 0:64], odkn_d[j].partition_broadcast(128)), writes=['gk'])
        S.op('dve', TS(gq[:, 0:64], gq[:, 0:64], 0.125, ALU.mult), reads=['gq'], writes=['gq'])
        segs = [(t0 * 128, ntl * 128) for (t0, ntl, ctx) in SEQS]
        conv_w = lambda c_: load_w([W[:, i * 512 + c_ * 128:i * 512 + (c_ + 1) * 128] for i in range(4)])
        nxt = pre_w.pop('c0') if 'c0' in pre_w else conv_w(0)
        wqk_of = lambda hg_: load_w([W[:, 2048 + hg_ * 256:2048 + (hg_ + 1) * 256], W[:, 2560 + hg_ * 256:2560 + (hg_ + 1) * 256]])
        wvg_of = lambda hg_: load_w([W[:, 3072 + hg_ * 256:3072 + (hg_ + 1) * 256], W[:, 3584 + hg_ * 256:3584 + (hg_ + 1) * 256]])
        na_pre = {}
        for c4 in range(4):
            wv, wk_ = nxt
            if c4 < 3:
                nxt = conv_w(c4 + 1)
            else:
                na_pre['qk'] = wqk_of(0)
            mk = A.mark()
            u = A.alloc("u", [NTOK], F32)
            ab = A.alloc("ab", [NTOK], F32)
            y = A.alloc("y", [NTOK], F32)
            xs = A.alloc("xs", [512], F32)
            tgc = A.alloc("tgc", [512], F32)
            aa = A.alloc("aa", [512], F32)
            for blk in range(3):
                cs = slice(blk * 512, (blk + 1) * 512)
                hk = [('hsT', 4 * blk + i) for i in range(4)]
                bs = []
                for i in range(4):
                    b = P.get()
                    bs.append(b)
                    for kc in range(8):
                        S.op('pe', MM(P.ap(b), wv[:, kc, i * 128:(i + 1) * 128], hsT[:, kc, cs], start=(kc == 0), stop=(kc == 7)),
                             reads=[wk_] + hk, writes=PK(b))
                bx, bb, bc_, bg = bs
                S.op('act', ACTF(xs.ap, P.ap(bx), AF.Copy), reads=PK(bx), writes=[xs.key])
                S.op('dve', TT(u.ap[:, cs], P.ap(bc_), xs.ap, ALU.mult), reads=PK(bc_) + [xs.key], writes=[u.k(blk)])
                S.op('act', ACTF(tgc.ap, P.ap(bg), AF.Tanh, scale=0.5), reads=PK(bg), writes=[tgc.key])
                S.op('dve', STT(aa.ap, tgc.ap, 1.0, P.ap(bg), ALU.add, ALU.mult), reads=[tgc.key] + PK(bg), writes=[aa.key])
                S.op('dve', TT(ab.ap[:, cs], P.ap(bb), aa.ap, ALU.mult), reads=PK(bb) + [aa.key], writes=[ab.k(blk)])
                pump(1)
            S.op('dve', TS(y.ap, u.ap, convp[:, j, c4, 1:2], ALU.mult, convp[:, j, c4, 3:4], ALU.add),
                 reads=[u.key, ('convp', j)], writes=[y.key])
            for (a, L) in segs:
                S.op('dve', STT(y.ap[:, a + 1:a + L], u.ap[:, a:a + L - 1], convp[:, j, c4, 0:1], y.ap[:, a + 1:a + L], ALU.mult, ALU.add),
                     reads=[u.key, y.key, ('convp', j)], writes=[y.key])
                S.op('dve', STT(y.ap[:, a:a + L - 1], u.ap[:, a + 1:a + L], convp[:, j, c4, 2:3], y.ap[:, a:a + L - 1], ALU.mult, ALU.add),
                     reads=[u.key, y.key, ('convp', j)], writes=[y.key])
            S.op('dve', TT(mixT[:, c4, :], y.ap, ab.ap, ALU.mult), reads=[y.key, ab.key],
                 writes=[('mixT', t) for t in range(NT)])
            A.release(mk)
        if stop == 'conv':
            return
        for hg in range(2):
            wqk, wqkk = na_pre.pop('qk')
            wvg, wvgk = wvg_of(hg)
            if hg == 0:
                na_pre['qk'] = wqk_of(1)
            for si, (t0, ntl, ctx) in enumerate(SEQS):
                if hg == 1 and si == 2 and stop is None:
                    preload_out(odwout_d[j])
                mk3 = A.mark()
                NQ = ntl
                ktiles = ([12, 13] if ctx else []) + list(range(t0, t0 + ntl))
                NK = len(ktiles)
                qT = A.alloc("nqT", [4, NQ * 128], BF16)
                kT = A.alloc("nkT", [4, NK * 128], BF16)
                vaug = A.alloc("nvaug", [NK, 4, 66], BF16)
                att = A.alloc("natt", [NQ, 4, 64], BF16)
                abuf = A.alloc("nabuf", [NQ, 256], BF16)
                stage = [A.alloc("nstage%d" % i, [8, 64], F32) for i in range(2)]
                qkn = [A.alloc("nqkn%d" % i, [8, 64], BF16) for i in range(2)]
                sqs = A.alloc("nsqs", [8, 64], BF16)
                ss = A.alloc("nss", [8], F32)
                ms8 = A.alloc("nms8", [8], F32)
                rs8 = A.alloc("nrs8", [8], F32)
                na = A.alloc("nna", [8], F32)
                nb_ = A.alloc("nnb", [8], F32)
                tg = A.alloc("ntg", [256], F32)
                mt = [A.alloc("nmt%d" % i, [256], BF16) for i in range(2)]
                kvf = [A.alloc("kvf%d" % i, [2, 4, 64], F32) for i in range(2)]
                kvout = A.alloc("kvout", [2, 2, 4, 64], F32) if not ctx else None
                S.op('pool', MS(vaug.ap, 1.0), writes=[vaug.key])
                pendB = [None]
                for ki, t in enumerate(ktiles):
                    is_ctx = t >= 12
                    qi = ki - (2 if ctx else 0)
                    st = stage[ki % 2]
                    qk = qkn[ki % 2]
                    kf = kvf[ki % 2]
                    if is_ctx:
                        r0 = (t - 12) * 128
                        S.dma('sp', DMA(kf.ap[:, 0, :, :], nakc_d[j, r0:r0 + 128, hg * 4:(hg + 1) * 4, :]), writes=[kf.k(0)])
                        S.dma('sp', DMA(kf.ap[:, 1, :, :], navc_d[j, r0:r0 + 128, hg * 4:(hg + 1) * 4, :]), writes=[kf.k(1)])
                        S.op('dve', CP(qk.ap[:, 4:8, :], kf.ap[:, 0, :, :]), reads=[kf.k(0)], writes=[qk.key])
                        S.op('act', ACTF(vaug.ap[:, ki, :, 0:64], kf.ap[:, 1, :, :], AF.Copy), reads=[kf.k(1)], writes=[vaug.key])
                        lo = 4
                    else:
                        lo = 0
                        bqk = P.get()
                        for kc in range(8):
                            S.op('pe', MM(P.ap(bqk), hsT[:, kc, cols(t)], wqk[:, kc, :], start=(kc == 0), stop=(kc == 7)),
                                 reads=[('hsT', t), wqkk], writes=PK(bqk))
                        bvg = P.get()
                        for kc in range(8):
                            S.op('pe', MM(P.ap(bvg), hsT[:, kc, cols(t)], wvg[:, kc, :], start=(kc == 0), stop=(kc == 7)),
                                 reads=[('hsT', t), wvgk], writes=PK(bvg))
                        qkv = P.ap(bqk).rearrange("p (h e) -> p h e", e=64)
                        vv = P.ap(bvg)[:, 0:256].rearrange("p (h e) -> p h e", e=64)
                        S.op('act', ACTF(sqs.ap, qkv, AF.Square), reads=PK(bqk), writes=[sqs.key])
                        S.op('dve', RED(ss.ap, sqs.ap), reads=[sqs.key], writes=[ss.key])
                        S.op('dve', TS(ms8.ap, ss.ap, 1.0 / 64, ALU.mult, EPS, ALU.add), reads=[ss.key], writes=[ms8.key])
                        S.op('dve', TT(st.ap[:, 0:4, :], qkv[:, 0:4, :], gq[:, 0:64].unsqueeze(1).broadcast_to([128, 4, 64]), ALU.mult),
                             reads=PK(bqk) + ['gq'], writes=[st.key])
                        S.op('dve', TT(st.ap[:, 4:8, :], qkv[:, 4:8, :], gk[:, 0:64].unsqueeze(1).broadcast_to([128, 4, 64]), ALU.mult),
                             reads=PK(bqk) + ['gk'], writes=[st.key])
                        S.op('act', ACTF(vaug.ap[:, ki, :, 0:64], vv, AF.Copy), reads=PK(bvg), writes=[vaug.key])
                        if not ctx:
                            S.op('act', ACTF(kvout.ap[:, 1, qi, :, :], vv, AF.Copy), reads=PK(bvg), writes=[kvout.k((1, qi))])
                        rsqrt(rs8.ap, ms8.ap, na.ap, nb_.ap, [ms8.key], [rs8.key, na.key, nb_.key])
                        S.op('dve', TT(qk.ap, st.ap, rs8.ap.unsqueeze(2).broadcast_to([128, 8, 64]), ALU.mult),
                             reads=[st.key, rs8.key], writes=[qk.key])
                        S.op('act', ACTF(tg.ap, P.ap(bvg)[:, 256:512], AF.Tanh, scale=0.5), reads=PK(bvg), writes=[tg.key])
                        S.op('dve', STT(abuf.ap[:, qi, :], tg.ap, 1.0, P.ap(bvg)[:, 256:512], ALU.add, ALU.mult),
                             reads=[tg.key] + PK(bvg), writes=[abuf.k(qi)])
                        if not ctx:
                            for h_ in range(4):
                                S.op('act', ACTF(kvout.ap[:, 0, qi, h_, :], st.ap[:, 4 + h_, :], AF.Copy,
                                                 scale=rs8.ap[:, 4 + h_:5 + h_]),
                                     reads=[st.key, rs8.key], writes=[kvout.k((0, qi))])
                    def stageB(qk=qk, lo=lo, is_ctx=is_ctx, qi=qi, ki=ki):
                        bt = P.get(2)
                        for i in range(lo, 8):
                            S.op('pe', MM(P.ap(bt, 2)[0:64, i * 128:(i + 1) * 128], qk.ap[:, i, :], identb[:]),
                                 reads=[qk.key, 'identb'], writes=PK(bt, 2))
                        if not is_ctx:
                            S.op('act', ACTF(qT.ap[0:64, :, qi * 128:(qi + 1) * 128],
                                             P.ap(bt, 2)[0:64, 0:512].rearrange("p (h q) -> p h q", h=4), AF.Copy),
                                 reads=PK(bt, 2), writes=[qT.k(qi)])
                        S.op('act', ACTF(kT.ap[0:64, :, ki * 128:(ki + 1) * 128],
                                         P.ap(bt, 2)[0:64, 512:1024].rearrange("p (h q) -> p h q", h=4), AF.Copy),
                             reads=PK(bt, 2), writes=[kT.k(ki)])
                    if pendB[0] is not None:
                        pendB[0]()
                    pendB[0] = stageB
                    pump(1)
                pendB[0]()
                pendB[0] = None
                if not ctx:
                    for kv_, dst_ in ((0, onak_d), (1, onav_d)):
                        for qo in range(NQ):
                            S.dma('pool', DMA(dst_[si, j, qo * 128:(qo + 1) * 128, hg * 4:(hg + 1) * 4, :],
                                              kvout.ap[:, kv_, qo, :, :]), reads=[kvout.key], is_output=True)
                if ctx:
                    xraw = [A.alloc("xraw%d" % i, [1024], BF16) for i in range(2)]
                    xtab = [[A.alloc("xtab%d_%d" % (i, k_), [1024], BF16) for k_ in range(2)] for i in range(2)]

                    def hook(hh, hg=hg):
                        hd = hg * 4 + hh
                        xr = xraw[hh % 2]
                        S.dma('pool', DMA(xr.ap, rpbx_d[j, hd]), writes=[xr.key])
                        for k_ in range(2):
                            S.op('dve', TT(xtab[hh % 2][k_].ap, xr.ap, cmaskb[:, k_, :], ALU.add),
                                 reads=[xr.key, 'cmaskb'], writes=[xtab[hh % 2][k_].key])

                    def bias_fn(hh, qi, kidx):
                        if kidx < 2:
                            return None
                        jt = kidx - 2
                        w0 = 7 - 2 * (jt - qi)
                        k_ = 0 if qi in (0, 1, 6, 7) else 1
                        tb_ = xtab[hh % 2][k_]
                        return (tb_.ap[:, w0 * 64:(w0 + 2) * 64], [tb_.key])

                    if stop == 'na_prep':
                        return
                    attention(qT, kT, 64, vaug, att, NQ, na_blocks, bias_fn=bias_fn, head_hook=hook)
                else:
                    attention(qT, kT, 64, vaug, att, NQ, lambda qi_, NK=NK: list(range(NK)))
                if stop == 'na_att':
                    return
                for qi in range(NQ):
                    t = t0 + qi
                    m_ = mt[qi % 2]
                    S.op('dve', TT(m_.ap, abuf.ap[:, qi, :], att.ap[:, qi, :, :].rearrange("p h e -> p (h e)"), ALU.mult),
                         reads=[abuf.k(qi), att.k(qi)], writes=[m_.key])
                    bT = P.get()
                    for i in range(2):
                        S.op('pe', MM(P.ap(bT)[:, i * 128:(i + 1) * 128], m_.ap[:, i * 128:(i + 1) * 128], identb[:]),
                             reads=[m_.key, 'identb'], writes=PK(bT))
                    S.op('act', ACTF(mixT[:, 4 + hg * 2:4 + hg * 2 + 2, cols(t)],
                                     P.ap(bT)[:, 0:256].rearrange("p (c q) -> p c q", c=2), AF.Copy, scale=0.5),
                         reads=PK(bT), writes=[('mixT', t)])
                A.release(mk3)
                if stop == 'na_%d' % si:
                    return

    queue_mod(0)
    load_x()
    pump(24)
    for l in range(depth):
        if stop is None:
            if l % 2 == 0:
                pre_w['wa'] = load_w([evwin_d[l // 2][:, 0:416]])
            else:
                Wo = odwin_d[l // 2]
                pre_w['c0'] = load_w([Wo[:, i * 512:i * 512 + 128] for i in range(4)])
        phase_norm(l)
        if stop == 'norm':
            break
        if l + 1 < depth:
            queue_mod(l + 1)
        TAP('hsT%d' % l, hsT[:], [128, 8, NTOK], [('hsT', t) for t in range(NT)], BF16)
        last = (l == depth - 1)
        sb_ = (lambda blk: store_tiles(range(4 * blk, 4 * blk + 4))) if last else None
        if l % 2 == 0:
            even_layer(l)
            TAP('mixT%d' % l, mixT[:], [128, 8, NTOK], [('mixT', t) for t in range(NT)], BF16)
            pump(24)
            phase_out(l, evwout_d[l // 2], sb_)
        else:
            odd_layer(l)
            pump(24)
            phase_out(l, odwout_d[l // 2], sb_)
    if depth == 0:
        store_y()
    counts, nw = S.emit(nc, es)
    info = {'ops': counts, 'waits': nw, 'arena_peak': A.peak, 'taps': tap_list}
    es.close()
    return nc, info


def _constants():
    n = np.arange(1024)
    row = (n // GRID_W).astype(np.float32)
    col = (n % GRID_W).astype(np.float32)
    inv = (np.float32(ROPE_BASE) ** (-np.arange(8, dtype=np.float32) / np.float32(8))).astype(np.float32)
    ar = (row[:, None] * inv[None, :]).astype(np.float32)
    ac = (col[:, None] * inv[None, :]).astype(np.float32)
    C = np.concatenate([np.cos(ar), np.cos(ar), np.cos(ac), np.cos(ac)], axis=1).astype(np.float32)
    Sg = np.concatenate([-np.sin(ar), np.sin(ar), -np.sin(ac), np.sin(ac)], axis=1).astype(np.float32)
    rope_cs = np.stack([C, Sg]).astype(np.float32)
    s = np.arange(128)[:, None]
    t = np.arange(128)[None, :]
    same = (s // 64) == (t // 64)
    lmask = np.stack([np.where(same & (s <= t), 0.0, NEG), np.where(same & (s >= t), 0.0, NEG)]).astype(np.float32)
    kl = np.arange(2)[:, None, None, None]
    kc = np.arange(64)[None, :, None, None]
    w = np.arange(16)[None, None, :, None]
    qc = np.arange(64)[None, None, None, :]
    dr = np.broadcast_to(7 - w + kl, (2, 64, 16, 64))
    dc = np.broadcast_to(kc - qc, (2, 64, 16, 64))
    c0 = np.clip(qc - 8, 0, 48)
    col_ok = np.broadcast_to((kc >= c0) & (kc < c0 + 16), (2, 64, 16, 64))
    ok_full = col_ok & (np.abs(dr) <= 7)
    ok_int = ok_full & (dr >= -4) & (dr <= 3)
    cmask = np.stack([np.where(ok_full, 0.0, NEG), np.where(ok_int, 0.0, NEG)]).astype(np.float32).reshape(2, 128, 1024)
    idx_r = np.clip(dr + 7, 0, 14).reshape(128, 1024)
    idx_c = np.clip(dc + 15, 0, 30).reshape(128, 1024)
    return rope_cs, lmask, cmask, idx_r, idx_c


_CACHE = {}


def _get_program(depth, taps=(), stop=None):
    key = (depth, tuple(taps), stop)
    if key not in _CACHE:
        _CACHE[key] = build_nc(depth, taps, stop)
    return _CACHE[key]


def kernel(x_prompt, x_sample, c, cache_mla_ckv, cache_mla_kpe, state_mlstm_C, state_mlstm_n, state_mlstm_m,
           cache_na_k, cache_na_v, c_ctx, norm_w, ada_w, ada_b,
           ev_w_in, ev_q_a_norm, ev_kv_a_norm, ev_w_q_b, ev_w_kv_b, ev_q_norm, ev_k_norm, ev_b_i, ev_b_f,
           ev_h_norm, ev_w_out,
           od_w_in, od_conv_w, od_conv_b, od_q_norm, od_k_norm, od_rpb, od_w_out, _depth=4, _taps=(), _raw=False, _stop=None):
    f = lambda a: np.ascontiguousarray(np.asarray(a), dtype=np.float32)
    x_prompt, x_sample, c, c_ctx = f(x_prompt), f(x_sample), f(c), f(c_ctx)
    rope_cs, lmask, cmask, idx_r, idx_c = _constants()
    od_rpb = f(od_rpb)
    rpbx = np.ascontiguousarray(od_rpb[:, :, idx_r, idx_c])
    shared = {
        "norm_w": f(norm_w), "ada_w": f(ada_w), "ada_b": f(ada_b),
        "ev_w_in": f(ev_w_in), "ev_q_a_norm": f(ev_q_a_norm), "ev_kv_a_norm": f(ev_kv_a_norm),
        "ev_w_q_b": f(ev_w_q_b), "ev_w_kv_b": f(ev_w_kv_b), "ev_q_norm": f(ev_q_norm), "ev_k_norm": f(ev_k_norm),
        "ev_b_i": f(ev_b_i), "ev_b_f": f(ev_b_f), "ev_h_norm": f(ev_h_norm), "ev_w_out": f(ev_w_out),
        "od_w_in": f(od_w_in), "od_conv_w": f(od_conv_w), "od_conv_b": f(od_conv_b), "od_q_norm": f(od_q_norm),
        "od_k_norm": f(od_k_norm), "od_w_out": f(od_w_out), "rpbx": rpbx, "cmask": cmask, "rope_cs": rope_cs,
        "lmask": lmask,
    }
    cm_ckv, cm_kpe = f(cache_mla_ckv), f(cache_mla_kpe)
    sC, sn, sm = f(state_mlstm_C), f(state_mlstm_n), f(state_mlstm_m)
    nk, nv = f(cache_na_k), f(cache_na_v)
    in_maps = []
    for core in range(8):
        b = core // 4
        m = dict(shared)
        m["x"] = np.ascontiguousarray(np.concatenate([x_prompt[2 * core], x_prompt[2 * core + 1], x_sample[b]], axis=0))
        m["cond"] = np.ascontiguousarray(np.stack([c_ctx, c[b]]))
        m["ckv_c"] = cm_ckv[b]
        m["kpe_c"] = cm_kpe[b]
        m["stC"] = sC[b]
        m["stn"] = sn[b]
        m["stm"] = sm[b]
        m["nak_c"] = nk[b]
        m["nav_c"] = nv[b]
        in_maps.append(m)
    nc, info = _get_program(_depth, _taps, _stop)
    res = run_bass_kernel_spmd(nc, in_maps, core_ids=list(range(8)))
    R = res.results
    if _raw:
        return R, info
    yp = np.concatenate([R[i]["yp"].reshape(2, 256, D) for i in range(8)], axis=0)
    ys = np.stack([R[0]["ys"], R[4]["ys"]], axis=0)
    cat = lambda k: np.concatenate([R[i][k] for i in range(8)], axis=0)
    return (yp.astype(np.float32), ys.astype(np.float32), cat("o_ckv"), cat("o_kpe"), cat("o_C"), cat("o_n"),
            cat("o_m"), cat("o_nak"), cat("o_nav"))
```

```python
import os
from contextlib import ExitStack

import numpy as np
import concourse.bass as bass
import concourse.mybir as mybir
from concourse.bass_utils import run_bass_kernel_spmd

F32 = mybir.dt.float32
BF16 = mybir.dt.bfloat16
I32 = mybir.dt.int32
AF = mybir.ActivationFunctionType
ALU = mybir.AluOpType
AX = mybir.AxisListType

D = 1024
NT = 12
NTOK = 1536
EPS = 1e-6
NEG = -30000.0
GRID_W = 64
ROPE_BASE = 10000.0

ENGS = ['pe', 'act', 'dve', 'pool', 'sp']
N_DMA_SEMS = 24
SAME_ENG_WINDOW = 6


class _Op:
    __slots__ = ('eng', 'fn', 'deps_c', 'deps_d', 'sig', 'idx', 'is_dma', 'dsem', 'dval')


class Sched:
    def __init__(self):
        self.ops = {e: [] for e in ENGS}
        self.bufs = {}
        self.dma_cnt = [0] * N_DMA_SEMS
        self.dma_last = [None] * N_DMA_SEMS
        self.dma_rr = 0
        self.out_tokens = []

    def reg(self, name, rng=None):
        self.bufs[name] = {'range': rng, 'subs': {}}

    def _overlapping(self, name):
        b = self.bufs[name]
        res = [name]
        if b['range'] is None:
            return res
        a0, a1 = b['range']
        for n2, b2 in self.bufs.items():
            if n2 == name or b2['range'] is None:
                continue
            c0, c1 = b2['range']
            if c0 < a1 and a0 < c1 and b2['subs']:
                res.append(n2)
        return res

    @staticmethod
    def _norm(key):
        if isinstance(key, tuple):
            return key[0], key[1]
        return key, None

    def _collect(self, key, is_write, dc, dd):
        name, sub = self._norm(key)
        if name not in self.bufs:
            self.reg(name)
        for n2 in self._overlapping(name):
            subs = self.bufs[n2]['subs']
            if n2 == name and sub is not None:
                cands = [s for s in (sub, None) if s in subs]
            else:
                cands = list(subs.keys())
            for s in cands:
                st = subs[s]
                toks = [st[0]] if st[0] is not None else []
                if is_write:
                    toks = toks + list(st[1].values())
                for t in toks:
                    if t[0] == 'dma':
                        dd[t[1]] = max(dd.get(t[1], 0), t[2])
                    else:
                        dc[t[0]] = max(dc.get(t[0], -1), t[1])

    def _update(self, key, is_write, tok):
        name, sub = self._norm(key)
        subs = self.bufs[name]['subs']
        if is_write:
            if sub is None:
                for n2 in self._overlapping(name):
                    if n2 != name:
                        self.bufs[n2]['subs'] = {}
                subs.clear()
            subs[sub] = [tok, {}]
        else:
            if sub not in subs:
                subs[sub] = [None, {}]
            rk = tok[0] if tok[0] != 'dma' else ('dma', tok[1])
            subs[sub][1][rk] = tok

    def op(self, eng, fn, reads=(), writes=()):
        o = _Op()
        o.eng, o.fn, o.sig, o.is_dma = eng, fn, False, False
        o.idx = len(self.ops[eng])
        dc, dd = {}, {}
        for k in reads:
            self._collect(k, False, dc, dd)
        for k in writes:
            self._collect(k, True, dc, dd)
        o.deps_c, o.deps_d = dc, dd
        tok = (eng, o.idx)
        self.ops[eng].append(o)
        for k in reads:
            self._update(k, False, tok)
        for k in writes:
            self._update(k, True, tok)
        return o

    def dma(self, eng, fn, reads=(), writes=(), is_output=False):
        o = _Op()
        o.eng, o.fn, o.sig, o.is_dma = eng, fn, False, True
        o.idx = len(self.ops[eng])
        s = self.dma_rr
        self.dma_rr = (self.dma_rr + 1) % N_DMA_SEMS
        dc, dd = {}, {}
        if self.dma_last[s] is not None:
            dd[s] = self.dma_cnt[s]
        for k in reads:
            self._collect(k, False, dc, dd)
        for k in writes:
            self._collect(k, True, dc, dd)
        self.dma_cnt[s] += 16
        o.dsem, o.dval = s, self.dma_cnt[s]
        self.dma_last[s] = o
        o.deps_c, o.deps_d = dc, dd
        tok = ('dma', s, o.dval)
        self.ops[eng].append(o)
        for k in reads:
            self._update(k, False, tok)
        for k in writes:
            self._update(k, True, tok)
        if is_output:
            self.out_tokens.append(tok)
        return o

    @staticmethod
    def _need(o, x, j):
        if x != o.eng or o.is_dma:
            return True
        if o.eng == 'pe':
            return False
        return (o.idx - j) <= SAME_ENG_WINDOW

    def emit(self, nc, es):
        for e in ENGS:
            for o in self.ops[e]:
                for x, j in o.deps_c.items():
                    if self._need(o, x, j):
                        self.ops[x][j].sig = True
        cnt = {}
        for e in ENGS:
            c = 0
            arr = []
            for o in self.ops[e]:
                if o.sig and not o.is_dma:
                    c += 1
                arr.append(c)
            cnt[e] = arr
        sems = {e: es.enter_context(nc.semaphore('s_' + e)) for e in ENGS}
        dsems = [es.enter_context(nc.semaphore('d%d' % i)) for i in range(N_DMA_SEMS)]
        block = es.enter_context(nc.Block())
        handles = {'pe': block.tensor, 'act': block.scalar, 'dve': block.vector,
                   'pool': block.gpsimd, 'sp': block.sync}
        nwaits = [0]

        def run_engine(e):
            def body(eng):
                waited_c = {}
                waited_d = {}
                for o in self.ops[e]:
                    for x, j in o.deps_c.items():
                        if not self._need(o, x, j):
                            continue
                        v = cnt[x][j]
                        if v > waited_c.get(x, 0):
                            eng.wait_ge(sems[x], v)
                            waited_c[x] = v
                            nwaits[0] += 1
                    for s, v in o.deps_d.items():
                        if v > waited_d.get(s, 0):
                            eng.wait_ge(dsems[s], v)
                            waited_d[s] = v
                            nwaits[0] += 1
                    ins = o.fn(eng)
                    if o.is_dma:
                        ins.then_inc(dsems[o.dsem], 16)
                    elif o.sig:
                        ins.then_inc(sems[e], 1)
                if e == 'sp':
                    fin = {}
                    for t in self.out_tokens:
                        fin[t[1]] = max(fin.get(t[1], 0), t[2])
                    for s, v in fin.items():
                        if v > waited_d.get(s, 0):
                            eng.wait_ge(dsems[s], v)
            return body

        for e in ENGS:
            handles[e](run_engine(e))
        return {e: len(self.ops[e]) for e in ENGS}, nwaits[0]


def MM(out, lhsT, rhs, start=True, stop=True):
    return lambda e: e.matmul(out, lhsT=lhsT, rhs=rhs, start=start, stop=stop)


def ACTF(out, in_, func, bias=None, scale=None, accum=None):
    def f(e):
        kw = {}
        if bias is not None:
            kw['bias'] = bias
        if scale is not None:
            kw['scale'] = scale
        if accum is not None:
            kw['accum_out'] = accum
        return e.activation(out=out, in_=in_, func=func, **kw)
    return f


def TT(out, a, b, op):
    return lambda e: e.tensor_tensor(out=out, in0=a, in1=b, op=op)


def TS(out, a, s1, op0, s2=None, op1=None):
    def f(e):
        if op1 is None:
            return e.tensor_scalar(out=out, in0=a, scalar1=s1, scalar2=None, op0=op0)
        return e.tensor_scalar(out=out, in0=a, scalar1=s1, scalar2=s2, op0=op0, op1=op1)
    return f


def STT(out, a, s, b, op0, op1):
    return lambda e: e.scalar_tensor_tensor(out=out, in0=a, scalar=s, in1=b, op0=op0, op1=op1)


def CP(out, in_):
    return lambda e: e.tensor_copy(out=out, in_=in_)


def RED(out, in_, op=ALU.add):
    return lambda e: e.tensor_reduce(out=out, in_=in_, axis=AX.X, op=op)


def MS(ap, v):
    return lambda e: e.memset(ap, v)


def RCP(out, in_):
    return lambda e: e.reciprocal(out=out, in_=in_)


def SCAN(out, d0, d1, init, op0, op1):
    return lambda e: e.tensor_tensor_scan(out=out, data0=d0, data1=d1, initial=init, op0=op0, op1=op1)


def DMA(out, in_, **kw):
    return lambda e: e.dma_start(out=out, in_=in_, **kw)


def ASEL(out, in_, pattern, cmp, fill, base, cm):
    return lambda e: e.affine_select(out=out, in_=in_, pattern=pattern, compare_op=cmp, fill=fill,
                                     base=base, channel_multiplier=cm)


class Buf:
    __slots__ = ('ap', 'key')

    def __init__(self, ap, key):
        self.ap, self.key = ap, key

    def k(self, sub):
        return (self.key, sub)


class Arena:
    BLK = 32

    def __init__(self, S, tensor, words):
        self.S, self.t, self.words = S, tensor, words
        self.top = 0
        self.ctr = 0
        self.peak = 0
        self.live = []
        self.blocks = [dict() for _ in range(words // self.BLK + 2)]

    def mark(self):
        return self.top

    @staticmethod
    def _fold(dst, tok):
        rk = tok[0] if tok[0] != 'dma' else ('dma', tok[1])
        old = dst.get(rk)
        if old is None or tok[-1] > old[-1]:
            dst[rk] = tok

    def release(self, m):
        while self.live and self.live[-1][0] >= m:
            off, w, key = self.live.pop()
            st = self.S.bufs.pop(key, None)
            if st is None:
                continue
            toks = []
            for sub, (wt, rd) in st['subs'].items():
                if wt is not None:
                    toks.append(wt)
                toks.extend(rd.values())
            for bi in range(off // self.BLK, (off + w - 1) // self.BLK + 1):
                blk = self.blocks[bi]
                for t_ in toks:
                    self._fold(blk, t_)
        self.top = m

    def alloc(self, name, shape, dtype=F32, parts=128):
        n = 1
        for s in shape:
            n *= s
        w = n if dtype == F32 else (n + 1) // 2
        w = (w + 1) // 2 * 2
        off = self.top
        self.top += w
        self.peak = max(self.peak, self.top)
        assert self.top <= self.words, "arena overflow %s: %d > %d" % (name, self.top, self.words)
        ap = self.t[0:parts, off:off + w]
        if dtype != F32:
            ap = ap.bitcast(dtype)
        ap = ap[:, 0:n]
        if len(shape) > 1:
            names = ["d%d" % i for i in range(len(shape))]
            kw = {names[i]: shape[i] for i in range(len(shape))}
            ap = ap.rearrange("p (" + " ".join(names) + ") -> p " + " ".join(names), **kw)
        self.ctr += 1
        key = "%s#%d" % (name, self.ctr)
        self.S.reg(key, None)
        inh = {}
        for bi in range(off // self.BLK, (off + w - 1) // self.BLK + 1):
            for t_ in self.blocks[bi].values():
                self._fold(inh, t_)
        if inh:
            self.S.bufs[key]['subs'][None] = [None, inh]
        self.live.append((off, w, key))
        return Buf(ap, key)


class PSum:
    def __init__(self, ps):
        self.ps = ps
        self.next = 0
        self.limit = 7

    def get(self, nb=1):
        if self.next + nb > self.limit:
            self.next = 0
        b = self.next
        self.next = (self.next + nb) % self.limit
        return b

    def ap(self, b, nb=1):
        return self.ps[:, b * 512:(b + nb) * 512]

    @staticmethod
    def keys(b, nb=1):
        return [('ps', b + i) for i in range(nb)]


SEQS = [(0, 2, False), (2, 2, False), (4, 8, True)]
ARENA_WORDS = 15400
WSLOT = 4096
NWSLOT = 3


def build_nc(depth=4, taps=(), stop=None):
    nc = bass.Bass("TRN2", target_bir_lowering=False)
    S = Sched()
    es = ExitStack()

    def din(name, shape):
        return nc.dram_tensor(name, list(shape), F32, kind="ExternalInput").ap()

    def dout(name, shape):
        return nc.dram_tensor(name, list(shape), F32, kind="ExternalOutput").ap()

    x_d = din("x", [NTOK, D])
    cond_d = din("cond", [2, D])
    ckvc_d = din("ckv_c", [2, 256, 128])
    kpec_d = din("kpe_c", [2, 256, 32])
    stC_d = din("stC", [2, 2, 4, 128, 128])
    stn_d = din("stn", [2, 2, 4, 128])
    stm_d = din("stm", [2, 2, 4])
    nakc_d = din("nak_c", [2, 256, 8, 64])
    navc_d = din("nav_c", [2, 256, 8, 64])
    normw_d = din("norm_w", [4, D])
    adaw_d = din("ada_w", [4, D, 3 * D])
    adab_d = din("ada_b", [4, 3 * D])
    evwin_d = din("ev_w_in", [2, D, 3504])
    evqan_d = din("ev_q_a_norm", [2, 256])
    evkvan_d = din("ev_kv_a_norm", [2, 128])
    evwqb_d = din("ev_w_q_b", [2, 256, 768])
    evwkvb_d = din("ev_w_kv_b", [2, 128, 1024])
    evqn_d = din("ev_q_norm", [2, 96])
    evkn_d = din("ev_k_norm", [2, 96])
    evbi_d = din("ev_b_i", [2, 2, 4])
    evbf_d = din("ev_b_f", [2, 2, 4])
    evhn_d = din("ev_h_norm", [2, 512])
    evwout_d = din("ev_w_out", [2, D, D])
    odwin_d = din("od_w_in", [2, D, 4096])
    odcw_d = din("od_conv_w", [2, 3, 512])
    odcb_d = din("od_conv_b", [2, 512])
    odqn_d = din("od_q_norm", [2, 64])
    odkn_d = din("od_k_norm", [2, 64])
    odwout_d = din("od_w_out", [2, D, D])
    rpbx_d = din("rpbx", [2, 8, 128, 1024])
    cmask_d = din("cmask", [2, 128, 1024])
    ropecs_d = din("rope_cs", [2, 1024, 32])
    lmask_d = din("lmask", [2, 128, 128])

    yp_d = dout("yp", [512, D])
    ys_d = dout("ys", [1024, D])
    ockv_d = dout("o_ckv", [2, 2, 256, 128])
    okpe_d = dout("o_kpe", [2, 2, 256, 32])
    oC_d = dout("o_C", [2, 2, 2, 4, 128, 128])
    on_d = dout("o_n", [2, 2, 2, 4, 128])
    om_d = dout("o_m", [2, 2, 2, 4])
    onak_d = dout("o_nak", [2, 2, 256, 8, 64])
    onav_d = dout("o_nav", [2, 2, 256, 8, 64])

    tap_list = []

    def sb(name, shape, dt):
        return es.enter_context(nc.sbuf_tensor(name, list(shape), dt))

    yT = sb("yT", [128, 8, NTOK], F32)
    hsT = sb("hsT", [128, 8, NTOK], BF16)
    mixT = sb("mixT", [128, 8, NTOK], BF16)
    wring = [sb("wring%d" % i, [128, WSLOT], BF16) for i in range(NWSLOT)]
    scb = sb("scb", [128, 8, 2], BF16)
    wmod = [sb("wmod%d" % i, [128, 8, 128], BF16) for i in range(2)]
    mrow = sb("mrow", [2, 128], F32)
    wqb = sb("wqb", [128, 2, 768], BF16)
    wkvb = sb("wkvb", [128, 1024], BF16)
    wgate = sb("wgate", [128, 8, 16], BF16)
    identf = sb("identf", [128, 128], F32)
    identb = sb("identb", [128, 128], BF16)
    onesb = sb("onesb", [128, 128], BF16)
    sel = sb("sel", [36, 4, 128], F32)
    lmaskb = sb("lmaskb", [128, 2, 128], BF16)
    ropeC = sb("ropeC", [128, 8, 32], F32)
    ropeS = sb("ropeS", [128, 8, 32], F32)
    cmaskb = sb("cmaskb", [128, 2, 1024], BF16)
    pvA = sb("pvA", [128, 128], F32)
    pvB = sb("pvB", [128, 64], F32)
    scond = sb("scond", [128, 16], F32)
    modT = sb("modT", [128, 4, 24, 2], F32)
    modA = sb("modA", [128, 4, 8, 2], F32)
    gq = sb("gq", [128, 96], F32)
    gk = sb("gk", [128, 96], F32)
    ghn = sb("ghn", [128, 512], F32)
    bif = sb("bif", [36, 2, 2], F32)
    m0t = sb("m0t", [36, 2], F32)
    convp = sb("convp", [128, 2, 4, 4], F32)
    arena_t = sb("arena", [128, ARENA_WORDS], F32)
    ps_t = es.enter_context(nc.psum_tensor("ps", [128, 4096], F32))

    A = Arena(S, arena_t, ARENA_WORDS)
    P = PSum(ps_t)
    PK = PSum.keys

    def TAP(name, ap, shape, reads, dt=F32):
        if name not in taps:
            return
        d = nc.dram_tensor("tap_" + name, list(shape), dt, kind="ExternalOutput").ap()
        S.dma('sp', DMA(d, ap), reads=reads, is_output=True)
        tap_list.append(name)

    def cols(t, n=1):
        return slice(128 * t, 128 * (t + n))

    S.op('pool', MS(identf[:], 1.0), writes=['identf'])
    S.op('pool', ASEL(identf[:], identf[:], [[-1, 128]], ALU.is_equal, 0.0, 0, 1),
         reads=['identf'], writes=['identf'])
    S.op('dve', CP(identb[:], identf[:]), reads=['identf'], writes=['identb'])
    S.op('pool', MS(onesb[:], 1.0), writes=['onesb'])
    S.op('pool', MS(sel[:], 1.0), writes=['sel'])
    for g0 in (0, 32):
        for h in range(4):
            S.op('pool', ASEL(sel[g0:g0 + 4, h, :], sel[g0:g0 + 4, h, :], [[0, 128]], ALU.is_equal, 0.0, -h, 1),
                 reads=['sel'], writes=['sel'])
    S.dma('pool', DMA(lmaskb[:], lmask_d.rearrange("d p q -> p d q")), writes=['lmaskb'])
    S.dma('pool', DMA(cmaskb[:], cmask_d.rearrange("d p q -> p d q")), writes=['cmaskb'])
    S.dma('sp', DMA(ropeC[:], ropecs_d[0].rearrange("(t p) e -> p t e", p=128)), writes=['ropeC'])
    S.dma('sp', DMA(ropeS[:], ropecs_d[1].rearrange("(t p) e -> p t e", p=128)), writes=['ropeS'])

    mk = A.mark()
    stgA = A.alloc("stgA", [128], F32)
    stgB = A.alloc("stgB", [128], F32, parts=64)
    S.dma('sp', DMA(stgA.ap[0:96, :], adab_d.rearrange("l (c p) -> (l c) p", p=128)), writes=[stgA.k(0)])
    S.dma('sp', DMA(stgA.ap[96:128, :], normw_d.rearrange("l (c p) -> (l c) p", p=128)), writes=[stgA.k(1)])
    S.dma('sp', DMA(stgB.ap[0:16, :], cond_d.rearrange("g (c p) -> (g c) p", p=128)), writes=[stgB.k(0)])
    S.dma('sp', DMA(stgB.ap[16:20, :], evqan_d.rearrange("l (c p) -> (l c) p", p=128)), writes=[stgB.k(1)])
    S.dma('sp', DMA(stgB.ap[20:22, :], evkvan_d), writes=[stgB.k(2)])
    S.dma('sp', DMA(stgB.ap[22:46, :], odcw_d.rearrange("l k (c p) -> (l k c) p", p=128)), writes=[stgB.k(3)])
    S.dma('sp', DMA(stgB.ap[46:54, :], odcb_d.rearrange("l (c p) -> (l c) p", p=128)), writes=[stgB.k(4)])
    b = P.get()
    S.op('pe', MM(P.ap(b)[:, 0:128], stgA.ap, identf[:]), reads=[stgA.key, 'identf'], writes=PK(b))
    S.op('dve', CP(pvA[:], P.ap(b)[:, 0:128]), reads=PK(b), writes=['pvA'])
    b = P.get()
    S.op('pe', MM(P.ap(b)[:, 0:54], stgB.ap[0:54, :], identf[0:54, 0:54]), reads=[stgB.key, 'identf'], writes=PK(b))
    S.op('dve', CP(pvB[:, 0:54], P.ap(b)[:, 0:54]), reads=PK(b), writes=['pvB'])
    A.release(mk)

    def pv_adab(l):
        return pvA[:, l * 24:(l + 1) * 24]

    def pv_normw(l):
        return pvA[:, 96 + l * 8:96 + (l + 1) * 8]

    def pv_qan(j, c2):
        return pvB[:, 16 + j * 2 + c2:16 + j * 2 + c2 + 1]

    def pv_kvan(j):
        return pvB[:, 20 + j:21 + j]

    mk = A.mark()
    tnh = A.alloc("tnh", [16], F32)
    S.op('act', ACTF(tnh.ap, pvB[:, 0:16], AF.Tanh, scale=0.5), reads=['pvB'], writes=[tnh.key])
    S.op('dve', STT(tnh.ap, tnh.ap, 1.0, pvB[:, 0:16], ALU.add, ALU.mult), reads=[tnh.key, 'pvB'], writes=[tnh.key])
    S.op('dve', TS(scond[:], tnh.ap, 0.5, ALU.mult), reads=[tnh.key], writes=['scond'])
    S.op('dve', CP(scb[:], scond[:].rearrange("p (g c) -> p c g", g=2)), reads=['scond'], writes=['scb'])
    A.release(mk)
    for j in range(2):
        wv = pvB[:, 22 + j * 12:22 + (j + 1) * 12].rearrange("p (k c) -> p c k", k=3)
        S.op('dve', TS(convp[:, j, :, 0:3], wv, 0.5, ALU.mult), reads=['pvB'], writes=[('convp', j)])
        bv = pvB[:, 46 + j * 4:46 + (j + 1) * 4]
        S.op('dve', TS(convp[:, j, :, 3], bv, 0.5, ALU.mult), reads=['pvB'], writes=[('convp', j)])
    for d_, g0 in ((0, 0), (1, 32)):
        S.dma('sp', DMA(bif[g0:g0 + 4, 0, :], evbi_d[:, d_, :].rearrange("j h -> h j"), allow_slow_non_contiguous=True),
              writes=['bif'])
        S.dma('sp', DMA(bif[g0:g0 + 4, 1, :], evbf_d[:, d_, :].rearrange("j h -> h j"), allow_slow_non_contiguous=True),
              writes=['bif'])
        S.dma('sp', DMA(m0t[g0:g0 + 4, :], stm_d[:, d_, :].rearrange("j h -> h j"), allow_slow_non_contiguous=True),
              writes=['m0t'])

    wstate = {'i': 0}
    pre_w = {}

    def load_w(pieces):
        i = wstate['i']
        wstate['i'] = (i + 1) % NWSLOT
        tot = sum(p.shape[1] for p in pieces)
        assert 8 * tot <= WSLOT, tot
        view = wring[i][:, 0:8 * tot].rearrange("p (k n) -> p k n", k=8)
        key = 'wring%d' % i
        off = 0
        for pi, pc in enumerate(pieces):
            n = pc.shape[1]
            S.dma('pool', DMA(view[:, :, off:off + n], pc.rearrange("(k p) n -> p k n", p=128)),
                  writes=[(key, pi)])
            off += n
        return view, key

    def rsqrt(dst, src, t1, t2, rk, wk):
        S.op('dve', TS(dst.bitcast(I32), src.bitcast(I32), -0.5, ALU.mult, float(0x5f3759df), ALU.add),
             reads=rk, writes=wk)
        for _ in range(2):
            S.op('dve', TT(t2, src, dst, ALU.mult), reads=rk + wk, writes=wk)
            S.op('dve', STT(t2, t2, -0.5, dst, ALU.mult, ALU.mult), reads=wk, writes=wk)
            S.op('dve', STT(dst, t2, 1.5, dst, ALU.add, ALU.mult), reads=wk, writes=wk)

    modq = []

    def queue_mod(l):
        def dma_piece(cg):
            S.dma('pool', DMA(wmod[cg % 2][:], adaw_d[l][:, cg * 128:(cg + 1) * 128].rearrange("(k p) n -> p k n", p=128)),
                  writes=['wmod%d' % (cg % 2)])

        def step(cg):
            def f():
                if cg == 0:
                    dma_piece(0)
                if cg + 1 < 24:
                    dma_piece(cg + 1)
                wv, wkey = wmod[cg % 2], 'wmod%d' % (cg % 2)
                for kc in range(8):
                    S.op('pe', MM(P.ap(7)[:, cg * 2:cg * 2 + 2], wv[:, kc, :], scb[:, kc, :], start=(kc == 0), stop=(kc == 7)),
                         reads=[wkey, 'scb'], writes=PK(7))
                if cg == 23:
                    pb = P.ap(7)
                    S.op('dve', TT(modT[:, l, :, :], pb[:, 0:48].rearrange("p (c g) -> p c g", g=2),
                                   pv_adab(l).unsqueeze(2).broadcast_to([128, 24, 2]), ALU.add),
                         reads=PK(7) + ['pvA'], writes=[('modT', l)])
                    S.op('dve', TS(modA[:, l, :, :], modT[:, l, 8:16, :], 1.0, ALU.add), reads=[('modT', l)], writes=[('modA', l)])
                    S.op('dve', TT(modA[:, l, :, :], modA[:, l, :, :], pv_normw(l).unsqueeze(2).broadcast_to([128, 8, 2]),
                                   ALU.mult), reads=[('modA', l), 'pvA'], writes=[('modA', l)])
            return f
        for cg in range(24):
            modq.append(step(cg))

    def pump(n=1):
        for _ in range(n):
            if modq:
                modq.pop(0)()

    def load_x():
        mk = A.mark()
        xin = [A.alloc("xin%d" % i, [1024], F32) for i in range(2)]
        for t in range(NT):
            xb = xin[t % 2]
            S.dma('sp', DMA(xb.ap, x_d[128 * t:128 * (t + 1), :]), writes=[xb.key])
            b = P.get(2)
            for c in range(8):
                S.op('pe', MM(P.ap(b, 2)[:, c * 128:(c + 1) * 128], xb.ap[:, c * 128:(c + 1) * 128], identf[:]),
                     reads=[xb.key, 'identf'], writes=PK(b, 2))
            eng = 'act' if t % 2 else 'dve'
            src = P.ap(b, 2).rearrange("p (c q) -> p c q", c=8)
            if eng == 'act':
                S.op('act', ACTF(yT[:, :, cols(t)], src, AF.Copy), reads=PK(b, 2), writes=[('yT', t)])
            else:
                S.op('dve', CP(yT[:, :, cols(t)], src), reads=PK(b, 2), writes=[('yT', t)])
            pump(2)
        A.release(mk)

    yo_bufs = []

    def store_tiles(tiles):
        if not yo_bufs:
            yo_bufs.extend([A.alloc("yo%d" % i, [1024], F32) for i in range(2)])
        for t in tiles:
            ob = yo_bufs[t % 2]
            b = P.get(2)
            for c in range(8):
                S.op('pe', MM(P.ap(b, 2)[:, c * 128:(c + 1) * 128], yT[:, c, cols(t)], identf[:]),
                     reads=[('yT', t), 'identf'], writes=PK(b, 2))
            if t % 2:
                S.op('act', ACTF(ob.ap, P.ap(b, 2), AF.Copy), reads=PK(b, 2), writes=[ob.key])
            else:
                S.op('dve', CP(ob.ap, P.ap(b, 2)), reads=PK(b, 2), writes=[ob.key])
            dst = yp_d[128 * t:128 * (t + 1), :] if t < 4 else ys_d[128 * (t - 4):128 * (t - 3), :]
            S.dma('sp', DMA(dst, ob.ap), reads=[ob.key], is_output=True)

    def store_y():
        store_tiles(range(NT))

    def phase_norm(l):
        mk = A.mark()
        sqs_ = [A.alloc("sq%d" % i, [8, 512], BF16) for i in range(3)]
        ms = A.alloc("ms", [3, 512], F32)
        rstd = A.alloc("rstd", [3, 512], F32)
        t1 = A.alloc("nt1", [3, 512], F32)
        t2 = A.alloc("nt2", [3, 512], F32)
        tmp = [A.alloc("ntmp%d" % i, [512], F32) for i in range(2)]
        for blk in range(3):
            cs = slice(blk * 512, (blk + 1) * 512)
            yk = [('yT', 4 * blk + i) for i in range(4)]
            sq = sqs_[blk]
            S.op('act', ACTF(sq.ap, yT[:, :, cs], AF.Square), reads=yk, writes=[sq.key])
            b = P.get()
            for c in range(8):
                S.op('pe', MM(P.ap(b), onesb[:], sq.ap[:, c, :], start=(c == 0), stop=(c == 7)),
                     reads=[sq.key, 'onesb'], writes=PK(b))
            S.op('dve', TS(ms.ap[:, blk, :], P.ap(b), 1.0 / D, ALU.mult, EPS, ALU.add), reads=PK(b), writes=[ms.k(blk)])
        rsqrt(rstd.ap, ms.ap, t1.ap, t2.ap, [ms.key], [rstd.key, t1.key, t2.key])
        for blk in range(3):
            g = 0 if blk == 0 else 1
            cs = slice(blk * 512, (blk + 1) * 512)
            yk = [('yT', 4 * blk + i) for i in range(4)]
            hk = [('hsT', 4 * blk + i) for i in range(4)]
            for c in range(8):
                tb = tmp[c % 2]
                S.op('dve', STT(tb.ap, yT[:, c, cs], modA[:, l, c, g:g + 1], rstd.ap[:, blk, :], ALU.mult, ALU.mult),
                     reads=yk + [('modA', l), rstd.key], writes=[tb.key])
                S.op('act', ACTF(hsT[:, c, cs], tb.ap, AF.Identity, bias=modT[:, l, c, g:g + 1], scale=1.0),
                     reads=[tb.key, ('modT', l)], writes=hk)
        A.release(mk)

    pre_out = [None]

    def preload_out(wout_d):
        pre_out[0] = load_w([wout_d[:, 0:512]])

    def phase_out(l, wout_d, store_blk=None):
        halves = [pre_out[0] if pre_out[0] is not None else load_w([wout_d[:, 0:512]]), None]
        pre_out[0] = None
        halves[1] = load_w([wout_d[:, 512:1024]])
        for blk in range(3):
            g = 0 if blk == 0 else 1
            cs = slice(blk * 512, (blk + 1) * 512)
            mk_ = [('mixT', 4 * blk + i) for i in range(4)]
            yk = [('yT', 4 * blk + i) for i in range(4)]
            for half in range(2):
                wv, wkey = halves[half]
                for dc in range(4):
                    c = half * 4 + dc
                    b = P.get()
                    for jc in range(8):
                        S.op('pe', MM(P.ap(b), wv[:, jc, dc * 128:(dc + 1) * 128], mixT[:, jc, cs],
                                      start=(jc == 0), stop=(jc == 7)),
                             reads=[wkey] + mk_, writes=PK(b))
                    S.op('dve', STT(yT[:, c, cs], P.ap(b), modT[:, l, 16 + c, g:g + 1], yT[:, c, cs], ALU.mult, ALU.add),
                         reads=PK(b) + [('modT', l)] + yk, writes=yk)
            if store_blk is not None:
                store_blk(blk)

    def attention(qT, kT, dk, vaug, att, NQ, blocks_for_q, bias_fn=None, head_hook=None):
        mk = A.mark()
        maxb = max(len(blocks_for_q(qi)) for qi in range(NQ))
        gsz = maxb if maxb <= 8 else (maxb + 1) // 2
        assert gsz <= 8
        npt = 3 if gsz <= 2 else 2
        PT = [A.alloc("PT%d" % i, [gsz * 128], BF16) for i in range(npt)]
        rd = A.alloc("rd", [4], F32)
        items = []
        for hh in range(4):
            for qi in range(NQ):
                bl = blocks_for_q(qi)
                grps = [bl[i:i + gsz] for i in range(0, len(bl), gsz)]
                for gi, g_ in enumerate(grps):
                    items.append((hh, qi, gi, len(grps), g_))
        state = {'pt': 0, 'ob': None, 'on': 0}
        P.limit = 5
        if P.next >= 5:
            P.next = 0

        def emit_S(it):
            hh, qi, gi, ng, g_ = it
            if head_hook is not None and qi == 0 and gi == 0:
                head_hook(hh)
            b = P.get(2)
            reg = P.ap(b, 2)
            for bi, kidx in enumerate(g_):
                bias = bias_fn(hh, qi, kidx) if bias_fn is not None else None
                S.op('pe', MM(reg[:, bi * 128:(bi + 1) * 128], kT.ap[0:dk, hh, kidx * 128:(kidx + 1) * 128],
                              qT.ap[0:dk, hh, qi * 128:(qi + 1) * 128], start=True, stop=(bias is None)),
                     reads=[kT.key, qT.key], writes=PK(b, 2))
                if bias is not None:
                    S.op('pe', MM(reg[:, bi * 128:(bi + 1) * 128], identb[:], bias[0], start=False, stop=True),
                         reads=['identb'] + bias[1], writes=PK(b, 2))
            pt = PT[state["pt"] % npt]
            state['pt'] += 1
            n = len(g_) * 128
            S.op('act', ACTF(pt.ap[:, 0:n], reg[:, 0:n], AF.Exp), reads=PK(b, 2), writes=[pt.key])
            return pt

        def emit_PV(it, pt):
            hh, qi, gi, ng, g_ = it
            if gi == 0 and qi % 4 == 0:
                state['ob'] = 5 + state['on'] % 2
                state['on'] += 1
            ob = state['ob']
            oq = qi % 4
            for bi, kidx in enumerate(g_):
                S.op('pe', MM(P.ap(ob)[:, oq * 66:oq * 66 + 65], pt.ap[:, bi * 128:(bi + 1) * 128],
                              vaug.ap[:, kidx, hh, 0:65], start=(gi == 0 and bi == 0),
                              stop=(gi == ng - 1 and bi == len(g_) - 1)),
                     reads=[pt.key, vaug.key], writes=PK(ob))
            if gi == ng - 1 and (oq == 3 or qi == NQ - 1):
                nq = oq + 1
                q0 = qi - oq
                ov = P.ap(ob)[:, 0:nq * 66].rearrange("p (q e) -> p q e", e=66)
                S.op('dve', RCP(rd.ap[:, 0:nq], ov[:, :, 64]), reads=PK(ob), writes=[rd.key])
                S.op('dve', TT(att.ap[:, q0:q0 + nq, hh, :], ov[:, :, 0:64],
                               rd.ap[:, 0:nq].unsqueeze(2).broadcast_to([128, nq, 64]), ALU.mult),
                     reads=PK(ob) + [rd.key], writes=[att.k(q) for q in range(q0, q0 + nq)])

        prev = None
        for it in items:
            pt = emit_S(it)
            if prev is not None:
                emit_PV(*prev)
            prev = (it, pt)
        emit_PV(*prev)
        P.limit = 7
        A.release(mk)

    def even_layer(l):
        j = l // 2
        W = evwin_d[j]
        S.dma('pool', DMA(wqb[:], evwqb_d[j].rearrange("(c p) n -> p c n", p=128)), writes=['wqb'])
        S.dma('pool', DMA(wkvb[:], evwkvb_d[j]), writes=['wkvb'])
        S.dma('pool', DMA(wgate[:], W[:, 2464:2480].rearrange("(k p) n -> p k n", p=128)), writes=['wgate'])
        S.dma('sp', DMA(gq[:, 0:96], evqn_d[j].partition_broadcast(128)), writes=['gq'])
        S.dma('sp', DMA(gk[:, 0:96], evkn_d[j].partition_broadcast(128)), writes=['gk'])
        S.dma('sp', DMA(ghn[:], evhn_d[j].partition_broadcast(128)), writes=['ghn'])
        S.op('dve', TS(gq[:, 0:96], gq[:, 0:96], 96 ** -0.5, ALU.mult), reads=['gq'], writes=['gq'])
        S.op('dve', TS(ghn[:], ghn[:], 0.25, ALU.mult), reads=['ghn'], writes=['ghn'])

        mk1 = A.mark()
        qanT = A.alloc("qanT", [2, NTOK], BF16)
        ckvT = A.alloc("ckvT", [NTOK + 256], BF16)
        kpeall = A.alloc("kpeall", [14, 32], F32)
        krg = A.alloc("krg", [14, 32], F32)
        sskpe = A.alloc("sskpe", [14], F32)
        mk2 = A.mark()
        wa, wakey = pre_w.pop('wa') if 'wa' in pre_w else load_w([W[:, 0:416]])
        wga_of = lambda hg_: load_w([W[:, 416 + hg_ * 256:416 + (hg_ + 1) * 256]])
        wga_next = wga_of(0)
        sq = A.alloc("sq1", [3, 512], BF16)
        ms = A.alloc("ms1", [2, 512], F32)
        rstd = A.alloc("rstd1", [2, 512], F32)
        n1 = A.alloc("n1", [2, 512], F32)
        n2 = A.alloc("n2", [2, 512], F32)
        ckvTf = A.alloc("ckvTf", [512], F32)
        for blk in range(3):
            cs = slice(blk * 512, (blk + 1) * 512)
            hk = [('hsT', 4 * blk + i) for i in range(4)]
            bs = []
            for ci in range(3):
                b = P.get()
                bs.append(b)
                for kc in range(8):
                    S.op('pe', MM(P.ap(b), wa[:, kc, ci * 128:(ci + 1) * 128], hsT[:, kc, cs],
                                  start=(kc == 0), stop=(kc == 7)), reads=[wakey] + hk, writes=PK(b))
                S.op('act', ACTF(sq.ap[:, ci, :], P.ap(b), AF.Square), reads=PK(b), writes=[sq.k(ci)])
            bq_ = P.get()
            for ci in range(2):
                S.op('pe', MM(P.ap(bq_), onesb[:], sq.ap[:, ci, :], start=(ci == 0), stop=(ci == 1)),
                     reads=[sq.k(ci), 'onesb'], writes=PK(bq_))
            bk_ = P.get()
            S.op('pe', MM(P.ap(bk_), onesb[:], sq.ap[:, 2, :]), reads=[sq.k(2), 'onesb'], writes=PK(bk_))
            S.op('dve', TS(ms.ap[:, 0, :], P.ap(bq_), 1.0 / 256, ALU.mult, EPS, ALU.add), reads=PK(bq_), writes=[ms.key])
            S.op('dve', TS(ms.ap[:, 1, :], P.ap(bk_), 1.0 / 128, ALU.mult, EPS, ALU.add), reads=PK(bk_), writes=[ms.key])
            rsqrt(rstd.ap, ms.ap, n1.ap, n2.ap, [ms.key], [rstd.key, n1.key, n2.key])
            for c2 in range(2):
                S.op('dve', STT(qanT.ap[:, c2, cs], P.ap(bs[c2]), pv_qan(j, c2), rstd.ap[:, 0, :], ALU.mult, ALU.mult),
                     reads=PK(bs[c2]) + ['pvB', rstd.key], writes=[qanT.k(4 * blk + i) for i in range(4)])
            S.op('dve', STT(ckvT.ap[:, cs], P.ap(bs[2]), pv_kvan(j), rstd.ap[:, 1, :], ALU.mult, ALU.mult),
                 reads=PK(bs[2]) + ['pvB', rstd.key], writes=[ckvT.k(4 * blk + i) for i in range(4)])
            if blk == 0:
                S.op('dve', STT(ckvTf.ap, P.ap(bs[2]), pv_kvan(j), rstd.ap[:, 1, :], ALU.mult, ALU.mult),
                     reads=PK(bs[2]) + ['pvB', rstd.key], writes=[ckvTf.key])
        b = P.get()
        for t in range(NT):
            for kc in range(8):
                S.op('pe', MM(P.ap(b)[:, t * 32:(t + 1) * 32], hsT[:, kc, cols(t)], wa[:, kc, 384:416],
                              start=(kc == 0), stop=(kc == 7)), reads=[wakey, ('hsT', t)], writes=PK(b))
        S.op('dve', CP(kpeall.ap[:, 0:12, :], P.ap(b)[:, 0:384].rearrange("p (t e) -> p t e", e=32)),
             reads=PK(b), writes=[kpeall.key])
        S.dma('sp', DMA(kpeall.ap[:, 12:14, :], kpec_d[j].rearrange("(t p) e -> p t e", p=128)), writes=[kpeall.key])
        for s_ in range(2):
            S.dma('sp', DMA(okpe_d[s_, j].rearrange("(t p) e -> p t e", p=128), kpeall.ap[:, 2 * s_:2 * s_ + 2, :]),
                  reads=[kpeall.key], is_output=True)
        ktmp = A.alloc("ktmp", [14, 32], F32)
        ktmp2 = A.alloc("ktmp2", [8, 32], F32)
        S.op('act', ACTF(ktmp.ap, kpeall.ap, AF.Square), reads=[kpeall.key], writes=[ktmp.key])
        S.op('dve', RED(sskpe.ap, ktmp.ap), reads=[ktmp.key], writes=[sskpe.key])
        S.op('dve', TT(krg.ap, kpeall.ap, gk[:, 64:96].unsqueeze(1).broadcast_to([128, 14, 32]), ALU.mult),
             reads=[kpeall.key, 'gk'], writes=[krg.key])
        S.op('dve', TT(ktmp.ap[:, 0:8, :], krg.ap[:, 4:12, :], ropeC[:], ALU.mult),
             reads=[krg.key, 'ropeC', ktmp.key], writes=[ktmp.key])
        for g in range(2):
            gs = slice(g * 16, (g + 1) * 16)
            S.op('dve', TT(ktmp2.ap[:, :, gs].rearrange("p t (a e) -> p t a e", a=2),
                           krg.ap[:, 4:12, gs].rearrange("p t (a e) -> p t a e", a=2)[:, :, ::-1, :],
                           ropeS[:, :, gs].rearrange("p t (a e) -> p t a e", a=2), ALU.mult),
                 reads=[krg.key, 'ropeS'], writes=[ktmp2.key])
        S.op('dve', TT(krg.ap[:, 4:12, :], ktmp.ap[:, 0:8, :], ktmp2.ap, ALU.add),
             reads=[ktmp.key, ktmp2.key], writes=[krg.key])
        ckvo = A.alloc("ckvo", [4, 128], F32)
        b = P.get()
        for t in range(4):
            S.op('pe', MM(P.ap(b)[:, t * 128:(t + 1) * 128], ckvTf.ap[:, cols(t)], identf[:]),
                 reads=[ckvTf.key, 'identf'], writes=PK(b))
        S.op('act', ACTF(ckvo.ap, P.ap(b).rearrange("p (t e) -> p t e", e=128), AF.Copy), reads=PK(b), writes=[ckvo.key])
        for s_ in range(2):
            S.dma('sp', DMA(ockv_d[s_, j].rearrange("(t p) e -> p t e", p=128), ckvo.ap[:, 2 * s_:2 * s_ + 2, :]),
                  reads=[ckvo.key], is_output=True)
        cc = A.alloc("cc", [2, 128], F32)
        S.dma('sp', DMA(cc.ap, ckvc_d[j].rearrange("(t p) e -> p t e", p=128)), writes=[cc.key])
        b = P.get()
        for t in range(2):
            S.op('pe', MM(P.ap(b)[:, t * 128:(t + 1) * 128], cc.ap[:, t, :], identf[:]),
                 reads=[cc.key, 'identf'], writes=PK(b))
        S.op('act', ACTF(ckvT.ap[:, NTOK:NTOK + 256], P.ap(b)[:, 0:256], AF.Copy), reads=PK(b),
             writes=[ckvT.k(12), ckvT.k(13)])
        A.release(mk2)

        if stop == 'mla_pre':
            A.release(mk1)
            return
        for hg in range(2):
            wga, wgakey = wga_next
            if hg == 0:
                wga_next = wga_of(1)
            for (t0, ntl, ctx) in SEQS:
                mk3 = A.mark()
                NQ = ntl
                ktiles = ([12, 13] if ctx else []) + list(range(t0, t0 + ntl))
                NK = len(ktiles)
                qT = A.alloc("qT", [4, NQ * 128], BF16)
                kT = A.alloc("kT", [4, NK * 128], BF16)
                vaug = A.alloc("vaug", [NK, 4, 66], BF16)
                att = A.alloc("att", [NQ, 4, 64], BF16)
                abuf = A.alloc("abuf", [NQ, 256], BF16)
                stage = [A.alloc("stage0", [8, 96], F32)] * 2
                qkn = [A.alloc("qkn%d" % i, [8, 96], BF16) for i in range(2)]
                sqs = A.alloc("sqs", [8, 96], BF16)
                ss = A.alloc("ss", [8], F32)
                ms8 = A.alloc("ms8", [8], F32)
                rs8 = A.alloc("rs8", [8], F32)
                na = A.alloc("na", [8], F32)
                nb_ = A.alloc("nb", [8], F32)
                r1 = A.alloc("r1", [4, 32], F32)
                r2 = A.alloc("r2", [4, 32], F32)
                tg = A.alloc("tg", [256], F32)
                mt = [A.alloc("mt%d" % i, [256], BF16) for i in range(2)]
                S.op('pool', MS(vaug.ap, 1.0), writes=[vaug.key])
                pendB = [None]
                for ki, t in enumerate(ktiles):
                    is_ctx = t >= 12
                    lo = 4 if is_ctx else 0
                    qi = ki - (2 if ctx else 0)
                    ccols = slice(NTOK + 128 * (t - 12), NTOK + 128 * (t - 11)) if is_ctx else cols(t)
                    st = stage[ki % 2]
                    qk = qkn[ki % 2]
                    bkv = P.get()
                    S.op('pe', MM(P.ap(bkv), ckvT.ap[:, ccols], wkvb[:, hg * 512:(hg + 1) * 512]),
                         reads=[ckvT.k(t), 'wkvb'], writes=PK(bkv))
                    kvv = P.ap(bkv).rearrange("p (h e) -> p h e", h=4)
                    if not is_ctx:
                        bq = P.get()
                        for c2 in range(2):
                            S.op('pe', MM(P.ap(bq)[:, 0:384], qanT.ap[:, c2, cols(t)], wqb[:, c2, hg * 384:(hg + 1) * 384],
                                          start=(c2 == 0), stop=(c2 == 1)), reads=[qanT.k(t), 'wqb'], writes=PK(bq))
                        qv = P.ap(bq)[:, 0:384].rearrange("p (h e) -> p h e", h=4)
                        bg = P.get()
                        for kc in range(8):
                            S.op('pe', MM(P.ap(bg)[:, 0:256], hsT[:, kc, cols(t)], wga[:, kc, :],
                                          start=(kc == 0), stop=(kc == 7)), reads=[('hsT', t), wgakey], writes=PK(bg))
                        S.op('act', ACTF(sqs.ap[:, 0:4, :], qv, AF.Square), reads=PK(bq), writes=[sqs.k(0)])
                    S.op('act', ACTF(sqs.ap[:, 4:8, 0:64], kvv[:, :, 0:64], AF.Square), reads=PK(bkv), writes=[sqs.k(1)])
                    if not is_ctx:
                        S.op('dve', RED(ss.ap[:, 0:4], sqs.ap[:, 0:4, :]), reads=[sqs.k(0)], writes=[ss.key])
                    S.op('dve', RED(ss.ap[:, 4:8], sqs.ap[:, 4:8, 0:64]), reads=[sqs.k(1)], writes=[ss.key])
                    S.op('dve', TS(ss.ap[:, 4:8], ss.ap[:, 4:8], sskpe.ap[:, t:t + 1], ALU.add),
                         reads=[ss.key, sskpe.key], writes=[ss.key])
                    S.op('dve', TS(ms8.ap[:, lo:8], ss.ap[:, lo:8], 1.0 / 96, ALU.mult, EPS, ALU.add),
                         reads=[ss.key], writes=[ms8.key])
                    if not is_ctx:
                        S.op('dve', TT(st.ap[:, 0:4, :], qv, gq[:, 0:96].unsqueeze(1).broadcast_to([128, 4, 96]), ALU.mult),
                             reads=PK(bq) + ['gq'], writes=[st.key])
                    S.op('dve', TT(st.ap[:, 4:8, 0:64], kvv[:, :, 0:64],
                                   gk[:, 0:64].unsqueeze(1).broadcast_to([128, 4, 64]), ALU.mult),
                         reads=PK(bkv) + ['gk'], writes=[st.key])
                    S.op('pool', CP(st.ap[:, 4:8, 64:96], krg.ap[:, t, :].unsqueeze(1).broadcast_to([128, 4, 32])),
                         reads=[krg.key], writes=[st.key])
                    S.op('act', ACTF(vaug.ap[:, ki, :, 0:64], kvv[:, :, 64:128], AF.Copy), reads=PK(bkv), writes=[vaug.key])
                    if ctx and not is_ctx:
                        tl = t - 4
                        S.op('dve', TT(r1.ap, st.ap[:, 0:4, 64:96], ropeC[:, tl, :].unsqueeze(1).broadcast_to([128, 4, 32]),
                                       ALU.mult), reads=[st.key, 'ropeC'], writes=[r1.key])
                        for g in range(2):
                            gs = slice(g * 16, (g + 1) * 16)
                            gs2 = slice(64 + g * 16, 64 + (g + 1) * 16)
                            S.op('dve', TT(r2.ap[:, :, gs].rearrange("p h (a e) -> p h a e", a=2),
                                           st.ap[:, 0:4, gs2].rearrange("p h (a e) -> p h a e", a=2)[:, :, ::-1, :],
                                           ropeS[:, tl, gs].rearrange("p (a e) -> p a e", a=2).unsqueeze(1)
                                           .broadcast_to([128, 4, 2, 8]), ALU.mult),
                                 reads=[st.key, 'ropeS'], writes=[r2.key])
                        S.op('dve', TT(st.ap[:, 0:4, 64:96], r1.ap, r2.ap, ALU.add), reads=[r1.key, r2.key], writes=[st.key])
                    rsqrt(rs8.ap[:, lo:8], ms8.ap[:, lo:8], na.ap[:, lo:8], nb_.ap[:, lo:8], [ms8.key],
                          [rs8.key, na.key, nb_.key])
                    S.op('dve', TT(qk.ap[:, lo:8, :], st.ap[:, lo:8, :],
                                   rs8.ap[:, lo:8].unsqueeze(2).broadcast_to([128, 8 - lo, 96]), ALU.mult),
                         reads=[st.key, rs8.key], writes=[qk.key])
                    if not is_ctx:
                        S.op('act', ACTF(tg.ap, P.ap(bg)[:, 0:256], AF.Tanh, scale=0.5), reads=PK(bg), writes=[tg.key])
                        S.op('dve', STT(abuf.ap[:, qi, :], tg.ap, 1.0, P.ap(bg)[:, 0:256], ALU.add, ALU.mult),
                             reads=[tg.key] + PK(bg), writes=[abuf.k(qi)])
                    def stageB(qk=qk, lo=lo, is_ctx=is_ctx, qi=qi, ki=ki):
                        bt = P.get(2)
                        for i in range(lo, 8):
                            S.op('pe', MM(P.ap(bt, 2)[0:96, i * 128:(i + 1) * 128], qk.ap[:, i, :], identb[:]),
                                 reads=[qk.key, 'identb'], writes=PK(bt, 2))
                        if not is_ctx:
                            S.op('act', ACTF(qT.ap[0:96, :, qi * 128:(qi + 1) * 128],
                                             P.ap(bt, 2)[0:96, 0:512].rearrange("p (h q) -> p h q", h=4), AF.Copy),
                                 reads=PK(bt, 2), writes=[qT.k(qi)])
                        S.op('act', ACTF(kT.ap[0:96, :, ki * 128:(ki + 1) * 128],
                                         P.ap(bt, 2)[0:96, 512:1024].rearrange("p (h q) -> p h q", h=4), AF.Copy),
                             reads=PK(bt, 2), writes=[kT.k(ki)])
                    if pendB[0] is not None:
                        pendB[0]()
                    pendB[0] = stageB
                    pump(1)
                pendB[0]()
                pendB[0] = None
                attention(qT, kT, 96, vaug, att, NQ, lambda qi_, NK=NK: list(range(NK)))
                for qi in range(NQ):
                    t = t0 + qi
                    m_ = mt[qi % 2]
                    S.op('dve', TT(m_.ap, abuf.ap[:, qi, :], att.ap[:, qi, :, :].rearrange("p h e -> p (h e)"), ALU.mult),
                         reads=[abuf.k(qi), att.k(qi)], writes=[m_.key])
                    bT = P.get()
                    for i in range(2):
                        S.op('pe', MM(P.ap(bT)[:, i * 128:(i + 1) * 128], m_.ap[:, i * 128:(i + 1) * 128], identb[:]),
                             reads=[m_.key, 'identb'], writes=PK(bT))
                    S.op('act', ACTF(mixT[:, hg * 2:hg * 2 + 2, cols(t)],
                                     P.ap(bT)[:, 0:256].rearrange("p (c q) -> p c q", c=2), AF.Copy, scale=0.5),
                         reads=PK(bT), writes=[('mixT', t)])
                A.release(mk3)
        A.release(mk1)
        if stop == 'mla':
            return
        mlstm_phase(l)

    def mlstm_phase(l):
        j = l // 2
        W = evwin_d[j]
        mkA = A.mark()
        negR = A.alloc("negR", [NTOK], F32, parts=36)
        wint = A.alloc("wint", [NTOK], F32, parts=36)
        tmq = A.alloc("tmq", [12, 2, 3, 4], F32)
        cst = A.alloc("cst01", [2], F32, parts=36)
        w1_of = lambda h_: load_w([W[:, 928 + h_ * 128:928 + (h_ + 1) * 128], W[:, 1440 + h_ * 128:1440 + (h_ + 1) * 128]])
        w1_next = w1_of(0)
        qk_pair = [(A.alloc("qTbP%d" % i, [1024], BF16), A.alloc("kTbP%d" % i, [1024], BF16)) for i in range(2)]

        fm_done = set()

        def proj_fm(it_, w1v, w1key):
            if it_ in fm_done:
                return
            fm_done.add(it_)
            t0_, ntl_, _c = SEQS[it_ % 3]
            L_ = ntl_ * 128
            a_ = t0_ * 128
            qb_, kb_ = qk_pair[it_ % 2]
            for bi in range((L_ + 511) // 512):
                n = min(512, L_ - bi * 512)
                cs = slice(a_ + bi * 512, a_ + bi * 512 + n)
                lc = slice(bi * 512, bi * 512 + n)
                hk = [('hsT', (a_ + bi * 512) // 128 + i) for i in range(n // 128)]
                bq = P.get()
                for kc in range(8):
                    S.op('pe', MM(P.ap(bq)[:, 0:n], w1v[:, kc, 0:128], hsT[:, kc, cs], start=(kc == 0), stop=(kc == 7)),
                         reads=[w1key] + hk, writes=PK(bq))
                bk = P.get()
                for kc in range(8):
                    S.op('pe', MM(P.ap(bk)[:, 0:n], w1v[:, kc, 128:256], hsT[:, kc, cs], start=(kc == 0), stop=(kc == 7)),
                         reads=[w1key] + hk, writes=PK(bk))
                S.op('act', ACTF(qb_.ap[:, lc], P.ap(bq)[:, 0:n], AF.Copy), reads=PK(bq), writes=[qb_.k(bi)])
                S.op('act', ACTF(kb_.ap[:, lc], P.ap(bk)[:, 0:n], AF.Copy, scale=128 ** -0.5), reads=PK(bk), writes=[kb_.k(bi)])

        mkB = A.mark()
        gtm = A.alloc("gtm", [12, 16], F32)
        T1 = A.alloc("T1", [NTOK], F32, parts=36)
        T2 = A.alloc("T2", [NTOK], F32, parts=36)
        T3 = A.alloc("T3", [NTOK], F32, parts=36)
        T4 = A.alloc("T4", [NTOK], F32, parts=36)
        T5 = A.alloc("T5", [NTOK], F32, parts=36)
        S.op('pool', MS(cst.ap[:, 0:1], 1.0), writes=[cst.key])
        S.op('pool', MS(cst.ap[:, 1:2], 0.0), writes=[cst.key])
        for Tb in (T1, T3):
            S.op('pool', MS(Tb.ap, 0.0), writes=[Tb.key])
        b = P.get()
        for t in range(NT):
            for kc in range(8):
                S.op('pe', MM(P.ap(b)[:, t * 16:(t + 1) * 16], hsT[:, kc, cols(t)], wgate[:, kc, :],
                              start=(kc == 0), stop=(kc == 7)), reads=[('hsT', t), 'wgate'], writes=PK(b))
        S.op('dve', CP(gtm.ap, P.ap(b)[:, 0:192].rearrange("p (t e) -> p t e", e=16)), reads=PK(b), writes=[gtm.key])
        for which, Tdst in ((0, T3), (1, T1)):
            b3 = P.get(3)
            for d_ in range(2):
                g0 = 32 * d_
                for t in range(NT):
                    S.op('pe', MM(P.ap(b3, 3)[g0:g0 + 4, t * 128:(t + 1) * 128],
                                  gtm.ap[:, t, which * 8 + d_ * 4:which * 8 + d_ * 4 + 4], identf[:]),
                         reads=[gtm.key, 'identf'], writes=PK(b3, 3))
            for d_ in range(2):
                g0 = 32 * d_
                S.op('act', ACTF(Tdst.ap[g0:g0 + 4, :], P.ap(b3, 3)[g0:g0 + 4, :], AF.Identity,
                                 bias=bif[g0:g0 + 4, which, j:j + 1], scale=1.0),
                     reads=PK(b3, 3) + ['bif'], writes=[Tdst.key])
        allk = [T1.key, T2.key, T3.key, T4.key, T5.key]
        S.op('dve', STT(T2.ap, T1.ap, -1.0, T1.ap, ALU.mult, ALU.max), reads=[T1.key], writes=[T2.key])
        S.op('act', ACTF(T2.ap, T2.ap, AF.Exp, scale=-1.0), reads=[T2.key], writes=[T2.key])
        S.op('act', ACTF(T2.ap, T2.ap, AF.Ln, bias=1.0, scale=1.0), reads=[T2.key], writes=[T2.key])
        S.op('dve', TS(T1.ap, T1.ap, 0.0, ALU.min), reads=[T1.key], writes=[T1.key])
        S.op('dve', TT(T1.ap, T1.ap, T2.ap, ALU.subtract), reads=[T1.key, T2.key], writes=[T1.key])
        segs = [(t0 * 128, ntl * 128, ctx) for (t0, ntl, ctx) in SEQS]

        def dirview(ap, d_, a, L):
            v = ap[32 * d_:32 * d_ + 4, a:a + L]
            return v[:, ::-1] if d_ else v

        for d_ in range(2):
            g0 = 32 * d_
            for (a, L, ctx) in segs:
                S.op('dve', SCAN(dirview(T2.ap, d_, a, L), cst.ap[g0:g0 + 4, 0:1].broadcast_to([4, L]),
                                 dirview(T1.ap, d_, a, L), 0.0, ALU.mult, ALU.add),
                     reads=[T1.key, cst.key], writes=[T2.key])
        S.op('dve', TT(T3.ap, T3.ap, T2.ap, ALU.subtract), reads=[T3.key, T2.key], writes=[T3.key])
        for d_ in range(2):
            g0 = 32 * d_
            for (a, L, ctx) in segs:
                init = m0t[g0:g0 + 4, j:j + 1] if ctx else 0.0
                S.op('dve', SCAN(dirview(T1.ap, d_, a, L), cst.ap[g0:g0 + 4, 1:2].broadcast_to([4, L]),
                                 dirview(T3.ap, d_, a, L), init, ALU.add, ALU.max),
                     reads=[T3.key, cst.key, 'm0t', T1.key], writes=[T1.key])
        S.op('dve', TT(T2.ap, T2.ap, T1.ap, ALU.add), reads=[T2.key, T1.key], writes=[T2.key])
        for s_ in range(2):
            a, L, _ = segs[s_]
            for d_ in range(2):
                g0 = 32 * d_
                idx = a + L - 1 if d_ == 0 else a
                S.dma('sp', DMA(om_d[s_, j, d_, :].rearrange("(h o) -> h o", o=1), T2.ap[g0:g0 + 4, idx:idx + 1]),
                      reads=[T2.key], is_output=True)
        for d_ in range(2):
            g0 = 32 * d_
            for (a, L, ctx) in segs:
                nck = L // 64
                first = slice(a, a + 64) if d_ == 0 else slice(a + L - 64, a + L)
                if ctx:
                    S.op('dve', CP(T4.ap[g0:g0 + 4, first], m0t[g0:g0 + 4, j:j + 1].broadcast_to([4, 64])),
                         reads=['m0t', T4.key], writes=[T4.key])
                else:
                    S.op('dve', MS(T4.ap[g0:g0 + 4, first], 0.0), reads=[T4.key], writes=[T4.key])
                if d_ == 0:
                    dst = T4.ap[g0:g0 + 4, a + 64:a + L].rearrange("p (k e) -> p k e", e=64)
                    src = T1.ap[g0:g0 + 4, a + 63:a + L - 1:64]
                    src2 = T1.ap[g0:g0 + 4, a + 63:a + L:64]
                else:
                    dst = T4.ap[g0:g0 + 4, a:a + L - 64].rearrange("p (k e) -> p k e", e=64)
                    src = T1.ap[g0:g0 + 4, a + 64:a + L:64]
                    src2 = T1.ap[g0:g0 + 4, a:a + L:64]
                S.op('dve', CP(dst, src.unsqueeze(2).broadcast_to([4, nck - 1, 64])), reads=[T1.key, T4.key], writes=[T4.key])
                S.op('dve', CP(T5.ap[g0:g0 + 4, a:a + L].rearrange("p (k e) -> p k e", e=64),
                               src2.unsqueeze(2).broadcast_to([4, nck, 64])), reads=[T1.key, T5.key], writes=[T5.key])
        for d_ in range(2):
            g0 = 32 * d_
            r = slice(g0, g0 + 4)
            S.op('dve', TT(T4.ap[r, :], T4.ap[r, :], T1.ap[r, :], ALU.subtract), reads=[T4.key, T1.key], writes=[T4.key])
            S.op('act', ACTF(wint.ap[r, :], T4.ap[r, :], AF.Exp), reads=[T4.key], writes=[wint.key])
            S.op('dve', TT(T5.ap[r, :], T3.ap[r, :], T5.ap[r, :], ALU.subtract), reads=[T3.key, T5.key], writes=[T5.key])
            S.op('act', ACTF(T5.ap[r, :], T5.ap[r, :], AF.Exp), reads=[T5.key], writes=[T5.key])
            S.op('act', ACTF(T2.ap[r, :], T2.ap[r, :], AF.Exp, scale=-1.0), reads=[T2.key], writes=[T2.key])
            S.op('dve', TS(negR.ap[r, :], T1.ap[r, :], -1.0, ALU.mult), reads=[T1.key], writes=[negR.key])
        if stop is None:
            proj_fm(0, w1_next[0], w1_next[1])
            proj_fm(1, w1_next[0], w1_next[1])
        b = P.get()
        for t in range(NT):
            for d_ in range(2):
                g0 = 32 * d_
                for q_, Tq in enumerate((T3, T5, T2)):
                    c0 = ((t * 2 + d_) * 3 + q_) * 4
                    S.op('pe', MM(P.ap(b)[:, c0:c0 + 4], Tq.ap[g0:g0 + 4, cols(t)], identf[g0:g0 + 4, g0:g0 + 4]),
                         reads=[Tq.key, 'identf'], writes=PK(b))
        S.op('dve', CP(tmq.ap, P.ap(b)[:, 0:288].rearrange("p (t d q h) -> p t d q h", t=12, d=2, q=3)),
             reads=PK(b), writes=[tmq.key])
        TAP('negR', negR.ap, [36, NTOK], [negR.key])
        TAP('wint', wint.ap, [36, NTOK], [wint.key])
        TAP('tmq', tmq.ap, [128, 12, 2, 3, 4], [tmq.key])
        A.release(mkB)
        if stop == 'rows':
            A.release(mkA)
            return

        ksc = 128 ** -0.5
        for h in range(4):
            w1, w1k = w1_next
            if h == 0:
                proj_fm(0, w1, w1k)
            w2, w2k = load_w([W[:, 1952 + h * 128:1952 + (h + 1) * 128], W[:, 2480 + h * 128:2480 + (h + 1) * 128],
                              W[:, 2992 + h * 128:2992 + (h + 1) * 128]])
            if h < 3:
                w1_next = w1_of(h + 1)
            for si, (t0, ntl, ctx) in enumerate(SEQS):
                if h == 3 and si == 2 and stop is None:
                    preload_out(evwout_d[j])
                mkC = A.mark()
                L = ntl * 128
                a = t0 * 128
                nck = L // 64
                it = h * 3 + si
                qTb, kTb = qk_pair[it % 2]
                ktm = A.alloc("ktm", [ntl, 128], BF16)
                va2 = A.alloc("va2", [ntl, 130], BF16)
                og = A.alloc("og", [ntl, 128], BF16)
                hacc = A.alloc("hacc", [ntl, 128], F32)
                qTw = A.alloc("qTw", [2, L], BF16)
                Cb = A.alloc("Cb", [2, 130], BF16)
                kw = A.alloc("kw", [2, ntl, 128], BF16)
                Cst = A.alloc("Cst", [2, 130], F32)
                wsb = A.alloc("wsb", [2, nck], F32)
                Dt = [A.alloc("Dt%d" % i, [128], BF16) for i in range(2)]
                scT = [A.alloc("scT%d" % i, [128], BF16) for i in range(4)]
                tog = A.alloc("tog", [256], F32)
                ag = A.alloc("ag", [128], F32)
                den = [A.alloc("den%d" % i, [4], F32) for i in range(2)]
                junk = A.alloc("junk", [128], F32)
                ssh = A.alloc("ssh", [ntl], F32)
                msh = A.alloc("msh", [ntl], F32)
                rsh = A.alloc("rsh", [ntl], F32)
                nh1 = A.alloc("nh1", [ntl], F32)
                nh2 = A.alloc("nh2", [ntl], F32)
                htmp = [A.alloc("htmp%d" % i, [128], F32) for i in range(2)]
                hmt = [A.alloc("hmt%d" % i, [128], BF16) for i in range(2)]
                S.op('pool', MS(va2.ap, 1.0), writes=[va2.key])
                for tt in range(ntl):
                    t = t0 + tt
                    bv = P.get()
                    for kc in range(8):
                        S.op('pe', MM(P.ap(bv)[:, 0:384], hsT[:, kc, cols(t)], w2[:, kc, :], start=(kc == 0), stop=(kc == 7)),
                             reads=[w2k, ('hsT', t)], writes=PK(bv))
                    bk2 = P.get()
                    for kc in range(8):
                        S.op('pe', MM(P.ap(bk2)[:, 0:128], hsT[:, kc, cols(t)], w1[:, kc, 128:256],
                                      start=(kc == 0), stop=(kc == 7)), reads=[w1k, ('hsT', t)], writes=PK(bk2))
                    S.op('act', ACTF(ktm.ap[:, tt, :], P.ap(bk2)[:, 0:128], AF.Copy, scale=ksc), reads=PK(bk2), writes=[ktm.k(tt)])
                    S.op('act', ACTF(va2.ap[:, tt, 0:128], P.ap(bv)[:, 0:128], AF.Copy), reads=PK(bv), writes=[va2.k(tt)])
                    S.op('act', ACTF(tog.ap, P.ap(bv)[:, 128:384], AF.Tanh, scale=0.5), reads=PK(bv), writes=[tog.key])
                    S.op('dve', STT(ag.ap, tog.ap[:, 128:256], 1.0, P.ap(bv)[:, 256:384], ALU.add, ALU.mult),
                         reads=[tog.key] + PK(bv), writes=[ag.key])
                    S.op('dve', STT(og.ap[:, tt, :], tog.ap[:, 0:128], 1.0, ag.ap, ALU.add, ALU.mult),
                         reads=[tog.key, ag.key], writes=[og.k(tt)])
                if stop == 'mproj':
                    return
                if ctx:
                    for d_ in range(2):
                        S.dma('sp', DMA(Cst.ap[:, d_, 0:128], stC_d[j, d_, h]), writes=[Cst.k(d_)])
                        S.dma('sp', DMA(Cst.ap[:, d_, 128:129], stn_d[j, d_, h].rearrange("(p o) -> p o", o=1)),
                              writes=[Cst.k(d_)])
                else:
                    S.op('pool', MS(Cst.ap, 0.0), writes=[Cst.key])
                for d_ in range(2):
                    S.op('act', ACTF(Cb.ap[:, d_, 0:129], Cst.ap[:, d_, 0:129], AF.Copy), reads=[Cst.k(d_), Cst.key],
                         writes=[Cb.k(d_)])
                if stop == 'mSdma' and ctx:
                    return
                bw = P.get()
                wsr = A.alloc("wsr", [nck], F32, parts=36)
                for d_ in range(2):
                    g0 = 32 * d_
                    samp = wint.ap[g0:g0 + 4, a + 63:a + L:64] if d_ == 0 else wint.ap[g0:g0 + 4, a:a + L:64]
                    S.op('dve', CP(wsr.ap[g0:g0 + 4, :], samp), reads=[wint.key, wsr.key], writes=[wsr.key])
                    S.op('pe', MM(P.ap(bw)[:, d_ * nck:(d_ + 1) * nck], sel[g0:g0 + 4, h, :], wsr.ap[g0:g0 + 4, :]),
                         reads=[wsr.key, 'sel'], writes=PK(bw))
                S.op('dve', CP(wsb.ap, P.ap(bw)[:, 0:2 * nck].rearrange("p (d k) -> p d k", d=2)), reads=PK(bw), writes=[wsb.key])
                for d_ in range(2):
                    g0 = 32 * d_
                    for bi in range(L // 256):
                        n = 256
                        cs = slice(a + bi * 256, a + bi * 256 + n)
                        lc = slice(bi * 256, bi * 256 + n)
                        bb = P.get()
                        S.op('pe', MM(P.ap(bb)[:, 0:n], sel[g0:g0 + 4, h, :], wint.ap[g0:g0 + 4, cs]),
                             reads=[wint.key, 'sel'], writes=PK(bb))
                        S.op('dve', TT(qTw.ap[:, d_, lc], P.ap(bb)[:, 0:n], qTb.ap[:, lc], ALU.mult),
                             reads=PK(bb) + [qTb.key], writes=[qTw.k(d_)])
                    for tt in range(ntl):
                        S.op('act', ACTF(kw.ap[:, d_, tt, :], ktm.ap[:, tt, :], AF.Copy,
                                         scale=tmq.ap[:, t0 + tt, d_, 1, h:h + 1]),
                             reads=[ktm.k(tt), tmq.key], writes=[kw.k((d_, tt))])
                if stop == 'mprep':
                    return
                cnt = {'dt': 0, 'sc': 0, 'den': 0}
                done = set()

                def intra(tt, d_):
                    t = t0 + tt
                    g0 = 32 * d_
                    lc = slice(tt * 128, (tt + 1) * 128)
                    bS = P.get()
                    S.op('pe', MM(P.ap(bS)[:, 0:128], kTb.ap[:, lc], qTb.ap[:, lc]), reads=[kTb.key, qTb.key], writes=PK(bS))
                    S.op('pe', MM(P.ap(bS)[:, 128:256], sel[g0:g0 + 4, h, :], negR.ap[g0:g0 + 4, cols(t)], start=True, stop=False),
                         reads=[negR.key, 'sel'], writes=PK(bS))
                    S.op('pe', MM(P.ap(bS)[:, 128:256], identb[:], lmaskb[:, d_, :], start=False, stop=True),
                         reads=['identb', 'lmaskb'], writes=PK(bS))
                    dt_ = Dt[cnt['dt'] % 2]
                    cnt['dt'] += 1
                    sc_ = scT[cnt['sc'] % 4]
                    cnt['sc'] += 1
                    S.op('act', ACTF(dt_.ap, P.ap(bS)[:, 128:256], AF.Exp, bias=tmq.ap[:, t, d_, 0, h:h + 1], scale=1.0),
                         reads=PK(bS) + [tmq.key], writes=[dt_.key])
                    S.op('dve', TT(sc_.ap, P.ap(bS)[:, 0:128], dt_.ap, ALU.mult), reads=PK(bS) + [dt_.key], writes=[sc_.key])
                    return sc_

                def num_open(tt, d_, sc_):
                    bN = P.get()
                    S.op('pe', MM(P.ap(bN)[:, 0:129], sc_.ap, va2.ap[:, tt, 0:129], start=True, stop=False),
                         reads=[sc_.key, va2.k(tt)], writes=PK(bN))
                    return bN

                def inter(tt, d_, hf, bN, last):
                    S.op('pe', MM(P.ap(bN)[hf * 64:(hf + 1) * 64, 0:129], qTw.ap[:, d_, tt * 128 + hf * 64:tt * 128 + (hf + 1) * 64],
                                  Cb.ap[:, d_, 0:129], start=False, stop=last),
                         reads=[qTw.k(d_), Cb.k(d_)], writes=PK(bN))

                def update(tt, d_, hf):
                    cidx = tt * 2 + hf
                    bU = P.get()
                    S.op('pe', MM(P.ap(bU)[:, 0:129], kw.ap[hf * 64:(hf + 1) * 64, d_, tt, :],
                                  va2.ap[hf * 64:(hf + 1) * 64, tt, 0:129]),
                         reads=[kw.k((d_, tt)), va2.k(tt)], writes=PK(bU))
                    S.op('dve', STT(Cb.ap[:, d_, 0:129], Cst.ap[:, d_, 0:129], wsb.ap[:, d_, cidx:cidx + 1],
                                    P.ap(bU)[:, 0:129], ALU.mult, ALU.add),
                         reads=[Cst.k(d_), wsb.key] + PK(bU), writes=[Cb.k(d_)])
                    S.op('dve', STT(Cst.ap[:, d_, 0:129], Cst.ap[:, d_, 0:129], wsb.ap[:, d_, cidx:cidx + 1],
                                    P.ap(bU)[:, 0:129], ALU.mult, ALU.add),
                         reads=[Cst.k(d_), wsb.key] + PK(bU), writes=[Cst.k(d_)])

                def epilogue(tt, d_, bN):
                    t = t0 + tt
                    dn = den[cnt['den'] % 2]
                    cnt['den'] += 1
                    qn = P.ap(bN)[:, 128:129]
                    S.op('dve', TS(dn.ap[:, 0:1], qn, tmq.ap[:, t, d_, 2, h:h + 1], ALU.max), reads=PK(bN) + [tmq.key], writes=[dn.key])
                    S.op('dve', STT(dn.ap[:, 1:2], qn, -1.0, dn.ap[:, 0:1], ALU.mult, ALU.max), reads=PK(bN) + [dn.key], writes=[dn.key])
                    S.op('dve', RCP(dn.ap[:, 2:3], dn.ap[:, 1:2]), reads=[dn.key], writes=[dn.key])
                    if tt not in done:
                        done.add(tt)
                        S.op('dve', TS(hacc.ap[:, tt, :], P.ap(bN)[:, 0:128], dn.ap[:, 2:3], ALU.mult),
                             reads=PK(bN) + [dn.key], writes=[hacc.k(tt)])
                    else:
                        S.op('dve', STT(hacc.ap[:, tt, :], P.ap(bN)[:, 0:128], dn.ap[:, 2:3], hacc.ap[:, tt, :], ALU.mult, ALU.add),
                             reads=PK(bN) + [dn.key, hacc.k(tt)], writes=[hacc.k(tt)])
                        S.op('act', ACTF(junk.ap, hacc.ap[:, tt, :], AF.Square, accum=ssh.ap[:, tt:tt + 1]),
                             reads=[hacc.k(tt)], writes=[junk.key, ssh.k(tt)])

                if stop == 'mSprep' and ctx:
                    return
                for step in range(ntl):
                    tf, tb = step, ntl - 1 - step
                    scf = intra(tf, 0)
                    scb = intra(tb, 1)
                    bNf = num_open(tf, 0, scf)
                    inter(tf, 0, 0, bNf, False)
                    update(tf, 0, 0)
                    bNb = num_open(tb, 1, scb)
                    inter(tb, 1, 1, bNb, False)
                    update(tb, 1, 1)
                    inter(tf, 0, 1, bNf, True)
                    update(tf, 0, 1)
                    epilogue(tf, 0, bNf)
                    inter(tb, 1, 0, bNb, True)
                    update(tb, 1, 0)
                    epilogue(tb, 1, bNb)
                if h == 0 and si == 0:
                    TAP('hacc', hacc.ap, [128, ntl, 128], [hacc.key])
                    TAP('qTb', qTb.ap, [128, L], [qTb.key], BF16)
                    TAP('kTb', kTb.ap, [128, L], [kTb.key], BF16)
                    TAP('qTw', qTw.ap, [128, 2, L], [qTw.key], BF16)
                    TAP('og', og.ap, [128, ntl, 128], [og.key], BF16)
                if stop == 'mloop':
                    return
                if not ctx:
                    for d_ in range(2):
                        S.dma('sp', DMA(oC_d[si, j, d_, h], Cst.ap[:, d_, 0:128]), reads=[Cst.k(d_)], is_output=True)
                        S.dma('sp', DMA(on_d[si, j, d_, h, :].rearrange("(p o) -> p o", o=1), Cst.ap[:, d_, 128:129]),
                              reads=[Cst.k(d_)], is_output=True)
                if stop == 'mout':
                    return
                if si < 2:
                    proj_fm(it + 1, w1, w1k)
                elif h < 3:
                    proj_fm(it + 1, w1_next[0], w1_next[1])
                S.op('dve', TS(msh.ap, ssh.ap, 1.0 / 128, ALU.mult, EPS, ALU.add), reads=[ssh.key], writes=[msh.key])
                rsqrt(rsh.ap, msh.ap, nh1.ap, nh2.ap, [msh.key], [rsh.key, nh1.key, nh2.key])
                bT = None
                for tt in range(ntl):
                    t = t0 + tt
                    ht = htmp[tt % 2]
                    hm = hmt[tt % 2]
                    S.op('dve', STT(ht.ap, hacc.ap[:, tt, :], rsh.ap[:, tt:tt + 1], ghn[:, h * 128:(h + 1) * 128], ALU.mult, ALU.mult),
                         reads=[hacc.k(tt), rsh.key, 'ghn'], writes=[ht.key])
                    S.op('dve', TT(hm.ap, ht.ap, og.ap[:, tt, :], ALU.mult), reads=[ht.key, og.k(tt)], writes=[hm.key])
                    if tt % 4 == 0:
                        bT = P.get()
                    S.op('pe', MM(P.ap(bT)[:, (tt % 4) * 128:(tt % 4 + 1) * 128], hm.ap, identb[:]),
                         reads=[hm.key, 'identb'], writes=PK(bT))
                    if tt % 4 == 3 or tt == ntl - 1:
                        n = tt % 4 + 1
                        tfirst = t - (n - 1)
                        S.op('act', ACTF(mixT[:, 4 + h, cols(tfirst, n)], P.ap(bT)[:, 0:n * 128], AF.Copy),
                             reads=PK(bT), writes=[('mixT', tfirst + i) for i in range(n)])
                A.release(mkC)
                if stop is not None and stop.startswith('mstep') and (h * 3 + si + 1) >= int(stop[5:]):
                    return
        A.release(mkA)

    def na_blocks(qi):
        if qi in (0, 1):
            js = [0, 1, 2, 3]
        elif qi in (6, 7):
            js = [4, 5, 6, 7]
        else:
            js = list(range(qi - 2, qi + 3))
        return [0, 1] + [2 + jt for jt in js]

    def odd_layer(l):
        j = l // 2
        W = odwin_d[j]
        S.dma('sp', DMA(gq[:, 0:64], odqn_d[j].partition_broadcast(128)), writes=['gq'])
        S.dma('sp', DMA(gk[:, 0:64], odkn_d[j].partition_broadcast(128)), writes=['gk'])
        S.op('dve', TS(gq[:, 0:64], gq[:, 0:64], 0.125, ALU.mult), reads=['gq'], writes=['gq'])
        segs = [(t0 * 128, ntl * 128) for (t0, ntl, ctx) in SEQS]
        conv_w = lambda c_: load_w([W[:, i * 512 + c_ * 128:i * 512 + (c_ + 1) * 128] for i in range(4)])
        nxt = pre_w.pop('c0') if 'c0' in pre_w else conv_w(0)
        wqk_of = lambda hg_: load_w([W[:, 2048 + hg_ * 256:2048 + (hg_ + 1) * 256], W[:, 2560 + hg_ * 256:2560 + (hg_ + 1) * 256]])
        wvg_of = lambda hg_: load_w([W[:, 3072 + hg_ * 256:3072 + (hg_ + 1) * 256], W[:, 3584 + hg_ * 256:3584 + (hg_ + 1) * 256]])
        na_pre = {}
        for c4 in range(4):
            wv, wk_ = nxt
            if c4 < 3:
                nxt = conv_w(c4 + 1)
            else:
                na_pre['qk'] = wqk_of(0)
            mk = A.mark()
            u = A.alloc("u", [NTOK], F32)
            ab = A.alloc("ab", [NTOK], F32)
            y = A.alloc("y", [NTOK], F32)
            xs = A.alloc("xs", [512], F32)
            tgc = A.alloc("tgc", [512], F32)
            aa = A.alloc("aa", [512], F32)
            for blk in range(3):
                cs = slice(blk * 512, (blk + 1) * 512)
                hk = [('hsT', 4 * blk + i) for i in range(4)]
                bs = []
                for i in range(4):
                    b = P.get()
                    bs.append(b)
                    for kc in range(8):
                        S.op('pe', MM(P.ap(b), wv[:, kc, i * 128:(i + 1) * 128], hsT[:, kc, cs], start=(kc == 0), stop=(kc == 7)),
                             reads=[wk_] + hk, writes=PK(b))
                bx, bb, bc_, bg = bs
                S.op('act', ACTF(xs.ap, P.ap(bx), AF.Copy), reads=PK(bx), writes=[xs.key])
                S.op('dve', TT(u.ap[:, cs], P.ap(bc_), xs.ap, ALU.mult), reads=PK(bc_) + [xs.key], writes=[u.k(blk)])
                S.op('act', ACTF(tgc.ap, P.ap(bg), AF.Tanh, scale=0.5), reads=PK(bg), writes=[tgc.key])
                S.op('dve', STT(aa.ap, tgc.ap, 1.0, P.ap(bg), ALU.add, ALU.mult), reads=[tgc.key] + PK(bg), writes=[aa.key])
                S.op('dve', TT(ab.ap[:, cs], P.ap(bb), aa.ap, ALU.mult), reads=PK(bb) + [aa.key], writes=[ab.k(blk)])
                pump(1)
            S.op('dve', TS(y.ap, u.ap, convp[:, j, c4, 1:2], ALU.mult, convp[:, j, c4, 3:4], ALU.add),
                 reads=[u.key, ('convp', j)], writes=[y.key])
            for (a, L) in segs:
                S.op('dve', STT(y.ap[:, a + 1:a + L], u.ap[:, a:a + L - 1], convp[:, j, c4, 0:1], y.ap[:, a + 1:a + L], ALU.mult, ALU.add),
                     reads=[u.key, y.key, ('convp', j)], writes=[y.key])
                S.op('dve', STT(y.ap[:, a:a + L - 1], u.ap[:, a + 1:a + L], convp[:, j, c4, 2:3], y.ap[:, a:a + L - 1], ALU.mult, ALU.add),
                     reads=[u.key, y.key, ('convp', j)], writes=[y.key])
            S.op('dve', TT(mixT[:, c4, :], y.ap, ab.ap, ALU.mult), reads=[y.key, ab.key],
                 writes=[('mixT', t) for t in range(NT)])
            A.release(mk)
        if stop == 'conv':
            return
        for hg in range(2):
            wqk, wqkk = na_pre.pop('qk')
            wvg, wvgk = wvg_of(hg)
            if hg == 0:
                na_pre['qk'] = wqk_of(1)
            for si, (t0, ntl, ctx) in enumerate(SEQS):
                if hg == 1 and si == 2 and stop is None:
                    preload_out(odwout_d[j])
                mk3 = A.mark()
                NQ = ntl
                ktiles = ([12, 13] if ctx else []) + list(range(t0, t0 + ntl))
                NK = len(ktiles)
                qT = A.alloc("nqT", [4, NQ * 128], BF16)
                kT = A.alloc("nkT", [4, NK * 128], BF16)
                vaug = A.alloc("nvaug", [NK, 4, 66], BF16)
                att = A.alloc("natt", [NQ, 4, 64], BF16)
                abuf = A.alloc("nabuf", [NQ, 256], BF16)
                stage = [A.alloc("nstage%d" % i, [8, 64], F32) for i in range(2)]
                qkn = [A.alloc("nqkn%d" % i, [8, 64], BF16) for i in range(2)]
                sqs = A.alloc("nsqs", [8, 64], BF16)
                ss = A.alloc("nss", [8], F32)
                ms8 = A.alloc("nms8", [8], F32)
                rs8 = A.alloc("nrs8", [8], F32)
                na = A.alloc("nna", [8], F32)
                nb_ = A.alloc("nnb", [8], F32)
                tg = A.alloc("ntg", [256], F32)
                mt = [A.alloc("nmt%d" % i, [256], BF16) for i in range(2)]
                kvf = [A.alloc("kvf%d" % i, [2, 4, 64], F32) for i in range(2)]
                kvout = A.alloc("kvout", [2, 2, 4, 64], F32) if not ctx else None
                S.op('pool', MS(vaug.ap, 1.0), writes=[vaug.key])
                pendB = [None]
                for ki, t in enumerate(ktiles):
                    is_ctx = t >= 12
                    qi = ki - (2 if ctx else 0)
                    st = stage[ki % 2]
                    qk = qkn[ki % 2]
                    kf = kvf[ki % 2]
                    if is_ctx:
                        r0 = (t - 12) * 128
                        S.dma('sp', DMA(kf.ap[:, 0, :, :], nakc_d[j, r0:r0 + 128, hg * 4:(hg + 1) * 4, :]), writes=[kf.k(0)])
                        S.dma('sp', DMA(kf.ap[:, 1, :, :], navc_d[j, r0:r0 + 128, hg * 4:(hg + 1) * 4, :]), writes=[kf.k(1)])
                        S.op('dve', CP(qk.ap[:, 4:8, :], kf.ap[:, 0, :, :]), reads=[kf.k(0)], writes=[qk.key])
                        S.op('act', ACTF(vaug.ap[:, ki, :, 0:64], kf.ap[:, 1, :, :], AF.Copy), reads=[kf.k(1)], writes=[vaug.key])
                        lo = 4
                    else:
                        lo = 0
                        bqk = P.get()
                        for kc in range(8):
                            S.op('pe', MM(P.ap(bqk), hsT[:, kc, cols(t)], wqk[:, kc, :], start=(kc == 0), stop=(kc == 7)),
                                 reads=[('hsT', t), wqkk], writes=PK(bqk))
                        bvg = P.get()
                        for kc in range(8):
                            S.op('pe', MM(P.ap(bvg), hsT[:, kc, cols(t)], wvg[:, kc, :], start=(kc == 0), stop=(kc == 7)),
                                 reads=[('hsT', t), wvgk], writes=PK(bvg))
                        qkv = P.ap(bqk).rearrange("p (h e) -> p h e", e=64)
                        vv = P.ap(bvg)[:, 0:256].rearrange("p (h e) -> p h e", e=64)
                        S.op('act', ACTF(sqs.ap, qkv, AF.Square), reads=PK(bqk), writes=[sqs.key])
                        S.op('dve', RED(ss.ap, sqs.ap), reads=[sqs.key], writes=[ss.key])
                        S.op('dve', TS(ms8.ap, ss.ap, 1.0 / 64, ALU.mult, EPS, ALU.add), reads=[ss.key], writes=[ms8.key])
                        S.op('dve', TT(st.ap[:, 0:4, :], qkv[:, 0:4, :], gq[:, 0:64].unsqueeze(1).broadcast_to([128, 4, 64]), ALU.mult),
                             reads=PK(bqk) + ['gq'], writes=[st.key])
                        S.op('dve', TT(st.ap[:, 4:8, :], qkv[:, 4:8, :], gk[:, 0:64].unsqueeze(1).broadcast_to([128, 4, 64]), ALU.mult),
                             reads=PK(bqk) + ['gk'], writes=[st.key])
                        S.op('act', ACTF(vaug.ap[:, ki, :, 0:64], vv, AF.Copy), reads=PK(bvg), writes=[vaug.key])
                        if not ctx:
                            S.op('act', ACTF(kvout.ap[:, 1, qi, :, :], vv, AF.Copy), reads=PK(bvg), writes=[kvout.k((1, qi))])
                        rsqrt(rs8.ap, ms8.ap, na.ap, nb_.ap, [ms8.key], [rs8.key, na.key, nb_.key])
                        S.op('dve', TT(qk.ap, st.ap, rs8.ap.unsqueeze(2).broadcast_to([128, 8, 64]), ALU.mult),
                             reads=[st.key, rs8.key], writes=[qk.key])
                        S.op('act', ACTF(tg.ap, P.ap(bvg)[:, 256:512], AF.Tanh, scale=0.5), reads=PK(bvg), writes=[tg.key])
                        S.op('dve', STT(abuf.ap[:, qi, :], tg.ap, 1.0, P.ap(bvg)[:, 256:512], ALU.add, ALU.mult),
                             reads=[tg.key] + PK(bvg), writes=[abuf.k(qi)])
                        if not ctx:
                            for h_ in range(4):
                                S.op('act', ACTF(kvout.ap[:, 0, qi, h_, :], st.ap[:, 4 + h_, :], AF.Copy,
                                                 scale=rs8.ap[:, 4 + h_:5 + h_]),
                                     reads=[st.key, rs8.key], writes=[kvout.k((0, qi))])
                    def stageB(qk=qk, lo=lo, is_ctx=is_ctx, qi=qi, ki=ki):
                        bt = P.get(2)
                        for i in range(lo, 8):
                            S.op('pe', MM(P.ap(bt, 2)[0:64, i * 128:(i + 1) * 128], qk.ap[:, i, :], identb[:]),
                                 reads=[qk.key, 'identb'], writes=PK(bt, 2))
                        if not is_ctx:
                            S.op('act', ACTF(qT.ap[0:64, :, qi * 128:(qi + 1) * 128],
                                             P.ap(bt, 2)[0:64, 0:512].rearrange("p (h q) -> p h q", h=4), AF.Copy),
                                 reads=PK(bt, 2), writes=[qT.k(qi)])
                        S.op('act', ACTF(kT.ap[0:64, :, ki * 128:(ki + 1) * 128],
                                         P.ap(bt, 2)[0:64, 512:1024].rearrange("p (h q) -> p h q", h=4), AF.Copy),
                             reads=PK(bt, 2), writes=[kT.k(ki)])
                    if pendB[0] is not None:
                        pendB[0]()
                    pendB[0] = stageB
                    pump(1)
                pendB[0]()
                pendB[0] = None
                if not ctx:
                    for kv_, dst_ in ((0, onak_d), (1, onav_d)):
                        for qo in range(NQ):
                            S.dma('pool', DMA(dst_[si, j, qo * 128:(qo + 1) * 128, hg * 4:(hg + 1) * 4, :],
                                              kvout.ap[:, kv_, qo, :, :]), reads=[kvout.key], is_output=True)
                if ctx:
                    xraw = [A.alloc("xraw%d" % i, [1024], BF16) for i in range(2)]
                    xtab = [[A.alloc("xtab%d_%d" % (i, k_), [1024], BF16) for k_ in range(2)] for i in range(2)]

                    def hook(hh, hg=hg):
                        hd = hg * 4 + hh
                        xr = xraw[hh % 2]
                        S.dma('pool', DMA(xr.ap, rpbx_d[j, hd]), writes=[xr.key])
                        for k_ in range(2):
                            S.op('dve', TT(xtab[hh % 2][k_].ap, xr.ap, cmaskb[:, k_, :], ALU.add),
                                 reads=[xr.key, 'cmaskb'], writes=[xtab[hh % 2][k_].key])

                    def bias_fn(hh, qi, kidx):
                        if kidx < 2:
                            return None
                        jt = kidx - 2
                        w0 = 7 - 2 * (jt - qi)
                        k_ = 0 if qi in (0, 1, 6, 7) else 1
                        tb_ = xtab[hh % 2][k_]
                        return (tb_.ap[:, w0 * 64:(w0 + 2) * 64], [tb_.key])

                    if stop == 'na_prep':
                        return
                    attention(qT, kT, 64, vaug, att, NQ, na_blocks, bias_fn=bias_fn, head_hook=hook)
                else:
                    attention(qT, kT, 64, vaug, att, NQ, lambda qi_, NK=NK: list(range(NK)))
                if stop == 'na_att':
                    return
                for qi in range(NQ):
                    t = t0 + qi
                    m_ = mt[qi % 2]
                    S.op('dve', TT(m_.ap, abuf.ap[:, qi, :], att.ap[:, qi, :, :].rearrange("p h e -> p (h e)"), ALU.mult),
                         reads=[abuf.k(qi), att.k(qi)], writes=[m_.key])
                    bT = P.get()
                    for i in range(2):
                        S.op('pe', MM(P.ap(bT)[:, i * 128:(i + 1) * 128], m_.ap[:, i * 128:(i + 1) * 128], identb[:]),
                             reads=[m_.key, 'identb'], writes=PK(bT))
                    S.op('act', ACTF(mixT[:, 4 + hg * 2:4 + hg * 2 + 2, cols(t)],
                                     P.ap(bT)[:, 0:256].rearrange("p (c q) -> p c q", c=2), AF.Copy, scale=0.5),
                         reads=PK(bT), writes=[('mixT', t)])
                A.release(mk3)
                if stop == 'na_%d' % si:
                    return

    queue_mod(0)
    load_x()
    pump(24)
    for l in range(depth):
        if stop is None:
            if l % 2 == 0:
                pre_w['wa'] = load_w([evwin_d[l // 2][:, 0:416]])
            else:
                Wo = odwin_d[l // 2]
                pre_w['c0'] = load_w([Wo[:, i * 512:i * 512 + 128] for i in range(4)])
        phase_norm(l)
        if stop == 'norm':
            break
        if l + 1 < depth:
            queue_mod(l + 1)
        TAP('hsT%d' % l, hsT[:], [128, 8, NTOK], [('hsT', t) for t in range(NT)], BF16)
        last = (l == depth - 1)
        sb_ = (lambda blk: store_tiles(range(4 * blk, 4 * blk + 4))) if last else None
        if l % 2 == 0:
            even_layer(l)
            TAP('mixT%d' % l, mixT[:], [128, 8, NTOK], [('mixT', t) for t in range(NT)], BF16)
            pump(24)
            phase_out(l, evwout_d[l // 2], sb_)
        else:
            odd_layer(l)
            pump(24)
            phase_out(l, odwout_d[l // 2], sb_)
    if depth == 0:
        store_y()
    counts, nw = S.emit(nc, es)
    info = {'ops': counts, 'waits': nw, 'arena_peak': A.peak, 'taps': tap_list}
    es.close()
    return nc, info


def _constants():
    n = np.arange(1024)
    row = (n // GRID_W).astype(np.float32)
    col = (n % GRID_W).astype(np.float32)
    inv = (np.float32(ROPE_BASE) ** (-np.arange(8, dtype=np.float32) / np.float32(8))).astype(np.float32)
    ar = (row[:, None] * inv[None, :]).astype(np.float32)
    ac = (col[:, None] * inv[None, :]).astype(np.float32)
    C = np.concatenate([np.cos(ar), np.cos(ar), np.cos(ac), np.cos(ac)], axis=1).astype(np.float32)
    Sg = np.concatenate([-np.sin(ar), np.sin(ar), -np.sin(ac), np.sin(ac)], axis=1).astype(np.float32)
    rope_cs = np.stack([C, Sg]).astype(np.float32)
    s = np.arange(128)[:, None]
    t = np.arange(128)[None, :]
    same = (s // 64) == (t // 64)
    lmask = np.stack([np.where(same & (s <= t), 0.0, NEG), np.where(same & (s >= t), 0.0, NEG)]).astype(np.float32)
    kl = np.arange(2)[:, None, None, None]
    kc = np.arange(64)[None, :, None, None]
    w = np.arange(16)[None, None, :, None]
    qc = np.arange(64)[None, None, None, :]
    dr = np.broadcast_to(7 - w + kl, (2, 64, 16, 64))
    dc = np.broadcast_to(kc - qc, (2, 64, 16, 64))
    c0 = np.clip(qc - 8, 0, 48)
    col_ok = np.broadcast_to((kc >= c0) & (kc < c0 + 16), (2, 64, 16, 64))
    ok_full = col_ok & (np.abs(dr) <= 7)
    ok_int = ok_full & (dr >= -4) & (dr <= 3)
    cmask = np.stack([np.where(ok_full, 0.0, NEG), np.where(ok_int, 0.0, NEG)]).astype(np.float32).reshape(2, 128, 1024)
    idx_r = np.clip(dr + 7, 0, 14).reshape(128, 1024)
    idx_c = np.clip(dc + 15, 0, 30).reshape(128, 1024)
    return rope_cs, lmask, cmask, idx_r, idx_c


_CACHE = {}


def _get_program(depth, taps=(), stop=None):
    key = (depth, tuple(taps), stop)
    if key not in _CACHE:
        _CACHE[key] = build_nc(depth, taps, stop)
    return _CACHE[key]


def kernel(x_prompt, x_sample, c, cache_mla_ckv, cache_mla_kpe, state_mlstm_C, state_mlstm_n, state_mlstm_m,
           cache_na_k, cache_na_v, c_ctx, norm_w, ada_w, ada_b,
           ev_w_in, ev_q_a_norm, ev_kv_a_norm, ev_w_q_b, ev_w_kv_b, ev_q_norm, ev_k_norm, ev_b_i, ev_b_f,
           ev_h_norm, ev_w_out,
           od_w_in, od_conv_w, od_conv_b, od_q_norm, od_k_norm, od_rpb, od_w_out, _depth=4, _taps=(), _raw=False, _stop=None):
    f = lambda a: np.ascontiguousarray(np.asarray(a), dtype=np.float32)
    x_prompt, x_sample, c, c_ctx = f(x_prompt), f(x_sample), f(c), f(c_ctx)
    rope_cs, lmask, cmask, idx_r, idx_c = _constants()
    od_rpb = f(od_rpb)
    rpbx = np.ascontiguousarray(od_rpb[:, :, idx_r, idx_c])
    shared = {
        "norm_w": f(norm_w), "ada_w": f(ada_w), "ada_b": f(ada_b),
        "ev_w_in": f(ev_w_in), "ev_q_a_norm": f(ev_q_a_norm), "ev_kv_a_norm": f(ev_kv_a_norm),
        "ev_w_q_b": f(ev_w_q_b), "ev_w_kv_b": f(ev_w_kv_b), "ev_q_norm": f(ev_q_norm), "ev_k_norm": f(ev_k_norm),
        "ev_b_i": f(ev_b_i), "ev_b_f": f(ev_b_f), "ev_h_norm": f(ev_h_norm), "ev_w_out": f(ev_w_out),
        "od_w_in": f(od_w_in), "od_conv_w": f(od_conv_w), "od_conv_b": f(od_conv_b), "od_q_norm": f(od_q_norm),
        "od_k_norm": f(od_k_norm), "od_w_out": f(od_w_out), "rpbx": rpbx, "cmask": cmask, "rope_cs": rope_cs,
        "lmask": lmask,
    }
    cm_ckv, cm_kpe = f(cache_mla_ckv), f(cache_mla_kpe)
    sC, sn, sm = f(state_mlstm_C), f(state_mlstm_n), f(state_mlstm_m)
    nk, nv = f(cache_na_k), f(cache_na_v)
    in_maps = []
    for core in range(8):
        b = core // 4
        m = dict(shared)
        m["x"] = np.ascontiguousarray(np.concatenate([x_prompt[2 * core], x_prompt[2 * core + 1], x_sample[b]], axis=0))
        m["cond"] = np.ascontiguousarray(np.stack([c_ctx, c[b]]))
        m["ckv_c"] = cm_ckv[b]
        m["kpe_c"] = cm_kpe[b]
        m["stC"] = sC[b]
        m["stn"] = sn[b]
        m["stm"] = sm[b]
        m["nak_c"] = nk[b]
        m["nav_c"] = nv[b]
        in_maps.append(m)
    nc, info = _get_program(_depth, _taps, _stop)
    res = run_bass_kernel_spmd(nc, in_maps, core_ids=list(range(8)))
    R = res.results
    if _raw:
        return R, info
    yp = np.concatenate([R[i]["yp"].reshape(2, 256, D) for i in range(8)], axis=0)
    ys = np.stack([R[0]["ys"], R[4]["ys"]], axis=0)
    cat = lambda k: np.concatenate([R[i][k] for i in range(8)], axis=0)
    return (yp.astype(np.float32), ys.astype(np.float32), cat("o_ckv"), cat("o_kpe"), cat("o_C"), cat("o_n"),
            cat("o_m"), cat("o_nak"), cat("o_nav"))
```
